# Optimizing a Trainium2 kernel written in Bass

```python
import jax, jax.numpy as jnp
from jax import lax
import numpy as np

D_MODEL = 1024
BATCH = 32
SEQ = 256
DEPTH = 4
DEC_BATCH = 4
DEC_SEQ = 4096
PAST_LEN = 256

GRID_W = 64
N_EVEN = (DEPTH + 1) // 2
N_ODD = DEPTH // 2
EPS = 1e-6
NEG_INF = -1e30
A_HEADS = 8
A_KV_HEADS = 2
GQA_GROUP = A_HEADS // A_KV_HEADS
HEAD_DIM = 64
WINDOW = 128
BLOCK = 128
ROPE_BASE = 10000.0
B_HEADS = 4
B_DK = 64
B_DV = 128
GATE_RANK = 16
GATE_NORMALIZER = 16.0
GLA_CHUNK = 16
D_RNN = D_MODEL
RG_BLOCKS = 16
RG_BW = D_RNN // RG_BLOCKS
RG_C = 8.0
CONV_W = 4
D_FF = 2816
FFN_CONV_W = 3
A_Q = A_HEADS * HEAD_DIM
A_KV = A_KV_HEADS * HEAD_DIM
B_QK = B_HEADS * B_DK
B_V = B_HEADS * B_DV
EVEN_IN = A_Q + 2 * A_KV + 2 * B_QK + 2 * B_V + 2 * GATE_RANK
MIX_OUT = A_Q + B_V

kernel_name = 'hybrid_swa_gla_rglru_prefix_diffusion_step'


def _rmsnorm(x, g):
    xf = x.astype(jnp.float32)
    y = xf * lax.rsqrt(jnp.mean(xf * xf, axis=-1, keepdims=True) + EPS)
    return (y * g.astype(jnp.float32)).astype(x.dtype)


def _modulation(cond, w_ada, b_ada):
    m = jax.nn.silu(cond) @ w_ada + b_ada
    return jnp.split(m[:, None, :], 6, axis=-1)


def _modulate(x, g, shift, scale):
    return _rmsnorm(x, g) * (1.0 + scale) + shift


def _dwconv(x, w, b):
    width = w.shape[0]
    left = width // 2
    t_len = x.shape[1]
    xp = jnp.pad(x, ((0, 0), (left, width - 1 - left), (0, 0)))
    out = xp[:, 0:t_len] * w[0]
    for k in range(1, width):
        out = out + xp[:, k:k + t_len] * w[k]
    return out + b


def _axial_rope_tables(t_len, dtype):
    rows = t_len // GRID_W
    row = jnp.repeat(jnp.arange(rows, dtype=jnp.float32), GRID_W)
    col = jnp.tile(jnp.arange(GRID_W, dtype=jnp.float32), rows)
    n_freq = HEAD_DIM // 4
    inv_freq = jnp.power(ROPE_BASE, -jnp.arange(n_freq, dtype=jnp.float32) / n_freq)
    ang_r = row[:, None, None] * inv_freq
    ang_c = col[:, None, None] * inv_freq
    return (jnp.cos(ang_r).astype(dtype), jnp.sin(ang_r).astype(dtype),
            jnp.cos(ang_c).astype(dtype), jnp.sin(ang_c).astype(dtype))


def _rot_half(t, cos, sin):
    n = t.shape[-1] // 2
    t1, t2 = t[..., :n], t[..., n:]
    return jnp.concatenate([t1 * cos - t2 * sin, t1 * sin + t2 * cos], axis=-1)


def _axial_rope(x, cos_r, sin_r, cos_c, sin_c):
    half = HEAD_DIM // 2
    return jnp.concatenate([_rot_half(x[..., :half], cos_r, sin_r),
                            _rot_half(x[..., half:], cos_c, sin_c)], axis=-1)


def _attn_context(q, k, v, sink):
    bsz, s_len = q.shape[0], q.shape[1]
    nq = s_len // BLOCK
    qb = (q * HEAD_DIM ** -0.5).reshape(bsz, nq, BLOCK, A_KV_HEADS, GQA_GROUP, HEAD_DIM).transpose(1, 0, 2, 3, 4, 5)
    sink_l = sink.astype(jnp.float32).reshape(A_KV_HEADS, GQA_GROUP)[None, :, :, None, None]

    def one_block(qblk):
        s = jnp.einsum('bqkgd,bskd->bkgqs', qblk, k, preferred_element_type=jnp.float32)
        logits = jnp.concatenate([s, jnp.broadcast_to(sink_l, s.shape[:-1] + (1,))], axis=-1)
        p = jax.nn.softmax(logits, axis=-1)[..., :-1].astype(v.dtype)
        return jnp.einsum('bkgqs,bskd->bqkgd', p, v)

    o = lax.map(one_block, qb)
    return o.transpose(1, 0, 2, 3, 4, 5).reshape(bsz, s_len, A_Q)


def _attn_latent(q, k, v, k_ctx, v_ctx, sink):
    bsz, t_len = q.shape[0], q.shape[1]
    nb = t_len // BLOCK
    qb = (q * HEAD_DIM ** -0.5).reshape(bsz, nb, BLOCK, A_KV_HEADS, GQA_GROUP, HEAD_DIM).transpose(1, 0, 2, 3, 4, 5)

    def neighbours(t):
        tb = jnp.pad(t, ((0, 0), (BLOCK, BLOCK), (0, 0), (0, 0))).reshape(bsz, nb + 2, BLOCK, A_KV_HEADS, HEAD_DIM)
        win = jnp.concatenate([tb[:, :-2], tb[:, 1:-1], tb[:, 2:]], axis=2)
        return win.transpose(1, 0, 2, 3, 4)

    kw, vw = neighbours(k), neighbours(v)
    qi = jnp.arange(BLOCK)[:, None]
    kj = jnp.arange(3 * BLOCK)[None, :]
    rel = kj - BLOCK - qi
    kpos = (jnp.arange(nb)[:, None, None] - 1) * BLOCK + kj[None]
    valid = (jnp.abs(rel) <= WINDOW)[None] & (kpos >= 0) & (kpos < t_len)
    sink_l = sink.astype(jnp.float32).reshape(A_KV_HEADS, GQA_GROUP)[None, :, :, None, None]

    def one_block(args):
        qblk, kblk, vblk, ok = args
        s_w = jnp.einsum('bqkgd,bskd->bkgqs', qblk, kblk, preferred_element_type=jnp.float32)
        s_w = jnp.where(ok[None, None, None], s_w, NEG_INF)
        s_c = jnp.einsum('bqkgd,bpkd->bkgqp', qblk, k_ctx, preferred_element_type=jnp.float32)
        snk = jnp.broadcast_to(sink_l, s_w.shape[:-1] + (1,))
        p = jax.nn.softmax(jnp.concatenate([s_w, s_c, snk], axis=-1), axis=-1).astype(v.dtype)
        return (jnp.einsum('bkgqs,bskd->bqkgd', p[..., :3 * BLOCK], vblk)
                + jnp.einsum('bkgqp,bpkd->bqkgd', p[..., 3 * BLOCK:-1], v_ctx))

    o = lax.map(one_block, (qb, kw, vw, valid))
    return o.transpose(1, 0, 2, 3, 4, 5).reshape(bsz, t_len, A_Q)


def _gla_chunked(q, k, v, log_a, s0):
    bsz, t_len, nh, dk = q.shape
    dv = v.shape[-1]
    n = t_len // GLA_CHUNK

    def chunks(t):
        return t.reshape(bsz, n, GLA_CHUNK, nh, t.shape[-1]).transpose(1, 0, 3, 2, 4).astype(jnp.float32)

    qc, kc, vc, lc = chunks(q), chunks(k), chunks(v), chunks(log_a)
    b = jnp.cumsum(lc, axis=3)
    causal = jnp.tril(jnp.ones((GLA_CHUNK, GLA_CHUNK), dtype=bool))[:, :, None]
    rel = b[..., :, None, :] - b[..., None, :, :]
    decay = jnp.exp(jnp.where(causal, rel, NEG_INF))
    scores = jnp.einsum('nbhid,nbhijd,nbhjd->nbhij', qc, decay, kc)
    o_intra = jnp.einsum('nbhij,nbhjv->nbhiv', scores, vc)
    b_last = b[..., -1:, :]
    q_in = qc * jnp.exp(b)
    k_in = kc * jnp.exp(b_last - b)
    a_chunk = jnp.exp(b_last[..., 0, :])

    def step(s, xs):
        qi, ki, vi, ai = xs
        o = jnp.einsum('bhcd,bhdv->bhcv', qi, s)
        s = ai[..., None] * s + jnp.einsum('bhcd,bhcv->bhdv', ki, vi)
        return s, o

    s_fin, o_inter = lax.scan(step, s0.astype(jnp.float32), (q_in, k_in, vc, a_chunk))
    o = (o_intra + o_inter).transpose(1, 0, 3, 2, 4).reshape(bsz, t_len, nh, dv)
    return o.astype(v.dtype), s_fin.astype(v.dtype)


def _gla_bidir(q, k, v, la_f, la_b, s0_f, s0_b):
    o_f, s_f = _gla_chunked(q, k, v, la_f, s0_f)
    o_b, s_b = _gla_chunked(q[:, ::-1], k[:, ::-1], v[:, ::-1], la_b[:, ::-1], s0_b)
    return o_f + o_b[:, ::-1], s_f, s_b


def _even_project(h, w_in, w_gate_f, b_gate_f, w_gate_b, b_gate_b):
    bsz, t_len = h.shape[0], h.shape[1]
    sizes = (A_Q, A_KV, A_KV, B_QK, B_QK, B_V, B_V, GATE_RANK, GATE_RANK)
    cuts = [sum(sizes[:n]) for n in range(1, len(sizes))]
    q_a, k_a, v_a, q_b, k_b, v_b, g_b, r_f, r_b = jnp.split(h @ w_in, cuts, axis=-1)
    q_a = q_a.reshape(bsz, t_len, A_HEADS, HEAD_DIM)
    k_a = k_a.reshape(bsz, t_len, A_KV_HEADS, HEAD_DIM)
    v_a = v_a.reshape(bsz, t_len, A_KV_HEADS, HEAD_DIM)
    q_b = q_b.reshape(bsz, t_len, B_HEADS, B_DK) * (B_DK ** -0.5)
    k_b = k_b.reshape(bsz, t_len, B_HEADS, B_DK)
    v_b = v_b.reshape(bsz, t_len, B_HEADS, B_DV)
    la_f = jax.nn.log_sigmoid((r_f @ w_gate_f + b_gate_f).astype(jnp.float32)).reshape(bsz, t_len, B_HEADS, B_DK) / GATE_NORMALIZER
    la_b = jax.nn.log_sigmoid((r_b @ w_gate_b + b_gate_b).astype(jnp.float32)).reshape(bsz, t_len, B_HEADS, B_DK) / GATE_NORMALIZER
    return q_a, k_a, v_a, q_b, k_b, v_b, g_b, la_f, la_b


def _even_output(o_a, o_b, g_b, gla_norm, w_out):
    bsz, t_len = o_a.shape[0], o_a.shape[1]
    o_b = _rmsnorm(o_b, gla_norm).reshape(bsz, t_len, B_V) * jax.nn.silu(g_b)
    return jnp.concatenate([o_a, o_b], axis=-1) @ w_out


def _even_context(h, w_in, sink, w_gate_f, b_gate_f, w_gate_b, b_gate_b, gla_norm, w_out):
    q_a, k_a, v_a, q_b, k_b, v_b, g_b, la_f, la_b = _even_project(h, w_in, w_gate_f, b_gate_f, w_gate_b, b_gate_b)
    o_a = _attn_context(q_a, k_a, v_a, sink)
    s0 = jnp.zeros((h.shape[0], B_HEADS, B_DK, B_DV), jnp.float32)
    o_b, s_f, s_b = _gla_bidir(q_b, k_b, v_b, la_f, la_b, s0, s0)
    return _even_output(o_a, o_b, g_b, gla_norm, w_out), k_a, v_a, s_f, s_b


def _even_latent(h, k_ctx, v_ctx, s0_f, s0_b, rope, w_in, sink, w_gate_f, b_gate_f, w_gate_b, b_gate_b, gla_norm, w_out):
    q_a, k_a, v_a, q_b, k_b, v_b, g_b, la_f, la_b = _even_project(h, w_in, w_gate_f, b_gate_f, w_gate_b, b_gate_b)
    q_a = _axial_rope(q_a, rope[0], rope[1], rope[2], rope[3])
    k_a = _axial_rope(k_a, rope[0], rope[1], rope[2], rope[3])
    o_a = _attn_latent(q_a, k_a, v_a, k_ctx, v_ctx, sink)
    o_b, _, _ = _gla_bidir(q_b, k_b, v_b, la_f, la_b, s0_f, s0_b)
    return _even_output(o_a, o_b, g_b, gla_norm, w_out)


def _block_diag(x, w, b):
    xb = x.reshape(x.shape[:-1] + (RG_BLOCKS, RG_BW))
    return jnp.einsum('btnd,nde->btne', xb, w).reshape(x.shape) + b


def _combine(left, right):
    a_l, u_l = left
    a_r, u_r = right
    return a_l * a_r, a_r * u_l + u_r


def _rglru(x, w_a, b_a, w_i, b_i, lam, h0):
    xf = x.astype(jnp.float32)
    r = jax.nn.sigmoid(_block_diag(xf, w_a, b_a))
    i = jax.nn.sigmoid(_block_diag(xf, w_i, b_i))
    log_a = RG_C * r * jax.nn.log_sigmoid(lam.astype(jnp.float32))
    a = jnp.exp(log_a)
    u = jnp.sqrt(-jnp.expm1(2.0 * log_a)) * (i * xf)
    a_cum, u_cum = lax.associative_scan(_combine, (a, u), axis=1)
    h = a_cum * h0.astype(jnp.float32)[:, None, :] + u_cum
    return h.astype(x.dtype)


def _odd_mixer(h, h0_f, h0_b, w_in, conv_w, conv_b, w_a, b_a, w_i, b_i, lam, w_out):
    y, xr = jnp.split(h @ w_in, 2, axis=-1)
    xr = _dwconv(xr, conv_w, conv_b)
    h_f = _rglru(xr, w_a[0], b_a[0], w_i[0], b_i[0], lam[0], h0_f)
    h_b = _rglru(xr[:, ::-1], w_a[1], b_a[1], w_i[1], b_i[1], lam[1], h0_b)[:, ::-1]
    out = (jax.nn.gelu(y) * (h_f + h_b)) @ w_out
    return out, h_f[:, -1], h_b[:, 0]


def _conv_ffn(h, w_up, conv_w, conv_b, w_down):
    u = _dwconv(h @ w_up, conv_w, conv_b)
    gate, val = jnp.split(u, 2, axis=-1)
    return (jax.nn.silu(gate) * val) @ w_down


def setup_inputs(seed: int = 0) -> dict:
    key = jax.random.key(seed)
    ks = jax.random.split(key, 48)
    f32 = jnp.float32

    def nrm(i, shape, scale):
        return jax.random.normal(ks[i], shape, f32) * scale

    u = jax.random.uniform(ks[40], (N_ODD, 2, D_RNN), f32, 0.9, 0.999)
    s = u ** (1.0 / RG_C)
    lam = jnp.log(s) - jnp.log1p(-s)
    return {
        'x_prompt': nrm(0, (BATCH, SEQ, D_MODEL), 1.0),
        'x_sample': nrm(1, (DEC_BATCH, DEC_SEQ, D_MODEL), 1.0),
        'cache_attn_k': nrm(2, (DEC_BATCH, N_EVEN, PAST_LEN, A_KV_HEADS, HEAD_DIM), 1.0),
        'cache_attn_v': nrm(3, (DEC_BATCH, N_EVEN, PAST_LEN, A_KV_HEADS, HEAD_DIM), 1.0),
        'state_gla': nrm(4, (DEC_BATCH, N_EVEN, 2, B_HEADS, B_DK, B_DV), 1.0),
        'state_rglru': nrm(5, (DEC_BATCH, N_ODD, 2, D_RNN), 0.5),
        'c': nrm(6, (DEC_BATCH, D_MODEL), 1.0),
        'c_ctx': nrm(7, (D_MODEL,), 1.0),
        'norm_mix': 1.0 + nrm(8, (DEPTH, D_MODEL), 0.02),
        'norm_ffn': 1.0 + nrm(9, (DEPTH, D_MODEL), 0.02),
        'w_ada': nrm(10, (DEPTH, D_MODEL, 6 * D_MODEL), 0.2 * D_MODEL ** -0.5),
        'b_ada': nrm(11, (DEPTH, 6 * D_MODEL), 0.01),
        'ev_w_in': nrm(12, (N_EVEN, D_MODEL, EVEN_IN), D_MODEL ** -0.5),
        'ev_sink': nrm(13, (N_EVEN, A_HEADS), 1.0),
        'ev_w_gate_f': nrm(14, (N_EVEN, GATE_RANK, B_QK), GATE_RANK ** -0.5),
        'ev_b_gate_f': nrm(15, (N_EVEN, B_QK), 0.1),
        'ev_w_gate_b': nrm(16, (N_EVEN, GATE_RANK, B_QK), GATE_RANK ** -0.5),
        'ev_b_gate_b': nrm(17, (N_EVEN, B_QK), 0.1),
        'ev_gla_norm': 1.0 + nrm(18, (N_EVEN, B_DV), 0.02),
        'ev_w_out': nrm(19, (N_EVEN, MIX_OUT, D_MODEL), MIX_OUT ** -0.5),
        'od_w_in': nrm(20, (N_ODD, D_MODEL, 2 * D_RNN), D_MODEL ** -0.5),
        'od_conv_w': nrm(21, (N_ODD, CONV_W, D_RNN), CONV_W ** -0.5),
        'od_conv_b': nrm(22, (N_ODD, D_RNN), 0.01),
        'od_w_a': nrm(23, (N_ODD, 2, RG_BLOCKS, RG_BW, RG_BW), RG_BW ** -0.5),
        'od_b_a': nrm(24, (N_ODD, 2, D_RNN), 0.01),
        'od_w_i': nrm(25, (N_ODD, 2, RG_BLOCKS, RG_BW, RG_BW), RG_BW ** -0.5),
        'od_b_i': nrm(26, (N_ODD, 2, D_RNN), 0.01),
        'od_lambda': lam,
        'od_w_out': nrm(27, (N_ODD, D_RNN, D_MODEL), D_RNN ** -0.5),
        'ffn_w_up': nrm(28, (DEPTH, D_MODEL, 2 * D_FF), D_MODEL ** -0.5),
        'ffn_conv_w': nrm(29, (DEPTH, FFN_CONV_W, 2 * D_FF), FFN_CONV_W ** -0.5),
        'ffn_conv_b': nrm(30, (DEPTH, 2 * D_FF), 0.01),
        'ffn_w_down': nrm(31, (DEPTH, D_FF, D_MODEL), D_FF ** -0.5),
        'final_norm': 1.0 + nrm(32, (D_MODEL,), 0.02),
    }


def reference(x_prompt, x_sample, cache_attn_k, cache_attn_v, state_gla, state_rglru, c, c_ctx,
              norm_mix, norm_ffn, w_ada, b_ada,
              ev_w_in, ev_sink, ev_w_gate_f, ev_b_gate_f, ev_w_gate_b, ev_b_gate_b, ev_gla_norm, ev_w_out,
              od_w_in, od_conv_w, od_conv_b, od_w_a, od_b_a, od_w_i, od_b_i, od_lambda, od_w_out,
              ffn_w_up, ffn_conv_w, ffn_conv_b, ffn_w_down, final_norm):
    rope = _axial_rope_tables(x_sample.shape[1], x_sample.dtype)
    xp, xs = x_prompt, x_sample
    new_k, new_v, new_gla, new_rg = [], [], [], []
    for layer in range(DEPTH):
        j = layer // 2
        sh_mp, sc_mp, g_mp, sh_fp, sc_fp, g_fp = _modulation(c_ctx[None, :], w_ada[layer], b_ada[layer])
        sh_ms, sc_ms, g_ms, sh_fs, sc_fs, g_fs = _modulation(c, w_ada[layer], b_ada[layer])
        hp = _modulate(xp, norm_mix[layer], sh_mp, sc_mp)
        hs = _modulate(xs, norm_mix[layer], sh_ms, sc_ms)
        if layer % 2 == 0:
            ew = (ev_w_in[j], ev_sink[j], ev_w_gate_f[j], ev_b_gate_f[j], ev_w_gate_b[j], ev_b_gate_b[j],
                  ev_gla_norm[j], ev_w_out[j])
            out_p, k_c, v_c, s_f, s_b = _even_context(hp, *ew)
            out_s = _even_latent(hs, cache_attn_k[:, j], cache_attn_v[:, j], state_gla[:, j, 0], state_gla[:, j, 1],
                                 rope, *ew)
            new_k.append(k_c)
            new_v.append(v_c)
            new_gla.append(jnp.stack([s_f, s_b], axis=1))
        else:
            ow = (od_w_in[j], od_conv_w[j], od_conv_b[j], od_w_a[j], od_b_a[j], od_w_i[j], od_b_i[j],
                  od_lambda[j], od_w_out[j])
            h0 = jnp.zeros((xp.shape[0], D_RNN), xp.dtype)
            out_p, r_f, r_b = _odd_mixer(hp, h0, h0, *ow)
            out_s, _, _ = _odd_mixer(hs, state_rglru[:, j, 0], state_rglru[:, j, 1], *ow)
            new_rg.append(jnp.stack([r_f, r_b], axis=1))
        xp = xp + g_mp * out_p
        xs = xs + g_ms * out_s
        hp = _modulate(xp, norm_ffn[layer], sh_fp, sc_fp)
        hs = _modulate(xs, norm_ffn[layer], sh_fs, sc_fs)
        xp = xp + g_fp * _conv_ffn(hp, ffn_w_up[layer], ffn_conv_w[layer], ffn_conv_b[layer], ffn_w_down[layer])
        xs = xs + g_fs * _conv_ffn(hs, ffn_w_up[layer], ffn_conv_w[layer], ffn_conv_b[layer], ffn_w_down[layer])
    y_prompt = _rmsnorm(xp, final_norm)
    y_sample = _rmsnorm(xs, final_norm)
    new_attn_k = jnp.stack(new_k, axis=1)
    new_attn_v = jnp.stack(new_v, axis=1)
    new_state_gla = jnp.stack(new_gla, axis=1)
    new_state_rglru = jnp.stack(new_rg, axis=1)
    return (y_prompt, y_sample, new_attn_k, new_attn_v, new_state_gla, new_state_rglru)
```

```python
import contextlib
import numpy as np
import concourse.bass as bass
import concourse.mybir as mybir
from concourse.bass_utils import run_bass_kernel_spmd

F32 = mybir.dt.float32
BF16 = mybir.dt.bfloat16
AF = mybir.ActivationFunctionType
ALU = mybir.AluOpType

D = 1024
KC = 8
BATCH, SEQ = 32, 256
DEC_BATCH, DEC_SEQ = 4, 4096
DEPTH = 4
N_CORES = 8
NPC = BATCH // N_CORES
TS = DEC_SEQ
TP = NPC * SEQ
TT = TS + TP
D_FF = 2816
NJ = D_FF // 128
EPS = 1e-6

SAME_SYNC = True
RELAX = [10 ** 9]
PE_RELAX = [False]
DMA_K = 8


class Tok:
    __slots__ = ("sem", "semid", "val", "key")

    def __init__(self, sem, semid, val, key):
        self.sem, self.semid, self.val, self.key = sem, semid, val, key


class Res:
    __slots__ = ("name", "w", "r", "ex")

    def __init__(self, name="", ex=False):
        self.name = name
        self.w = None
        self.r = []
        self.ex = ex


def PRes():
    return Res("psum", ex=True)


class RG(list):
    pass


def _flat(lst):
    out = []
    for x in lst:
        if isinstance(x, RG):
            out.extend(x)
        else:
            out.append(x)
    return out


class Prog:
    ENG = ("pe", "dve", "act", "pool", "sp")

    def __init__(self, nc):
        self.nc = nc
        self.es = contextlib.ExitStack()
        self.eng = {"pe": nc.tensor, "dve": nc.vector, "act": nc.scalar, "pool": nc.gpsimd, "sp": nc.sync}
        self.sem = {}
        self.seq = {}
        self.waited = {}
        self._nid = 0
        for e in self.ENG:
            self.sem[e] = (self.es.enter_context(nc.semaphore("s_" + e)), self._sid())
            self.seq[e] = 0
        self.dsem = {}
        self.dman = {}
        for q in ("sp", "act", "pool"):
            self.dsem[q] = [(self.es.enter_context(nc.semaphore(f"d_{q}{i}")), self._sid()) for i in range(DMA_K)]
            self.dman[q] = 0
        self.stage_es = None
        self.n_inst = 0

    def _sid(self):
        self._nid += 1
        return self._nid

    def sb(self, name, shape, dtype, glob=False):
        es = self.es if (glob or self.stage_es is None) else self.stage_es
        self._nid += 1
        return es.enter_context(self.nc.sbuf_tensor(f"{name}_{self._nid}", list(shape), dtype))

    def ps(self, name, shape, dtype=F32):
        es = self.es if self.stage_es is None else self.stage_es
        self._nid += 1
        return es.enter_context(self.nc.psum_tensor(f"{name}_{self._nid}", list(shape), dtype))

    @contextlib.contextmanager
    def stage(self):
        assert self.stage_es is None
        self.stage_es = contextlib.ExitStack()
        try:
            yield
        finally:
            self.barrier()
            self.stage_es.close()
            self.stage_es = None

    def _deps(self, reads, writes):
        toks = []
        for r in reads:
            if r.w is not None:
                toks.append(r.w)
            if r.ex:
                toks.extend(r.r)
        for w in writes:
            if w.w is not None:
                toks.append(w.w)
            toks.extend(w.r)
        return toks

    def _need(self, e, toks, strict_same=False):
        need = {}
        for t in toks:
            if t.key == e and not strict_same:
                if (not SAME_SYNC) or (e == "pe" and PE_RELAX[0]) or (e != "pe" and (self.seq[e] - t.val) >= RELAX[0]):
                    continue
            if self.waited.get((e, t.semid), 0) >= t.val:
                continue
            if t.semid not in need or need[t.semid].val < t.val:
                need[t.semid] = t
        return list(need.values())

    def _emit_waits(self, e, need):
        for t in need:
            self.eng[e].wait_ge(t.sem, t.val)
            self.waited[(e, t.semid)] = t.val
            self.n_inst += 1

    def _record(self, tok, reads, writes):
        for r in reads:
            if r.ex:
                r.w = tok
                r.r = []
            else:
                r.r = [x for x in r.r if x.key != tok.key] + [tok]
        for w in writes:
            w.w = tok
            w.r = []

    def op(self, e, fn, reads=(), writes=()):
        reads, writes = _flat(reads), _flat(writes)
        need = self._need(e, self._deps(reads, writes))
        attach = need.pop() if need else None
        self._emit_waits(e, need)
        first_holder = []
        ins = fn(self.eng[e], first_holder)
        tgt = first_holder[0] if first_holder else ins
        if attach is not None:
            tgt._wait_ge(attach.sem, attach.val)
            self.waited[(e, attach.semid)] = attach.val
        self.seq[e] += 1
        sem, semid = self.sem[e]
        ins.then_inc(sem, 1)
        self.n_inst += 1
        tok = Tok(sem, semid, self.seq[e], e)
        self._record(tok, reads, writes)
        return tok

    def dma(self, q, out, in_, reads=(), writes=(), **kw):
        e = q
        reads, writes = _flat(reads), _flat(writes)
        need = self._need(e, self._deps(reads, writes), strict_same=True)
        n = self.dman[q]
        slot, rnd = n % DMA_K, n // DMA_K
        sem, semid = self.dsem[q][slot]
        if rnd > 0 and self.waited.get((e, semid), 0) < 16 * rnd:
            need.append(Tok(sem, semid, 16 * rnd, ("q", q, slot)))
        self._emit_waits(e, need)
        self.eng[e].dma_start(out=out, in_=in_, **kw).then_inc(sem, 16)
        self.n_inst += 1
        self.dman[q] = n + 1
        tok = Tok(sem, semid, 16 * (rnd + 1), ("q", q, slot))
        self._record(tok, reads, writes)
        return tok

    def all_tokens(self):
        toks = []
        for e in self.ENG:
            if self.seq[e] > 0:
                sem, semid = self.sem[e]
                toks.append(Tok(sem, semid, self.seq[e], e))
        for q in self.dsem:
            n = self.dman[q]
            for slot in range(DMA_K):
                cnt = (n - slot + DMA_K - 1) // DMA_K if n > slot else 0
                if cnt > 0:
                    sem, semid = self.dsem[q][slot]
                    toks.append(Tok(sem, semid, 16 * cnt, ("q", q, slot)))
        return toks

    def barrier(self, engines=None):
        toks = self.all_tokens()
        for e in (engines or self.ENG):
            need = []
            for t in toks:
                if t.key == e:
                    continue
                if self.waited.get((e, t.semid), 0) >= t.val:
                    continue
                need.append(t)
            self._emit_waits(e, need)


def I(method, *a, **k):
    def fn(eng, fh):
        return getattr(eng, method)(*a, **k)
    return fn


def MM(mms):
    def fn(eng, fh):
        ins = None
        for i, m in enumerate(mms):
            ins = eng.matmul(m["out"], lhsT=m["lhsT"], rhs=m["rhs"], start=m["start"], stop=m["stop"])
            if i == 0:
                fh.append(ins)
        return ins
    return fn


def TR(trs, ident):
    def fn(eng, fh):
        ins = None
        for i, (o, a) in enumerate(trs):
            ins = eng.transpose(out=o, in_=a, identity=ident)
            if i == 0:
                fh.append(ins)
        return ins
    return fn


def fm(v):
    v = np.asarray(v, np.float32)
    n = v.shape[-1] // 128
    a = v.reshape(v.shape[:-1] + (n, 128))
    return np.moveaxis(a, -1, 0)


class VecPack:
    def __init__(self):
        self.items = []
        self.off = {}
        self.n = 0

    def add(self, name, arr):
        arr = np.ascontiguousarray(arr, np.float32).reshape(128, -1)
        self.off[name] = (self.n, arr.shape[1])
        self.items.append(arr)
        self.n += arr.shape[1]

    def build(self):
        return np.ascontiguousarray(np.concatenate(self.items, axis=1))


VEC_SPEC = [
    ("cvec", 16), ("b_ada", 4 * 48), ("norm_mix", 32), ("norm_ffn", 32), ("final_norm", 8),
    ("ffn_conv_w", 4 * 3 * 44), ("ffn_conv_b", 4 * 44),
    ("od_conv_w", 2 * 4 * 8), ("od_conv_b", 16), ("od_b_a", 32), ("od_b_i", 32), ("od_lambda", 32),
    ("st_rg", 32), ("gla_norm", 2), ("b_gate", 2 * 2 * 2), ("sink", 16),
]
VEC_OFF = {}
_o = 0
for _n, _c in VEC_SPEC:
    VEC_OFF[_n] = (_o, _c)
    _o += _c
NV = _o

C_IDENT = 0
C_MF = 128
C_MB = 256
C_RST = 384
C_AMASK = 896
C_ONES = 1280
C_PERM = 1408
NCONST = 1536


def make_consts():
    c = np.zeros((128, NCONST), np.float32)
    c[:, C_IDENT:C_IDENT + 128] = np.eye(128, dtype=np.float32)
    j = np.arange(128)[:, None]
    i = np.arange(128)[None, :]
    c[:, C_MF:C_MF + 128] = (j <= i)
    c[:, C_MB:C_MB + 128] = (j >= i)
    t = np.arange(512)
    c[:, C_RST:C_RST + 512] = (t % 128 != 0)[None, :]
    qi = np.arange(128)[:, None]
    kj = np.arange(384)[None, :]
    rel = kj - 128 - qi
    c[:, C_AMASK:C_AMASK + 384] = np.where(np.abs(rel) <= 128, 0.0, -1e30)
    c[:, C_ONES:C_ONES + 128] = 1.0 / 1024.0
    for m in range(128):
        d = m % 64
        if (d % 32) < 16:
            c[m + 16, C_PERM + m] = -1.0
        else:
            c[m - 16, C_PERM + m] = 1.0
    return c


def make_rope():
    t = np.arange(DEC_SEQ)
    row = (t // 64).astype(np.float32)
    col = (t % 64).astype(np.float32)
    inv = np.power(np.float32(10000.0), -np.arange(16, dtype=np.float32) / np.float32(16.0)).astype(np.float32)
    ang_r = (row[None, :] * inv[:, None]).astype(np.float32)
    ang_c = (col[None, :] * inv[:, None]).astype(np.float32)
    tab = np.zeros((128, 2, DEC_SEQ), np.float32)
    for p in range(128):
        d = p % 64
        a = ang_r if d < 32 else ang_c
        f = d % 16
        tab[p, 0] = np.cos(a[f])
        tab[p, 1] = np.sin(a[f])
    return tab


class Builder:
    def __init__(self, layers=DEPTH, do_mixer=True, do_ffn=True, do_even=True, dbg_seqs=None, dbg_cut=99):
        self.dbg_seqs = dbg_seqs
        self.dbg_cut = dbg_cut
        self.layers = layers
        self.do_mixer = do_mixer
        self.do_ffn = do_ffn
        self.do_even = do_even
        nc = bass.Bass("TRN2", target_bir_lowering=False)
        self.nc = nc
        self.P = Prog(nc)
        dt = lambda name, shape, kind=None, dtype=F32: (
            nc.dram_tensor(name, list(shape), dtype, kind=kind) if kind else nc.dram_tensor(name, list(shape), dtype))
        self.xin = dt("xin", [TT, D], "ExternalInput").ap()
        self.vec_d = dt("vec", [128, NV], "ExternalInput").ap()
        self.const_d = dt("consts", [128, NCONST], "ExternalInput").ap()
        self.w_ada = dt("w_ada", [DEPTH, D, 6 * D], "ExternalInput").ap()
        self.ffn_up = dt("ffn_up_t", [DEPTH, NJ, 128, KC * 256], "ExternalInput").ap()
        self.ffn_dn = dt("ffn_dn_t", [DEPTH, KC, 128, NJ * 128], "ExternalInput").ap()
        self.od_w_in = dt("od_w_in", [2, D, 2 * D], "ExternalInput").ap()
        self.od_w_out = dt("od_w_out", [2, D, D], "ExternalInput").ap()
        self.od_bd = dt("od_bd", [2, 2, 2, 128, KC * 128], "ExternalInput").ap()
        self.w_even = dt("w_even", [2, 128, KC * 2624], "ExternalInput").ap()
        self.ev_w_out = dt("ev_w_out", [2, D, D], "ExternalInput").ap()
        self.wg = dt("wg", [2, 64, 512], "ExternalInput").ap()
        self.cache_k = dt("cache_k", [2, 256, 128], "ExternalInput").ap()
        self.cache_v = dt("cache_v", [2, 256, 128], "ExternalInput").ap()
        self.state_gla = dt("state_gla", [2, 2, 4, 64, 128], "ExternalInput").ap()
        self.rope = dt("rope", [128, 2, DEC_SEQ], "ExternalInput").ap()
        self.y = dt("y", [TT, D], "ExternalOutput").ap()
        self.nk = dt("nk", [NPC, 2, SEQ, 128], "ExternalOutput").ap()
        self.nv = dt("nv", [NPC, 2, SEQ, 128], "ExternalOutput").ap()
        self.ngla = dt("ngla", [NPC, 2, 2, 4, 64, 128], "ExternalOutput").ap()
        self.nrg = dt("nrg", [NPC * 2 * 2 * KC, 128], "ExternalOutput").ap()
        self.GY = dt("GY", [D, TT], dtype=BF16).ap().rearrange("(k p) t -> p k t", p=128)
        self.XC = dt("XC", [D, TT]).ap().rearrange("(k p) t -> p k t", p=128)
        self.HF = dt("HF", [D, TT]).ap().rearrange("(k p) t -> p k t", p=128)
        self.XT = dt("XT", [D, TT]).ap().rearrange("(k p) t -> p k t", p=128)
        self.build()

    def setup_globals(self):
        P = self.P
        self.VEC = P.sb("VEC", [128, NV], F32, glob=True)
        self.CON = P.sb("CON", [128, NCONST], F32, glob=True)
        self.CONB = P.sb("CONB", [128, 384], BF16, glob=True)
        self.MOD = P.sb("MOD", [128, DEPTH, 48, 2], F32, glob=True)
        self.AMF = P.sb("AMF", [128, DEPTH, 2, 2, 8], F32, glob=True)
        self.r_vec = Res("VEC")
        self.r_con = Res("CON")
        self.r_conb = Res("CONB")
        self.r_mod = Res("MOD")
        self.r_amf = Res("AMF")
        P.dma("sp", self.VEC[:], self.vec_d[:, :], writes=[self.r_vec])
        P.dma("sp", self.CON[:], self.const_d[:, :], writes=[self.r_con])
        P.op("dve", I("tensor_copy", out=self.CONB[:], in_=self.CON[:, 0:384]), reads=[self.r_con], writes=[self.r_conb])

    def vec(self, name, *idx_shape):
        o, n = VEC_OFF[name]
        return self.VEC[:, o:o + n]

    def stage0(self):
        P = self.P
        with P.stage():
            XA = [P.sb(f"s0_xa{i}", [128, D], F32) for i in range(3)]
            XB = [P.sb(f"s0_xb{i}", [128, KC, 128], F32) for i in range(2)]
            rXA = [Res() for _ in range(3)]
            rXB = [Res() for _ in range(2)]
            TPp = [P.ps(f"s0_tp{i}", [128, 512], F32) for i in range(4)]
            rTP = [PRes() for _ in range(4)]
            ident = self.CON[:, C_IDENT:C_IDENT + 128]
            SC = P.sb("s0_sc", [128, KC, 2], BF16)
            rSC = Res()
            WA = [P.sb(f"s0_wa{i}", [128, KC, 1024], BF16) for i in range(2)]
            rWA = [Res() for _ in range(2)]
            PM = P.ps("s0_pm", [128, 512], F32)
            rPM = PRes()
            o, n = VEC_OFF["cvec"]
            P.op("act", I("activation", out=SC[:].rearrange("p k c -> p (k c)"), in_=self.VEC[:, o:o + n], func=AF.Silu),
                 reads=[self.r_vec], writes=[rSC])
            nblk = TT // 128
            mod_jobs = [(l, pc) for l in range(DEPTH) for pc in range(6)]
            mj = 0

            def do_mod(l, pc, idx):
                wa, rwa = WA[idx % 2], rWA[idx % 2]
                wv = self.w_ada[l].rearrange("(k p) n -> p k n", p=128)
                P.dma("pool", wa[:], wv[:, :, pc * 1024:(pc + 1) * 1024], writes=[rwa])
                for jj in range(8):
                    mms = [dict(out=PM[:, jj * 2:jj * 2 + 2], lhsT=wa[:, k, jj * 128:(jj + 1) * 128], rhs=SC[:, k, :],
                                start=(k == 0), stop=(k == KC - 1)) for k in range(KC)]
                    P.op("pe", MM(mms), reads=[rwa, rSC], writes=[rPM])
                ob, _ = VEC_OFF["b_ada"]
                bsl = self.VEC[:, ob + l * 48 + pc * 8: ob + l * 48 + pc * 8 + 8]
                for c in range(2):
                    P.op("dve", I("tensor_tensor", out=self.MOD[:, l, pc * 8:(pc + 1) * 8, c], in0=PM[:, c:16:2], in1=bsl, op=ALU.add),
                         reads=[rPM, self.r_vec], writes=[self.r_mod])

            for blk in range(nblk):
                xa, rxa = XA[blk % 3], rXA[blk % 3]
                xb, rxb = XB[blk % 2], rXB[blk % 2]
                P.dma("sp", xa[:], self.xin[blk * 128:(blk + 1) * 128, :], writes=[rxa])
                for hf in range(2):
                    tp, rtp = TPp[(blk % 2) * 2 + hf], rTP[(blk % 2) * 2 + hf]
                    trs = [(tp[:, kk * 128:(kk + 1) * 128], xa[:, (hf * 4 + kk) * 128:(hf * 4 + kk + 1) * 128]) for kk in range(4)]
                    P.op("pe", TR(trs, ident), reads=[rxa, self.r_con], writes=[rtp])
                    dst = xb[:, hf * 4:(hf + 1) * 4, :]
                    src = tp[:].rearrange("p (a b) -> p a b", a=4)
                    if hf == 0:
                        P.op("act", I("copy", out=dst, in_=src), reads=[rtp], writes=[rxb])
                    else:
                        P.op("dve", I("tensor_copy", out=dst, in_=src), reads=[rtp], writes=[rxb])
                P.dma("sp", self.XT[:, :, blk * 128:(blk + 1) * 128], xb[:], reads=[rxb], writes=[self.r_xt])
                if mj < len(mod_jobs) and blk % 1 == 0:
                    do_mod(*mod_jobs[mj], mj)
                    mj += 1
            while mj < len(mod_jobs):
                do_mod(*mod_jobs[mj], mj)
                mj += 1
            for l in range(DEPTH):
                for mf, (nname, sc0) in enumerate((("norm_mix", 8), ("norm_ffn", 32))):
                    on, _ = VEC_OFF[nname]
                    for c in range(2):
                        P.op("dve", I("scalar_tensor_tensor", out=self.AMF[:, l, mf, c, :], in0=self.MOD[:, l, sc0:sc0 + 8, c], scalar=1.0,
                                      in1=self.VEC[:, on + l * 8: on + l * 8 + 8], op0=ALU.add, op1=ALU.mult),
                             reads=[self.r_mod, self.r_vec], writes=[self.r_amf])

    def modA(self, l, mf, c, k):
        return self.AMF[:, l, mf, c, k:k + 1]

    def modB(self, l, mf, c, k):
        base = 0 if mf == 0 else 24
        return self.MOD[:, l, base + k, c:c + 1]

    def modG(self, l, mf, c, k):
        base = 16 if mf == 0 else 40
        return self.MOD[:, l, base + k, c:c + 1]

    def norm_mod(self, xt, rx, ht, rh, W, l, mf, c, pfx, tiles):
        P = self.P
        sq, rsq = tiles["sq"], tiles["rsq"]
        ps, rps = tiles["ps"], tiles["rps"]
        rstd, rrstd = tiles["rstd"], tiles["rrstd"]
        ones = self.CON[:, C_ONES:C_ONES + 128]
        P.op("act", I("activation", out=sq[:, :, 0:W], in_=xt[:, :, 0:W], func=AF.Square), reads=[rx], writes=[rsq])
        mms = [dict(out=ps[:, 0:W], lhsT=ones, rhs=sq[:, k, 0:W], start=(k == 0), stop=(k == KC - 1)) for k in range(KC)]
        P.op("pe", MM(mms), reads=[rsq, self.r_con], writes=[rps])
        P.op("act", I("activation", out=rstd[:, 0:W], in_=ps[:, 0:W], func=AF.Sqrt, bias=self.EPSB[:, 0:1], scale=1.0),
             reads=[rps], writes=[rrstd])
        P.op("dve", I("reciprocal", out=rstd[:, 0:W], in_=rstd[:, 0:W]), reads=[rrstd], writes=[rrstd])
        P.op("dve", I("tensor_tensor", out=sq[:, :, 0:W], in0=xt[:, :, 0:W], in1=rstd[:, 0:W].unsqueeze(1).to_broadcast([128, KC, W]), op=ALU.mult),
             reads=[rx, rrstd, rsq], writes=[rsq])
        for k in range(KC):
            e = "act" if k % 2 == 0 else "pool"
            if e == "act":
                P.op("act", I("activation", out=ht[:, k, 0:W], in_=sq[:, k, 0:W], func=AF.Identity,
                              bias=self.modB(l, mf, c, k), scale=self.modA(l, mf, c, k)),
                     reads=[rsq, self.r_amf, self.r_mod], writes=[rh])
            else:
                P.op("pool", I("tensor_scalar", out=ht[:, k, 0:W], in0=sq[:, k, 0:W], scalar1=self.modA(l, mf, c, k),
                               scalar2=self.modB(l, mf, c, k), op0=ALU.mult, op1=ALU.add),
                     reads=[rsq, self.r_amf, self.r_mod], writes=[rh])

    def ffn_stage(self, l):
        P = self.P
        with P.stage():
            NT = 1024
            XTt = P.sb("f_xt", [128, KC, NT + 8], F32)
            rX = Res()
            SQ = P.sb("f_sq", [128, KC, 520], F32)
            rSQ = Res()
            RSTD = P.sb("f_rstd", [128, 520], F32)
            rRSTD = Res()
            HT = P.sb("f_ht", [128, KC, NT + 8], BF16)
            rH = RG(Res() for _ in range(KC))
            PB = P.sb("f_p", [128, NJ, NT], BF16)
            rPB = [Res() for _ in range(NJ)]
            UPW = [P.sb(f"f_upw{i}", [128, KC, 256], BF16) for i in range(4)]
            rUPW = [Res() for _ in range(4)]
            DNW = [P.sb(f"f_dnw{i}", [128, NJ, 128], BF16) for i in range(2)]
            rDNW = [Res() for _ in range(2)]
            UG = [P.sb(f"f_ug{i}", [128, 516], F32) for i in range(2)]
            UV = [P.sb(f"f_uv{i}", [128, 516], F32) for i in range(2)]
            rUG = [Res() for _ in range(2)]
            rUV = [Res() for _ in range(2)]
            AG = [P.sb(f"f_ag{i}", [128, 512], F32) for i in range(2)]
            AV = [P.sb(f"f_av{i}", [128, 512], F32) for i in range(2)]
            SG = [P.sb(f"f_sg{i}", [128, 512], F32) for i in range(2)]
            rAG = [Res() for _ in range(2)]
            rAV = [Res() for _ in range(2)]
            rSG = [Res() for _ in range(2)]
            SG2 = [P.sb(f"f_sg2{i}", [128, 512], F32) for i in range(2)]
            rSG2 = [Res() for _ in range(2)]
            PG = [P.ps(f"f_pg{i}", [128, 512], F32) for i in range(2)]
            PV = [P.ps(f"f_pv{i}", [128, 512], F32) for i in range(2)]
            PH = [P.ps(f"f_ph{i}", [128, 512], F32) for i in range(2)]
            PD = [P.ps(f"f_pd{i}", [128, 512], F32) for i in range(2)]
            rPG = [PRes() for _ in range(2)]
            rPV = [PRes() for _ in range(2)]
            rPH = [PRes() for _ in range(2)]
            rPD = [PRes() for _ in range(2)]
            ocw, _ = VEC_OFF["ffn_conv_w"]
            ocb, _ = VEC_OFF["ffn_conv_b"]

            def cw(tap, ch):
                o = ocw + (l * 3 + tap) * 44 + ch
                return self.VEC[:, o:o + 1]

            def cb(ch):
                o = ocb + l * 44 + ch
                return self.VEC[:, o:o + 1]

            tiles = []
            for i in range(TS // NT):
                units = []
                for u in range(NT // 512):
                    c0 = i * NT + u * 512
                    units.append((c0, 512, c0 > 0, c0 + 512 < TS))
                tiles.append((1, units))
            for g in range(NPC // 4):
                tiles.append((0, [(TS + (g * 4 + b) * SEQ, SEQ, False, False) for b in range(4)]))
            upi = 0
            dni = 0
            uix = 0
            n_up = len(tiles) * NJ
            n_dn = len(tiles) * KC
            issued = {"up": 0, "dn": 0}

            def ensure_up(i):
                while issued["up"] < min(i + 4, n_up):
                    q = issued["up"]
                    P.dma("pool", UPW[q % 4][:].rearrange("p k n -> p (k n)"), self.ffn_up[l, q % NJ], writes=[rUPW[q % 4]])
                    issued["up"] = q + 1

            def ensure_dn(i):
                while issued["dn"] < min(i + 2, n_dn):
                    q = issued["dn"]
                    P.dma("pool", DNW[q % 2][:].rearrange("p j n -> p (j n)"), self.ffn_dn[l, q % KC], writes=[rDNW[q % 2]])
                    issued["dn"] = q + 1

            for cond, units in tiles:
                offs = []
                off = 0
                for (c0, N, hl, hr) in units:
                    offs.append(off)
                    W = N + 2
                    lo = c0 - 1 if hl else c0
                    hi = c0 + N + 1 if hr else c0 + N
                    dlo = off + (0 if hl else 1)
                    P.dma("sp", XTt[:, :, dlo:dlo + (hi - lo)], self.XT[:, :, lo:hi], reads=[self.r_xt], writes=[rX])
                    if not hl:
                        P.op("pool", I("memset", XTt[:, :, off:off + 1], 0.0), writes=[rX])
                    if not hr:
                        P.op("pool", I("memset", XTt[:, :, off + W - 1:off + W], 0.0), writes=[rX])
                    off += W
                for ui, (c0, N, hl, hr) in enumerate(units):
                    W = N + 2
                    o = offs[ui]
                    tl = dict(sq=SQ, rsq=rSQ, ps=PD[0], rps=rPD[0], rstd=RSTD, rrstd=rRSTD)
                    self.norm_mod_wide(XTt, rX, HT, rH, o, W, l, 1, cond, tl, PD, rPD)
                    if not hl:
                        P.op("pool", I("memset", HT[:, :, o:o + 1], 0.0), writes=[rH])
                    if not hr:
                        P.op("pool", I("memset", HT[:, :, o + W - 1:o + W], 0.0), writes=[rH])
                for j in range(NJ):
                    ensure_up(upi)
                    ensure_dn(dni)
                    w, rw = UPW[upi % 4], rUPW[upi % 4]
                    upi += 1
                    pcol = 0
                    for ui, (c0, N, hl, hr) in enumerate(units):
                        W = N + 2
                        o = offs[ui]
                        b = uix % 2
                        uix += 1
                        Wm = min(W, 512)
                        for half, (pt, rpt) in enumerate(((PG[b], rPG[b]), (PV[b], rPV[b]))):
                            mms = [dict(out=pt[:, 0:Wm], lhsT=w[:, k, half * 128:(half + 1) * 128], rhs=HT[:, k, o:o + Wm],
                                        start=(k == 0), stop=(k == KC - 1)) for k in range(KC)]
                            P.op("pe", MM(mms), reads=[rw, rH], writes=[rpt])
                        if W > 512:
                            mms = []
                            for half in range(2):
                                mms += [dict(out=PH[b][:, half * 2:half * 2 + 2], lhsT=w[:, k, half * 128:(half + 1) * 128],
                                             rhs=HT[:, k, o + 512:o + W], start=(k == 0), stop=(k == KC - 1)) for k in range(KC)]
                            P.op("pe", MM(mms), reads=[rw, rH], writes=[rPH[b]])
                        P.op("act", I("copy", out=UG[b][:, 0:Wm], in_=PG[b][:, 0:Wm]), reads=[rPG[b]], writes=[rUG[b]])
                        P.op("act", I("copy", out=UV[b][:, 0:Wm], in_=PV[b][:, 0:Wm]), reads=[rPV[b]], writes=[rUV[b]])
                        if W > 512:
                            P.op("act", I("copy", out=UG[b][:, 512:514], in_=PH[b][:, 0:2]), reads=[rPH[b]], writes=[rUG[b]])
                            P.op("act", I("copy", out=UV[b][:, 512:514], in_=PH[b][:, 2:4]), reads=[rPH[b]], writes=[rUV[b]])
                        chg, chv = j, NJ + j
                        P.op("dve", I("tensor_scalar", out=AG[b][:, 0:N], in0=UG[b][:, 0:N], scalar1=cw(0, chg), scalar2=cb(chg), op0=ALU.mult, op1=ALU.add),
                             reads=[rUG[b], self.r_vec], writes=[rAG[b]])
                        P.op("dve", I("tensor_scalar", out=AV[b][:, 0:N], in0=UV[b][:, 0:N], scalar1=cw(0, chv), scalar2=cb(chv), op0=ALU.mult, op1=ALU.add),
                             reads=[rUV[b], self.r_vec], writes=[rAV[b]])
                        for tap in (1, 2):
                            P.op("dve", I("scalar_tensor_tensor", out=AG[b][:, 0:N], in0=UG[b][:, tap:N + tap], scalar=cw(tap, chg), in1=AG[b][:, 0:N],
                                          op0=ALU.mult, op1=ALU.add), reads=[rUG[b], rAG[b], self.r_vec], writes=[rAG[b]])
                            P.op("dve", I("scalar_tensor_tensor", out=AV[b][:, 0:N], in0=UV[b][:, tap:N + tap], scalar=cw(tap, chv), in1=AV[b][:, 0:N],
                                          op0=ALU.mult, op1=ALU.add), reads=[rUV[b], rAV[b], self.r_vec], writes=[rAV[b]])
                        P.op("act", I("activation", out=SG[b][:, 0:N], in_=AG[b][:, 0:N], func=AF.Silu), reads=[rAG[b]], writes=[rSG[b]])
                        P.op("pool", I("tensor_tensor", out=PB[:, j, pcol:pcol + N], in0=SG[b][:, 0:N], in1=AV[b][:, 0:N], op=ALU.mult),
                             reads=[rSG[b], rAV[b]], writes=[rPB[j]])
                        pcol += N
                for m in range(KC):
                    ensure_dn(dni)
                    ensure_up(upi)
                    w, rw = DNW[dni % 2], rDNW[dni % 2]
                    dni += 1
                    pcol = 0
                    for ui, (c0, N, hl, hr) in enumerate(units):
                        o = offs[ui]
                        b = uix % 2
                        uix += 1
                        mms = [dict(out=PD[b][:, 0:N], lhsT=w[:, j, :], rhs=PB[:, j, pcol:pcol + N], start=(j == 0), stop=(j == NJ - 1))
                               for j in range(NJ)]
                        P.op("pe", MM(mms), reads=[rw] + rPB, writes=[rPD[b]])
                        xs = XTt[:, m, o + 1:o + 1 + N]
                        P.op("dve", I("scalar_tensor_tensor", out=xs, in0=PD[b][:, 0:N], scalar=self.modG(l, 1, cond, m), in1=xs, op0=ALU.mult, op1=ALU.add),
                             reads=[rPD[b], rX, self.r_mod], writes=[rX])
                        pcol += N
                for ui, (c0, N, hl, hr) in enumerate(units):
                    o = offs[ui]
                    P.dma("sp", self.XT[:, :, c0:c0 + N], XTt[:, :, o + 1:o + 1 + N], reads=[rX], writes=[self.r_xt])

    def norm_mod_wide(self, XTt, rX, HT, rH, o, W, l, mf, cond, tl, PD, rPD, sqw=None):
        P = self.P
        SQ, rSQ, RSTD, rRSTD = tl["sq"], tl["rsq"], tl["rstd"], tl["rrstd"]
        ones = self.CON[:, C_ONES:C_ONES + 128]
        segs = [(0, min(W, 512))] + ([(512, W)] if W > 512 else [])
        for si, (a, bnd) in enumerate(segs):
            w = bnd - a
            ps, rps = PD[si % 2], rPD[si % 2]
            P.op("act", I("activation", out=SQ[:, :, 0:w], in_=XTt[:, :, o + a:o + bnd], func=AF.Square), reads=[rX], writes=[rSQ])
            mms = [dict(out=ps[:, 0:w], lhsT=ones, rhs=SQ[:, k, 0:w], start=(k == 0), stop=(k == KC - 1)) for k in range(KC)]
            P.op("pe", MM(mms), reads=[rSQ, self.r_con], writes=[rps])
            P.op("act", I("activation", out=RSTD[:, a:bnd], in_=ps[:, 0:w], func=AF.Sqrt, bias=self.EPSB[:, 0:1], scale=1.0),
                 reads=[rps], writes=[rRSTD])
            P.op("dve", I("reciprocal", out=RSTD[:, a:bnd], in_=RSTD[:, a:bnd]), reads=[rRSTD], writes=[rRSTD])
            P.op("dve", I("tensor_tensor", out=SQ[:, :, 0:w], in0=XTt[:, :, o + a:o + bnd], in1=RSTD[:, a:bnd].unsqueeze(1).to_broadcast([128, KC, w]), op=ALU.mult),
                 reads=[rX, rRSTD, rSQ], writes=[rSQ])
            for k in range(KC):
                if k % 2 == 0:
                    P.op("act", I("activation", out=HT[:, k, o + a:o + bnd], in_=SQ[:, k, 0:w], func=AF.Identity,
                                  bias=self.modB(l, mf, cond, k), scale=self.modA(l, mf, cond, k)),
                         reads=[rSQ, self.r_amf, self.r_mod], writes=[rH[k] if isinstance(rH, RG) else rH])
                else:
                    P.op("pool", I("tensor_scalar", out=HT[:, k, o + a:o + bnd], in0=SQ[:, k, 0:w], scalar1=self.modA(l, mf, cond, k),
                                   scalar2=self.modB(l, mf, cond, k), op0=ALU.mult, op1=ALU.add),
                         reads=[rSQ, self.r_amf, self.r_mod], writes=[rH[k] if isinstance(rH, RG) else rH])


    def even_stage(self, l):
        P = self.P
        j = l // 2
        N = 256
        NCH = N // 128
        QA, KD, QB, KB, GB, RR, TM = 0, 512, 768, 1024, 1280, 1792, 1856
        NCE = TM + 768
        with P.stage():
            WE = P.sb("e_we", [128, KC, NCE], BF16)
            WO = P.sb("e_wo", [128, KC, D], BF16)
            WG = P.sb("e_wg", [64, 512], BF16)
            rW = Res()
            P.dma("pool", WE[:].rearrange("p k n -> p (k n)"), self.w_even[j], writes=[rW])
            P.dma("pool", WO[:], self.ev_w_out[j].rearrange("(k p) n -> p k n", p=128), writes=[rW])
            P.dma("pool", WG[:], self.wg[j], writes=[rW])
            identf = self.CON[:, C_IDENT:C_IDENT + 128]
            identb = self.CONB[:, 0:128]
            ones = self.CON[:, C_ONES:C_ONES + 128]
            permf = self.CON[:, C_PERM:C_PERM + 128]
            PJ = [P.ps(f"e_pj{i}", [128, 512], F32) for i in range(3)]
            rPJ = [PRes() for _ in range(3)]
            ATp = [P.ps(f"e_at{i}", [128, 512], F32) for i in range(2)]
            rAT = [PRes() for _ in range(2)]
            OTp = P.ps("e_ot", [128, 512], F32)
            rOT = PRes()
            KVp = P.ps("e_kv", [128, 512], F32)
            rKV = PRes()
            TRb = P.ps("e_trb", [128, 1024], BF16)
            rTRb = PRes()
            pjc = [0]

            def pj():
                b = pjc[0] % 2
                pjc[0] += 1
                return PJ[b], rPJ[b]

            KCX = P.sb("e_kcx", [128, 2, 256], BF16)
            VCX = P.sb("e_vcx", [128, 2, 128], BF16)
            CKD = P.sb("e_ckd", [128, 2, 2, 2, 64], F32)
            rKCX, rVCX, rCKD = Res(), Res(), Res()
            ckv = self.cache_k[j].rearrange("(b t) (h d) -> t b h d", t=128, h=2)
            for dup in range(2):
                for blk in range(2):
                    P.dma("sp", CKD[:, blk, :, dup, :], ckv[:, blk], writes=[rCKD])
            for blk in range(2):
                for kvh in range(2):
                    pt, rpt = pj()
                    P.op("pe", TR([(pt[:, 0:128], CKD[:, blk, kvh].rearrange("t u d -> t (u d)"))], identf), reads=[rCKD, self.r_con], writes=[rpt])
                    P.op("act", I("copy", out=KCX[:, kvh, blk * 128:(blk + 1) * 128], in_=pt[:, 0:128]), reads=[rpt], writes=[rKCX])
            P.dma("pool", VCX[:], self.cache_v[j].rearrange("(b t) f -> t b f", t=128), writes=[rVCX])

            KR = P.sb("e_kr", [128, 2, TS], BF16)
            VA = P.sb("e_va", [128, TS // 128, 128], BF16)
            SBS = P.sb("e_sbs", [128, TS // 128, 2, 128], BF16)
            rKR = [Res() for _ in range(TS // N)]
            rVA = [Res() for _ in range(TS // N)]
            rSBS = [Res() for _ in range(TS // N)]
            SF = P.sb("e_sf", [128, 2, 256], F32)
            SB_ = P.sb("e_sb", [128, 2, 256], F32)
            SFb = P.sb("e_sfb", [128, 2, 256], BF16)
            rSF, rSB, rSFb = Res(), Res(), Res()
            XTt = P.sb("e_xt", [128, KC, N], F32)
            XR = P.sb("e_xr", [128, KC, N], F32)
            RSTD = P.sb("e_rstd", [128, N], F32)
            HT = P.sb("e_ht", [128, KC, N], BF16)
            ROP = P.sb("e_rop", [128, 2, N], F32)
            QS = P.sb("e_qs", [128, N], F32)
            T1 = P.sb("e_t1", [128, N], F32)
            T2 = P.sb("e_t2", [128, N], F32)
            QR = [P.sb(f"e_qr{i}", [128, 4, N], BF16) for i in range(2)]
            OB = [P.sb(f"e_ob{i}", [128, 4, N], BF16) for i in range(2)]
            OAT = P.sb("e_oat", [128, 4, N], BF16)
            GS = P.sb("e_gs", [128, 4, N], BF16)
            QBr = P.sb("e_qbr", [128, 2, N], F32)
            KBr = P.sb("e_kbr", [128, 2, N], F32)
            RT = P.sb("e_rt", [64, N], BF16)
            VB = P.sb("e_vb", [128, NCH, 512], BF16)
            KVO = P.sb("e_kvo", [128, NCH, 256], F32)
            Lg = P.sb("e_l", [128, 2, N], F32)
            CS = P.sb("e_cs", [128, 2, N], F32)
            E1 = P.sb("e_e1", [128, 2, N], F32)
            E2 = P.sb("e_e2", [128, 2, N], F32)
            E3 = P.sb("e_e3", [128, 2, N], F32)
            NB = P.sb("e_nb", [128, 2, NCH], F32)
            AD = P.sb("e_ad", [128, 2, NCH], F32)
            QT = [P.sb(f"e_qt{d}", [128, 2, N], BF16) for d in range(2)]
            KT = [P.sb(f"e_kt{d}", [128, 2, N], BF16) for d in range(2)]
            KHT = P.sb("e_kht", [128, 2, N], BF16)
            KH = P.sb("e_kh", [128, NCH, 2, 128], BF16)
            ATm = [P.sb(f"e_atm{d}", [128, 512], BF16) for d in range(2)]
            OSQ = P.sb("e_osq", [128, 512], F32)
            ORS = P.sb("e_ors", [128, 512], F32)
            OBF = P.sb("e_obf", [128, 512], F32)
            SM = [P.sb(f"e_sm{i}", [128, 640], F32) for i in range(2)]
            PX = [P.sb(f"e_px{i}", [128, 640], BF16) for i in range(2)]
            PTs = [P.sb(f"e_pts{i}", [128, 640], BF16) for i in range(2)]
            OAs = P.sb("e_oas", [128, 512], BF16)
            SMALL = [P.sb(f"e_small{i}", [128, 8], F32) for i in range(2)]
            RDEN = [P.sb(f"e_rden{i}", [128, 8], F32) for i in range(2)]
            rSMl = [Res() for _ in range(2)]
            rPXl = [Res() for _ in range(2)]
            rPTl = [Res() for _ in range(2)]
            rSMALLl = [Res() for _ in range(2)]
            rRDEN = [Res() for _ in range(2)]
            NBG = P.sb("e_nbg", [128, 4], F32)
            (rX, rXR, rRSTD, rH_, rROP, rQS, rT1, rT2, rOAT, rGS, rQBr, rKBr, rRT, rVB, rKVO, rL, rCS, rE1, rE2, rE3, rNB, rAD,
             rKHT, rKH, rOSQ, rORS, rOBF, rSM, rPX, rPTs, rOAs, rSMALL, rNBG) = (Res() for _ in range(33))
            rH = RG(Res() for _ in range(KC))
            rQR = [Res() for _ in range(2)]
            rOB = [Res() for _ in range(2)]
            rQT = [Res() for _ in range(2)]
            rKT = [Res() for _ in range(2)]
            rATm = [Res() for _ in range(2)]
            obg, _ = VEC_OFF["b_gate"]
            P.op("dve", I("tensor_scalar", out=NBG[:], in0=self.VEC[:, obg + j * 4: obg + j * 4 + 4], scalar1=-1.0, scalar2=None, op0=ALU.mult),
                 reads=[self.r_vec], writes=[rNBG])
            osk, _ = VEC_OFF["sink"]
            ogn, _ = VEC_OFF["gla_norm"]
            gn = self.VEC[:, ogn + j: ogn + j + 1]

            def proj_fm(col, M, dst_rows=128):
                pt, rpt = pj()
                mms = [dict(out=pt[0:M, 0:N], lhsT=WE[:, k, col:col + M], rhs=HT[:, k, 0:N], start=(k == 0), stop=(k == KC - 1)) for k in range(KC)]
                P.op("pe", MM(mms), reads=[rW, rH], writes=[rpt])
                return pt, rpt

            def gates(d, rev):
                for p in range(2):
                    pt, rpt = pj()
                    P.op("pe", MM([dict(out=pt[:, 0:N], lhsT=WG[0:64, d * 256 + p * 128: d * 256 + (p + 1) * 128], rhs=RT[0:64, 0:N], start=True, stop=True)]),
                         reads=[rW, rRT], writes=[rpt])
                    P.op("act", I("activation", out=Lg[:, p, :], in_=pt[:, 0:N], func=AF.Exp, bias=NBG[:, d * 2 + p: d * 2 + p + 1], scale=-1.0),
                         reads=[rpt, rNBG], writes=[rL])
                P.op("act", I("activation", out=Lg[:].rearrange("p a n -> p (a n)"), in_=Lg[:].rearrange("p a n -> p (a n)"), func=AF.Ln,
                              bias=self.ONEB[:, 0:1], scale=1.0), reads=[rL], writes=[rL])
                rst = self.CON[:, C_RST:C_RST + N]
                for p in range(2):
                    if not rev:
                        P.op("dve", I("tensor_tensor_scan", out=CS[:, p, :], data0=rst, data1=Lg[:, p, :], initial=0.0, op0=ALU.mult, op1=ALU.add),
                             reads=[rL, self.r_con], writes=[rCS])
                    else:
                        P.op("dve", I("tensor_tensor_scan", out=CS[:, p, :][:, ::-1], data0=rst, data1=Lg[:, p, :][:, ::-1], initial=0.0,
                                      op0=ALU.mult, op1=ALU.add), reads=[rL, self.r_con], writes=[rCS])

            def e3_and_decay(rev):
                first = 0 if rev else 127
                P.op("dve", I("tensor_scalar", out=NB[:], in0=CS[:, :, first::128], scalar1=-1.0 / 16.0, scalar2=None, op0=ALU.mult),
                     reads=[rCS], writes=[rNB])
                P.op("act", I("activation", out=AD[:].rearrange("p a c -> p (a c)"), in_=NB[:].rearrange("p a c -> p (a c)"), func=AF.Exp),
                     reads=[rNB], writes=[rAD])
                for p in range(2):
                    for c in range(NCH):
                        P.op("act", I("activation", out=E3[:, p, c * 128:(c + 1) * 128], in_=CS[:, p, c * 128:(c + 1) * 128], func=AF.Exp,
                                      bias=NB[:, p, c:c + 1], scale=1.0 / 16.0), reads=[rCS, rNB], writes=[rE3])
                P.op("dve", I("tensor_tensor", out=KHT[:], in0=KBr[:], in1=E3[:], op=ALU.mult), reads=[rKBr, rE3], writes=[rKHT])
                for c in range(NCH):
                    trs = [(TRb[:, (c * 2 + p) * 128:(c * 2 + p + 1) * 128], KHT[:, p, c * 128:(c + 1) * 128]) for p in range(2)]
                    P.op("pe", TR(trs, identb), reads=[rKHT, self.r_conb], writes=[rTRb])
                P.op("act", I("copy", out=KH[:].rearrange("t c p f -> t (c p f)"), in_=TRb[:, 0:NCH * 256]), reads=[rTRb], writes=[rKH])

            def kv_update(c, S, rS, extra_reads=()):
                mms = [dict(out=KVp[:, p * 256:(p + 1) * 256], lhsT=KH[:, c, p, :], rhs=VB[:, c, p * 256:(p + 1) * 256], start=True, stop=True) for p in range(2)]
                P.op("pe", MM(mms), reads=[rKH, rVB], writes=[rKV])
                for p in range(2):
                    P.op("dve", I("scalar_tensor_tensor", out=S[:, p, :], in0=S[:, p, :], scalar=AD[:, p, c:c + 1], in1=KVp[:, p * 256:(p + 1) * 256],
                                  op0=ALU.mult, op1=ALU.add), reads=[rKV, rAD, rS] + list(extra_reads), writes=[rS])

            def load_norm(t0, cond):
                P.dma("sp", XTt[:], self.XT[:, :, t0:t0 + N], reads=[self.r_xt], writes=[rX])
                tl = dict(sq=XR, rsq=rXR, rstd=RSTD, rrstd=rRSTD)
                self.norm_mod_wide(XTt, rX, HT, rH, 0, N, l, 0, cond, tl, [PJ[0], PJ[1]], [rPJ[0], rPJ[1]])

            def vb_proj():
                for c in range(NCH):
                    pt, rpt = pj()
                    mms = [dict(out=pt[:, 0:512], lhsT=HT[:, k, c * 128:(c + 1) * 128], rhs=WE[:, k, TM + 256:TM + 768], start=(k == 0), stop=(k == KC - 1))
                           for k in range(KC)]
                    P.op("pe", MM(mms), reads=[rW, rH], writes=[rpt])
                    P.op("act", I("copy", out=VB[:, c, :], in_=pt[:, 0:512]), reads=[rpt], writes=[rVB])

            def r_proj():
                pt, rpt = proj_fm(RR, 64)
                P.op("act", I("copy", out=RT[:, :], in_=pt[0:64, 0:N]), reads=[rpt], writes=[rRT])

            def kb_proj():
                for p in range(2):
                    pt, rpt = proj_fm(KB + p * 128, 128)
                    P.op("act", I("copy", out=KBr[:, p, :], in_=pt[:, 0:N]), reads=[rpt], writes=[rKBr])

            seqs = [(0, TS, 1, None)] + [(TS + b * SEQ, SEQ, 0, b) for b in range(NPC)]
            if self.dbg_seqs is not None:
                seqs = [seqs[i] for i in self.dbg_seqs]
            if self.dbg_cut <= 1:
                seqs = []
            for (s0, T, cond, pb) in seqs:
                ntile = T // N
                nblk = T // 128
                sample = pb is None
                for (S, rS, d) in ((SF, rSF, 0), (SB_, rSB, 1)):
                    P.op("pool", I("memset", S[:], 0.0), writes=[rS])
                    if sample:
                        for p in range(2):
                            for hh in range(2):
                                P.dma("sp", S[hh * 64:(hh + 1) * 64, p, hh * 128:(hh + 1) * 128], self.state_gla[j, d, 2 * p + hh], writes=[rS])
                for ti in range(ntile - 1, -1, -1) if self.dbg_cut > 2 else []:
                    t0 = s0 + ti * N
                    load_norm(t0, cond)
                    r_proj()
                    kb_proj()
                    vb_proj()
                    gates(1, True)
                    e3_and_decay(True)
                    for c in range(NCH - 1, -1, -1):
                        gc = ti * NCH + c
                        for p in range(2):
                            P.op("pool", I("tensor_copy", out=SBS[0:64, gc, p, :], in_=SB_[0:64, p, 0:128]), reads=[rSB], writes=[rSBS[ti]])
                            P.op("pool", I("tensor_copy", out=SBS[64:128, gc, p, :], in_=SB_[64:128, p, 128:256]), reads=[rSB], writes=[rSBS[ti]])
                        kv_update(c, SB_, rSB)
                if not sample:
                    for p in range(2):
                        for hh in range(2):
                            P.dma("sp", self.ngla[pb, j, 1, 2 * p + hh], SB_[hh * 64:(hh + 1) * 64, p, hh * 128:(hh + 1) * 128], reads=[rSB], writes=[self.r_out])
                P.op("act", I("copy", out=SFb[:], in_=SF[:]), reads=[rSF], writes=[rSFb])

                def phase_a(ti):
                    t0 = s0 + ti * N
                    bq = ti % 2
                    load_norm(t0, cond)
                    if sample:
                        P.dma("sp", ROP[:], self.rope[:, :, ti * N:(ti + 1) * N], writes=[rROP])

                    def rope_or_copy(pt, rpt, dst, rdst):
                        if not sample:
                            P.op("act", I("copy", out=dst, in_=pt[:, 0:N]), reads=[rpt], writes=[rdst])
                            return
                        P.op("act", I("copy", out=QS[:], in_=pt[:, 0:N]), reads=[rpt], writes=[rQS])
                        p2, rp2 = pj()
                        P.op("pe", MM([dict(out=p2[:, 0:N], lhsT=permf, rhs=QS[:], start=True, stop=True)]), reads=[rQS, self.r_con], writes=[rp2])
                        P.op("pool", I("tensor_tensor", out=T1[:], in0=QS[:], in1=ROP[:, 0, :], op=ALU.mult), reads=[rQS, rROP], writes=[rT1])
                        P.op("dve", I("tensor_tensor", out=T2[:], in0=p2[:, 0:N], in1=ROP[:, 1, :], op=ALU.mult), reads=[rp2, rROP], writes=[rT2])
                        P.op("pool", I("tensor_tensor", out=dst, in0=T1[:], in1=T2[:], op=ALU.add), reads=[rT1, rT2], writes=[rdst])

                    if self.dbg_cut <= 4.1:
                        return
                    for cc in range(4):
                        pt, rpt = proj_fm(QA + cc * 128, 128)
                        rope_or_copy(pt, rpt, QR[bq][:, cc, :], rQR[bq])
                    for kvh in range(2):
                        pt, rpt = proj_fm(KD + kvh * 128, 128)
                        rope_or_copy(pt, rpt, KR[:, kvh, ti * N:(ti + 1) * N], rKR[ti])
                    if self.dbg_cut <= 4.2:
                        return
                    for p in range(2):
                        pt, rpt = proj_fm(QB + p * 128, 128)
                        P.op("act", I("activation", out=QBr[:, p, :], in_=pt[:, 0:N], func=AF.Identity, scale=0.125), reads=[rpt], writes=[rQBr])
                    kb_proj()
                    for h in range(4):
                        pt, rpt = proj_fm(GB + h * 128, 128)
                        P.op("act", I("activation", out=GS[:, h, :], in_=pt[:, 0:N], func=AF.Silu), reads=[rpt], writes=[rGS])
                    r_proj()
                    vb_proj()
                    if self.dbg_cut <= 4.3:
                        return
                    for c in range(NCH):
                        pt, rpt = pj()
                        mms = [dict(out=pt[:, 0:256], lhsT=HT[:, k, c * 128:(c + 1) * 128], rhs=WE[:, k, TM:TM + 256], start=(k == 0), stop=(k == KC - 1))
                               for k in range(KC)]
                        P.op("pe", MM(mms), reads=[rW, rH], writes=[rpt])
                        P.op("act", I("copy", out=VA[:, ti * NCH + c, :], in_=pt[:, 128:256]), reads=[rpt], writes=[rVA[ti]])
                        if not sample:
                            P.op("dve", I("tensor_copy", out=KVO[:, c, :], in_=pt[:, 0:256]), reads=[rpt], writes=[rKVO])
                        if not sample and self.dbg_cut != 4.35:
                            P.dma("sp", self.nk[pb, j, (ti * NCH + c) * 128:(ti * NCH + c + 1) * 128, :], KVO[:, c, 0:128], reads=[rKVO], writes=[self.r_out])
                            P.dma("sp", self.nv[pb, j, (ti * NCH + c) * 128:(ti * NCH + c + 1) * 128, :], KVO[:, c, 128:256], reads=[rKVO], writes=[self.r_out])
                    if self.dbg_cut <= 4.4:
                        return
                    for d in range(2):
                        gates(d, d == 1)
                        flat = lambda t: t[:].rearrange("p a n -> p (a n)")
                        P.op("act", I("activation", out=flat(E1), in_=flat(CS), func=AF.Exp, scale=-1.0 / 16.0), reads=[rCS], writes=[rE1])
                        P.op("act", I("activation", out=flat(E2), in_=flat(CS), func=AF.Exp, scale=1.0 / 16.0), reads=[rCS], writes=[rE2])
                        P.op("dve", I("tensor_tensor", out=QT[d][:], in0=QBr[:], in1=E1[:], op=ALU.mult), reads=[rQBr, rE1], writes=[rQT[d]])
                        P.op("pool", I("tensor_tensor", out=KT[d][:], in0=KBr[:], in1=E2[:], op=ALU.mult), reads=[rKBr, rE2], writes=[rKT[d]])
                        if d == 0:
                            e3_and_decay(False)
                    if self.dbg_cut <= 4.5:
                        return
                    for c in range(NCH):
                        gc = ti * NCH + c
                        cs_ = slice(c * 128, (c + 1) * 128)
                        for hh in range(2):
                            mms = []
                            for d in range(2):
                                for p in range(2):
                                    mms.append(dict(out=ATp[hh][:, (d * 2 + p) * 128:(d * 2 + p + 1) * 128], lhsT=KT[d][hh * 64:(hh + 1) * 64, p, cs_],
                                                    rhs=QT[d][hh * 64:(hh + 1) * 64, p, cs_], start=True, stop=True))
                            P.op("pe", MM(mms), reads=[rKT[0], rKT[1], rQT[0], rQT[1]], writes=[rAT[hh]])
                            for d in range(2):
                                mo_ = C_MF if d == 0 else C_MB
                                mk = self.CON[:, mo_:mo_ + 128].unsqueeze(1).to_broadcast([128, 2, 128])
                                P.op("dve", I("tensor_tensor", out=ATm[hh][:, d * 256:(d + 1) * 256].rearrange("p (h n) -> p h n", h=2),
                                              in0=ATp[hh][:, d * 256:(d + 1) * 256].rearrange("p (h n) -> p h n", h=2), in1=mk, op=ALU.mult),
                                     reads=[rAT[hh], self.r_con], writes=[rATm[hh]])
                        mms = []
                        for h in range(4):
                            p, hh = h // 2, h % 2
                            o = OTp[:, h * 128:(h + 1) * 128]
                            mms.append(dict(out=o, lhsT=VB[:, c, h * 128:(h + 1) * 128], rhs=ATm[hh][:, (0 * 2 + p) * 128:(0 * 2 + p + 1) * 128], start=True, stop=False))
                            mms.append(dict(out=o, lhsT=VB[:, c, h * 128:(h + 1) * 128], rhs=ATm[hh][:, (1 * 2 + p) * 128:(1 * 2 + p + 1) * 128], start=False, stop=False))
                            mms.append(dict(out=o, lhsT=SFb[hh * 64:(hh + 1) * 64, p, hh * 128:(hh + 1) * 128], rhs=QT[0][hh * 64:(hh + 1) * 64, p, cs_],
                                            start=False, stop=False))
                            mms.append(dict(out=o, lhsT=SBS[hh * 64:(hh + 1) * 64, gc, p, :], rhs=QT[1][hh * 64:(hh + 1) * 64, p, cs_],
                                            start=False, stop=True))
                        P.op("pe", MM(mms), reads=[rVB, rATm[0], rATm[1], rSFb, rSBS[ti], rQT[0], rQT[1]], writes=[rOT])
                        kv_update(c, SF, rSF)
                        P.op("act", I("copy", out=SFb[:], in_=SF[:]), reads=[rSF], writes=[rSFb])
                        P.op("act", I("activation", out=OSQ[:], in_=OTp[:], func=AF.Square), reads=[rOT], writes=[rOSQ])
                        pt, rpt = pj()
                        P.op("pe", MM([dict(out=pt[:, 0:512], lhsT=ones, rhs=OSQ[:], start=True, stop=True)]), reads=[rOSQ, self.r_con], writes=[rpt])
                        P.op("act", I("activation", out=ORS[:], in_=pt[:, 0:512], func=AF.Sqrt, bias=self.EPSB[:, 0:1], scale=8.0), reads=[rpt], writes=[rORS])
                        P.op("dve", I("reciprocal", out=ORS[:], in_=ORS[:]), reads=[rORS], writes=[rORS])
                        P.op("dve", I("tensor_tensor", out=OBF[:], in0=OTp[:], in1=ORS[:], op=ALU.mult), reads=[rOT, rORS], writes=[rOBF])
                        P.op("dve", I("scalar_tensor_tensor", out=OB[bq][:, :, cs_], in0=OBF[:].rearrange("p (h n) -> p h n", h=4), scalar=gn,
                                      in1=GS[:, :, cs_], op0=ALU.mult, op1=ALU.mult), reads=[rOBF, rGS, self.r_vec], writes=[rOB[bq]])

                def phase_b(ti):
                    t0 = s0 + ti * N
                    bq = ti % 2
                    P.dma("sp", XR[:], self.XT[:, :, t0:t0 + N], reads=[self.r_xt], writes=[rXR])
                    for c in range(NCH):
                        qb = ti * NCH + c
                        cs_ = slice(c * 128, (c + 1) * 128)
                        if sample:
                            kb0, kb1 = max(0, qb - 1), min(nblk, qb + 2)
                        else:
                            kb0, kb1 = 0, nblk
                        nw = (kb1 - kb0) * 128
                        ntot = nw + (256 if sample else 0)
                        kread = sorted(set((b * 128) // N for b in range(kb0, kb1)))
                        pa, rpa = PJ[2], rPJ[2]
                        rd, rrd = RDEN[qb % 2], rRDEN[qb % 2]
                        def qk(h):
                            cc, hh, kvh = h // 2, h % 2, h // 4
                            par = h % 2
                            rows = slice(hh * 64, (hh + 1) * 64)
                            sw, rsw = (ATp[0], rAT[0]) if par == 0 else (OTp, rOT)
                            sc, rsc = (ATp[1], rAT[1]) if par == 0 else (KVp, rKV)
                            P.op("pe", MM([dict(out=sw[:, 0:nw], lhsT=QR[bq][rows, cc, cs_], rhs=KR[rows, kvh, kb0 * 128:kb1 * 128], start=True, stop=True)]),
                                 reads=[rQR[bq]] + [rKR[t] for t in kread], writes=[rsw])
                            if sample:
                                P.op("pe", MM([dict(out=sc[:, 0:256], lhsT=QR[bq][rows, cc, cs_], rhs=KCX[rows, kvh, :], start=True, stop=True)]),
                                     reads=[rQR[bq], rKCX], writes=[rsc])

                        def stage1(h):
                            par = h % 2
                            sw, rsw = (ATp[0], rAT[0]) if par == 0 else (OTp, rOT)
                            sc, rsc = (ATp[1], rAT[1]) if par == 0 else (KVp, rKV)
                            SMh, rSMh, PXh, rPXh, SMALLh, rSMALLh = SM[par], rSMl[par], PX[par], rPXl[par], SMALL[par], rSMALLl[par]
                            if sample:
                                mo = C_AMASK + (kb0 - (qb - 1)) * 128
                                P.op("dve", I("tensor_tensor", out=SMh[:, 0:nw], in0=sw[:, 0:nw], in1=self.CON[:, mo:mo + nw], op=ALU.add),
                                     reads=[rsw, self.r_con], writes=[rSMh])
                                P.op("act", I("copy", out=SMh[:, nw:ntot], in_=sc[:, 0:256]), reads=[rsc], writes=[rSMh])
                            else:
                                P.op("act", I("copy", out=SMh[:, 0:nw], in_=sw[:, 0:nw]), reads=[rsw], writes=[rSMh])
                            sk = self.VEC[:, osk + j * 8 + h: osk + j * 8 + h + 1]
                            P.op("dve", I("reduce_max", out=SMALLh[:, 0:1], in_=SMh[:, 0:ntot], axis=mybir.AxisListType.X), reads=[rSMh], writes=[rSMALLh])
                            P.op("dve", I("tensor_scalar", out=SMALLh[:, 1:2], in0=SMALLh[:, 0:1], scalar1=0.125, scalar2=sk, op0=ALU.mult, op1=ALU.max),
                                 reads=[rSMALLh, self.r_vec], writes=[rSMALLh])
                            P.op("dve", I("tensor_scalar", out=SMALLh[:, 2:3], in0=SMALLh[:, 1:2], scalar1=-1.0, scalar2=None, op0=ALU.mult),
                                 reads=[rSMALLh], writes=[rSMALLh])
                            P.op("act", I("activation", out=PXh[:, 0:ntot], in_=SMh[:, 0:ntot], func=AF.Exp, bias=SMALLh[:, 2:3], scale=0.125, accum_out=SMALLh[:, 3:4]),
                                 reads=[rSMh, rSMALLh], writes=[rPXh, rSMALLh])
                            P.op("act", I("activation", out=SMALLh[:, 4:5], in_=sk, func=AF.Exp, bias=SMALLh[:, 2:3], scale=1.0),
                                 reads=[rSMALLh, self.r_vec], writes=[rSMALLh])

                        def stage2(h):
                            kvh = h // 4
                            par = h % 2
                            PXh, rPXh, PTh, rPTh, SMALLh, rSMALLh = PX[par], rPXl[par], PTs[par], rPTl[par], SMALL[par], rSMALLl[par]
                            P.op("dve", I("tensor_tensor", out=SMALLh[:, 5:6], in0=SMALLh[:, 3:4], in1=SMALLh[:, 4:5], op=ALU.add), reads=[rSMALLh], writes=[rSMALLh])
                            P.op("dve", I("reciprocal", out=rd[:, h:h + 1], in_=SMALLh[:, 5:6]), reads=[rSMALLh], writes=[rrd])
                            nkb = ntot // 128
                            trs = [(TRb[:, b * 128:(b + 1) * 128], PXh[:, b * 128:(b + 1) * 128]) for b in range(nkb)]
                            P.op("pe", TR(trs, identb), reads=[rPXh, self.r_conb], writes=[rTRb])
                            P.op("dve", I("tensor_copy", out=PTh[:, 0:ntot], in_=TRb[:, 0:ntot]), reads=[rTRb], writes=[rPTh])
                            mms = []
                            for b in range(nkb):
                                if b < kb1 - kb0:
                                    vv = VA[:, kb0 + b, kvh * 64:(kvh + 1) * 64]
                                else:
                                    vv = VCX[:, b - (kb1 - kb0), kvh * 64:(kvh + 1) * 64]
                                mms.append(dict(out=pa[:, h * 64:(h + 1) * 64], lhsT=PTh[:, b * 128:(b + 1) * 128], rhs=vv, start=(b == 0), stop=(b == nkb - 1)))
                            P.op("pe", MM(mms), reads=[rPTh, rVCX] + [rVA[t] for t in kread], writes=[rpa])

                        qk(0)
                        qk(1)
                        stage1(0)
                        for h in range(8):
                            if h + 1 < 8:
                                stage1(h + 1)
                            if h + 2 < 8:
                                qk(h + 2)
                            stage2(h)
                        P.op("dve", I("tensor_tensor", out=OAs[:].rearrange("p (h d) -> p h d", h=8), in0=pa[:].rearrange("p (h d) -> p h d", h=8),
                                      in1=rd[:].unsqueeze(2).to_broadcast([128, 8, 64]), op=ALU.mult), reads=[rpa, rrd], writes=[rOAs])
                        trs = [(TRb[:, k4 * 128:(k4 + 1) * 128], OAs[:, k4 * 128:(k4 + 1) * 128]) for k4 in range(4)]
                        P.op("pe", TR(trs, identb), reads=[rOAs, self.r_conb], writes=[rTRb])
                        P.op("dve", I("tensor_copy", out=OAT[:, :, cs_], in_=TRb[:, 0:512].rearrange("p (k n) -> p k n", k=4)), reads=[rTRb], writes=[rOAT])
                    for m in range(KC):
                        pt, rpt = pj()
                        mms = []
                        for k in range(KC):
                            rhs = OAT[:, k, :] if k < 4 else OB[bq][:, k - 4, :]
                            mms.append(dict(out=pt[:, 0:N], lhsT=WO[:, k, m * 128:(m + 1) * 128], rhs=rhs, start=(k == 0), stop=(k == KC - 1)))
                        P.op("pe", MM(mms), reads=[rW, rOAT, rOB[bq]], writes=[rpt])
                        P.op("dve", I("scalar_tensor_tensor", out=XR[:, m, :], in0=pt[:, 0:N], scalar=self.modG(l, 0, cond, m), in1=XR[:, m, :], op0=ALU.mult, op1=ALU.add),
                             reads=[rpt, rXR, self.r_mod], writes=[rXR])
                    P.dma("sp", self.XT[:, :, t0:t0 + N], XR[:], reads=[rXR], writes=[self.r_xt])

                for ti in range(ntile + 1):
                    if ti < ntile and self.dbg_cut > 3:
                        phase_a(ti)
                    if ti >= 1 and self.dbg_cut > 4:
                        phase_b(ti - 1)
                if not sample:
                    for p in range(2):
                        for hh in range(2):
                            P.dma("sp", self.ngla[pb, j, 0, 2 * p + hh], SF[hh * 64:(hh + 1) * 64, p, hh * 128:(hh + 1) * 128], reads=[rSF], writes=[self.r_out])

    def odd_stage(self, l):
        P = self.P
        j = l // 2
        NMAX = 512
        WMAX = NMAX + 3
        with P.stage():
            WIN = P.sb("o_win", [128, KC, 2048], BF16)
            WOUT = P.sb("o_wout", [128, KC, D], BF16)
            BD = P.sb("o_bd", [128, 4, KC, 128], BF16)
            rW = Res()
            P.dma("pool", WIN[:], self.od_w_in[j].rearrange("(k p) n -> p k n", p=128), writes=[rW])
            P.dma("pool", WOUT[:], self.od_w_out[j].rearrange("(k p) n -> p k n", p=128), writes=[rW])
            for di in range(2):
                for ai in range(2):
                    P.dma("pool", BD[:, di * 2 + ai].rearrange("p k n -> p (k n)"), self.od_bd[j, di, ai], writes=[rW])
            CL = P.sb("o_cl", [128, 2, 2, KC], F32)
            rCL = Res()
            ol, _ = VEC_OFF["od_lambda"]
            lam = self.VEC[:, ol + j * 16: ol + j * 16 + 16]
            cl0 = CL[:, 0].rearrange("p d k -> p (d k)")
            cl1 = CL[:, 1].rearrange("p d k -> p (d k)")
            P.op("act", I("activation", out=cl0, in_=lam, func=AF.Exp, scale=-1.0), reads=[self.r_vec], writes=[rCL])
            P.op("act", I("activation", out=cl0, in_=cl0, func=AF.Ln, bias=self.ONEB[:, 0:1], scale=1.0), reads=[rCL], writes=[rCL])
            P.op("dve", I("tensor_scalar", out=cl1, in0=cl0, scalar1=-8.0, scalar2=None, op0=ALU.mult), reads=[rCL], writes=[rCL])
            P.op("dve", I("tensor_scalar", out=cl0, in0=cl0, scalar1=-4.0, scalar2=None, op0=ALU.mult), reads=[rCL], writes=[rCL])
            HBA = P.sb("o_hba", [128, 2, 16], F32)
            rHBA = Res()
            oba_, _ = VEC_OFF["od_b_a"]
            obi_, _ = VEC_OFF["od_b_i"]
            P.op("dve", I("tensor_scalar", out=HBA[:, 0, :], in0=self.VEC[:, oba_ + j * 16: oba_ + j * 16 + 16], scalar1=0.5, scalar2=None, op0=ALU.mult),
                 reads=[self.r_vec], writes=[rHBA])
            P.op("dve", I("tensor_scalar", out=HBA[:, 1, :], in0=self.VEC[:, obi_ + j * 16: obi_ + j * 16 + 16], scalar1=0.5, scalar2=None, op0=ALU.mult),
                 reads=[self.r_vec], writes=[rHBA])

            XTt = P.sb("o_xt", [128, KC, WMAX + 1], F32)
            HT = P.sb("o_ht", [128, KC, WMAX + 1], BF16)
            GYt = P.sb("o_gy", [128, KC, NMAX], BF16)
            U = [P.sb(f"o_u{i}", [128, WMAX + 1], F32) for i in range(2)]
            XCt = P.sb("o_xc", [128, KC, NMAX], F32)
            XCb = P.sb("o_xcb", [128, KC, NMAX], BF16)
            HFt = P.sb("o_hf", [128, KC, NMAX], F32)
            HBt = P.sb("o_hb", [128, KC, NMAX], F32)
            Zt = P.sb("o_z", [128, KC, NMAX], BF16)
            RSTD = P.sb("o_rstd", [128, WMAX + 1], F32)
            CAR = P.sb("o_car", [128, KC], F32)
            GG = 4
            Rr = [P.sb(f"o_r{i}", [128, NMAX], F32) for i in range(GG)]
            Ii = [P.sb(f"o_i{i}", [128, NMAX], F32) for i in range(GG)]
            Aa = [P.sb(f"o_a{i}", [128, NMAX], F32) for i in range(GG)]
            Mm = [P.sb(f"o_m{i}", [128, NMAX], F32) for i in range(GG)]
            rX, rH_, rGY, rXC_, rXCb_, rHF_, rHB_, rZ_, rRSTD, rCAR = (Res() for _ in range(10))
            rH = RG(Res() for _ in range(KC))
            rXC = RG(Res() for _ in range(KC))
            rXCb = RG(Res() for _ in range(KC))
            rHF = RG(Res() for _ in range(KC))
            rHB = RG(Res() for _ in range(KC))
            rZ = RG(Res() for _ in range(KC))
            rU = [Res() for _ in range(2)]
            rR = [Res() for _ in range(4)]
            rI = [Res() for _ in range(4)]
            rA = [Res() for _ in range(4)]
            rM = [Res() for _ in range(4)]
            PA = [P.ps(f"o_pa{i}", [128, 512], F32) for i in range(2)]
            PBk = [P.ps(f"o_pb{i}", [128, 512], F32) for i in range(2)]
            PHh = [P.ps(f"o_ph{i}", [128, 512], F32) for i in range(2)]
            PD = [P.ps(f"o_pd{i}", [128, 512], F32) for i in range(2)]
            rPA = [PRes() for _ in range(2)]
            rPB = [PRes() for _ in range(2)]
            rPH = [PRes() for _ in range(2)]
            rPD = [PRes() for _ in range(2)]
            ocw, _ = VEC_OFF["od_conv_w"]
            ocb, _ = VEC_OFF["od_conv_b"]
            oba, _ = VEC_OFF["od_b_a"]
            obi, _ = VEC_OFF["od_b_i"]
            ost, _ = VEC_OFF["st_rg"]

            def vcol(o):
                return self.VEC[:, o:o + 1]

            cwt = lambda tap, k: vcol(ocw + (j * 4 + tap) * 8 + k)
            cbt = lambda k: vcol(ocb + j * 8 + k)
            bat = lambda di, k: vcol(oba + (j * 2 + di) * 8 + k)
            bit = lambda di, k: vcol(obi + (j * 2 + di) * 8 + k)
            h0t = lambda di, k: vcol(ost + (j * 2 + di) * 8 + k)
            cnt = [0]

            def gates_group(di, N, ks, HO, rHO, inits):
                for g, k in enumerate(ks):
                    b = cnt[0] % 2
                    cnt[0] += 1
                    P.op("pe", MM([dict(out=PA[b][:, 0:N], lhsT=BD[:, di * 2 + 0, k, :], rhs=XCb[:, k, 0:N], start=True, stop=True)]),
                         reads=[rW, rXCb[k]], writes=[rPA[b]])
                    P.op("pe", MM([dict(out=PBk[b][:, 0:N], lhsT=BD[:, di * 2 + 1, k, :], rhs=XCb[:, k, 0:N], start=True, stop=True)]),
                         reads=[rW, rXCb[k]], writes=[rPB[b]])
                    P.op("act", I("activation", out=Rr[g][:, 0:N], in_=PA[b][:, 0:N], func=AF.Tanh, bias=HBA[:, 0, di * 8 + k: di * 8 + k + 1], scale=0.5),
                         reads=[rPA[b], rHBA], writes=[rR[g]])
                    P.op("act", I("activation", out=Ii[g][:, 0:N], in_=PBk[b][:, 0:N], func=AF.Tanh, bias=HBA[:, 1, di * 8 + k: di * 8 + k + 1], scale=0.5),
                         reads=[rPB[b], rHBA], writes=[rI[g]])
                for g, k in enumerate(ks):
                    P.op("act", I("activation", out=Aa[g][:, 0:N], in_=Rr[g][:, 0:N], func=AF.Exp, scale=CL[:, 0, di, k:k + 1], bias=CL[:, 0, di, k:k + 1]),
                         reads=[rR[g], rCL], writes=[rA[g]])
                    P.op("act", I("activation", out=Mm[g][:, 0:N], in_=Rr[g][:, 0:N], func=AF.Exp, scale=CL[:, 1, di, k:k + 1], bias=CL[:, 1, di, k:k + 1]),
                         reads=[rR[g], rCL], writes=[rM[g]])
                    P.op("dve", I("scalar_tensor_tensor", out=Ii[g][:, 0:N], in0=Ii[g][:, 0:N], scalar=1.0, in1=XCt[:, k, 0:N], op0=ALU.add, op1=ALU.mult),
                         reads=[rI[g], rXC[k]], writes=[rI[g]])
                for g, k in enumerate(ks):
                    P.op("act", I("activation", out=Mm[g][:, 0:N], in_=Mm[g][:, 0:N], func=AF.Sqrt, bias=self.QTRB[:, 0:1], scale=-0.25),
                         reads=[rM[g]], writes=[rM[g]])
                    P.op("dve", I("tensor_tensor", out=Mm[g][:, 0:N], in0=Mm[g][:, 0:N], in1=Ii[g][:, 0:N], op=ALU.mult),
                         reads=[rM[g], rI[g]], writes=[rM[g]])
                    init_ap, rinit = inits[g]
                    if di == 0:
                        P.op("dve", I("tensor_tensor_scan", out=HO[:, k, 0:N], data0=Aa[g][:, 0:N], data1=Mm[g][:, 0:N], initial=init_ap,
                                      op0=ALU.mult, op1=ALU.add), reads=[rA[g], rM[g]] + rinit, writes=[rHO[k]])
                    else:
                        P.op("dve", I("tensor_tensor_scan", out=HO[:, k, 0:N][:, ::-1],
                                      data0=Aa[g][:, 0:N][:, ::-1], data1=Mm[g][:, 0:N][:, ::-1], initial=init_ap,
                                      op0=ALU.mult, op1=ALU.add), reads=[rA[g], rM[g]] + rinit, writes=[rHO[k]])

            seqs = [(0, TS, 1, None)] + [(TS + b * SEQ, SEQ, 0, b) for b in range(NPC)]
            for (s0, T, cond, pb) in seqs:
                N = min(NMAX, T)
                ntile = T // N
                W = N + 3
                for ti in range(ntile):
                    t0 = s0 + ti * N
                    hl = ti > 0
                    hr = ti < ntile - 1
                    lo = t0 - 2 if hl else t0
                    hi = t0 + N + 1 if hr else t0 + N
                    dlo = 0 if hl else 2
                    P.dma("sp", XTt[:, :, dlo:dlo + (hi - lo)], self.XT[:, :, lo:hi], reads=[self.r_xt], writes=[rX])
                    if not hl:
                        P.op("pool", I("memset", XTt[:, :, 0:2], 0.0), writes=[rX])
                    if not hr:
                        P.op("pool", I("memset", XTt[:, :, W - 1:W], 0.0), writes=[rX])
                    tl = dict(sq=HBt, rsq=rHB, rstd=RSTD, rrstd=rRSTD)
                    self.norm_mod_wide(XTt, rX, HT, rH, 0, W, l, 0, cond, tl, PD, rPD, sqw=NMAX)
                    if not hl:
                        P.op("pool", I("memset", HT[:, :, 0:2], 0.0), writes=[rH])
                    if not hr:
                        P.op("pool", I("memset", HT[:, :, W - 1:W], 0.0), writes=[rH])
                    for k in range(KC):
                        b = cnt[0] % 2
                        cnt[0] += 1
                        mms = [dict(out=PA[b][:, 0:N], lhsT=WIN[:, kk, k * 128:(k + 1) * 128], rhs=HT[:, kk, 2:N + 2],
                                    start=(kk == 0), stop=(kk == KC - 1)) for kk in range(KC)]
                        P.op("pe", MM(mms), reads=[rW, rH], writes=[rPA[b]])
                        P.op("act", I("activation", out=GYt[:, k, 0:N], in_=PA[b][:, 0:N], func=AF.Gelu), reads=[rPA[b]], writes=[rGY])
                    P.dma("sp", self.GY[:, :, t0:t0 + N], GYt[:, :, 0:N], reads=[rGY], writes=[self.r_gy])
                    for k in range(KC):
                        b = cnt[0] % 2
                        cnt[0] += 1
                        Wm = min(W, 512)
                        mms = [dict(out=PBk[b][:, 0:Wm], lhsT=WIN[:, kk, D + k * 128:D + (k + 1) * 128], rhs=HT[:, kk, 0:Wm],
                                    start=(kk == 0), stop=(kk == KC - 1)) for kk in range(KC)]
                        P.op("pe", MM(mms), reads=[rW, rH], writes=[rPB[b]])
                        P.op("act", I("copy", out=U[b][:, 0:Wm], in_=PBk[b][:, 0:Wm]), reads=[rPB[b]], writes=[rU[b]])
                        if W > 512:
                            mms = [dict(out=PHh[b][:, 0:W - 512], lhsT=WIN[:, kk, D + k * 128:D + (k + 1) * 128], rhs=HT[:, kk, 512:W],
                                        start=(kk == 0), stop=(kk == KC - 1)) for kk in range(KC)]
                            P.op("pe", MM(mms), reads=[rW, rH], writes=[rPH[b]])
                            P.op("act", I("copy", out=U[b][:, 512:W], in_=PHh[b][:, 0:W - 512]), reads=[rPH[b]], writes=[rU[b]])
                        P.op("dve", I("tensor_scalar", out=XCt[:, k, 0:N], in0=U[b][:, 0:N], scalar1=cwt(0, k), scalar2=cbt(k), op0=ALU.mult, op1=ALU.add),
                             reads=[rU[b], self.r_vec], writes=[rXC[k]])
                        for tap in (1, 2, 3):
                            P.op("dve", I("scalar_tensor_tensor", out=XCt[:, k, 0:N], in0=U[b][:, tap:N + tap], scalar=cwt(tap, k), in1=XCt[:, k, 0:N],
                                          op0=ALU.mult, op1=ALU.add), reads=[rU[b], rXC[k], self.r_vec], writes=[rXC[k]])
                        P.op("pool", I("tensor_copy", out=XCb[:, k, 0:N], in_=XCt[:, k, 0:N]), reads=[rXC[k]], writes=[rXCb[k]])
                    P.dma("sp", self.XC[:, :, t0:t0 + N], XCt[:, :, 0:N], reads=[rXC], writes=[self.r_xc])
                    for k0 in range(0, KC, GG):
                        inits = []
                        for k in range(k0, k0 + GG):
                            if ti == 0:
                                inits.append((h0t(0, k), [self.r_vec]) if pb is None else (0.0, []))
                            else:
                                inits.append((CAR[:, k:k + 1], [rCAR]))
                        gates_group(0, N, list(range(k0, k0 + GG)), HFt, rHF, inits)
                    P.op("pool", I("tensor_copy", out=CAR[:], in_=HFt[:, :, N - 1]), reads=[rHF], writes=[rCAR])
                    P.dma("sp", self.HF[:, :, t0:t0 + N], HFt[:, :, 0:N], reads=[rHF], writes=[self.r_hf])
                    if pb is not None and ti == ntile - 1:
                        c0 = ((pb * 2 + j) * 2 + 0) * 8
                        P.op("pool", I("tensor_copy", out=self.RGO[:, c0:c0 + 8], in_=HFt[:, :, N - 1]), reads=[rHF], writes=[self.r_rgo])
                for ti in range(ntile - 1, -1, -1):
                    t0 = s0 + ti * N
                    P.dma("sp", XCt[:, :, 0:N], self.XC[:, :, t0:t0 + N], reads=[self.r_xc], writes=[rXC])
                    for k in range(KC):
                        P.op("pool", I("tensor_copy", out=XCb[:, k, 0:N], in_=XCt[:, k, 0:N]), reads=[rXC[k]], writes=[rXCb[k]])
                    P.dma("sp", HFt[:, :, 0:N], self.HF[:, :, t0:t0 + N], reads=[self.r_hf], writes=[rHF])
                    P.dma("sp", GYt[:, :, 0:N], self.GY[:, :, t0:t0 + N], reads=[self.r_gy], writes=[rGY])
                    P.dma("sp", XTt[:, :, 0:N], self.XT[:, :, t0:t0 + N], reads=[self.r_xt], writes=[rX])
                    for k0 in range(0, KC, GG):
                        inits = []
                        for k in range(k0, k0 + GG):
                            if ti == ntile - 1:
                                inits.append((h0t(1, k), [self.r_vec]) if pb is None else (0.0, []))
                            else:
                                inits.append((CAR[:, k:k + 1], [rCAR]))
                        gates_group(1, N, list(range(k0, k0 + GG)), HBt, rHB, inits)
                    P.op("pool", I("tensor_copy", out=CAR[:], in_=HBt[:, :, 0]), reads=[rHB], writes=[rCAR])
                    if pb is not None and ti == 0:
                        c0 = ((pb * 2 + j) * 2 + 1) * 8
                        P.op("pool", I("tensor_copy", out=self.RGO[:, c0:c0 + 8], in_=HBt[:, :, 0]), reads=[rHB], writes=[self.r_rgo])
                    for k in range(KC):
                        P.op("dve", I("tensor_tensor", out=HFt[:, k, 0:N], in0=HFt[:, k, 0:N], in1=HBt[:, k, 0:N], op=ALU.add),
                             reads=[rHB[k], rHF[k]], writes=[rHF[k]])
                        P.op("dve", I("tensor_tensor", out=Zt[:, k, 0:N], in0=HFt[:, k, 0:N], in1=GYt[:, k, 0:N], op=ALU.mult),
                             reads=[rHF[k], rGY], writes=[rZ[k]])
                    for m in range(KC):
                        b = cnt[0] % 2
                        cnt[0] += 1
                        mms = [dict(out=PD[b][:, 0:N], lhsT=WOUT[:, kk, m * 128:(m + 1) * 128], rhs=Zt[:, kk, 0:N],
                                    start=(kk == 0), stop=(kk == KC - 1)) for kk in range(KC)]
                        P.op("pe", MM(mms), reads=[rW, rZ], writes=[rPD[b]])
                        xs = XTt[:, m, 0:N]
                        P.op("dve", I("scalar_tensor_tensor", out=xs, in0=PD[b][:, 0:N], scalar=self.modG(l, 0, cond, m), in1=xs, op0=ALU.mult, op1=ALU.add),
                             reads=[rPD[b], rX, self.r_mod], writes=[rX])
                    P.dma("sp", self.XT[:, :, t0:t0 + N], XTt[:, :, 0:N], reads=[rX], writes=[self.r_xt])

    def final_stage(self):
        P = self.P
        with P.stage():
            XF = [P.sb(f"fin_x{i}", [128, KC, 128], F32) for i in range(2)]
            SQ = [P.sb(f"fin_sq{i}", [128, KC, 128], F32) for i in range(2)]
            RS = [P.sb(f"fin_rs{i}", [128, 128], F32) for i in range(2)]
            YB = [P.sb(f"fin_y{i}", [128, D], F32) for i in range(2)]
            rXF = [Res() for _ in range(2)]
            rSQ = [Res() for _ in range(2)]
            rRS = [Res() for _ in range(2)]
            rYB = [Res() for _ in range(2)]
            PSS = [P.ps(f"fin_ps{i}", [128, 512], F32) for i in range(2)]
            rPSS = [PRes() for _ in range(2)]
            TPp = [P.ps(f"fin_tp{i}", [128, 512], F32) for i in range(4)]
            rTP = [PRes() for _ in range(4)]
            ident = self.CON[:, C_IDENT:C_IDENT + 128]
            ones = self.CON[:, C_ONES:C_ONES + 128]
            ofn, _ = VEC_OFF["final_norm"]
            fn_b = self.VEC[:, ofn:ofn + 8].unsqueeze(2).to_broadcast([128, KC, 128])
            for blk in range(TT // 128):
                b = blk % 2
                P.dma("sp", XF[b][:], self.XT[:, :, blk * 128:(blk + 1) * 128], reads=[self.r_xt], writes=[rXF[b]])
                P.op("act", I("activation", out=SQ[b][:], in_=XF[b][:], func=AF.Square), reads=[rXF[b]], writes=[rSQ[b]])
                mms = [dict(out=PSS[b][:, 0:128], lhsT=ones, rhs=SQ[b][:, k, :], start=(k == 0), stop=(k == KC - 1)) for k in range(KC)]
                P.op("pe", MM(mms), reads=[rSQ[b], self.r_con], writes=[rPSS[b]])
                P.op("act", I("activation", out=RS[b][:], in_=PSS[b][:, 0:128], func=AF.Sqrt, bias=self.EPSB[:, 0:1], scale=1.0),
                     reads=[rPSS[b]], writes=[rRS[b]])
                P.op("dve", I("reciprocal", out=RS[b][:], in_=RS[b][:]), reads=[rRS[b]], writes=[rRS[b]])
                P.op("dve", I("tensor_tensor", out=SQ[b][:], in0=XF[b][:], in1=RS[b][:].unsqueeze(1).to_broadcast([128, KC, 128]), op=ALU.mult),
                     reads=[rXF[b], rRS[b], rSQ[b]], writes=[rSQ[b]])
                P.op("pool", I("tensor_tensor", out=SQ[b][:], in0=SQ[b][:], in1=fn_b, op=ALU.mult),
                     reads=[rSQ[b], self.r_vec], writes=[rSQ[b]])
                for hf in range(2):
                    tp, rtp = TPp[b * 2 + hf], rTP[b * 2 + hf]
                    trs = [(tp[:, kk * 128:(kk + 1) * 128], SQ[b][:, hf * 4 + kk, :]) for kk in range(4)]
                    P.op("pe", TR(trs, ident), reads=[rSQ[b], self.r_con], writes=[rtp])
                    if hf == 0:
                        P.op("act", I("copy", out=YB[b][:, 0:512], in_=tp[:]), reads=[rtp], writes=[rYB[b]])
                    else:
                        P.op("dve", I("tensor_copy", out=YB[b][:, 512:1024], in_=tp[:]), reads=[rtp], writes=[rYB[b]])
                P.dma("sp", self.y[blk * 128:(blk + 1) * 128, :], YB[b][:], reads=[rYB[b]], writes=[self.r_y])


    def out_stage(self):
        P = self.P
        with P.stage():
            ident = self.CON[:, C_IDENT:C_IDENT + 128]
            TPp = P.ps("os_tp", [128, 512], F32)
            rTP = PRes()
            OB = P.sb("os_ob", [128, 128], F32)
            rOB = Res()
            ncol = NPC * 2 * 2 * KC
            for g in range((ncol + 127) // 128):
                w = min(128, ncol - g * 128)
                P.op("pe", TR([(TPp[0:w, 0:128], self.RGO[:, g * 128:g * 128 + w])], ident), reads=[self.r_rgo, self.r_con], writes=[rTP])
                P.op("dve", I("tensor_copy", out=OB[0:w, :], in_=TPp[0:w, 0:128]), reads=[rTP], writes=[rOB])
                P.dma("sp", self.nrg[g * 128:g * 128 + w, :], OB[0:w, :], reads=[rOB], writes=[self.r_out])

    def build(self):
        P = self.P
        self.r_xt = Res("XT")
        self.r_y = Res("y")
        self.r_gy, self.r_xc, self.r_hf, self.r_rgo, self.r_out = Res(), Res(), Res(), Res(), Res()
        self.setup_globals()
        self.EPSB = P.sb("EPSB", [128, 1], F32, glob=True)
        P.op("pool", I("memset", self.EPSB[:], EPS), writes=[self.r_con])
        self.QTRB = P.sb("QTRB", [128, 1], F32, glob=True)
        P.op("pool", I("memset", self.QTRB[:], 0.25), writes=[self.r_con])
        self.ONEB = P.sb("ONEB", [128, 1], F32, glob=True)
        P.op("pool", I("memset", self.ONEB[:], 1.0), writes=[self.r_con])
        self.RGO = P.sb("RGO", [128, NPC * 2 * 2 * KC], F32, glob=True)
        P.op("pool", I("memset", self.RGO[:], 0.0), writes=[self.r_rgo])
        self.stage0()
        for l in range(self.layers):
            if self.do_mixer:
                if l % 2 == 1:
                    self.odd_stage(l)
                elif self.do_even:
                    self.even_stage(l)
            if self.do_ffn:
                self.ffn_stage(l)
        self.final_stage()
        self.out_stage()
        P.barrier(engines=["sp"])


_CACHE = {}


def pack_vec(inp, core):
    s = core % DEC_BATCH
    vp = VecPack()
    vp.add("cvec", np.stack([fm(inp["c_ctx"]), fm(inp["c"][s])], axis=-1))
    vp.add("b_ada", fm(inp["b_ada"]))
    vp.add("norm_mix", fm(inp["norm_mix"]))
    vp.add("norm_ffn", fm(inp["norm_ffn"]))
    vp.add("final_norm", fm(inp["final_norm"]))
    vp.add("ffn_conv_w", fm(inp["ffn_conv_w"]))
    vp.add("ffn_conv_b", fm(inp["ffn_conv_b"]))
    vp.add("od_conv_w", fm(inp["od_conv_w"]))
    vp.add("od_conv_b", fm(inp["od_conv_b"]))
    vp.add("od_b_a", fm(inp["od_b_a"]))
    vp.add("od_b_i", fm(inp["od_b_i"]))
    vp.add("od_lambda", fm(inp["od_lambda"]))
    vp.add("st_rg", fm(inp["state_rglru"][s]))
    vp.add("gla_norm", fm(inp["ev_gla_norm"]))
    vp.add("b_gate", np.stack([fm(inp["ev_b_gate_f"]), fm(inp["ev_b_gate_b"])], axis=2))
    vp.add("sink", np.broadcast_to(inp["ev_sink"].reshape(1, 16), (128, 16)))
    for n, c in VEC_SPEC:
        assert vp.off[n] == VEC_OFF[n], (n, vp.off[n], VEC_OFF[n])
    return vp.build()


def pack_shared(inp):
    sh = {}
    sh["consts"] = make_consts()
    sh["w_ada"] = np.ascontiguousarray(inp["w_ada"], np.float32)
    up = np.asarray(inp["ffn_w_up"], np.float32)
    upk = up.reshape(DEPTH, KC, 128, 2, NJ, 128)
    sh["ffn_up_t"] = np.ascontiguousarray(upk.transpose(0, 4, 2, 1, 3, 5)).reshape(DEPTH, NJ, 128, KC * 256)
    dn = np.asarray(inp["ffn_w_down"], np.float32)
    dnk = dn.reshape(DEPTH, NJ, 128, KC, 128)
    sh["ffn_dn_t"] = np.ascontiguousarray(dnk.transpose(0, 3, 2, 1, 4)).reshape(DEPTH, KC, 128, NJ * 128)
    wi = np.asarray(inp["ev_w_in"], np.float32)
    z16 = np.zeros((2, D, 16), np.float32)
    cols = [wi[:, :, 0:512],
            wi[:, :, 512:576], wi[:, :, 512:576], wi[:, :, 576:640], wi[:, :, 576:640],
            wi[:, :, 768:1024], wi[:, :, 1024:1280], wi[:, :, 1792:2304],
            wi[:, :, 2304:2320], z16, wi[:, :, 2320:2336], z16,
            wi[:, :, 512:640], wi[:, :, 640:768], wi[:, :, 1280:1792]]
    we = np.concatenate(cols, axis=2)
    assert we.shape[2] == 2624
    sh["w_even"] = np.ascontiguousarray(we.reshape(2, KC, 128, 2624).transpose(0, 2, 1, 3)).reshape(2, 128, KC * 2624)
    sh["ev_w_out"] = np.ascontiguousarray(inp["ev_w_out"], np.float32)
    wg = np.zeros((2, 64, 512), np.float32)
    wg[:, 0:16, 0:256] = inp["ev_w_gate_f"]
    wg[:, 32:48, 256:512] = inp["ev_w_gate_b"]
    sh["wg"] = wg
    sh["rope"] = make_rope()
    sh["od_w_in"] = np.ascontiguousarray(inp["od_w_in"], np.float32)
    sh["od_w_out"] = np.ascontiguousarray(inp["od_w_out"], np.float32)
    bd = np.zeros((2, 2, 2, 128, KC, 128), np.float32)
    for ai, nm in enumerate(("od_w_a", "od_w_i")):
        w = np.asarray(inp[nm], np.float32)
        for k in range(KC):
            for hb in range(2):
                bd[:, :, ai, hb * 64:(hb + 1) * 64, k, hb * 64:(hb + 1) * 64] = w[:, :, 2 * k + hb]
    sh["od_bd"] = bd.reshape(2, 2, 2, 128, KC * 128)
    return sh


def core_inputs(inp, sh, core):
    s = core % DEC_BATCH
    xin = np.concatenate([inp["x_sample"][s], inp["x_prompt"][core * NPC:(core + 1) * NPC].reshape(TP, D)], axis=0)
    m = dict(sh)
    m["xin"] = np.ascontiguousarray(xin, np.float32)
    m["vec"] = pack_vec(inp, core)
    m["cache_k"] = np.ascontiguousarray(inp["cache_attn_k"][s].reshape(2, 256, 128), np.float32)
    m["cache_v"] = np.ascontiguousarray(inp["cache_attn_v"][s].reshape(2, 256, 128), np.float32)
    m["state_gla"] = np.ascontiguousarray(inp["state_gla"][s], np.float32)
    return m


def get_builder(**kw):
    key = tuple(sorted(kw.items()))
    if key not in _CACHE:
        _CACHE[key] = Builder(**kw)
    return _CACHE[key]


CFG = {}


def set_cores(n):
    global N_CORES, NPC, TP, TT
    N_CORES = n
    NPC = BATCH // n
    TP = NPC * SEQ
    TT = TS + TP

CORE_OFF = [0]


def kernel(**inputs):
    inp = {k: np.asarray(v) for k, v in inputs.items()}
    cfg = dict(CFG)
    ncr = cfg.pop("n_cores", N_CORES)
    bld = Builder(**cfg)
    sh = pack_shared(inp)
    coff = cfg.pop("core_off", 0) if False else CORE_OFF[0]
    in_maps = [core_inputs(inp, sh, c + coff) for c in range(ncr)]
    res = run_bass_kernel_spmd(bld.nc, in_maps, core_ids=list(range(ncr)))
    R = list(res.results)
    while len(R) < N_CORES:
        R.append(R[0])
    y_prompt = np.stack([R[c]["y"][TS:].reshape(NPC, SEQ, D) for c in range(N_CORES)]).reshape(BATCH, SEQ, D)
    y_sample = np.stack([R[s]["y"][:TS] for s in range(DEC_BATCH)])
    nrg = np.stack([R[c]["nrg"].reshape(NPC, 2, 2, D) for c in range(N_CORES)]).reshape(BATCH, 2, 2, D)
    nk = np.stack([R[c]["nk"] for c in range(N_CORES)]).reshape(BATCH, 2, SEQ, 2, 64)
    nv = np.stack([R[c]["nv"] for c in range(N_CORES)]).reshape(BATCH, 2, SEQ, 2, 64)
    ngla = np.stack([R[c]["ngla"] for c in range(N_CORES)]).reshape(BATCH, 2, 2, 4, 64, 128)
    f = lambda a: np.ascontiguousarray(a, dtype=np.float32)
    return (f(y_prompt), f(y_sample), f(nk), f(nv), f(ngla), f(nrg))
```

```python
import contextlib
import numpy as np
import concourse.bass as bass
import concourse.mybir as mybir
from concourse.bass_utils import run_bass_kernel_spmd

F32 = mybir.dt.float32
BF16 = mybir.dt.bfloat16
AF = mybir.ActivationFunctionType
ALU = mybir.AluOpType

D = 1024
KC = 8
BATCH, SEQ = 32, 256
DEC_BATCH, DEC_SEQ = 4, 4096
DEPTH = 4
N_CORES = 8
NPC = BATCH // N_CORES
TS = DEC_SEQ
TP = NPC * SEQ
TT = TS + TP
D_FF = 2816
NJ = D_FF // 128
EPS = 1e-6

SAME_SYNC = True
RELAX = [10 ** 9]
PE_RELAX = [False]
DMA_K = 8


class Tok:
    __slots__ = ("sem", "semid", "val", "key")

    def __init__(self, sem, semid, val, key):
        self.sem, self.semid, self.val, self.key = sem, semid, val, key


class Res:
    __slots__ = ("name", "w", "r", "ex")

    def __init__(self, name="", ex=False):
        self.name = name
        self.w = None
        self.r = []
        self.ex = ex


def PRes():
    return Res("psum", ex=True)


class RG(list):
    pass


def _flat(lst):
    out = []
    for x in lst:
        if isinstance(x, RG):
            out.extend(x)
        else:
            out.append(x)
    return out


class Prog:
    ENG = ("pe", "dve", "act", "pool", "sp")

    def __init__(self, nc):
        self.nc = nc
        self.es = contextlib.ExitStack()
        self.eng = {"pe": nc.tensor, "dve": nc.vector, "act": nc.scalar, "pool": nc.gpsimd, "sp": nc.sync}
        self.sem = {}
        self.seq = {}
        self.waited = {}
        self._nid = 0
        for e in self.ENG:
            self.sem[e] = (self.es.enter_context(nc.semaphore("s_" + e)), self._sid())
            self.seq[e] = 0
        self.dsem = {}
        self.dman = {}
        for q in ("sp", "act", "pool"):
            self.dsem[q] = [(self.es.enter_context(nc.semaphore(f"d_{q}{i}")), self._sid()) for i in range(DMA_K)]
            self.dman[q] = 0
        self.stage_es = None
        self.n_inst = 0

    def _sid(self):
        self._nid += 1
        return self._nid

    def sb(self, name, shape, dtype, glob=False):
        es = self.es if (glob or self.stage_es is None) else self.stage_es
        self._nid += 1
        return es.enter_context(self.nc.sbuf_tensor(f"{name}_{self._nid}", list(shape), dtype))

    def ps(self, name, shape, dtype=F32):
        es = self.es if self.stage_es is None else self.stage_es
        self._nid += 1
        return es.enter_context(self.nc.psum_tensor(f"{name}_{self._nid}", list(shape), dtype))

    @contextlib.contextmanager
    def stage(self):
        assert self.stage_es is None
        self.stage_es = contextlib.ExitStack()
        try:
            yield
        finally:
            self.barrier()
            self.stage_es.close()
            self.stage_es = None

    def _deps(self, reads, writes):
        toks = []
        for r in reads:
            if r.w is not None:
                toks.append(r.w)
            if r.ex:
                toks.extend(r.r)
        for w in writes:
            if w.w is not None:
                toks.append(w.w)
            toks.extend(w.r)
        return toks

    def _need(self, e, toks, strict_same=False):
        need = {}
        for t in toks:
            if t.key == e and not strict_same:
                if (not SAME_SYNC) or (e == "pe" and PE_RELAX[0]) or (e != "pe" and (self.seq[e] - t.val) >= RELAX[0]):
                    continue
            if self.waited.get((e, t.semid), 0) >= t.val:
                continue
            if t.semid not in need or need[t.semid].val < t.val:
                need[t.semid] = t
        return list(need.values())

    def _emit_waits(self, e, need):
        for t in need:
            self.eng[e].wait_ge(t.sem, t.val)
            self.waited[(e, t.semid)] = t.val
            self.n_inst += 1

    def _record(self, tok, reads, writes):
        for r in reads:
            if r.ex:
                r.w = tok
                r.r = []
            else:
                r.r = [x for x in r.r if x.key != tok.key] + [tok]
        for w in writes:
            w.w = tok
            w.r = []

    def op(self, e, fn, reads=(), writes=()):
        reads, writes = _flat(reads), _flat(writes)
        need = self._need(e, self._deps(reads, writes))
        attach = need.pop() if need else None
        self._emit_waits(e, need)
        first_holder = []
        ins = fn(self.eng[e], first_holder)
        tgt = first_holder[0] if first_holder else ins
        if attach is not None:
            tgt._wait_ge(attach.sem, attach.val)
            self.waited[(e, attach.semid)] = attach.val
        self.seq[e] += 1
        sem, semid = self.sem[e]
        ins.then_inc(sem, 1)
        self.n_inst += 1
        tok = Tok(sem, semid, self.seq[e], e)
        self._record(tok, reads, writes)
        return tok

    def dma(self, q, out, in_, reads=(), writes=(), **kw):
        e = q
        reads, writes = _flat(reads), _flat(writes)
        need = self._need(e, self._deps(reads, writes), strict_same=True)
        n = self.dman[q]
        slot, rnd = n % DMA_K, n // DMA_K
        sem, semid = self.dsem[q][slot]
        if rnd > 0 and self.waited.get((e, semid), 0) < 16 * rnd:
            need.append(Tok(sem, semid, 16 * rnd, ("q", q, slot)))
        self._emit_waits(e, need)
        self.eng[e].dma_start(out=out, in_=in_, **kw).then_inc(sem, 16)
        self.n_inst += 1
        self.dman[q] = n + 1
        tok = Tok(sem, semid, 16 * (rnd + 1), ("q", q, slot))
        self._record(tok, reads, writes)
        return tok

    def all_tokens(self):
        toks = []
        for e in self.ENG:
            if self.seq[e] > 0:
                sem, semid = self.sem[e]
                toks.append(Tok(sem, semid, self.seq[e], e))
        for q in self.dsem:
            n = self.dman[q]
            for slot in range(DMA_K):
                cnt = (n - slot + DMA_K - 1) // DMA_K if n > slot else 0
                if cnt > 0:
                    sem, semid = self.dsem[q][slot]
                    toks.append(Tok(sem, semid, 16 * cnt, ("q", q, slot)))
        return toks

    def barrier(self, engines=None):
        toks = self.all_tokens()
        for e in (engines or self.ENG):
            need = []
            for t in toks:
                if t.key == e:
                    continue
                if self.waited.get((e, t.semid), 0) >= t.val:
                    continue
                need.append(t)
            self._emit_waits(e, need)


def I(method, *a, **k):
    def fn(eng, fh):
        return getattr(eng, method)(*a, **k)
    return fn


def MM(mms):
    def fn(eng, fh):
        ins = None
        for i, m in enumerate(mms):
            ins = eng.matmul(m["out"], lhsT=m["lhsT"], rhs=m["rhs"], start=m["start"], stop=m["stop"])
            if i == 0:
                fh.append(ins)
        return ins
    return fn


def TR(trs, ident):
    def fn(eng, fh):
        ins = None
        for i, (o, a) in enumerate(trs):
            ins = eng.transpose(out=o, in_=a, identity=ident)
            if i == 0:
                fh.append(ins)
        return ins
    return fn


def fm(v):
    v = np.asarray(v, np.float32)
    n = v.shape[-1] // 128
    a = v.reshape(v.shape[:-1] + (n, 128))
    return np.moveaxis(a, -1, 0)


class VecPack:
    def __init__(self):
        self.items = []
        self.off = {}
        self.n = 0

    def add(self, name, arr):
        arr = np.ascontiguousarray(arr, np.float32).reshape(128, -1)
        self.off[name] = (self.n, arr.shape[1])
        self.items.append(arr)
        self.n += arr.shape[1]

    def build(self):
        return np.ascontiguousarray(np.concatenate(self.items, axis=1))


VEC_SPEC = [
    ("cvec", 16), ("b_ada", 4 * 48), ("norm_mix", 32), ("norm_ffn", 32), ("final_norm", 8),
    ("ffn_conv_w", 4 * 3 * 44), ("ffn_conv_b", 4 * 44),
    ("od_conv_w", 2 * 4 * 8), ("od_conv_b", 16), ("od_b_a", 32), ("od_b_i", 32), ("od_lambda", 32),
    ("st_rg", 32), ("gla_norm", 2), ("b_gate", 2 * 2 * 2), ("sink", 16),
]
VEC_OFF = {}
_o = 0
for _n, _c in VEC_SPEC:
    VEC_OFF[_n] = (_o, _c)
    _o += _c
NV = _o

C_IDENT = 0
C_MF = 128
C_MB = 256
C_RST = 384
C_AMASK = 896
C_ONES = 1280
C_PERM = 1408
NCONST = 1536


def make_consts():
    c = np.zeros((128, NCONST), np.float32)
    c[:, C_IDENT:C_IDENT + 128] = np.eye(128, dtype=np.float32)
    j = np.arange(128)[:, None]
    i = np.arange(128)[None, :]
    c[:, C_MF:C_MF + 128] = (j <= i)
    c[:, C_MB:C_MB + 128] = (j >= i)
    t = np.arange(512)
    c[:, C_RST:C_RST + 512] = (t % 128 != 0)[None, :]
    qi = np.arange(128)[:, None]
    kj = np.arange(384)[None, :]
    rel = kj - 128 - qi
    c[:, C_AMASK:C_AMASK + 384] = np.where(np.abs(rel) <= 128, 0.0, -1e30)
    c[:, C_ONES:C_ONES + 128] = 1.0 / 1024.0
    for m in range(128):
        d = m % 64
        if (d % 32) < 16:
            c[m + 16, C_PERM + m] = -1.0
        else:
            c[m - 16, C_PERM + m] = 1.0
    return c


def make_rope():
    t = np.arange(DEC_SEQ)
    row = (t // 64).astype(np.float32)
    col = (t % 64).astype(np.float32)
    inv = np.power(np.float32(10000.0), -np.arange(16, dtype=np.float32) / np.float32(16.0)).astype(np.float32)
    ang_r = (row[None, :] * inv[:, None]).astype(np.float32)
    ang_c = (col[None, :] * inv[:, None]).astype(np.float32)
    tab = np.zeros((128, 2, DEC_SEQ), np.float32)
    for p in range(128):
        d = p % 64
        a = ang_r if d < 32 else ang_c
        f = d % 16
        tab[p, 0] = np.cos(a[f])
        tab[p, 1] = np.sin(a[f])
    return tab


class Builder:
    def __init__(self, layers=DEPTH, do_mixer=True, do_ffn=True, do_even=True, dbg_seqs=None, dbg_cut=99):
        self.dbg_seqs = dbg_seqs
        self.dbg_cut = dbg_cut
        self.layers = layers
        self.do_mixer = do_mixer
        self.do_ffn = do_ffn
        self.do_even = do_even
        nc = bass.Bass("TRN2", target_bir_lowering=False)
        self.nc = nc
        self.P = Prog(nc)
        dt = lambda name, shape, kind=None, dtype=F32: (
            nc.dram_tensor(name, list(shape), dtype, kind=kind) if kind else nc.dram_tensor(name, list(shape), dtype))
        self.xin = dt("xin", [TT, D], "ExternalInput").ap()
        self.vec_d = dt("vec", [128, NV], "ExternalInput").ap()
        self.const_d = dt("consts", [128, NCONST], "ExternalInput").ap()
        self.w_ada = dt("w_ada", [DEPTH, D, 6 * D], "ExternalInput").ap()
        self.ffn_up = dt("ffn_up_t", [DEPTH, NJ, 128, KC * 256], "ExternalInput").ap()
        self.ffn_dn = dt("ffn_dn_t", [DEPTH, KC, 128, NJ * 128], "ExternalInput").ap()
        self.od_w_in = dt("od_w_in", [2, D, 2 * D], "ExternalInput").ap()
        self.od_w_out = dt("od_w_out", [2, D, D], "ExternalInput").ap()
        self.od_bd = dt("od_bd", [2, 2, 2, 128, KC * 128], "ExternalInput").ap()
        self.w_even = dt("w_even", [2, 128, KC * 2624], "ExternalInput").ap()
        self.ev_w_out = dt("ev_w_out", [2, D, D], "ExternalInput").ap()
        self.wg = dt("wg", [2, 64, 512], "ExternalInput").ap()
        self.cache_k = dt("cache_k", [2, 256, 128], "ExternalInput").ap()
        self.cache_v = dt("cache_v", [2, 256, 128], "ExternalInput").ap()
        self.state_gla = dt("state_gla", [2, 2, 4, 64, 128], "ExternalInput").ap()
        self.rope = dt("rope", [128, 2, DEC_SEQ], "ExternalInput").ap()
        self.y = dt("y", [TT, D], "ExternalOutput").ap()
        self.nk = dt("nk", [NPC, 2, SEQ, 128], "ExternalOutput").ap()
        self.nv = dt("nv", [NPC, 2, SEQ, 128], "ExternalOutput").ap()
        self.ngla = dt("ngla", [NPC, 2, 2, 4, 64, 128], "ExternalOutput").ap()
        self.nrg = dt("nrg", [NPC * 2 * 2 * KC, 128], "ExternalOutput").ap()
        self.GY = dt("GY", [D, TT], dtype=BF16).ap().rearrange("(k p) t -> p k t", p=128)
        self.XC = dt("XC", [D, TT]).ap().rearrange("(k p) t -> p k t", p=128)
        self.HF = dt("HF", [D, TT]).ap().rearrange("(k p) t -> p k t", p=128)
        self.XT = dt("XT", [D, TT]).ap().rearrange("(k p) t -> p k t", p=128)
        self.build()

    def setup_globals(self):
        P = self.P
        self.VEC = P.sb("VEC", [128, NV], F32, glob=True)
        self.CON = P.sb("CON", [128, NCONST], F32, glob=True)
        self.CONB = P.sb("CONB", [128, 384], BF16, glob=True)
        self.MOD = P.sb("MOD", [128, DEPTH, 48, 2], F32, glob=True)
        self.AMF = P.sb("AMF", [128, DEPTH, 2, 2, 8], F32, glob=True)
        self.r_vec = Res("VEC")
        self.r_con = Res("CON")
        self.r_conb = Res("CONB")
        self.r_mod = Res("MOD")
        self.r_amf = Res("AMF")
        P.dma("sp", self.VEC[:], self.vec_d[:, :], writes=[self.r_vec])
        P.dma("sp", self.CON[:], self.const_d[:, :], writes=[self.r_con])
        P.op("dve", I("tensor_copy", out=self.CONB[:], in_=self.CON[:, 0:384]), reads=[self.r_con], writes=[self.r_conb])

    def vec(self, name, *idx_shape):
        o, n = VEC_OFF[name]
        return self.VEC[:, o:o + n]

    def stage0(self):
        P = self.P
        with P.stage():
            XA = [P.sb(f"s0_xa{i}", [128, D], F32) for i in range(3)]
            XB = [P.sb(f"s0_xb{i}", [128, KC, 128], F32) for i in range(2)]
            rXA = [Res() for _ in range(3)]
            rXB = [Res() for _ in range(2)]
            TPp = [P.ps(f"s0_tp{i}", [128, 512], F32) for i in range(4)]
            rTP = [PRes() for _ in range(4)]
            ident = self.CON[:, C_IDENT:C_IDENT + 128]
            SC = P.sb("s0_sc", [128, KC, 2], BF16)
            rSC = Res()
            WA = [P.sb(f"s0_wa{i}", [128, KC, 1024], BF16) for i in range(2)]
            rWA = [Res() for _ in range(2)]
            PM = P.ps("s0_pm", [128, 512], F32)
            rPM = PRes()
            o, n = VEC_OFF["cvec"]
            P.op("act", I("activation", out=SC[:].rearrange("p k c -> p (k c)"), in_=self.VEC[:, o:o + n], func=AF.Silu),
                 reads=[self.r_vec], writes=[rSC])
            nblk = TT // 128
            mod_jobs = [(l, pc) for l in range(DEPTH) for pc in range(6)]
            mj = 0

            def do_mod(l, pc, idx):
                wa, rwa = WA[idx % 2], rWA[idx % 2]
                wv = self.w_ada[l].rearrange("(k p) n -> p k n", p=128)
                P.dma("pool", wa[:], wv[:, :, pc * 1024:(pc + 1) * 1024], writes=[rwa])
                for jj in range(8):
                    mms = [dict(out=PM[:, jj * 2:jj * 2 + 2], lhsT=wa[:, k, jj * 128:(jj + 1) * 128], rhs=SC[:, k, :],
                                start=(k == 0), stop=(k == KC - 1)) for k in range(KC)]
                    P.op("pe", MM(mms), reads=[rwa, rSC], writes=[rPM])
                ob, _ = VEC_OFF["b_ada"]
                bsl = self.VEC[:, ob + l * 48 + pc * 8: ob + l * 48 + pc * 8 + 8]
                for c in range(2):
                    P.op("dve", I("tensor_tensor", out=self.MOD[:, l, pc * 8:(pc + 1) * 8, c], in0=PM[:, c:16:2], in1=bsl, op=ALU.add),
                         reads=[rPM, self.r_vec], writes=[self.r_mod])

            for blk in range(nblk):
                xa, rxa = XA[blk % 3], rXA[blk % 3]
                xb, rxb = XB[blk % 2], rXB[blk % 2]
                P.dma("sp", xa[:], self.xin[blk * 128:(blk + 1) * 128, :], writes=[rxa])
                for hf in range(2):
                    tp, rtp = TPp[(blk % 2) * 2 + hf], rTP[(blk % 2) * 2 + hf]
                    trs = [(tp[:, kk * 128:(kk + 1) * 128], xa[:, (hf * 4 + kk) * 128:(hf * 4 + kk + 1) * 128]) for kk in range(4)]
                    P.op("pe", TR(trs, ident), reads=[rxa, self.r_con], writes=[rtp])
                    dst = xb[:, hf * 4:(hf + 1) * 4, :]
                    src = tp[:].rearrange("p (a b) -> p a b", a=4)
                    if hf == 0:
                        P.op("act", I("copy", out=dst, in_=src), reads=[rtp], writes=[rxb])
                    else:
                        P.op("dve", I("tensor_copy", out=dst, in_=src), reads=[rtp], writes=[rxb])
                P.dma("sp", self.XT[:, :, blk * 128:(blk + 1) * 128], xb[:], reads=[rxb], writes=[self.r_xt])
                if mj < len(mod_jobs) and blk % 1 == 0:
                    do_mod(*mod_jobs[mj], mj)
                    mj += 1
            while mj < len(mod_jobs):
                do_mod(*mod_jobs[mj], mj)
                mj += 1
            for l in range(DEPTH):
                for mf, (nname, sc0) in enumerate((("norm_mix", 8), ("norm_ffn", 32))):
                    on, _ = VEC_OFF[nname]
                    for c in range(2):
                        P.op("dve", I("scalar_tensor_tensor", out=self.AMF[:, l, mf, c, :], in0=self.MOD[:, l, sc0:sc0 + 8, c], scalar=1.0,
                                      in1=self.VEC[:, on + l * 8: on + l * 8 + 8], op0=ALU.add, op1=ALU.mult),
                             reads=[self.r_mod, self.r_vec], writes=[self.r_amf])

    def modA(self, l, mf, c, k):
        return self.AMF[:, l, mf, c, k:k + 1]

    def modB(self, l, mf, c, k):
        base = 0 if mf == 0 else 24
        return self.MOD[:, l, base + k, c:c + 1]

    def modG(self, l, mf, c, k):
        base = 16 if mf == 0 else 40
        return self.MOD[:, l, base + k, c:c + 1]

    def norm_mod(self, xt, rx, ht, rh, W, l, mf, c, pfx, tiles):
        P = self.P
        sq, rsq = tiles["sq"], tiles["rsq"]
        ps, rps = tiles["ps"], tiles["rps"]
        rstd, rrstd = tiles["rstd"], tiles["rrstd"]
        ones = self.CON[:, C_ONES:C_ONES + 128]
        P.op("act", I("activation", out=sq[:, :, 0:W], in_=xt[:, :, 0:W], func=AF.Square), reads=[rx], writes=[rsq])
        mms = [dict(out=ps[:, 0:W], lhsT=ones, rhs=sq[:, k, 0:W], start=(k == 0), stop=(k == KC - 1)) for k in range(KC)]
        P.op("pe", MM(mms), reads=[rsq, self.r_con], writes=[rps])
        P.op("act", I("activation", out=rstd[:, 0:W], in_=ps[:, 0:W], func=AF.Sqrt, bias=self.EPSB[:, 0:1], scale=1.0),
             reads=[rps], writes=[rrstd])
        P.op("dve", I("reciprocal", out=rstd[:, 0:W], in_=rstd[:, 0:W]), reads=[rrstd], writes=[rrstd])
        P.op("dve", I("tensor_tensor", out=sq[:, :, 0:W], in0=xt[:, :, 0:W], in1=rstd[:, 0:W].unsqueeze(1).to_broadcast([128, KC, W]), op=ALU.mult),
             reads=[rx, rrstd, rsq], writes=[rsq])
        for k in range(KC):
            e = "act" if k % 2 == 0 else "pool"
            if e == "act":
                P.op("act", I("activation", out=ht[:, k, 0:W], in_=sq[:, k, 0:W], func=AF.Identity,
                              bias=self.modB(l, mf, c, k), scale=self.modA(l, mf, c, k)),
                     reads=[rsq, self.r_amf, self.r_mod], writes=[rh])
            else:
                P.op("pool", I("tensor_scalar", out=ht[:, k, 0:W], in0=sq[:, k, 0:W], scalar1=self.modA(l, mf, c, k),
                               scalar2=self.modB(l, mf, c, k), op0=ALU.mult, op1=ALU.add),
                     reads=[rsq, self.r_amf, self.r_mod], writes=[rh])

    def ffn_stage(self, l):
        P = self.P
        with P.stage():
            NT = 1024
            XTt = P.sb("f_xt", [128, KC, NT + 8], F32)
            rX = Res()
            SQ = P.sb("f_sq", [128, KC, 520], F32)
            rSQ = Res()
            RSTD = P.sb("f_rstd", [128, 520], F32)
            rRSTD = Res()
            HT = P.sb("f_ht", [128, KC, NT + 8], BF16)
            rH = RG(Res() for _ in range(KC))
            PB = P.sb("f_p", [128, NJ, NT], BF16)
            rPB = [Res() for _ in range(NJ)]
            UPW = [P.sb(f"f_upw{i}", [128, KC, 256], BF16) for i in range(4)]
            rUPW = [Res() for _ in range(4)]
            DNW = [P.sb(f"f_dnw{i}", [128, NJ, 128], BF16) for i in range(2)]
            rDNW = [Res() for _ in range(2)]
            UG = [P.sb(f"f_ug{i}", [128, 516], F32) for i in range(2)]
            UV = [P.sb(f"f_uv{i}", [128, 516], F32) for i in range(2)]
            rUG = [Res() for _ in range(2)]
            rUV = [Res() for _ in range(2)]
            AG = [P.sb(f"f_ag{i}", [128, 512], F32) for i in range(2)]
            AV = [P.sb(f"f_av{i}", [128, 512], F32) for i in range(2)]
            SG = [P.sb(f"f_sg{i}", [128, 512], F32) for i in range(2)]
            rAG = [Res() for _ in range(2)]
            rAV = [Res() for _ in range(2)]
            rSG = [Res() for _ in range(2)]
            SG2 = [P.sb(f"f_sg2{i}", [128, 512], F32) for i in range(2)]
            rSG2 = [Res() for _ in range(2)]
            PG = [P.ps(f"f_pg{i}", [128, 512], F32) for i in range(2)]
            PV = [P.ps(f"f_pv{i}", [128, 512], F32) for i in range(2)]
            PH = [P.ps(f"f_ph{i}", [128, 512], F32) for i in range(2)]
            PD = [P.ps(f"f_pd{i}", [128, 512], F32) for i in range(2)]
            rPG = [PRes() for _ in range(2)]
            rPV = [PRes() for _ in range(2)]
            rPH = [PRes() for _ in range(2)]
            rPD = [PRes() for _ in range(2)]
            ocw, _ = VEC_OFF["ffn_conv_w"]
            ocb, _ = VEC_OFF["ffn_conv_b"]

            def cw(tap, ch):
                o = ocw + (l * 3 + tap) * 44 + ch
                return self.VEC[:, o:o + 1]

            def cb(ch):
                o = ocb + l * 44 + ch
                return self.VEC[:, o:o + 1]

            tiles = []
            for i in range(TS // NT):
                units = []
                for u in range(NT // 512):
                    c0 = i * NT + u * 512
                    units.append((c0, 512, c0 > 0, c0 + 512 < TS))
                tiles.append((1, units))
            for g in range(NPC // 4):
                tiles.append((0, [(TS + (g * 4 + b) * SEQ, SEQ, False, False) for b in range(4)]))
            upi = 0
            dni = 0
            uix = 0
            n_up = len(tiles) * NJ
            n_dn = len(tiles) * KC
            issued = {"up": 0, "dn": 0}

            def ensure_up(i):
                while issued["up"] < min(i + 4, n_up):
                    q = issued["up"]
                    P.dma("pool", UPW[q % 4][:].rearrange("p k n -> p (k n)"), self.ffn_up[l, q % NJ], writes=[rUPW[q % 4]])
                    issued["up"] = q + 1

            def ensure_dn(i):
                while issued["dn"] < min(i + 2, n_dn):
                    q = issued["dn"]
                    P.dma("pool", DNW[q % 2][:].rearrange("p j n -> p (j n)"), self.ffn_dn[l, q % KC], writes=[rDNW[q % 2]])
                    issued["dn"] = q + 1

            for cond, units in tiles:
                offs = []
                off = 0
                for (c0, N, hl, hr) in units:
                    offs.append(off)
                    W = N + 2
                    lo = c0 - 1 if hl else c0
                    hi = c0 + N + 1 if hr else c0 + N
                    dlo = off + (0 if hl else 1)
                    P.dma("sp", XTt[:, :, dlo:dlo + (hi - lo)], self.XT[:, :, lo:hi], reads=[self.r_xt], writes=[rX])
                    if not hl:
                        P.op("pool", I("memset", XTt[:, :, off:off + 1], 0.0), writes=[rX])
                    if not hr:
                        P.op("pool", I("memset", XTt[:, :, off + W - 1:off + W], 0.0), writes=[rX])
                    off += W
                for ui, (c0, N, hl, hr) in enumerate(units):
                    W = N + 2
                    o = offs[ui]
                    tl = dict(sq=SQ, rsq=rSQ, ps=PD[0], rps=rPD[0], rstd=RSTD, rrstd=rRSTD)
                    self.norm_mod_wide(XTt, rX, HT, rH, o, W, l, 1, cond, tl, PD, rPD)
                    if not hl:
                        P.op("pool", I("memset", HT[:, :, o:o + 1], 0.0), writes=[rH])
                    if not hr:
                        P.op("pool", I("memset", HT[:, :, o + W - 1:o + W], 0.0), writes=[rH])
                for j in range(NJ):
                    ensure_up(upi)
                    ensure_dn(dni)
                    w, rw = UPW[upi % 4], rUPW[upi % 4]
                    upi += 1
                    pcol = 0
                    for ui, (c0, N, hl, hr) in enumerate(units):
                        W = N + 2
                        o = offs[ui]
                        b = uix % 2
                        uix += 1
                        Wm = min(W, 512)
                        for half, (pt, rpt) in enumerate(((PG[b], rPG[b]), (PV[b], rPV[b]))):
                            mms = [dict(out=pt[:, 0:Wm], lhsT=w[:, k, half * 128:(half + 1) * 128], rhs=HT[:, k, o:o + Wm],
                                        start=(k == 0), stop=(k == KC - 1)) for k in range(KC)]
                            P.op("pe", MM(mms), reads=[rw, rH], writes=[rpt])
                        if W > 512:
                            mms = []
                            for half in range(2):
                                mms += [dict(out=PH[b][:, half * 2:half * 2 + 2], lhsT=w[:, k, half * 128:(half + 1) * 128],
                                             rhs=HT[:, k, o + 512:o + W], start=(k == 0), stop=(k == KC - 1)) for k in range(KC)]
                            P.op("pe", MM(mms), reads=[rw, rH], writes=[rPH[b]])
                        P.op("act", I("copy", out=UG[b][:, 0:Wm], in_=PG[b][:, 0:Wm]), reads=[rPG[b]], writes=[rUG[b]])
                        P.op("act", I("copy", out=UV[b][:, 0:Wm], in_=PV[b][:, 0:Wm]), reads=[rPV[b]], writes=[rUV[b]])
                        if W > 512:
                            P.op("act", I("copy", out=UG[b][:, 512:514], in_=PH[b][:, 0:2]), reads=[rPH[b]], writes=[rUG[b]])
                            P.op("act", I("copy", out=UV[b][:, 512:514], in_=PH[b][:, 2:4]), reads=[rPH[b]], writes=[rUV[b]])
                        chg, chv = j, NJ + j
                        P.op("act", I("activation", out=AG[b][:, 0:N], in_=PG[b][:, 0:N], func=AF.Identity, scale=cw(0, chg), bias=cb(chg)),
                             reads=[rPG[b], self.r_vec], writes=[rAG[b]])
                        P.op("act", I("activation", out=AV[b][:, 0:N], in_=PV[b][:, 0:N], func=AF.Identity, scale=cw(0, chv), bias=cb(chv)),
                             reads=[rPV[b], self.r_vec], writes=[rAV[b]])
                        for tap in (1, 2):
                            P.op("dve", I("scalar_tensor_tensor", out=AG[b][:, 0:N], in0=UG[b][:, tap:N + tap], scalar=cw(tap, chg), in1=AG[b][:, 0:N],
                                          op0=ALU.mult, op1=ALU.add), reads=[rUG[b], rAG[b], self.r_vec], writes=[rAG[b]])
                            P.op("dve", I("scalar_tensor_tensor", out=AV[b][:, 0:N], in0=UV[b][:, tap:N + tap], scalar=cw(tap, chv), in1=AV[b][:, 0:N],
                                          op0=ALU.mult, op1=ALU.add), reads=[rUV[b], rAV[b], self.r_vec], writes=[rAV[b]])
                        P.op("act", I("activation", out=SG[b][:, 0:N], in_=AG[b][:, 0:N], func=AF.Silu), reads=[rAG[b]], writes=[rSG[b]])
                        P.op("pool", I("tensor_tensor", out=PB[:, j, pcol:pcol + N], in0=SG[b][:, 0:N], in1=AV[b][:, 0:N], op=ALU.mult),
                             reads=[rSG[b], rAV[b]], writes=[rPB[j]])
                        pcol += N
                for m in range(KC):
                    ensure_dn(dni)
                    ensure_up(upi)
                    w, rw = DNW[dni % 2], rDNW[dni % 2]
                    dni += 1
                    pcol = 0
                    for ui, (c0, N, hl, hr) in enumerate(units):
                        o = offs[ui]
                        b = uix % 2
                        uix += 1
                        mms = [dict(out=PD[b][:, 0:N], lhsT=w[:, j, :], rhs=PB[:, j, pcol:pcol + N], start=(j == 0), stop=(j == NJ - 1))
                               for j in range(NJ)]
                        P.op("pe", MM(mms), reads=[rw] + rPB, writes=[rPD[b]])
                        xs = XTt[:, m, o + 1:o + 1 + N]
                        P.op("dve", I("scalar_tensor_tensor", out=xs, in0=PD[b][:, 0:N], scalar=self.modG(l, 1, cond, m), in1=xs, op0=ALU.mult, op1=ALU.add),
                             reads=[rPD[b], rX, self.r_mod], writes=[rX])
                        pcol += N
                for ui, (c0, N, hl, hr) in enumerate(units):
                    o = offs[ui]
                    P.dma("sp", self.XT[:, :, c0:c0 + N], XTt[:, :, o + 1:o + 1 + N], reads=[rX], writes=[self.r_xt])

    def norm_mod_wide(self, XTt, rX, HT, rH, o, W, l, mf, cond, tl, PD, rPD, sqw=None):
        P = self.P
        SQ, rSQ, RSTD, rRSTD = tl["sq"], tl["rsq"], tl["rstd"], tl["rrstd"]
        ones = self.CON[:, C_ONES:C_ONES + 128]
        segs = [(0, min(W, 512))] + ([(512, W)] if W > 512 else [])
        for si, (a, bnd) in enumerate(segs):
            w = bnd - a
            ps, rps = PD[si % 2], rPD[si % 2]
            P.op("act", I("activation", out=SQ[:, :, 0:w], in_=XTt[:, :, o + a:o + bnd], func=AF.Square), reads=[rX], writes=[rSQ])
            mms = [dict(out=ps[:, 0:w], lhsT=ones, rhs=SQ[:, k, 0:w], start=(k == 0), stop=(k == KC - 1)) for k in range(KC)]
            P.op("pe", MM(mms), reads=[rSQ, self.r_con], writes=[rps])
            P.op("act", I("activation", out=RSTD[:, a:bnd], in_=ps[:, 0:w], func=AF.Sqrt, bias=self.EPSB[:, 0:1], scale=1.0),
                 reads=[rps], writes=[rRSTD])
            P.op("dve", I("reciprocal", out=RSTD[:, a:bnd], in_=RSTD[:, a:bnd]), reads=[rRSTD], writes=[rRSTD])
            P.op("dve", I("tensor_tensor", out=SQ[:, :, 0:w], in0=XTt[:, :, o + a:o + bnd], in1=RSTD[:, a:bnd].unsqueeze(1).to_broadcast([128, KC, w]), op=ALU.mult),
                 reads=[rX, rRSTD, rSQ], writes=[rSQ])
            for k in range(KC):
                if k % 2 == 0:
                    P.op("act", I("activation", out=HT[:, k, o + a:o + bnd], in_=SQ[:, k, 0:w], func=AF.Identity,
                                  bias=self.modB(l, mf, cond, k), scale=self.modA(l, mf, cond, k)),
                         reads=[rSQ, self.r_amf, self.r_mod], writes=[rH[k] if isinstance(rH, RG) else rH])
                else:
                    P.op("pool", I("tensor_scalar", out=HT[:, k, o + a:o + bnd], in0=SQ[:, k, 0:w], scalar1=self.modA(l, mf, cond, k),
                                   scalar2=self.modB(l, mf, cond, k), op0=ALU.mult, op1=ALU.add),
                         reads=[rSQ, self.r_amf, self.r_mod], writes=[rH[k] if isinstance(rH, RG) else rH])


    def even_stage(self, l):
        P = self.P
        j = l // 2
        N = 256
        NCH = N // 128
        QA, KD, QB, KB, GB, RR, TM = 0, 512, 768, 1024, 1280, 1792, 1856
        NCE = TM + 768
        with P.stage():
            WE = P.sb("e_we", [128, KC, NCE], BF16)
            WO = P.sb("e_wo", [128, KC, D], BF16)
            WG = P.sb("e_wg", [64, 512], BF16)
            rW = Res()
            P.dma("pool", WE[:].rearrange("p k n -> p (k n)"), self.w_even[j], writes=[rW])
            P.dma("pool", WO[:], self.ev_w_out[j].rearrange("(k p) n -> p k n", p=128), writes=[rW])
            P.dma("pool", WG[:], self.wg[j], writes=[rW])
            identf = self.CON[:, C_IDENT:C_IDENT + 128]
            identb = self.CONB[:, 0:128]
            ones = self.CON[:, C_ONES:C_ONES + 128]
            permf = self.CON[:, C_PERM:C_PERM + 128]
            PJ = [P.ps(f"e_pj{i}", [128, 512], F32) for i in range(3)]
            rPJ = [PRes() for _ in range(3)]
            ATp = [P.ps(f"e_at{i}", [128, 512], F32) for i in range(2)]
            rAT = [PRes() for _ in range(2)]
            OTp = P.ps("e_ot", [128, 512], F32)
            rOT = PRes()
            KVp = P.ps("e_kv", [128, 512], F32)
            rKV = PRes()
            TRb = P.ps("e_trb", [128, 1024], BF16)
            rTRb = PRes()
            pjc = [0]

            def pj():
                b = pjc[0] % 2
                pjc[0] += 1
                return PJ[b], rPJ[b]

            KCX = P.sb("e_kcx", [128, 2, 256], BF16)
            VCX = P.sb("e_vcx", [128, 2, 128], BF16)
            CKD = P.sb("e_ckd", [128, 2, 2, 2, 64], F32)
            rKCX, rVCX, rCKD = Res(), Res(), Res()
            ckv = self.cache_k[j].rearrange("(b t) (h d) -> t b h d", t=128, h=2)
            for dup in range(2):
                for blk in range(2):
                    P.dma("sp", CKD[:, blk, :, dup, :], ckv[:, blk], writes=[rCKD])
            for blk in range(2):
                for kvh in range(2):
                    pt, rpt = pj()
                    P.op("pe", TR([(pt[:, 0:128], CKD[:, blk, kvh].rearrange("t u d -> t (u d)"))], identf), reads=[rCKD, self.r_con], writes=[rpt])
                    P.op("act", I("copy", out=KCX[:, kvh, blk * 128:(blk + 1) * 128], in_=pt[:, 0:128]), reads=[rpt], writes=[rKCX])
            P.dma("pool", VCX[:], self.cache_v[j].rearrange("(b t) f -> t b f", t=128), writes=[rVCX])

            KR = P.sb("e_kr", [128, 2, TS], BF16)
            VA = P.sb("e_va", [128, TS // 128, 128], BF16)
            SBS = P.sb("e_sbs", [128, TS // 128, 2, 128], BF16)
            rKR = [Res() for _ in range(TS // N)]
            rVA = [Res() for _ in range(TS // N)]
            rSBS = [Res() for _ in range(TS // N)]
            SF = P.sb("e_sf", [128, 2, 256], F32)
            SB_ = P.sb("e_sb", [128, 2, 256], F32)
            SFb = P.sb("e_sfb", [128, 2, 256], BF16)
            rSF, rSB, rSFb = Res(), Res(), Res()
            XTt = P.sb("e_xt", [128, KC, N], F32)
            XR = P.sb("e_xr", [128, KC, N], F32)
            RSTD = P.sb("e_rstd", [128, N], F32)
            HT = P.sb("e_ht", [128, KC, N], BF16)
            ROP = P.sb("e_rop", [128, 2, N], F32)
            QS = P.sb("e_qs", [128, N], F32)
            T1 = P.sb("e_t1", [128, N], F32)
            T2 = P.sb("e_t2", [128, N], F32)
            QR = [P.sb(f"e_qr{i}", [128, 4, N], BF16) for i in range(2)]
            OB = [P.sb(f"e_ob{i}", [128, 4, N], BF16) for i in range(2)]
            OAT = P.sb("e_oat", [128, 4, N], BF16)
            GS = P.sb("e_gs", [128, 4, N], BF16)
            QBr = P.sb("e_qbr", [128, 2, N], F32)
            KBr = P.sb("e_kbr", [128, 2, N], F32)
            RT = P.sb("e_rt", [64, N], BF16)
            VB = P.sb("e_vb", [128, NCH, 512], BF16)
            KVO = P.sb("e_kvo", [128, NCH, 256], F32)
            Lg = P.sb("e_l", [128, 2, N], F32)
            CS = P.sb("e_cs", [128, 2, N], F32)
            E1 = P.sb("e_e1", [128, 2, N], F32)
            E2 = P.sb("e_e2", [128, 2, N], F32)
            E3 = P.sb("e_e3", [128, 2, N], F32)
            NB = P.sb("e_nb", [128, 2, NCH], F32)
            AD = P.sb("e_ad", [128, 2, NCH], F32)
            QT = [P.sb(f"e_qt{d}", [128, 2, N], BF16) for d in range(2)]
            KT = [P.sb(f"e_kt{d}", [128, 2, N], BF16) for d in range(2)]
            KHT = P.sb("e_kht", [128, 2, N], BF16)
            KH = P.sb("e_kh", [128, NCH, 2, 128], BF16)
            ATm = [P.sb(f"e_atm{d}", [128, 512], BF16) for d in range(2)]
            OSQ = P.sb("e_osq", [128, 512], F32)
            ORS = P.sb("e_ors", [128, 512], F32)
            OBF = P.sb("e_obf", [128, 512], F32)
            SM = [P.sb(f"e_sm{i}", [128, 640], F32) for i in range(2)]
            PX = [P.sb(f"e_px{i}", [128, 640], BF16) for i in range(2)]
            PTs = [P.sb(f"e_pts{i}", [128, 640], BF16) for i in range(2)]
            OAs = P.sb("e_oas", [128, 512], BF16)
            SMALL = [P.sb(f"e_small{i}", [128, 8], F32) for i in range(2)]
            RDEN = [P.sb(f"e_rden{i}", [128, 8], F32) for i in range(2)]
            rSMl = [Res() for _ in range(2)]
            rPXl = [Res() for _ in range(2)]
            rPTl = [Res() for _ in range(2)]
            rSMALLl = [Res() for _ in range(2)]
            rRDEN = [Res() for _ in range(2)]
            NBG = P.sb("e_nbg", [128, 4], F32)
            (rX, rXR, rRSTD, rH_, rROP, rQS, rT1, rT2, rOAT, rGS, rQBr, rKBr, rRT, rVB, rKVO, rL, rCS, rE1, rE2, rE3, rNB, rAD,
             rKHT, rKH, rOSQ, rORS, rOBF, rSM, rPX, rPTs, rOAs, rSMALL, rNBG) = (Res() for _ in range(33))
            rH = RG(Res() for _ in range(KC))
            rQR = [Res() for _ in range(2)]
            rOB = [Res() for _ in range(2)]
            rQT = [Res() for _ in range(2)]
            rKT = [Res() for _ in range(2)]
            rATm = [Res() for _ in range(2)]
            obg, _ = VEC_OFF["b_gate"]
            P.op("dve", I("tensor_scalar", out=NBG[:], in0=self.VEC[:, obg + j * 4: obg + j * 4 + 4], scalar1=-1.0, scalar2=None, op0=ALU.mult),
                 reads=[self.r_vec], writes=[rNBG])
            osk, _ = VEC_OFF["sink"]
            ogn, _ = VEC_OFF["gla_norm"]
            gn = self.VEC[:, ogn + j: ogn + j + 1]

            def proj_fm(col, M, dst_rows=128):
                pt, rpt = pj()
                mms = [dict(out=pt[0:M, 0:N], lhsT=WE[:, k, col:col + M], rhs=HT[:, k, 0:N], start=(k == 0), stop=(k == KC - 1)) for k in range(KC)]
                P.op("pe", MM(mms), reads=[rW, rH], writes=[rpt])
                return pt, rpt

            def gates(d, rev):
                for p in range(2):
                    pt, rpt = pj()
                    P.op("pe", MM([dict(out=pt[:, 0:N], lhsT=WG[0:64, d * 256 + p * 128: d * 256 + (p + 1) * 128], rhs=RT[0:64, 0:N], start=True, stop=True)]),
                         reads=[rW, rRT], writes=[rpt])
                    P.op("act", I("activation", out=Lg[:, p, :], in_=pt[:, 0:N], func=AF.Exp, bias=NBG[:, d * 2 + p: d * 2 + p + 1], scale=-1.0),
                         reads=[rpt, rNBG], writes=[rL])
                P.op("act", I("activation", out=Lg[:].rearrange("p a n -> p (a n)"), in_=Lg[:].rearrange("p a n -> p (a n)"), func=AF.Ln,
                              bias=self.ONEB[:, 0:1], scale=1.0), reads=[rL], writes=[rL])
                rst = self.CON[:, C_RST:C_RST + N]
                for p in range(2):
                    if not rev:
                        P.op("dve", I("tensor_tensor_scan", out=CS[:, p, :], data0=rst, data1=Lg[:, p, :], initial=0.0, op0=ALU.mult, op1=ALU.add),
                             reads=[rL, self.r_con], writes=[rCS])
                    else:
                        P.op("dve", I("tensor_tensor_scan", out=CS[:, p, :][:, ::-1], data0=rst, data1=Lg[:, p, :][:, ::-1], initial=0.0,
                                      op0=ALU.mult, op1=ALU.add), reads=[rL, self.r_con], writes=[rCS])

            def e3_and_decay(rev):
                first = 0 if rev else 127
                P.op("dve", I("tensor_scalar", out=NB[:], in0=CS[:, :, first::128], scalar1=-1.0 / 16.0, scalar2=None, op0=ALU.mult),
                     reads=[rCS], writes=[rNB])
                P.op("act", I("activation", out=AD[:].rearrange("p a c -> p (a c)"), in_=NB[:].rearrange("p a c -> p (a c)"), func=AF.Exp),
                     reads=[rNB], writes=[rAD])
                for p in range(2):
                    for c in range(NCH):
                        P.op("act", I("activation", out=E3[:, p, c * 128:(c + 1) * 128], in_=CS[:, p, c * 128:(c + 1) * 128], func=AF.Exp,
                                      bias=NB[:, p, c:c + 1], scale=1.0 / 16.0), reads=[rCS, rNB], writes=[rE3])
                P.op("dve", I("tensor_tensor", out=KHT[:], in0=KBr[:], in1=E3[:], op=ALU.mult), reads=[rKBr, rE3], writes=[rKHT])
                for c in range(NCH):
                    trs = [(TRb[:, (c * 2 + p) * 128:(c * 2 + p + 1) * 128], KHT[:, p, c * 128:(c + 1) * 128]) for p in range(2)]
                    P.op("pe", TR(trs, identb), reads=[rKHT, self.r_conb], writes=[rTRb])
                P.op("act", I("copy", out=KH[:].rearrange("t c p f -> t (c p f)"), in_=TRb[:, 0:NCH * 256]), reads=[rTRb], writes=[rKH])

            def kv_update(c, S, rS, extra_reads=()):
                mms = [dict(out=KVp[:, p * 256:(p + 1) * 256], lhsT=KH[:, c, p, :], rhs=VB[:, c, p * 256:(p + 1) * 256], start=True, stop=True) for p in range(2)]
                P.op("pe", MM(mms), reads=[rKH, rVB], writes=[rKV])
                for p in range(2):
                    P.op("dve", I("scalar_tensor_tensor", out=S[:, p, :], in0=S[:, p, :], scalar=AD[:, p, c:c + 1], in1=KVp[:, p * 256:(p + 1) * 256],
                                  op0=ALU.mult, op1=ALU.add), reads=[rKV, rAD, rS] + list(extra_reads), writes=[rS])

            def load_norm(t0, cond):
                P.dma("sp", XTt[:], self.XT[:, :, t0:t0 + N], reads=[self.r_xt], writes=[rX])
                tl = dict(sq=XR, rsq=rXR, rstd=RSTD, rrstd=rRSTD)
                self.norm_mod_wide(XTt, rX, HT, rH, 0, N, l, 0, cond, tl, [PJ[0], PJ[1]], [rPJ[0], rPJ[1]])

            def vb_proj():
                for c in range(NCH):
                    pt, rpt = pj()
                    mms = [dict(out=pt[:, 0:512], lhsT=HT[:, k, c * 128:(c + 1) * 128], rhs=WE[:, k, TM + 256:TM + 768], start=(k == 0), stop=(k == KC - 1))
                           for k in range(KC)]
                    P.op("pe", MM(mms), reads=[rW, rH], writes=[rpt])
                    P.op("act", I("copy", out=VB[:, c, :], in_=pt[:, 0:512]), reads=[rpt], writes=[rVB])

            def r_proj():
                pt, rpt = proj_fm(RR, 64)
                P.op("act", I("copy", out=RT[:, :], in_=pt[0:64, 0:N]), reads=[rpt], writes=[rRT])

            def kb_proj():
                for p in range(2):
                    pt, rpt = proj_fm(KB + p * 128, 128)
                    P.op("act", I("copy", out=KBr[:, p, :], in_=pt[:, 0:N]), reads=[rpt], writes=[rKBr])

            seqs = [(0, TS, 1, None)] + [(TS + b * SEQ, SEQ, 0, b) for b in range(NPC)]
            if self.dbg_seqs is not None:
                seqs = [seqs[i] for i in self.dbg_seqs]
            if self.dbg_cut <= 1:
                seqs = []
            for (s0, T, cond, pb) in seqs:
                ntile = T // N
                nblk = T // 128
                sample = pb is None
                for (S, rS, d) in ((SF, rSF, 0), (SB_, rSB, 1)):
                    P.op("pool", I("memset", S[:], 0.0), writes=[rS])
                    if sample:
                        for p in range(2):
                            for hh in range(2):
                                P.dma("sp", S[hh * 64:(hh + 1) * 64, p, hh * 128:(hh + 1) * 128], self.state_gla[j, d, 2 * p + hh], writes=[rS])
                for ti in range(ntile - 1, -1, -1) if self.dbg_cut > 2 else []:
                    t0 = s0 + ti * N
                    load_norm(t0, cond)
                    r_proj()
                    kb_proj()
                    vb_proj()
                    gates(1, True)
                    e3_and_decay(True)
                    for c in range(NCH - 1, -1, -1):
                        gc = ti * NCH + c
                        for p in range(2):
                            P.op("pool", I("tensor_copy", out=SBS[0:64, gc, p, :], in_=SB_[0:64, p, 0:128]), reads=[rSB], writes=[rSBS[ti]])
                            P.op("pool", I("tensor_copy", out=SBS[64:128, gc, p, :], in_=SB_[64:128, p, 128:256]), reads=[rSB], writes=[rSBS[ti]])
                        kv_update(c, SB_, rSB)
                if not sample:
                    for p in range(2):
                        for hh in range(2):
                            P.dma("sp", self.ngla[pb, j, 1, 2 * p + hh], SB_[hh * 64:(hh + 1) * 64, p, hh * 128:(hh + 1) * 128], reads=[rSB], writes=[self.r_out])
                P.op("act", I("copy", out=SFb[:], in_=SF[:]), reads=[rSF], writes=[rSFb])

                def phase_a(ti):
                    t0 = s0 + ti * N
                    bq = ti % 2
                    load_norm(t0, cond)
                    if sample:
                        P.dma("sp", ROP[:], self.rope[:, :, ti * N:(ti + 1) * N], writes=[rROP])

                    def rope_or_copy(pt, rpt, dst, rdst):
                        if not sample:
                            P.op("act", I("copy", out=dst, in_=pt[:, 0:N]), reads=[rpt], writes=[rdst])
                            return
                        P.op("act", I("copy", out=QS[:], in_=pt[:, 0:N]), reads=[rpt], writes=[rQS])
                        p2, rp2 = pj()
                        P.op("pe", MM([dict(out=p2[:, 0:N], lhsT=permf, rhs=QS[:], start=True, stop=True)]), reads=[rQS, self.r_con], writes=[rp2])
                        P.op("pool", I("tensor_tensor", out=T1[:], in0=QS[:], in1=ROP[:, 0, :], op=ALU.mult), reads=[rQS, rROP], writes=[rT1])
                        P.op("dve", I("tensor_tensor", out=T2[:], in0=p2[:, 0:N], in1=ROP[:, 1, :], op=ALU.mult), reads=[rp2, rROP], writes=[rT2])
                        P.op("pool", I("tensor_tensor", out=dst, in0=T1[:], in1=T2[:], op=ALU.add), reads=[rT1, rT2], writes=[rdst])

                    if self.dbg_cut <= 4.1:
                        return
                    for cc in range(4):
                        pt, rpt = proj_fm(QA + cc * 128, 128)
                        rope_or_copy(pt, rpt, QR[bq][:, cc, :], rQR[bq])
                    for kvh in range(2):
                        pt, rpt = proj_fm(KD + kvh * 128, 128)
                        rope_or_copy(pt, rpt, KR[:, kvh, ti * N:(ti + 1) * N], rKR[ti])
                    if self.dbg_cut <= 4.2:
                        return
                    for p in range(2):
                        pt, rpt = proj_fm(QB + p * 128, 128)
                        P.op("act", I("activation", out=QBr[:, p, :], in_=pt[:, 0:N], func=AF.Identity, scale=0.125), reads=[rpt], writes=[rQBr])
                    kb_proj()
                    for h in range(4):
                        pt, rpt = proj_fm(GB + h * 128, 128)
                        P.op("act", I("activation", out=GS[:, h, :], in_=pt[:, 0:N], func=AF.Silu), reads=[rpt], writes=[rGS])
                    r_proj()
                    vb_proj()
                    if self.dbg_cut <= 4.3:
                        return
                    for c in range(NCH):
                        pt, rpt = pj()
                        mms = [dict(out=pt[:, 0:256], lhsT=HT[:, k, c * 128:(c + 1) * 128], rhs=WE[:, k, TM:TM + 256], start=(k == 0), stop=(k == KC - 1))
                               for k in range(KC)]
                        P.op("pe", MM(mms), reads=[rW, rH], writes=[rpt])
                        P.op("act", I("copy", out=VA[:, ti * NCH + c, :], in_=pt[:, 128:256]), reads=[rpt], writes=[rVA[ti]])
                        if not sample:
                            P.op("dve", I("tensor_copy", out=KVO[:, c, :], in_=pt[:, 0:256]), reads=[rpt], writes=[rKVO])
                        if not sample and self.dbg_cut != 4.35:
                            P.dma("sp", self.nk[pb, j, (ti * NCH + c) * 128:(ti * NCH + c + 1) * 128, :], KVO[:, c, 0:128], reads=[rKVO], writes=[self.r_out])
                            P.dma("sp", self.nv[pb, j, (ti * NCH + c) * 128:(ti * NCH + c + 1) * 128, :], KVO[:, c, 128:256], reads=[rKVO], writes=[self.r_out])
                    if self.dbg_cut <= 4.4:
                        return
                    for d in range(2):
                        gates(d, d == 1)
                        flat = lambda t: t[:].rearrange("p a n -> p (a n)")
                        P.op("act", I("activation", out=flat(E1), in_=flat(CS), func=AF.Exp, scale=-1.0 / 16.0), reads=[rCS], writes=[rE1])
                        P.op("act", I("activation", out=flat(E2), in_=flat(CS), func=AF.Exp, scale=1.0 / 16.0), reads=[rCS], writes=[rE2])
                        P.op("dve", I("tensor_tensor", out=QT[d][:], in0=QBr[:], in1=E1[:], op=ALU.mult), reads=[rQBr, rE1], writes=[rQT[d]])
                        P.op("pool", I("tensor_tensor", out=KT[d][:], in0=KBr[:], in1=E2[:], op=ALU.mult), reads=[rKBr, rE2], writes=[rKT[d]])
                        if d == 0:
                            e3_and_decay(False)
                    if self.dbg_cut <= 4.5:
                        return
                    for c in range(NCH):
                        gc = ti * NCH + c
                        cs_ = slice(c * 128, (c + 1) * 128)
                        for hh in range(2):
                            mms = []
                            for d in range(2):
                                for p in range(2):
                                    mms.append(dict(out=ATp[hh][:, (d * 2 + p) * 128:(d * 2 + p + 1) * 128], lhsT=KT[d][hh * 64:(hh + 1) * 64, p, cs_],
                                                    rhs=QT[d][hh * 64:(hh + 1) * 64, p, cs_], start=True, stop=True))
                            P.op("pe", MM(mms), reads=[rKT[0], rKT[1], rQT[0], rQT[1]], writes=[rAT[hh]])
                            for d in range(2):
                                mo_ = C_MF if d == 0 else C_MB
                                mk = self.CON[:, mo_:mo_ + 128].unsqueeze(1).to_broadcast([128, 2, 128])
                                P.op("dve", I("tensor_tensor", out=ATm[hh][:, d * 256:(d + 1) * 256].rearrange("p (h n) -> p h n", h=2),
                                              in0=ATp[hh][:, d * 256:(d + 1) * 256].rearrange("p (h n) -> p h n", h=2), in1=mk, op=ALU.mult),
                                     reads=[rAT[hh], self.r_con], writes=[rATm[hh]])
                        mms = []
                        for h in range(4):
                            p, hh = h // 2, h % 2
                            o = OTp[:, h * 128:(h + 1) * 128]
                            mms.append(dict(out=o, lhsT=VB[:, c, h * 128:(h + 1) * 128], rhs=ATm[hh][:, (0 * 2 + p) * 128:(0 * 2 + p + 1) * 128], start=True, stop=False))
                            mms.append(dict(out=o, lhsT=VB[:, c, h * 128:(h + 1) * 128], rhs=ATm[hh][:, (1 * 2 + p) * 128:(1 * 2 + p + 1) * 128], start=False, stop=False))
                            mms.append(dict(out=o, lhsT=SFb[hh * 64:(hh + 1) * 64, p, hh * 128:(hh + 1) * 128], rhs=QT[0][hh * 64:(hh + 1) * 64, p, cs_],
                                            start=False, stop=False))
                            mms.append(dict(out=o, lhsT=SBS[hh * 64:(hh + 1) * 64, gc, p, :], rhs=QT[1][hh * 64:(hh + 1) * 64, p, cs_],
                                            start=False, stop=True))
                        P.op("pe", MM(mms), reads=[rVB, rATm[0], rATm[1], rSFb, rSBS[ti], rQT[0], rQT[1]], writes=[rOT])
                        kv_update(c, SF, rSF)
                        P.op("act", I("copy", out=SFb[:], in_=SF[:]), reads=[rSF], writes=[rSFb])
                        P.op("act", I("activation", out=OSQ[:], in_=OTp[:], func=AF.Square), reads=[rOT], writes=[rOSQ])
                        pt, rpt = pj()
                        P.op("pe", MM([dict(out=pt[:, 0:512], lhsT=ones, rhs=OSQ[:], start=True, stop=True)]), reads=[rOSQ, self.r_con], writes=[rpt])
                        P.op("act", I("activation", out=ORS[:], in_=pt[:, 0:512], func=AF.Sqrt, bias=self.EPSB[:, 0:1], scale=8.0), reads=[rpt], writes=[rORS])
                        P.op("dve", I("reciprocal", out=ORS[:], in_=ORS[:]), reads=[rORS], writes=[rORS])
                        P.op("dve", I("tensor_tensor", out=OBF[:], in0=OTp[:], in1=ORS[:], op=ALU.mult), reads=[rOT, rORS], writes=[rOBF])
                        P.op("dve", I("scalar_tensor_tensor", out=OB[bq][:, :, cs_], in0=OBF[:].rearrange("p (h n) -> p h n", h=4), scalar=gn,
                                      in1=GS[:, :, cs_], op0=ALU.mult, op1=ALU.mult), reads=[rOBF, rGS, self.r_vec], writes=[rOB[bq]])

                def phase_b(ti):
                    t0 = s0 + ti * N
                    bq = ti % 2
                    P.dma("sp", XR[:], self.XT[:, :, t0:t0 + N], reads=[self.r_xt], writes=[rXR])
                    for c in range(NCH):
                        qb = ti * NCH + c
                        cs_ = slice(c * 128, (c + 1) * 128)
                        if sample:
                            kb0, kb1 = max(0, qb - 1), min(nblk, qb + 2)
                        else:
                            kb0, kb1 = 0, nblk
                        nw = (kb1 - kb0) * 128
                        ntot = nw + (256 if sample else 0)
                        kread = sorted(set((b * 128) // N for b in range(kb0, kb1)))
                        pa, rpa = PJ[2], rPJ[2]
                        rd, rrd = RDEN[qb % 2], rRDEN[qb % 2]
                        def qk(h):
                            cc, hh, kvh = h // 2, h % 2, h // 4
                            par = h % 2
                            rows = slice(hh * 64, (hh + 1) * 64)
                            sw, rsw = (ATp[0], rAT[0]) if par == 0 else (OTp, rOT)
                            sc, rsc = (ATp[1], rAT[1]) if par == 0 else (KVp, rKV)
                            P.op("pe", MM([dict(out=sw[:, 0:nw], lhsT=QR[bq][rows, cc, cs_], rhs=KR[rows, kvh, kb0 * 128:kb1 * 128], start=True, stop=True)]),
                                 reads=[rQR[bq]] + [rKR[t] for t in kread], writes=[rsw])
                            if sample:
                                P.op("pe", MM([dict(out=sc[:, 0:256], lhsT=QR[bq][rows, cc, cs_], rhs=KCX[rows, kvh, :], start=True, stop=True)]),
                                     reads=[rQR[bq], rKCX], writes=[rsc])

                        def stage1(h):
                            par = h % 2
                            sw, rsw = (ATp[0], rAT[0]) if par == 0 else (OTp, rOT)
                            sc, rsc = (ATp[1], rAT[1]) if par == 0 else (KVp, rKV)
                            SMh, rSMh, PXh, rPXh, SMALLh, rSMALLh = SM[par], rSMl[par], PX[par], rPXl[par], SMALL[par], rSMALLl[par]
                            if sample:
                                mo = C_AMASK + (kb0 - (qb - 1)) * 128
                                P.op("dve", I("tensor_tensor", out=SMh[:, 0:nw], in0=sw[:, 0:nw], in1=self.CON[:, mo:mo + nw], op=ALU.add),
                                     reads=[rsw, self.r_con], writes=[rSMh])
                                P.op("act", I("copy", out=SMh[:, nw:ntot], in_=sc[:, 0:256]), reads=[rsc], writes=[rSMh])
                            else:
                                P.op("act", I("copy", out=SMh[:, 0:nw], in_=sw[:, 0:nw]), reads=[rsw], writes=[rSMh])
                            sk = self.VEC[:, osk + j * 8 + h: osk + j * 8 + h + 1]
                            P.op("dve", I("reduce_max", out=SMALLh[:, 0:1], in_=SMh[:, 0:ntot], axis=mybir.AxisListType.X), reads=[rSMh], writes=[rSMALLh])
                            P.op("dve", I("tensor_scalar", out=SMALLh[:, 1:2], in0=SMALLh[:, 0:1], scalar1=0.125, scalar2=sk, op0=ALU.mult, op1=ALU.max),
                                 reads=[rSMALLh, self.r_vec], writes=[rSMALLh])
                            P.op("dve", I("tensor_scalar", out=SMALLh[:, 2:3], in0=SMALLh[:, 1:2], scalar1=-1.0, scalar2=None, op0=ALU.mult),
                                 reads=[rSMALLh], writes=[rSMALLh])
                            P.op("act", I("activation", out=PXh[:, 0:ntot], in_=SMh[:, 0:ntot], func=AF.Exp, bias=SMALLh[:, 2:3], scale=0.125, accum_out=SMALLh[:, 3:4]),
                                 reads=[rSMh, rSMALLh], writes=[rPXh, rSMALLh])
                            P.op("act", I("activation", out=SMALLh[:, 4:5], in_=sk, func=AF.Exp, bias=SMALLh[:, 2:3], scale=1.0),
                                 reads=[rSMALLh, self.r_vec], writes=[rSMALLh])

                        def stage2(h):
                            kvh = h // 4
                            par = h % 2
                            PXh, rPXh, PTh, rPTh, SMALLh, rSMALLh = PX[par], rPXl[par], PTs[par], rPTl[par], SMALL[par], rSMALLl[par]
                            P.op("dve", I("tensor_tensor", out=SMALLh[:, 5:6], in0=SMALLh[:, 3:4], in1=SMALLh[:, 4:5], op=ALU.add), reads=[rSMALLh], writes=[rSMALLh])
                            P.op("dve", I("reciprocal", out=rd[:, h:h + 1], in_=SMALLh[:, 5:6]), reads=[rSMALLh], writes=[rrd])
                            nkb = ntot // 128
                            trs = [(TRb[:, b * 128:(b + 1) * 128], PXh[:, b * 128:(b + 1) * 128]) for b in range(nkb)]
                            P.op("pe", TR(trs, identb), reads=[rPXh, self.r_conb], writes=[rTRb])
                            P.op("dve", I("tensor_copy", out=PTh[:, 0:ntot], in_=TRb[:, 0:ntot]), reads=[rTRb], writes=[rPTh])
                            mms = []
                            for b in range(nkb):
                                if b < kb1 - kb0:
                                    vv = VA[:, kb0 + b, kvh * 64:(kvh + 1) * 64]
                                else:
                                    vv = VCX[:, b - (kb1 - kb0), kvh * 64:(kvh + 1) * 64]
                                mms.append(dict(out=pa[:, h * 64:(h + 1) * 64], lhsT=PTh[:, b * 128:(b + 1) * 128], rhs=vv, start=(b == 0), stop=(b == nkb - 1)))
                            P.op("pe", MM(mms), reads=[rPTh, rVCX] + [rVA[t] for t in kread], writes=[rpa])

                        qk(0)
                        qk(1)
                        stage1(0)
                        for h in range(8):
                            if h + 1 < 8:
                                stage1(h + 1)
                            if h + 2 < 8:
                                qk(h + 2)
                            stage2(h)
                        P.op("dve", I("tensor_tensor", out=OAs[:].rearrange("p (h d) -> p h d", h=8), in0=pa[:].rearrange("p (h d) -> p h d", h=8),
                                      in1=rd[:].unsqueeze(2).to_broadcast([128, 8, 64]), op=ALU.mult), reads=[rpa, rrd], writes=[rOAs])
                        trs = [(TRb[:, k4 * 128:(k4 + 1) * 128], OAs[:, k4 * 128:(k4 + 1) * 128]) for k4 in range(4)]
                        P.op("pe", TR(trs, identb), reads=[rOAs, self.r_conb], writes=[rTRb])
                        P.op("dve", I("tensor_copy", out=OAT[:, :, cs_], in_=TRb[:, 0:512].rearrange("p (k n) -> p k n", k=4)), reads=[rTRb], writes=[rOAT])
                    for m in range(KC):
                        pt, rpt = pj()
                        mms = []
                        for k in range(KC):
                            rhs = OAT[:, k, :] if k < 4 else OB[bq][:, k - 4, :]
                            mms.append(dict(out=pt[:, 0:N], lhsT=WO[:, k, m * 128:(m + 1) * 128], rhs=rhs, start=(k == 0), stop=(k == KC - 1)))
                        P.op("pe", MM(mms), reads=[rW, rOAT, rOB[bq]], writes=[rpt])
                        P.op("dve", I("scalar_tensor_tensor", out=XR[:, m, :], in0=pt[:, 0:N], scalar=self.modG(l, 0, cond, m), in1=XR[:, m, :], op0=ALU.mult, op1=ALU.add),
                             reads=[rpt, rXR, self.r_mod], writes=[rXR])
                    P.dma("sp", self.XT[:, :, t0:t0 + N], XR[:], reads=[rXR], writes=[self.r_xt])

                for ti in range(ntile + 1):
                    if ti < ntile and self.dbg_cut > 3:
                        phase_a(ti)
                    if ti >= 1 and self.dbg_cut > 4:
                        phase_b(ti - 1)
                if not sample:
                    for p in range(2):
                        for hh in range(2):
                            P.dma("sp", self.ngla[pb, j, 0, 2 * p + hh], SF[hh * 64:(hh + 1) * 64, p, hh * 128:(hh + 1) * 128], reads=[rSF], writes=[self.r_out])

    def odd_stage(self, l):
        P = self.P
        j = l // 2
        NMAX = 512
        WMAX = NMAX + 3
        with P.stage():
            WIN = P.sb("o_win", [128, KC, 2048], BF16)
            WOUT = P.sb("o_wout", [128, KC, D], BF16)
            BD = P.sb("o_bd", [128, 4, KC, 128], BF16)
            rW = Res()
            P.dma("pool", WIN[:], self.od_w_in[j].rearrange("(k p) n -> p k n", p=128), writes=[rW])
            P.dma("pool", WOUT[:], self.od_w_out[j].rearrange("(k p) n -> p k n", p=128), writes=[rW])
            for di in range(2):
                for ai in range(2):
                    P.dma("pool", BD[:, di * 2 + ai].rearrange("p k n -> p (k n)"), self.od_bd[j, di, ai], writes=[rW])
            CL = P.sb("o_cl", [128, 2, 2, KC], F32)
            rCL = Res()
            ol, _ = VEC_OFF["od_lambda"]
            lam = self.VEC[:, ol + j * 16: ol + j * 16 + 16]
            cl0 = CL[:, 0].rearrange("p d k -> p (d k)")
            cl1 = CL[:, 1].rearrange("p d k -> p (d k)")
            P.op("act", I("activation", out=cl0, in_=lam, func=AF.Exp, scale=-1.0), reads=[self.r_vec], writes=[rCL])
            P.op("act", I("activation", out=cl0, in_=cl0, func=AF.Ln, bias=self.ONEB[:, 0:1], scale=1.0), reads=[rCL], writes=[rCL])
            P.op("dve", I("tensor_scalar", out=cl1, in0=cl0, scalar1=-8.0, scalar2=None, op0=ALU.mult), reads=[rCL], writes=[rCL])
            P.op("dve", I("tensor_scalar", out=cl0, in0=cl0, scalar1=-4.0, scalar2=None, op0=ALU.mult), reads=[rCL], writes=[rCL])
            HBA = P.sb("o_hba", [128, 2, 16], F32)
            rHBA = Res()
            oba_, _ = VEC_OFF["od_b_a"]
            obi_, _ = VEC_OFF["od_b_i"]
            P.op("dve", I("tensor_scalar", out=HBA[:, 0, :], in0=self.VEC[:, oba_ + j * 16: oba_ + j * 16 + 16], scalar1=0.5, scalar2=None, op0=ALU.mult),
                 reads=[self.r_vec], writes=[rHBA])
            P.op("dve", I("tensor_scalar", out=HBA[:, 1, :], in0=self.VEC[:, obi_ + j * 16: obi_ + j * 16 + 16], scalar1=0.5, scalar2=None, op0=ALU.mult),
                 reads=[self.r_vec], writes=[rHBA])

            XTt = P.sb("o_xt", [128, KC, WMAX + 1], F32)
            HT = P.sb("o_ht", [128, KC, WMAX + 1], BF16)
            GYt = P.sb("o_gy", [128, KC, NMAX], BF16)
            U = [P.sb(f"o_u{i}", [128, WMAX + 1], F32) for i in range(2)]
            XCt = P.sb("o_xc", [128, KC, NMAX], F32)
            XCb = P.sb("o_xcb", [128, KC, NMAX], BF16)
            HFt = P.sb("o_hf", [128, KC, NMAX], F32)
            HBt = P.sb("o_hb", [128, KC, NMAX], F32)
            Zt = P.sb("o_z", [128, KC, NMAX], BF16)
            RSTD = P.sb("o_rstd", [128, WMAX + 1], F32)
            CAR = P.sb("o_car", [128, KC], F32)
            GG = 4
            Rr = [P.sb(f"o_r{i}", [128, NMAX], F32) for i in range(GG)]
            Ii = [P.sb(f"o_i{i}", [128, NMAX], F32) for i in range(GG)]
            Aa = [P.sb(f"o_a{i}", [128, NMAX], F32) for i in range(GG)]
            Mm = [P.sb(f"o_m{i}", [128, NMAX], F32) for i in range(GG)]
            rX, rH_, rGY, rXC_, rXCb_, rHF_, rHB_, rZ_, rRSTD, rCAR = (Res() for _ in range(10))
            rH = RG(Res() for _ in range(KC))
            rXC = RG(Res() for _ in range(KC))
            rXCb = RG(Res() for _ in range(KC))
            rHF = RG(Res() for _ in range(KC))
            rHB = RG(Res() for _ in range(KC))
            rZ = RG(Res() for _ in range(KC))
            rU = [Res() for _ in range(2)]
            rR = [Res() for _ in range(4)]
            rI = [Res() for _ in range(4)]
            rA = [Res() for _ in range(4)]
            rM = [Res() for _ in range(4)]
            PA = [P.ps(f"o_pa{i}", [128, 512], F32) for i in range(2)]
            PBk = [P.ps(f"o_pb{i}", [128, 512], F32) for i in range(2)]
            PHh = [P.ps(f"o_ph{i}", [128, 512], F32) for i in range(2)]
            PD = [P.ps(f"o_pd{i}", [128, 512], F32) for i in range(2)]
            rPA = [PRes() for _ in range(2)]
            rPB = [PRes() for _ in range(2)]
            rPH = [PRes() for _ in range(2)]
            rPD = [PRes() for _ in range(2)]
            ocw, _ = VEC_OFF["od_conv_w"]
            ocb, _ = VEC_OFF["od_conv_b"]
            oba, _ = VEC_OFF["od_b_a"]
            obi, _ = VEC_OFF["od_b_i"]
            ost, _ = VEC_OFF["st_rg"]

            def vcol(o):
                return self.VEC[:, o:o + 1]

            cwt = lambda tap, k: vcol(ocw + (j * 4 + tap) * 8 + k)
            cbt = lambda k: vcol(ocb + j * 8 + k)
            bat = lambda di, k: vcol(oba + (j * 2 + di) * 8 + k)
            bit = lambda di, k: vcol(obi + (j * 2 + di) * 8 + k)
            h0t = lambda di, k: vcol(ost + (j * 2 + di) * 8 + k)
            cnt = [0]

            def gates_group(di, N, ks, HO, rHO, inits):
                for g, k in enumerate(ks):
                    b = cnt[0] % 2
                    cnt[0] += 1
                    P.op("pe", MM([dict(out=PA[b][:, 0:N], lhsT=BD[:, di * 2 + 0, k, :], rhs=XCb[:, k, 0:N], start=True, stop=True)]),
                         reads=[rW, rXCb[k]], writes=[rPA[b]])
                    P.op("pe", MM([dict(out=PBk[b][:, 0:N], lhsT=BD[:, di * 2 + 1, k, :], rhs=XCb[:, k, 0:N], start=True, stop=True)]),
                         reads=[rW, rXCb[k]], writes=[rPB[b]])
                    P.op("act", I("activation", out=Rr[g][:, 0:N], in_=PA[b][:, 0:N], func=AF.Tanh, bias=HBA[:, 0, di * 8 + k: di * 8 + k + 1], scale=0.5),
                         reads=[rPA[b], rHBA], writes=[rR[g]])
                    P.op("act", I("activation", out=Ii[g][:, 0:N], in_=PBk[b][:, 0:N], func=AF.Tanh, bias=HBA[:, 1, di * 8 + k: di * 8 + k + 1], scale=0.5),
                         reads=[rPB[b], rHBA], writes=[rI[g]])
                for g, k in enumerate(ks):
                    P.op("act", I("activation", out=Aa[g][:, 0:N], in_=Rr[g][:, 0:N], func=AF.Exp, scale=CL[:, 0, di, k:k + 1], bias=CL[:, 0, di, k:k + 1]),
                         reads=[rR[g], rCL], writes=[rA[g]])
                    P.op("act", I("activation", out=Mm[g][:, 0:N], in_=Rr[g][:, 0:N], func=AF.Exp, scale=CL[:, 1, di, k:k + 1], bias=CL[:, 1, di, k:k + 1]),
                         reads=[rR[g], rCL], writes=[rM[g]])
                    P.op("dve", I("scalar_tensor_tensor", out=Ii[g][:, 0:N], in0=Ii[g][:, 0:N], scalar=1.0, in1=XCt[:, k, 0:N], op0=ALU.add, op1=ALU.mult),
                         reads=[rI[g], rXC[k]], writes=[rI[g]])
                for g, k in enumerate(ks):
                    P.op("act", I("activation", out=Mm[g][:, 0:N], in_=Mm[g][:, 0:N], func=AF.Sqrt, bias=self.QTRB[:, 0:1], scale=-0.25),
                         reads=[rM[g]], writes=[rM[g]])
                    P.op("dve", I("tensor_tensor", out=Mm[g][:, 0:N], in0=Mm[g][:, 0:N], in1=Ii[g][:, 0:N], op=ALU.mult),
                         reads=[rM[g], rI[g]], writes=[rM[g]])
                    init_ap, rinit = inits[g]
                    if di == 0:
                        P.op("dve", I("tensor_tensor_scan", out=HO[:, k, 0:N], data0=Aa[g][:, 0:N], data1=Mm[g][:, 0:N], initial=init_ap,
                                      op0=ALU.mult, op1=ALU.add), reads=[rA[g], rM[g]] + rinit, writes=[rHO[k]])
                    else:
                        P.op("dve", I("tensor_tensor_scan", out=HO[:, k, 0:N][:, ::-1],
                                      data0=Aa[g][:, 0:N][:, ::-1], data1=Mm[g][:, 0:N][:, ::-1], initial=init_ap,
                                      op0=ALU.mult, op1=ALU.add), reads=[rA[g], rM[g]] + rinit, writes=[rHO[k]])

            seqs = [(0, TS, 1, None)] + [(TS + b * SEQ, SEQ, 0, b) for b in range(NPC)]
            for (s0, T, cond, pb) in seqs:
                N = min(NMAX, T)
                ntile = T // N
                W = N + 3
                for ti in range(ntile):
                    t0 = s0 + ti * N
                    hl = ti > 0
                    hr = ti < ntile - 1
                    lo = t0 - 2 if hl else t0
                    hi = t0 + N + 1 if hr else t0 + N
                    dlo = 0 if hl else 2
                    P.dma("sp", XTt[:, :, dlo:dlo + (hi - lo)], self.XT[:, :, lo:hi], reads=[self.r_xt], writes=[rX])
                    if not hl:
                        P.op("pool", I("memset", XTt[:, :, 0:2], 0.0), writes=[rX])
                    if not hr:
                        P.op("pool", I("memset", XTt[:, :, W - 1:W], 0.0), writes=[rX])
                    tl = dict(sq=HBt, rsq=rHB, rstd=RSTD, rrstd=rRSTD)
                    self.norm_mod_wide(XTt, rX, HT, rH, 0, W, l, 0, cond, tl, PD, rPD, sqw=NMAX)
                    if not hl:
                        P.op("pool", I("memset", HT[:, :, 0:2], 0.0), writes=[rH])
                    if not hr:
                        P.op("pool", I("memset", HT[:, :, W - 1:W], 0.0), writes=[rH])
                    for k in range(KC):
                        b = cnt[0] % 2
                        cnt[0] += 1
                        mms = [dict(out=PA[b][:, 0:N], lhsT=WIN[:, kk, k * 128:(k + 1) * 128], rhs=HT[:, kk, 2:N + 2],
                                    start=(kk == 0), stop=(kk == KC - 1)) for kk in range(KC)]
                        P.op("pe", MM(mms), reads=[rW, rH], writes=[rPA[b]])
                        P.op("act", I("activation", out=GYt[:, k, 0:N], in_=PA[b][:, 0:N], func=AF.Gelu), reads=[rPA[b]], writes=[rGY])
                    P.dma("sp", self.GY[:, :, t0:t0 + N], GYt[:, :, 0:N], reads=[rGY], writes=[self.r_gy])
                    for k in range(KC):
                        b = cnt[0] % 2
                        cnt[0] += 1
                        Wm = min(W, 512)
                        mms = [dict(out=PBk[b][:, 0:Wm], lhsT=WIN[:, kk, D + k * 128:D + (k + 1) * 128], rhs=HT[:, kk, 0:Wm],
                                    start=(kk == 0), stop=(kk == KC - 1)) for kk in range(KC)]
                        P.op("pe", MM(mms), reads=[rW, rH], writes=[rPB[b]])
                        P.op("act", I("copy", out=U[b][:, 0:Wm], in_=PBk[b][:, 0:Wm]), reads=[rPB[b]], writes=[rU[b]])
                        if W > 512:
                            mms = [dict(out=PHh[b][:, 0:W - 512], lhsT=WIN[:, kk, D + k * 128:D + (k + 1) * 128], rhs=HT[:, kk, 512:W],
                                        start=(kk == 0), stop=(kk == KC - 1)) for kk in range(KC)]
                            P.op("pe", MM(mms), reads=[rW, rH], writes=[rPH[b]])
                            P.op("act", I("copy", out=U[b][:, 512:W], in_=PHh[b][:, 0:W - 512]), reads=[rPH[b]], writes=[rU[b]])
                        P.op("dve", I("tensor_scalar", out=XCt[:, k, 0:N], in0=U[b][:, 0:N], scalar1=cwt(0, k), scalar2=cbt(k), op0=ALU.mult, op1=ALU.add),
                             reads=[rU[b], self.r_vec], writes=[rXC[k]])
                        for tap in (1, 2, 3):
                            P.op("dve", I("scalar_tensor_tensor", out=XCt[:, k, 0:N], in0=U[b][:, tap:N + tap], scalar=cwt(tap, k), in1=XCt[:, k, 0:N],
                                          op0=ALU.mult, op1=ALU.add), reads=[rU[b], rXC[k], self.r_vec], writes=[rXC[k]])
                        P.op("pool", I("tensor_copy", out=XCb[:, k, 0:N], in_=XCt[:, k, 0:N]), reads=[rXC[k]], writes=[rXCb[k]])
                    P.dma("sp", self.XC[:, :, t0:t0 + N], XCt[:, :, 0:N], reads=[rXC], writes=[self.r_xc])
                    for k0 in range(0, KC, GG):
                        inits = []
                        for k in range(k0, k0 + GG):
                            if ti == 0:
                                inits.append((h0t(0, k), [self.r_vec]) if pb is None else (0.0, []))
                            else:
                                inits.append((CAR[:, k:k + 1], [rCAR]))
                        gates_group(0, N, list(range(k0, k0 + GG)), HFt, rHF, inits)
                    P.op("pool", I("tensor_copy", out=CAR[:], in_=HFt[:, :, N - 1]), reads=[rHF], writes=[rCAR])
                    P.dma("sp", self.HF[:, :, t0:t0 + N], HFt[:, :, 0:N], reads=[rHF], writes=[self.r_hf])
                    if pb is not None and ti == ntile - 1:
                        c0 = ((pb * 2 + j) * 2 + 0) * 8
                        P.op("pool", I("tensor_copy", out=self.RGO[:, c0:c0 + 8], in_=HFt[:, :, N - 1]), reads=[rHF], writes=[self.r_rgo])
                for ti in range(ntile - 1, -1, -1):
                    t0 = s0 + ti * N
                    P.dma("sp", XCt[:, :, 0:N], self.XC[:, :, t0:t0 + N], reads=[self.r_xc], writes=[rXC])
                    for k in range(KC):
                        P.op("pool", I("tensor_copy", out=XCb[:, k, 0:N], in_=XCt[:, k, 0:N]), reads=[rXC[k]], writes=[rXCb[k]])
                    P.dma("sp", HFt[:, :, 0:N], self.HF[:, :, t0:t0 + N], reads=[self.r_hf], writes=[rHF])
                    P.dma("sp", GYt[:, :, 0:N], self.GY[:, :, t0:t0 + N], reads=[self.r_gy], writes=[rGY])
                    P.dma("sp", XTt[:, :, 0:N], self.XT[:, :, t0:t0 + N], reads=[self.r_xt], writes=[rX])
                    for k0 in range(0, KC, GG):
                        inits = []
                        for k in range(k0, k0 + GG):
                            if ti == ntile - 1:
                                inits.append((h0t(1, k), [self.r_vec]) if pb is None else (0.0, []))
                            else:
                                inits.append((CAR[:, k:k + 1], [rCAR]))
                        gates_group(1, N, list(range(k0, k0 + GG)), HBt, rHB, inits)
                    P.op("pool", I("tensor_copy", out=CAR[:], in_=HBt[:, :, 0]), reads=[rHB], writes=[rCAR])
                    if pb is not None and ti == 0:
                        c0 = ((pb * 2 + j) * 2 + 1) * 8
                        P.op("pool", I("tensor_copy", out=self.RGO[:, c0:c0 + 8], in_=HBt[:, :, 0]), reads=[rHB], writes=[self.r_rgo])
                    for k in range(KC):
                        P.op("dve", I("tensor_tensor", out=HFt[:, k, 0:N], in0=HFt[:, k, 0:N], in1=HBt[:, k, 0:N], op=ALU.add),
                             reads=[rHB[k], rHF[k]], writes=[rHF[k]])
                        P.op("dve", I("tensor_tensor", out=Zt[:, k, 0:N], in0=HFt[:, k, 0:N], in1=GYt[:, k, 0:N], op=ALU.mult),
                             reads=[rHF[k], rGY], writes=[rZ[k]])
                    for m in range(KC):
                        b = cnt[0] % 2
                        cnt[0] += 1
                        mms = [dict(out=PD[b][:, 0:N], lhsT=WOUT[:, kk, m * 128:(m + 1) * 128], rhs=Zt[:, kk, 0:N],
                                    start=(kk == 0), stop=(kk == KC - 1)) for kk in range(KC)]
                        P.op("pe", MM(mms), reads=[rW, rZ], writes=[rPD[b]])
                        xs = XTt[:, m, 0:N]
                        P.op("dve", I("scalar_tensor_tensor", out=xs, in0=PD[b][:, 0:N], scalar=self.modG(l, 0, cond, m), in1=xs, op0=ALU.mult, op1=ALU.add),
                             reads=[rPD[b], rX, self.r_mod], writes=[rX])
                    P.dma("sp", self.XT[:, :, t0:t0 + N], XTt[:, :, 0:N], reads=[rX], writes=[self.r_xt])

    def final_stage(self):
        P = self.P
        with P.stage():
            XF = [P.sb(f"fin_x{i}", [128, KC, 128], F32) for i in range(2)]
            SQ = [P.sb(f"fin_sq{i}", [128, KC, 128], F32) for i in range(2)]
            RS = [P.sb(f"fin_rs{i}", [128, 128], F32) for i in range(2)]
            YB = [P.sb(f"fin_y{i}", [128, D], F32) for i in range(2)]
            rXF = [Res() for _ in range(2)]
            rSQ = [Res() for _ in range(2)]
            rRS = [Res() for _ in range(2)]
            rYB = [Res() for _ in range(2)]
            PSS = [P.ps(f"fin_ps{i}", [128, 512], F32) for i in range(2)]
            rPSS = [PRes() for _ in range(2)]
            TPp = [P.ps(f"fin_tp{i}", [128, 512], F32) for i in range(4)]
            rTP = [PRes() for _ in range(4)]
            ident = self.CON[:, C_IDENT:C_IDENT + 128]
            ones = self.CON[:, C_ONES:C_ONES + 128]
            ofn, _ = VEC_OFF["final_norm"]
            fn_b = self.VEC[:, ofn:ofn + 8].unsqueeze(2).to_broadcast([128, KC, 128])
            for blk in range(TT // 128):
                b = blk % 2
                P.dma("sp", XF[b][:], self.XT[:, :, blk * 128:(blk + 1) * 128], reads=[self.r_xt], writes=[rXF[b]])
                P.op("act", I("activation", out=SQ[b][:], in_=XF[b][:], func=AF.Square), reads=[rXF[b]], writes=[rSQ[b]])
                mms = [dict(out=PSS[b][:, 0:128], lhsT=ones, rhs=SQ[b][:, k, :], start=(k == 0), stop=(k == KC - 1)) for k in range(KC)]
                P.op("pe", MM(mms), reads=[rSQ[b], self.r_con], writes=[rPSS[b]])
                P.op("act", I("activation", out=RS[b][:], in_=PSS[b][:, 0:128], func=AF.Sqrt, bias=self.EPSB[:, 0:1], scale=1.0),
                     reads=[rPSS[b]], writes=[rRS[b]])
                P.op("dve", I("reciprocal", out=RS[b][:], in_=RS[b][:]), reads=[rRS[b]], writes=[rRS[b]])
                P.op("dve", I("tensor_tensor", out=SQ[b][:], in0=XF[b][:], in1=RS[b][:].unsqueeze(1).to_broadcast([128, KC, 128]), op=ALU.mult),
                     reads=[rXF[b], rRS[b], rSQ[b]], writes=[rSQ[b]])
                P.op("pool", I("tensor_tensor", out=SQ[b][:], in0=SQ[b][:], in1=fn_b, op=ALU.mult),
                     reads=[rSQ[b], self.r_vec], writes=[rSQ[b]])
                for hf in range(2):
                    tp, rtp = TPp[b * 2 + hf], rTP[b * 2 + hf]
                    trs = [(tp[:, kk * 128:(kk + 1) * 128], SQ[b][:, hf * 4 + kk, :]) for kk in range(4)]
                    P.op("pe", TR(trs, ident), reads=[rSQ[b], self.r_con], writes=[rtp])
                    if hf == 0:
                        P.op("act", I("copy", out=YB[b][:, 0:512], in_=tp[:]), reads=[rtp], writes=[rYB[b]])
                    else:
                        P.op("dve", I("tensor_copy", out=YB[b][:, 512:1024], in_=tp[:]), reads=[rtp], writes=[rYB[b]])
                P.dma("sp", self.y[blk * 128:(blk + 1) * 128, :], YB[b][:], reads=[rYB[b]], writes=[self.r_y])


    def out_stage(self):
        P = self.P
        with P.stage():
            ident = self.CON[:, C_IDENT:C_IDENT + 128]
            TPp = P.ps("os_tp", [128, 512], F32)
            rTP = PRes()
            OB = P.sb("os_ob", [128, 128], F32)
            rOB = Res()
            ncol = NPC * 2 * 2 * KC
            for g in range((ncol + 127) // 128):
                w = min(128, ncol - g * 128)
                P.op("pe", TR([(TPp[0:w, 0:128], self.RGO[:, g * 128:g * 128 + w])], ident), reads=[self.r_rgo, self.r_con], writes=[rTP])
                P.op("dve", I("tensor_copy", out=OB[0:w, :], in_=TPp[0:w, 0:128]), reads=[rTP], writes=[rOB])
                P.dma("sp", self.nrg[g * 128:g * 128 + w, :], OB[0:w, :], reads=[rOB], writes=[self.r_out])

    def build(self):
        P = self.P
        self.r_xt = Res("XT")
        self.r_y = Res("y")
        self.r_gy, self.r_xc, self.r_hf, self.r_rgo, self.r_out = Res(), Res(), Res(), Res(), Res()
        self.setup_globals()
        self.EPSB = P.sb("EPSB", [128, 1], F32, glob=True)
        P.op("pool", I("memset", self.EPSB[:], EPS), writes=[self.r_con])
        self.QTRB = P.sb("QTRB", [128, 1], F32, glob=True)
        P.op("pool", I("memset", self.QTRB[:], 0.25), writes=[self.r_con])
        self.ONEB = P.sb("ONEB", [128, 1], F32, glob=True)
        P.op("pool", I("memset", self.ONEB[:], 1.0), writes=[self.r_con])
        self.RGO = P.sb("RGO", [128, NPC * 2 * 2 * KC], F32, glob=True)
        P.op("pool", I("memset", self.RGO[:], 0.0), writes=[self.r_rgo])
        self.stage0()
        for l in range(self.layers):
            if self.do_mixer:
                if l % 2 == 1:
                    self.odd_stage(l)
                elif self.do_even:
                    self.even_stage(l)
            if self.do_ffn:
                self.ffn_stage(l)
        self.final_stage()
        self.out_stage()
        P.barrier(engines=["sp"])


_CACHE = {}


def pack_vec(inp, core):
    s = core % DEC_BATCH
    vp = VecPack()
    vp.add("cvec", np.stack([fm(inp["c_ctx"]), fm(inp["c"][s])], axis=-1))
    vp.add("b_ada", fm(inp["b_ada"]))
    vp.add("norm_mix", fm(inp["norm_mix"]))
    vp.add("norm_ffn", fm(inp["norm_ffn"]))
    vp.add("final_norm", fm(inp["final_norm"]))
    vp.add("ffn_conv_w", fm(inp["ffn_conv_w"]))
    vp.add("ffn_conv_b", fm(inp["ffn_conv_b"]))
    vp.add("od_conv_w", fm(inp["od_conv_w"]))
    vp.add("od_conv_b", fm(inp["od_conv_b"]))
    vp.add("od_b_a", fm(inp["od_b_a"]))
    vp.add("od_b_i", fm(inp["od_b_i"]))
    vp.add("od_lambda", fm(inp["od_lambda"]))
    vp.add("st_rg", fm(inp["state_rglru"][s]))
    vp.add("gla_norm", fm(inp["ev_gla_norm"]))
    vp.add("b_gate", np.stack([fm(inp["ev_b_gate_f"]), fm(inp["ev_b_gate_b"])], axis=2))
    vp.add("sink", np.broadcast_to(inp["ev_sink"].reshape(1, 16), (128, 16)))
    for n, c in VEC_SPEC:
        assert vp.off[n] == VEC_OFF[n], (n, vp.off[n], VEC_OFF[n])
    return vp.build()


def pack_shared(inp):
    sh = {}
    sh["consts"] = make_consts()
    sh["w_ada"] = np.ascontiguousarray(inp["w_ada"], np.float32)
    up = np.asarray(inp["ffn_w_up"], np.float32)
    upk = up.reshape(DEPTH, KC, 128, 2, NJ, 128)
    sh["ffn_up_t"] = np.ascontiguousarray(upk.transpose(0, 4, 2, 1, 3, 5)).reshape(DEPTH, NJ, 128, KC * 256)
    dn = np.asarray(inp["ffn_w_down"], np.float32)
    dnk = dn.reshape(DEPTH, NJ, 128, KC, 128)
    sh["ffn_dn_t"] = np.ascontiguousarray(dnk.transpose(0, 3, 2, 1, 4)).reshape(DEPTH, KC, 128, NJ * 128)
    wi = np.asarray(inp["ev_w_in"], np.float32)
    z16 = np.zeros((2, D, 16), np.float32)
    cols = [wi[:, :, 0:512],
            wi[:, :, 512:576], wi[:, :, 512:576], wi[:, :, 576:640], wi[:, :, 576:640],
            wi[:, :, 768:1024], wi[:, :, 1024:1280], wi[:, :, 1792:2304],
            wi[:, :, 2304:2320], z16, wi[:, :, 2320:2336], z16,
            wi[:, :, 512:640], wi[:, :, 640:768], wi[:, :, 1280:1792]]
    we = np.concatenate(cols, axis=2)
    assert we.shape[2] == 2624
    sh["w_even"] = np.ascontiguousarray(we.reshape(2, KC, 128, 2624).transpose(0, 2, 1, 3)).reshape(2, 128, KC * 2624)
    sh["ev_w_out"] = np.ascontiguousarray(inp["ev_w_out"], np.float32)
    wg = np.zeros((2, 64, 512), np.float32)
    wg[:, 0:16, 0:256] = inp["ev_w_gate_f"]
    wg[:, 32:48, 256:512] = inp["ev_w_gate_b"]
    sh["wg"] = wg
    sh["rope"] = make_rope()
    sh["od_w_in"] = np.ascontiguousarray(inp["od_w_in"], np.float32)
    sh["od_w_out"] = np.ascontiguousarray(inp["od_w_out"], np.float32)
    bd = np.zeros((2, 2, 2, 128, KC, 128), np.float32)
    for ai, nm in enumerate(("od_w_a", "od_w_i")):
        w = np.asarray(inp[nm], np.float32)
        for k in range(KC):
            for hb in range(2):
                bd[:, :, ai, hb * 64:(hb + 1) * 64, k, hb * 64:(hb + 1) * 64] = w[:, :, 2 * k + hb]
    sh["od_bd"] = bd.reshape(2, 2, 2, 128, KC * 128)
    return sh


def core_inputs(inp, sh, core):
    s = core % DEC_BATCH
    xin = np.concatenate([inp["x_sample"][s], inp["x_prompt"][core * NPC:(core + 1) * NPC].reshape(TP, D)], axis=0)
    m = dict(sh)
    m["xin"] = np.ascontiguousarray(xin, np.float32)
    m["vec"] = pack_vec(inp, core)
    m["cache_k"] = np.ascontiguousarray(inp["cache_attn_k"][s].reshape(2, 256, 128), np.float32)
    m["cache_v"] = np.ascontiguousarray(inp["cache_attn_v"][s].reshape(2, 256, 128), np.float32)
    m["state_gla"] = np.ascontiguousarray(inp["state_gla"][s], np.float32)
    return m


def get_builder(**kw):
    key = tuple(sorted(kw.items()))
    if key not in _CACHE:
        _CACHE[key] = Builder(**kw)
    return _CACHE[key]


CFG = {}


def set_cores(n):
    global N_CORES, NPC, TP, TT
    N_CORES = n
    NPC = BATCH // n
    TP = NPC * SEQ
    TT = TS + TP

CORE_OFF = [0]


def kernel(**inputs):
    inp = {k: np.asarray(v) for k, v in inputs.items()}
    cfg = dict(CFG)
    ncr = cfg.pop("n_cores", N_CORES)
    bld = Builder(**cfg)
    sh = pack_shared(inp)
    coff = cfg.pop("core_off", 0) if False else CORE_OFF[0]
    in_maps = [core_inputs(inp, sh, c + coff) for c in range(ncr)]
    res = run_bass_kernel_spmd(bld.nc, in_maps, core_ids=list(range(ncr)))
    R = list(res.results)
    while len(R) < N_CORES:
        R.append(R[0])
    y_prompt = np.stack([R[c]["y"][TS:].reshape(NPC, SEQ, D) for c in range(N_CORES)]).reshape(BATCH, SEQ, D)
    y_sample = np.stack([R[s]["y"][:TS] for s in range(DEC_BATCH)])
    nrg = np.stack([R[c]["nrg"].reshape(NPC, 2, 2, D) for c in range(N_CORES)]).reshape(BATCH, 2, 2, D)
    nk = np.stack([R[c]["nk"] for c in range(N_CORES)]).reshape(BATCH, 2, SEQ, 2, 64)
    nv = np.stack([R[c]["nv"] for c in range(N_CORES)]).reshape(BATCH, 2, SEQ, 2, 64)
    ngla = np.stack([R[c]["ngla"] for c in range(N_CORES)]).reshape(BATCH, 2, 2, 4, 64, 128)
    f = lambda a: np.ascontiguousarray(a, dtype=np.float32)
    return (f(y_prompt), f(y_sample), f(nk), f(nv), f(ngla), f(nrg))
```

```python
import contextlib
import numpy as np
import concourse.bass as bass
import concourse.mybir as mybir
from concourse.bass_utils import run_bass_kernel_spmd

F32 = mybir.dt.float32
BF16 = mybir.dt.bfloat16
AF = mybir.ActivationFunctionType
ALU = mybir.AluOpType

D = 1024
KC = 8
BATCH, SEQ = 32, 256
DEC_BATCH, DEC_SEQ = 4, 4096
DEPTH = 4
N_CORES = 8
NPC = BATCH // N_CORES
TS = DEC_SEQ
TP = NPC * SEQ
TT = TS + TP
D_FF = 2816
NJ = D_FF // 128
EPS = 1e-6

SAME_SYNC = True
RELAX = [10 ** 9]
PE_RELAX = [False]
DMA_K = 8


class Tok:
    __slots__ = ("sem", "semid", "val", "key")

    def __init__(self, sem, semid, val, key):
        self.sem, self.semid, self.val, self.key = sem, semid, val, key


class Res:
    __slots__ = ("name", "w", "r", "ex")

    def __init__(self, name="", ex=False):
        self.name = name
        self.w = None
        self.r = []
        self.ex = ex


def PRes():
    return Res("psum", ex=True)


class RG(list):
    pass


def _flat(lst):
    out = []
    for x in lst:
        if isinstance(x, RG):
            out.extend(x)
        else:
            out.append(x)
    return out


class Prog:
    ENG = ("pe", "dve", "act", "pool", "sp")

    def __init__(self, nc):
        self.nc = nc
        self.es = contextlib.ExitStack()
        self.eng = {"pe": nc.tensor, "dve": nc.vector, "act": nc.scalar, "pool": nc.gpsimd, "sp": nc.sync}
        self.sem = {}
        self.seq = {}
        self.waited = {}
        self._nid = 0
        for e in self.ENG:
            self.sem[e] = (self.es.enter_context(nc.semaphore("s_" + e)), self._sid())
            self.seq[e] = 0
        self.dsem = {}
        self.dman = {}
        for q in ("sp", "act", "pool"):
            self.dsem[q] = [(self.es.enter_context(nc.semaphore(f"d_{q}{i}")), self._sid()) for i in range(DMA_K)]
            self.dman[q] = 0
        self.stage_es = None
        self.n_inst = 0

    def _sid(self):
        self._nid += 1
        return self._nid

    def sb(self, name, shape, dtype, glob=False):
        es = self.es if (glob or self.stage_es is None) else self.stage_es
        self._nid += 1
        return es.enter_context(self.nc.sbuf_tensor(f"{name}_{self._nid}", list(shape), dtype))

    def ps(self, name, shape, dtype=F32):
        es = self.es if self.stage_es is None else self.stage_es
        self._nid += 1
        return es.enter_context(self.nc.psum_tensor(f"{name}_{self._nid}", list(shape), dtype))

    @contextlib.contextmanager
    def stage(self):
        assert self.stage_es is None
        self.stage_es = contextlib.ExitStack()
        try:
            yield
        finally:
            self.barrier()
            self.stage_es.close()
            self.stage_es = None

    def _deps(self, reads, writes):
        toks = []
        for r in reads:
            if r.w is not None:
                toks.append(r.w)
            if r.ex:
                toks.extend(r.r)
        for w in writes:
            if w.w is not None:
                toks.append(w.w)
            toks.extend(w.r)
        return toks

    def _need(self, e, toks, strict_same=False):
        need = {}
        for t in toks:
            if t.key == e and not strict_same:
                if (not SAME_SYNC) or (e == "pe" and PE_RELAX[0]) or (e != "pe" and (self.seq[e] - t.val) >= RELAX[0]):
                    continue
            if self.waited.get((e, t.semid), 0) >= t.val:
                continue
            if t.semid not in need or need[t.semid].val < t.val:
                need[t.semid] = t
        return list(need.values())

    def _emit_waits(self, e, need):
        for t in need:
            self.eng[e].wait_ge(t.sem, t.val)
            self.waited[(e, t.semid)] = t.val
            self.n_inst += 1

    def _record(self, tok, reads, writes):
        for r in reads:
            if r.ex:
                r.w = tok
                r.r = []
            else:
                r.r = [x for x in r.r if x.key != tok.key] + [tok]
        for w in writes:
            w.w = tok
            w.r = []

    def op(self, e, fn, reads=(), writes=()):
        reads, writes = _flat(reads), _flat(writes)
        need = self._need(e, self._deps(reads, writes))
        attach = need.pop() if need else None
        self._emit_waits(e, need)
        first_holder = []
        ins = fn(self.eng[e], first_holder)
        tgt = first_holder[0] if first_holder else ins
        if attach is not None:
            tgt._wait_ge(attach.sem, attach.val)
            self.waited[(e, attach.semid)] = attach.val
        self.seq[e] += 1
        sem, semid = self.sem[e]
        ins.then_inc(sem, 1)
        self.n_inst += 1
        tok = Tok(sem, semid, self.seq[e], e)
        self._record(tok, reads, writes)
        return tok

    def dma(self, q, out, in_, reads=(), writes=(), **kw):
        e = q
        reads, writes = _flat(reads), _flat(writes)
        need = self._need(e, self._deps(reads, writes), strict_same=True)
        n = self.dman[q]
        slot, rnd = n % DMA_K, n // DMA_K
        sem, semid = self.dsem[q][slot]
        if rnd > 0 and self.waited.get((e, semid), 0) < 16 * rnd:
            need.append(Tok(sem, semid, 16 * rnd, ("q", q, slot)))
        self._emit_waits(e, need)
        self.eng[e].dma_start(out=out, in_=in_, **kw).then_inc(sem, 16)
        self.n_inst += 1
        self.dman[q] = n + 1
        tok = Tok(sem, semid, 16 * (rnd + 1), ("q", q, slot))
        self._record(tok, reads, writes)
        return tok

    def all_tokens(self):
        toks = []
        for e in self.ENG:
            if self.seq[e] > 0:
                sem, semid = self.sem[e]
                toks.append(Tok(sem, semid, self.seq[e], e))
        for q in self.dsem:
            n = self.dman[q]
            for slot in range(DMA_K):
                cnt = (n - slot + DMA_K - 1) // DMA_K if n > slot else 0
                if cnt > 0:
                    sem, semid = self.dsem[q][slot]
                    toks.append(Tok(sem, semid, 16 * cnt, ("q", q, slot)))
        return toks

    def barrier(self, engines=None):
        toks = self.all_tokens()
        for e in (engines or self.ENG):
            need = []
            for t in toks:
                if t.key == e:
                    continue
                if self.waited.get((e, t.semid), 0) >= t.val:
                    continue
                need.append(t)
            self._emit_waits(e, need)


def I(method, *a, **k):
    def fn(eng, fh):
        return getattr(eng, method)(*a, **k)
    return fn


def MM(mms):
    def fn(eng, fh):
        ins = None
        for i, m in enumerate(mms):
            ins = eng.matmul(m["out"], lhsT=m["lhsT"], rhs=m["rhs"], start=m["start"], stop=m["stop"])
            if i == 0:
                fh.append(ins)
        return ins
    return fn


def TR(trs, ident):
    def fn(eng, fh):
        ins = None
        for i, (o, a) in enumerate(trs):
            ins = eng.transpose(out=o, in_=a, identity=ident)
            if i == 0:
                fh.append(ins)
        return ins
    return fn


def fm(v):
    v = np.asarray(v, np.float32)
    n = v.shape[-1] // 128
    a = v.reshape(v.shape[:-1] + (n, 128))
    return np.moveaxis(a, -1, 0)


class VecPack:
    def __init__(self):
        self.items = []
        self.off = {}
        self.n = 0

    def add(self, name, arr):
        arr = np.ascontiguousarray(arr, np.float32).reshape(128, -1)
        self.off[name] = (self.n, arr.shape[1])
        self.items.append(arr)
        self.n += arr.shape[1]

    def build(self):
        return np.ascontiguousarray(np.concatenate(self.items, axis=1))


VEC_SPEC = [
    ("cvec", 16), ("b_ada", 4 * 48), ("norm_mix", 32), ("norm_ffn", 32), ("final_norm", 8),
    ("ffn_conv_w", 4 * 3 * 44), ("ffn_conv_b", 4 * 44),
    ("od_conv_w", 2 * 4 * 8), ("od_conv_b", 16), ("od_b_a", 32), ("od_b_i", 32), ("od_lambda", 32),
    ("st_rg", 32), ("gla_norm", 2), ("b_gate", 2 * 2 * 2), ("sink", 16),
]
VEC_OFF = {}
_o = 0
for _n, _c in VEC_SPEC:
    VEC_OFF[_n] = (_o, _c)
    _o += _c
NV = _o

C_IDENT = 0
C_MF = 128
C_MB = 256
C_RST = 384
C_AMASK = 896
C_ONES = 1280
C_PERM = 1408
NCONST = 1536


def make_consts():
    c = np.zeros((128, NCONST), np.float32)
    c[:, C_IDENT:C_IDENT + 128] = np.eye(128, dtype=np.float32)
    j = np.arange(128)[:, None]
    i = np.arange(128)[None, :]
    c[:, C_MF:C_MF + 128] = (j <= i)
    c[:, C_MB:C_MB + 128] = (j >= i)
    t = np.arange(512)
    c[:, C_RST:C_RST + 512] = (t % 128 != 0)[None, :]
    qi = np.arange(128)[:, None]
    kj = np.arange(384)[None, :]
    rel = kj - 128 - qi
    c[:, C_AMASK:C_AMASK + 384] = np.where(np.abs(rel) <= 128, 0.0, -1e30)
    c[:, C_ONES:C_ONES + 128] = 1.0 / 1024.0
    for m in range(128):
        d = m % 64
        if (d % 32) < 16:
            c[m + 16, C_PERM + m] = -1.0
        else:
            c[m - 16, C_PERM + m] = 1.0
    return c


def make_rope():
    t = np.arange(DEC_SEQ)
    row = (t // 64).astype(np.float32)
    col = (t % 64).astype(np.float32)
    inv = np.power(np.float32(10000.0), -np.arange(16, dtype=np.float32) / np.float32(16.0)).astype(np.float32)
    ang_r = (row[None, :] * inv[:, None]).astype(np.float32)
    ang_c = (col[None, :] * inv[:, None]).astype(np.float32)
    tab = np.zeros((128, 2, DEC_SEQ), np.float32)
    for p in range(128):
        d = p % 64
        a = ang_r if d < 32 else ang_c
        f = d % 16
        tab[p, 0] = np.cos(a[f])
        tab[p, 1] = np.sin(a[f])
    return tab


class Builder:
    def __init__(self, layers=DEPTH, do_mixer=True, do_ffn=True, do_even=True, dbg_seqs=None, dbg_cut=99):
        self.dbg_seqs = dbg_seqs
        self.dbg_cut = dbg_cut
        self.layers = layers
        self.do_mixer = do_mixer
        self.do_ffn = do_ffn
        self.do_even = do_even
        nc = bass.Bass("TRN2", target_bir_lowering=False)
        self.nc = nc
        self.P = Prog(nc)
        dt = lambda name, shape, kind=None, dtype=F32: (
            nc.dram_tensor(name, list(shape), dtype, kind=kind) if kind else nc.dram_tensor(name, list(shape), dtype))
        self.xin = dt("xin", [TT, D], "ExternalInput").ap()
        self.vec_d = dt("vec", [128, NV], "ExternalInput").ap()
        self.const_d = dt("consts", [128, NCONST], "ExternalInput").ap()
        self.w_ada = dt("w_ada", [DEPTH, D, 6 * D], "ExternalInput").ap()
        self.ffn_up = dt("ffn_up_t", [DEPTH, NJ, 128, KC * 256], "ExternalInput").ap()
        self.ffn_dn = dt("ffn_dn_t", [DEPTH, KC, 128, NJ * 128], "ExternalInput").ap()
        self.od_w_in = dt("od_w_in", [2, D, 2 * D], "ExternalInput").ap()
        self.od_w_out = dt("od_w_out", [2, D, D], "ExternalInput").ap()
        self.od_bd = dt("od_bd", [2, 2, 2, 128, KC * 128], "ExternalInput").ap()
        self.w_even = dt("w_even", [2, 128, KC * 2624], "ExternalInput").ap()
        self.ev_w_out = dt("ev_w_out", [2, D, D], "ExternalInput").ap()
        self.wg = dt("wg", [2, 64, 512], "ExternalInput").ap()
        self.cache_k = dt("cache_k", [2, 256, 128], "ExternalInput").ap()
        self.cache_v = dt("cache_v", [2, 256, 128], "ExternalInput").ap()
        self.state_gla = dt("state_gla", [2, 2, 4, 64, 128], "ExternalInput").ap()
        self.rope = dt("rope", [128, 2, DEC_SEQ], "ExternalInput").ap()
        self.y = dt("y", [TT, D], "ExternalOutput").ap()
        self.nk = dt("nk", [NPC, 2, SEQ, 128], "ExternalOutput").ap()
        self.nv = dt("nv", [NPC, 2, SEQ, 128], "ExternalOutput").ap()
        self.ngla = dt("ngla", [NPC, 2, 2, 4, 64, 128], "ExternalOutput").ap()
        self.nrg = dt("nrg", [NPC * 2 * 2 * KC, 128], "ExternalOutput").ap()
        self.GY = dt("GY", [D, TT], dtype=BF16).ap().rearrange("(k p) t -> p k t", p=128)
        self.XC = dt("XC", [D, TT]).ap().rearrange("(k p) t -> p k t", p=128)
        self.HF = dt("HF", [D, TT]).ap().rearrange("(k p) t -> p k t", p=128)
        self.XT = dt("XT", [D, TT]).ap().rearrange("(k p) t -> p k t", p=128)
        self.build()

    def setup_globals(self):
        P = self.P
        self.VEC = P.sb("VEC", [128, NV], F32, glob=True)
        self.CON = P.sb("CON", [128, NCONST], F32, glob=True)
        self.CONB = P.sb("CONB", [128, 384], BF16, glob=True)
        self.MOD = P.sb("MOD", [128, DEPTH, 48, 2], F32, glob=True)
        self.AMF = P.sb("AMF", [128, DEPTH, 2, 2, 8], F32, glob=True)
        self.r_vec = Res("VEC")
        self.r_con = Res("CON")
        self.r_conb = Res("CONB")
        self.r_mod = Res("MOD")
        self.r_amf = Res("AMF")
        P.dma("sp", self.VEC[:], self.vec_d[:, :], writes=[self.r_vec])
        P.dma("sp", self.CON[:], self.const_d[:, :], writes=[self.r_con])
        P.op("dve", I("tensor_copy", out=self.CONB[:], in_=self.CON[:, 0:384]), reads=[self.r_con], writes=[self.r_conb])

    def vec(self, name, *idx_shape):
        o, n = VEC_OFF[name]
        return self.VEC[:, o:o + n]

    def stage0(self):
        P = self.P
        with P.stage():
            XA = [P.sb(f"s0_xa{i}", [128, D], F32) for i in range(3)]
            XB = [P.sb(f"s0_xb{i}", [128, KC, 128], F32) for i in range(2)]
            rXA = [Res() for _ in range(3)]
            rXB = [Res() for _ in range(2)]
            TPp = [P.ps(f"s0_tp{i}", [128, 512], F32) for i in range(4)]
            rTP = [PRes() for _ in range(4)]
            ident = self.CON[:, C_IDENT:C_IDENT + 128]
            SC = P.sb("s0_sc", [128, KC, 2], BF16)
            rSC = Res()
            WA = [P.sb(f"s0_wa{i}", [128, KC, 1024], BF16) for i in range(2)]
            rWA = [Res() for _ in range(2)]
            PM = P.ps("s0_pm", [128, 512], F32)
            rPM = PRes()
            o, n = VEC_OFF["cvec"]
            P.op("act", I("activation", out=SC[:].rearrange("p k c -> p (k c)"), in_=self.VEC[:, o:o + n], func=AF.Silu),
                 reads=[self.r_vec], writes=[rSC])
            nblk = TT // 128
            mod_jobs = [(l, pc) for l in range(DEPTH) for pc in range(6)]
            mj = 0

            mod_issued = [0]

            def issue_mod_dma(idx):
                l_, pc_ = mod_jobs[idx]
                wv = self.w_ada[l_].rearrange("(k p) n -> p k n", p=128)
                P.dma("pool", WA[idx % 2][:], wv[:, :, pc_ * 1024:(pc_ + 1) * 1024], writes=[rWA[idx % 2]])

            def do_mod(l, pc, idx):
                while mod_issued[0] <= min(idx + 1, len(mod_jobs) - 1):
                    issue_mod_dma(mod_issued[0])
                    mod_issued[0] += 1
                wa, rwa = WA[idx % 2], rWA[idx % 2]
                for jj in range(8):
                    mms = [dict(out=PM[:, jj * 2:jj * 2 + 2], lhsT=wa[:, k, jj * 128:(jj + 1) * 128], rhs=SC[:, k, :],
                                start=(k == 0), stop=(k == KC - 1)) for k in range(KC)]
                    P.op("pe", MM(mms), reads=[rwa, rSC], writes=[rPM])
                ob, _ = VEC_OFF["b_ada"]
                bsl = self.VEC[:, ob + l * 48 + pc * 8: ob + l * 48 + pc * 8 + 8]
                for c in range(2):
                    P.op("dve", I("tensor_tensor", out=self.MOD[:, l, pc * 8:(pc + 1) * 8, c], in0=PM[:, c:16:2], in1=bsl, op=ALU.add),
                         reads=[rPM, self.r_vec], writes=[self.r_mod])

            for blk in range(nblk):
                xa, rxa = XA[blk % 3], rXA[blk % 3]
                xb, rxb = XB[blk % 2], rXB[blk % 2]
                P.dma("sp", xa[:], self.xin[blk * 128:(blk + 1) * 128, :], writes=[rxa])
                for hf in range(2):
                    tp, rtp = TPp[(blk % 2) * 2 + hf], rTP[(blk % 2) * 2 + hf]
                    trs = [(tp[:, kk * 128:(kk + 1) * 128], xa[:, (hf * 4 + kk) * 128:(hf * 4 + kk + 1) * 128]) for kk in range(4)]
                    P.op("pe", TR(trs, ident), reads=[rxa, self.r_con], writes=[rtp])
                    dst = xb[:, hf * 4:(hf + 1) * 4, :]
                    src = tp[:].rearrange("p (a b) -> p a b", a=4)
                    if hf == 0:
                        P.op("act", I("copy", out=dst, in_=src), reads=[rtp], writes=[rxb])
                    else:
                        P.op("dve", I("tensor_copy", out=dst, in_=src), reads=[rtp], writes=[rxb])
                P.dma("sp", self.XT[:, :, blk * 128:(blk + 1) * 128], xb[:], reads=[rxb], writes=[self.r_xt])
                if mj < len(mod_jobs) and blk % 1 == 0:
                    do_mod(*mod_jobs[mj], mj)
                    mj += 1
            while mj < len(mod_jobs):
                do_mod(*mod_jobs[mj], mj)
                mj += 1
            for l in range(DEPTH):
                for mf, (nname, sc0) in enumerate((("norm_mix", 8), ("norm_ffn", 32))):
                    on, _ = VEC_OFF[nname]
                    for c in range(2):
                        P.op("dve", I("scalar_tensor_tensor", out=self.AMF[:, l, mf, c, :], in0=self.MOD[:, l, sc0:sc0 + 8, c], scalar=1.0,
                                      in1=self.VEC[:, on + l * 8: on + l * 8 + 8], op0=ALU.add, op1=ALU.mult),
                             reads=[self.r_mod, self.r_vec], writes=[self.r_amf])

    def modA(self, l, mf, c, k):
        return self.AMF[:, l, mf, c, k:k + 1]

    def modB(self, l, mf, c, k):
        base = 0 if mf == 0 else 24
        return self.MOD[:, l, base + k, c:c + 1]

    def modG(self, l, mf, c, k):
        base = 16 if mf == 0 else 40
        return self.MOD[:, l, base + k, c:c + 1]

    def norm_mod(self, xt, rx, ht, rh, W, l, mf, c, pfx, tiles):
        P = self.P
        sq, rsq = tiles["sq"], tiles["rsq"]
        ps, rps = tiles["ps"], tiles["rps"]
        rstd, rrstd = tiles["rstd"], tiles["rrstd"]
        ones = self.CON[:, C_ONES:C_ONES + 128]
        P.op("act", I("activation", out=sq[:, :, 0:W], in_=xt[:, :, 0:W], func=AF.Square), reads=[rx], writes=[rsq])
        mms = [dict(out=ps[:, 0:W], lhsT=ones, rhs=sq[:, k, 0:W], start=(k == 0), stop=(k == KC - 1)) for k in range(KC)]
        P.op("pe", MM(mms), reads=[rsq, self.r_con], writes=[rps])
        P.op("act", I("activation", out=rstd[:, 0:W], in_=ps[:, 0:W], func=AF.Sqrt, bias=self.EPSB[:, 0:1], scale=1.0),
             reads=[rps], writes=[rrstd])
        P.op("dve", I("reciprocal", out=rstd[:, 0:W], in_=rstd[:, 0:W]), reads=[rrstd], writes=[rrstd])
        P.op("dve", I("tensor_tensor", out=sq[:, :, 0:W], in0=xt[:, :, 0:W], in1=rstd[:, 0:W].unsqueeze(1).to_broadcast([128, KC, W]), op=ALU.mult),
             reads=[rx, rrstd, rsq], writes=[rsq])
        for k in range(KC):
            e = "act" if k % 2 == 0 else "pool"
            if e == "act":
                P.op("act", I("activation", out=ht[:, k, 0:W], in_=sq[:, k, 0:W], func=AF.Identity,
                              bias=self.modB(l, mf, c, k), scale=self.modA(l, mf, c, k)),
                     reads=[rsq, self.r_amf, self.r_mod], writes=[rh])
            else:
                P.op("pool", I("tensor_scalar", out=ht[:, k, 0:W], in0=sq[:, k, 0:W], scalar1=self.modA(l, mf, c, k),
                               scalar2=self.modB(l, mf, c, k), op0=ALU.mult, op1=ALU.add),
                     reads=[rsq, self.r_amf, self.r_mod], writes=[rh])

    def ffn_stage(self, l):
        P = self.P
        with P.stage():
            NT = 1024
            XTt = P.sb("f_xt", [128, KC, NT + 8], F32)
            rX = Res()
            SQ = P.sb("f_sq", [128, KC, 520], F32)
            rSQ = Res()
            RSTD = P.sb("f_rstd", [128, 520], F32)
            rRSTD = Res()
            HT = P.sb("f_ht", [128, KC, NT + 8], BF16)
            rH = RG(Res() for _ in range(KC))
            PB = P.sb("f_p", [128, NJ, NT], BF16)
            rPB = [Res() for _ in range(NJ)]
            UPW = [P.sb(f"f_upw{i}", [128, KC, 256], BF16) for i in range(4)]
            rUPW = [Res() for _ in range(4)]
            DNW = [P.sb(f"f_dnw{i}", [128, NJ, 128], BF16) for i in range(2)]
            rDNW = [Res() for _ in range(2)]
            UG = [P.sb(f"f_ug{i}", [128, 516], F32) for i in range(2)]
            UV = [P.sb(f"f_uv{i}", [128, 516], F32) for i in range(2)]
            rUG = [Res() for _ in range(2)]
            rUV = [Res() for _ in range(2)]
            AG = [P.sb(f"f_ag{i}", [128, 512], F32) for i in range(2)]
            AV = [P.sb(f"f_av{i}", [128, 512], F32) for i in range(2)]
            SG = [P.sb(f"f_sg{i}", [128, 512], F32) for i in range(2)]
            rAG = [Res() for _ in range(2)]
            rAV = [Res() for _ in range(2)]
            rSG = [Res() for _ in range(2)]
            SG2 = [P.sb(f"f_sg2{i}", [128, 512], F32) for i in range(2)]
            rSG2 = [Res() for _ in range(2)]
            PG = [P.ps(f"f_pg{i}", [128, 512], F32) for i in range(2)]
            PV = [P.ps(f"f_pv{i}", [128, 512], F32) for i in range(2)]
            PH = [P.ps(f"f_ph{i}", [128, 512], F32) for i in range(2)]
            PD = [P.ps(f"f_pd{i}", [128, 512], F32) for i in range(2)]
            rPG = [PRes() for _ in range(2)]
            rPV = [PRes() for _ in range(2)]
            rPH = [PRes() for _ in range(2)]
            rPD = [PRes() for _ in range(2)]
            ocw, _ = VEC_OFF["ffn_conv_w"]
            ocb, _ = VEC_OFF["ffn_conv_b"]

            def cw(tap, ch):
                o = ocw + (l * 3 + tap) * 44 + ch
                return self.VEC[:, o:o + 1]

            def cb(ch):
                o = ocb + l * 44 + ch
                return self.VEC[:, o:o + 1]

            tiles = []
            for i in range(TS // NT):
                units = []
                for u in range(NT // 512):
                    c0 = i * NT + u * 512
                    units.append((c0, 512, c0 > 0, c0 + 512 < TS))
                tiles.append((1, units))
            for g in range(NPC // 4):
                tiles.append((0, [(TS + (g * 4 + b) * SEQ, SEQ, False, False) for b in range(4)]))
            upi = 0
            dni = 0
            uix = 0
            n_up = len(tiles) * NJ
            n_dn = len(tiles) * KC
            issued = {"up": 0, "dn": 0}

            def ensure_up(i):
                while issued["up"] < min(i + 4, n_up):
                    q = issued["up"]
                    P.dma("pool", UPW[q % 4][:].rearrange("p k n -> p (k n)"), self.ffn_up[l, q % NJ], writes=[rUPW[q % 4]])
                    issued["up"] = q + 1

            def ensure_dn(i):
                while issued["dn"] < min(i + 2, n_dn):
                    q = issued["dn"]
                    P.dma("pool", DNW[q % 2][:].rearrange("p j n -> p (j n)"), self.ffn_dn[l, q % KC], writes=[rDNW[q % 2]])
                    issued["dn"] = q + 1

            for cond, units in tiles:
                offs = []
                off = 0
                for (c0, N, hl, hr) in units:
                    offs.append(off)
                    W = N + 2
                    lo = c0 - 1 if hl else c0
                    hi = c0 + N + 1 if hr else c0 + N
                    dlo = off + (0 if hl else 1)
                    P.dma("sp", XTt[:, :, dlo:dlo + (hi - lo)], self.XT[:, :, lo:hi], reads=[self.r_xt], writes=[rX])
                    if not hl:
                        P.op("pool", I("memset", XTt[:, :, off:off + 1], 0.0), writes=[rX])
                    if not hr:
                        P.op("pool", I("memset", XTt[:, :, off + W - 1:off + W], 0.0), writes=[rX])
                    off += W
                for ui, (c0, N, hl, hr) in enumerate(units):
                    W = N + 2
                    o = offs[ui]
                    tl = dict(sq=SQ, rsq=rSQ, ps=PD[0], rps=rPD[0], rstd=RSTD, rrstd=rRSTD)
                    self.norm_mod_wide(XTt, rX, HT, rH, o, W, l, 1, cond, tl, PD, rPD)
                    if not hl:
                        P.op("pool", I("memset", HT[:, :, o:o + 1], 0.0), writes=[rH])
                    if not hr:
                        P.op("pool", I("memset", HT[:, :, o + W - 1:o + W], 0.0), writes=[rH])
                for j in range(NJ):
                    ensure_up(upi)
                    ensure_dn(dni)
                    w, rw = UPW[upi % 4], rUPW[upi % 4]
                    upi += 1
                    pcol = 0
                    for ui, (c0, N, hl, hr) in enumerate(units):
                        W = N + 2
                        o = offs[ui]
                        b = uix % 2
                        uix += 1
                        Wm = min(W, 512)
                        for half, (pt, rpt) in enumerate(((PG[b], rPG[b]), (PV[b], rPV[b]))):
                            mms = [dict(out=pt[:, 0:Wm], lhsT=w[:, k, half * 128:(half + 1) * 128], rhs=HT[:, k, o:o + Wm],
                                        start=(k == 0), stop=(k == KC - 1)) for k in range(KC)]
                            P.op("pe", MM(mms), reads=[rw, rH], writes=[rpt])
                        if W > 512:
                            mms = []
                            for half in range(2):
                                mms += [dict(out=PH[b][:, half * 2:half * 2 + 2], lhsT=w[:, k, half * 128:(half + 1) * 128],
                                             rhs=HT[:, k, o + 512:o + W], start=(k == 0), stop=(k == KC - 1)) for k in range(KC)]
                            P.op("pe", MM(mms), reads=[rw, rH], writes=[rPH[b]])
                        P.op("act", I("copy", out=UG[b][:, 0:Wm], in_=PG[b][:, 0:Wm]), reads=[rPG[b]], writes=[rUG[b]])
                        P.op("act", I("copy", out=UV[b][:, 0:Wm], in_=PV[b][:, 0:Wm]), reads=[rPV[b]], writes=[rUV[b]])
                        if W > 512:
                            P.op("act", I("copy", out=UG[b][:, 512:514], in_=PH[b][:, 0:2]), reads=[rPH[b]], writes=[rUG[b]])
                            P.op("act", I("copy", out=UV[b][:, 512:514], in_=PH[b][:, 2:4]), reads=[rPH[b]], writes=[rUV[b]])
                        chg, chv = j, NJ + j
                        P.op("act", I("activation", out=AG[b][:, 0:N], in_=PG[b][:, 0:N], func=AF.Identity, scale=cw(0, chg), bias=cb(chg)),
                             reads=[rPG[b], self.r_vec], writes=[rAG[b]])
                        P.op("act", I("activation", out=AV[b][:, 0:N], in_=PV[b][:, 0:N], func=AF.Identity, scale=cw(0, chv), bias=cb(chv)),
                             reads=[rPV[b], self.r_vec], writes=[rAV[b]])
                        for tap in (1, 2):
                            P.op("dve", I("scalar_tensor_tensor", out=AG[b][:, 0:N], in0=UG[b][:, tap:N + tap], scalar=cw(tap, chg), in1=AG[b][:, 0:N],
                                          op0=ALU.mult, op1=ALU.add), reads=[rUG[b], rAG[b], self.r_vec], writes=[rAG[b]])
                            P.op("dve", I("scalar_tensor_tensor", out=AV[b][:, 0:N], in0=UV[b][:, tap:N + tap], scalar=cw(tap, chv), in1=AV[b][:, 0:N],
                                          op0=ALU.mult, op1=ALU.add), reads=[rUV[b], rAV[b], self.r_vec], writes=[rAV[b]])
                        P.op("act", I("activation", out=SG[b][:, 0:N], in_=AG[b][:, 0:N], func=AF.Silu), reads=[rAG[b]], writes=[rSG[b]])
                        P.op("pool", I("tensor_tensor", out=PB[:, j, pcol:pcol + N], in0=SG[b][:, 0:N], in1=AV[b][:, 0:N], op=ALU.mult),
                             reads=[rSG[b], rAV[b]], writes=[rPB[j]])
                        pcol += N
                for m in range(KC):
                    ensure_dn(dni)
                    ensure_up(upi)
                    w, rw = DNW[dni % 2], rDNW[dni % 2]
                    dni += 1
                    pcol = 0
                    for ui, (c0, N, hl, hr) in enumerate(units):
                        o = offs[ui]
                        b = uix % 2
                        uix += 1
                        mms = [dict(out=PD[b][:, 0:N], lhsT=w[:, j, :], rhs=PB[:, j, pcol:pcol + N], start=(j == 0), stop=(j == NJ - 1))
                               for j in range(NJ)]
                        P.op("pe", MM(mms), reads=[rw] + rPB, writes=[rPD[b]])
                        xs = XTt[:, m, o + 1:o + 1 + N]
                        P.op("dve", I("scalar_tensor_tensor", out=xs, in0=PD[b][:, 0:N], scalar=self.modG(l, 1, cond, m), in1=xs, op0=ALU.mult, op1=ALU.add),
                             reads=[rPD[b], rX, self.r_mod], writes=[rX])
                        pcol += N
                for ui, (c0, N, hl, hr) in enumerate(units):
                    o = offs[ui]
                    P.dma("sp", self.XT[:, :, c0:c0 + N], XTt[:, :, o + 1:o + 1 + N], reads=[rX], writes=[self.r_xt])

    def norm_mod_wide(self, XTt, rX, HT, rH, o, W, l, mf, cond, tl, PD, rPD, sqw=None):
        P = self.P
        SQ, rSQ, RSTD, rRSTD = tl["sq"], tl["rsq"], tl["rstd"], tl["rrstd"]
        ones = self.CON[:, C_ONES:C_ONES + 128]
        segs = [(0, min(W, 512))] + ([(512, W)] if W > 512 else [])
        for si, (a, bnd) in enumerate(segs):
            w = bnd - a
            ps, rps = PD[si % 2], rPD[si % 2]
            P.op("act", I("activation", out=SQ[:, :, 0:w], in_=XTt[:, :, o + a:o + bnd], func=AF.Square), reads=[rX], writes=[rSQ])
            mms = [dict(out=ps[:, 0:w], lhsT=ones, rhs=SQ[:, k, 0:w], start=(k == 0), stop=(k == KC - 1)) for k in range(KC)]
            P.op("pe", MM(mms), reads=[rSQ, self.r_con], writes=[rps])
            P.op("act", I("activation", out=RSTD[:, a:bnd], in_=ps[:, 0:w], func=AF.Sqrt, bias=self.EPSB[:, 0:1], scale=1.0),
                 reads=[rps], writes=[rRSTD])
            P.op("dve", I("reciprocal", out=RSTD[:, a:bnd], in_=RSTD[:, a:bnd]), reads=[rRSTD], writes=[rRSTD])
            P.op("dve", I("tensor_tensor", out=SQ[:, :, 0:w], in0=XTt[:, :, o + a:o + bnd], in1=RSTD[:, a:bnd].unsqueeze(1).to_broadcast([128, KC, w]), op=ALU.mult),
                 reads=[rX, rRSTD, rSQ], writes=[rSQ])
            for k in range(KC):
                if k % 2 == 0:
                    P.op("act", I("activation", out=HT[:, k, o + a:o + bnd], in_=SQ[:, k, 0:w], func=AF.Identity,
                                  bias=self.modB(l, mf, cond, k), scale=self.modA(l, mf, cond, k)),
                         reads=[rSQ, self.r_amf, self.r_mod], writes=[rH[k] if isinstance(rH, RG) else rH])
                else:
                    P.op("pool", I("tensor_scalar", out=HT[:, k, o + a:o + bnd], in0=SQ[:, k, 0:w], scalar1=self.modA(l, mf, cond, k),
                                   scalar2=self.modB(l, mf, cond, k), op0=ALU.mult, op1=ALU.add),
                         reads=[rSQ, self.r_amf, self.r_mod], writes=[rH[k] if isinstance(rH, RG) else rH])


    def even_stage(self, l):
        P = self.P
        j = l // 2
        N = 256
        NCH = N // 128
        QA, KD, QB, KB, GB, RR, TM = 0, 512, 768, 1024, 1280, 1792, 1856
        NCE = TM + 768
        with P.stage():
            WE = P.sb("e_we", [128, KC, NCE], BF16)
            WO = P.sb("e_wo", [128, KC, D], BF16)
            WG = P.sb("e_wg", [64, 512], BF16)
            rW = Res()
            P.dma("pool", WE[:].rearrange("p k n -> p (k n)"), self.w_even[j], writes=[rW])
            P.dma("pool", WO[:], self.ev_w_out[j].rearrange("(k p) n -> p k n", p=128), writes=[rW])
            P.dma("pool", WG[:], self.wg[j], writes=[rW])
            identf = self.CON[:, C_IDENT:C_IDENT + 128]
            identb = self.CONB[:, 0:128]
            ones = self.CON[:, C_ONES:C_ONES + 128]
            permf = self.CON[:, C_PERM:C_PERM + 128]
            PJ = [P.ps(f"e_pj{i}", [128, 512], F32) for i in range(3)]
            rPJ = [PRes() for _ in range(3)]
            ATp = [P.ps(f"e_at{i}", [128, 512], F32) for i in range(2)]
            rAT = [PRes() for _ in range(2)]
            OTp = P.ps("e_ot", [128, 512], F32)
            rOT = PRes()
            KVp = P.ps("e_kv", [128, 512], F32)
            rKV = PRes()
            TRb = P.ps("e_trb", [128, 1024], BF16)
            rTRb = PRes()
            pjc = [0]

            def pj():
                b = pjc[0] % 2
                pjc[0] += 1
                return PJ[b], rPJ[b]

            KCX = P.sb("e_kcx", [128, 2, 256], BF16)
            VCX = P.sb("e_vcx", [128, 2, 128], BF16)
            CKD = P.sb("e_ckd", [128, 2, 2, 2, 64], F32)
            rKCX, rVCX, rCKD = Res(), Res(), Res()
            ckv = self.cache_k[j].rearrange("(b t) (h d) -> t b h d", t=128, h=2)
            for dup in range(2):
                for blk in range(2):
                    P.dma("sp", CKD[:, blk, :, dup, :], ckv[:, blk], writes=[rCKD])
            for blk in range(2):
                for kvh in range(2):
                    pt, rpt = pj()
                    P.op("pe", TR([(pt[:, 0:128], CKD[:, blk, kvh].rearrange("t u d -> t (u d)"))], identf), reads=[rCKD, self.r_con], writes=[rpt])
                    P.op("act", I("copy", out=KCX[:, kvh, blk * 128:(blk + 1) * 128], in_=pt[:, 0:128]), reads=[rpt], writes=[rKCX])
            P.dma("pool", VCX[:], self.cache_v[j].rearrange("(b t) f -> t b f", t=128), writes=[rVCX])

            KR = P.sb("e_kr", [128, 2, TS], BF16)
            VA = P.sb("e_va", [128, TS // 128, 128], BF16)
            SBS = P.sb("e_sbs", [128, TS // 128, 2, 128], BF16)
            rKR = [Res() for _ in range(TS // N)]
            rVA = [Res() for _ in range(TS // N)]
            rSBS = [Res() for _ in range(TS // N)]
            SF = P.sb("e_sf", [128, 2, 256], F32)
            SB_ = P.sb("e_sb", [128, 2, 256], F32)
            SFb = P.sb("e_sfb", [128, 2, 256], BF16)
            rSF, rSB, rSFb = Res(), Res(), Res()
            XTt = P.sb("e_xt", [128, KC, N], F32)
            XR = P.sb("e_xr", [128, KC, N], F32)
            RSTD = P.sb("e_rstd", [128, N], F32)
            HT = P.sb("e_ht", [128, KC, N], BF16)
            ROP = P.sb("e_rop", [128, 2, N], F32)
            QS = P.sb("e_qs", [128, N], F32)
            T1 = P.sb("e_t1", [128, N], F32)
            T2 = P.sb("e_t2", [128, N], F32)
            QR = [P.sb(f"e_qr{i}", [128, 4, N], BF16) for i in range(2)]
            OB = [P.sb(f"e_ob{i}", [128, 4, N], BF16) for i in range(2)]
            OAT = P.sb("e_oat", [128, 4, N], BF16)
            GS = P.sb("e_gs", [128, 4, N], BF16)
            QBr = P.sb("e_qbr", [128, 2, N], F32)
            KBr = P.sb("e_kbr", [128, 2, N], F32)
            RT = P.sb("e_rt", [64, N], BF16)
            VB = P.sb("e_vb", [128, NCH, 512], BF16)
            KVO = P.sb("e_kvo", [128, NCH, 256], F32)
            Lg = P.sb("e_l", [128, 2, N], F32)
            CS = P.sb("e_cs", [128, 2, N], F32)
            E1 = P.sb("e_e1", [128, 2, N], F32)
            E2 = P.sb("e_e2", [128, 2, N], F32)
            E3 = P.sb("e_e3", [128, 2, N], F32)
            NB = P.sb("e_nb", [128, 2, NCH], F32)
            AD = P.sb("e_ad", [128, 2, NCH], F32)
            QT = [P.sb(f"e_qt{d}", [128, 2, N], BF16) for d in range(2)]
            KT = [P.sb(f"e_kt{d}", [128, 2, N], BF16) for d in range(2)]
            KHT = P.sb("e_kht", [128, 2, N], BF16)
            KH = P.sb("e_kh", [128, NCH, 2, 128], BF16)
            ATm = [P.sb(f"e_atm{d}", [128, 512], BF16) for d in range(2)]
            OSQ = P.sb("e_osq", [128, 512], F32)
            ORS = P.sb("e_ors", [128, 512], F32)
            OBF = P.sb("e_obf", [128, 512], F32)
            SM = [P.sb(f"e_sm{i}", [128, 640], F32) for i in range(2)]
            PX = [P.sb(f"e_px{i}", [128, 640], BF16) for i in range(2)]
            PTs = [P.sb(f"e_pts{i}", [128, 640], BF16) for i in range(2)]
            OAs = P.sb("e_oas", [128, 512], BF16)
            SMALL = [P.sb(f"e_small{i}", [128, 8], F32) for i in range(2)]
            RDEN = [P.sb(f"e_rden{i}", [128, 8], F32) for i in range(2)]
            rSMl = [Res() for _ in range(2)]
            rPXl = [Res() for _ in range(2)]
            rPTl = [Res() for _ in range(2)]
            rSMALLl = [Res() for _ in range(2)]
            rRDEN = [Res() for _ in range(2)]
            NBG = P.sb("e_nbg", [128, 4], F32)
            (rX, rXR, rRSTD, rH_, rROP, rQS, rT1, rT2, rOAT, rGS, rQBr, rKBr, rRT, rVB, rKVO, rL, rCS, rE1, rE2, rE3, rNB, rAD,
             rKHT, rKH, rOSQ, rORS, rOBF, rSM, rPX, rPTs, rOAs, rSMALL, rNBG) = (Res() for _ in range(33))
            rH = RG(Res() for _ in range(KC))
            rQR = [Res() for _ in range(2)]
            rOB = [Res() for _ in range(2)]
            rQT = [Res() for _ in range(2)]
            rKT = [Res() for _ in range(2)]
            rATm = [Res() for _ in range(2)]
            obg, _ = VEC_OFF["b_gate"]
            P.op("dve", I("tensor_scalar", out=NBG[:], in0=self.VEC[:, obg + j * 4: obg + j * 4 + 4], scalar1=-1.0, scalar2=None, op0=ALU.mult),
                 reads=[self.r_vec], writes=[rNBG])
            osk, _ = VEC_OFF["sink"]
            ogn, _ = VEC_OFF["gla_norm"]
            gn = self.VEC[:, ogn + j: ogn + j + 1]

            def proj_fm(col, M, dst_rows=128):
                pt, rpt = pj()
                mms = [dict(out=pt[0:M, 0:N], lhsT=WE[:, k, col:col + M], rhs=HT[:, k, 0:N], start=(k == 0), stop=(k == KC - 1)) for k in range(KC)]
                P.op("pe", MM(mms), reads=[rW, rH], writes=[rpt])
                return pt, rpt

            def gates(d, rev):
                for p in range(2):
                    pt, rpt = pj()
                    P.op("pe", MM([dict(out=pt[:, 0:N], lhsT=WG[0:64, d * 256 + p * 128: d * 256 + (p + 1) * 128], rhs=RT[0:64, 0:N], start=True, stop=True)]),
                         reads=[rW, rRT], writes=[rpt])
                    P.op("act", I("activation", out=Lg[:, p, :], in_=pt[:, 0:N], func=AF.Exp, bias=NBG[:, d * 2 + p: d * 2 + p + 1], scale=-1.0),
                         reads=[rpt, rNBG], writes=[rL])
                P.op("act", I("activation", out=Lg[:].rearrange("p a n -> p (a n)"), in_=Lg[:].rearrange("p a n -> p (a n)"), func=AF.Ln,
                              bias=self.ONEB[:, 0:1], scale=1.0), reads=[rL], writes=[rL])
                rst = self.CON[:, C_RST:C_RST + N]
                for p in range(2):
                    if not rev:
                        P.op("dve", I("tensor_tensor_scan", out=CS[:, p, :], data0=rst, data1=Lg[:, p, :], initial=0.0, op0=ALU.mult, op1=ALU.add),
                             reads=[rL, self.r_con], writes=[rCS])
                    else:
                        P.op("dve", I("tensor_tensor_scan", out=CS[:, p, :][:, ::-1], data0=rst, data1=Lg[:, p, :][:, ::-1], initial=0.0,
                                      op0=ALU.mult, op1=ALU.add), reads=[rL, self.r_con], writes=[rCS])

            def e3_and_decay(rev):
                first = 0 if rev else 127
                P.op("dve", I("tensor_scalar", out=NB[:], in0=CS[:, :, first::128], scalar1=-1.0 / 16.0, scalar2=None, op0=ALU.mult),
                     reads=[rCS], writes=[rNB])
                P.op("act", I("activation", out=AD[:].rearrange("p a c -> p (a c)"), in_=NB[:].rearrange("p a c -> p (a c)"), func=AF.Exp),
                     reads=[rNB], writes=[rAD])
                for p in range(2):
                    for c in range(NCH):
                        P.op("act", I("activation", out=E3[:, p, c * 128:(c + 1) * 128], in_=CS[:, p, c * 128:(c + 1) * 128], func=AF.Exp,
                                      bias=NB[:, p, c:c + 1], scale=1.0 / 16.0), reads=[rCS, rNB], writes=[rE3])
                P.op("dve", I("tensor_tensor", out=KHT[:], in0=KBr[:], in1=E3[:], op=ALU.mult), reads=[rKBr, rE3], writes=[rKHT])
                for c in range(NCH):
                    trs = [(TRb[:, (c * 2 + p) * 128:(c * 2 + p + 1) * 128], KHT[:, p, c * 128:(c + 1) * 128]) for p in range(2)]
                    P.op("pe", TR(trs, identb), reads=[rKHT, self.r_conb], writes=[rTRb])
                P.op("act", I("copy", out=KH[:].rearrange("t c p f -> t (c p f)"), in_=TRb[:, 0:NCH * 256]), reads=[rTRb], writes=[rKH])

            def kv_update(c, S, rS, extra_reads=()):
                mms = [dict(out=KVp[:, p * 256:(p + 1) * 256], lhsT=KH[:, c, p, :], rhs=VB[:, c, p * 256:(p + 1) * 256], start=True, stop=True) for p in range(2)]
                P.op("pe", MM(mms), reads=[rKH, rVB], writes=[rKV])
                for p in range(2):
                    P.op("dve", I("scalar_tensor_tensor", out=S[:, p, :], in0=S[:, p, :], scalar=AD[:, p, c:c + 1], in1=KVp[:, p * 256:(p + 1) * 256],
                                  op0=ALU.mult, op1=ALU.add), reads=[rKV, rAD, rS] + list(extra_reads), writes=[rS])

            def load_norm(t0, cond):
                P.dma("sp", XTt[:], self.XT[:, :, t0:t0 + N], reads=[self.r_xt], writes=[rX])
                tl = dict(sq=XR, rsq=rXR, rstd=RSTD, rrstd=rRSTD)
                self.norm_mod_wide(XTt, rX, HT, rH, 0, N, l, 0, cond, tl, [PJ[0], PJ[1]], [rPJ[0], rPJ[1]])

            def vb_proj():
                for c in range(NCH):
                    pt, rpt = pj()
                    mms = [dict(out=pt[:, 0:512], lhsT=HT[:, k, c * 128:(c + 1) * 128], rhs=WE[:, k, TM + 256:TM + 768], start=(k == 0), stop=(k == KC - 1))
                           for k in range(KC)]
                    P.op("pe", MM(mms), reads=[rW, rH], writes=[rpt])
                    P.op("act", I("copy", out=VB[:, c, :], in_=pt[:, 0:512]), reads=[rpt], writes=[rVB])

            def r_proj():
                pt, rpt = proj_fm(RR, 64)
                P.op("act", I("copy", out=RT[:, :], in_=pt[0:64, 0:N]), reads=[rpt], writes=[rRT])

            def kb_proj():
                for p in range(2):
                    pt, rpt = proj_fm(KB + p * 128, 128)
                    P.op("act", I("copy", out=KBr[:, p, :], in_=pt[:, 0:N]), reads=[rpt], writes=[rKBr])

            seqs = [(0, TS, 1, None)] + [(TS + b * SEQ, SEQ, 0, b) for b in range(NPC)]
            if self.dbg_seqs is not None:
                seqs = [seqs[i] for i in self.dbg_seqs]
            if self.dbg_cut <= 1:
                seqs = []
            for (s0, T, cond, pb) in seqs:
                ntile = T // N
                nblk = T // 128
                sample = pb is None
                for (S, rS, d) in ((SF, rSF, 0), (SB_, rSB, 1)):
                    P.op("pool", I("memset", S[:], 0.0), writes=[rS])
                    if sample:
                        for p in range(2):
                            for hh in range(2):
                                P.dma("sp", S[hh * 64:(hh + 1) * 64, p, hh * 128:(hh + 1) * 128], self.state_gla[j, d, 2 * p + hh], writes=[rS])
                for ti in range(ntile - 1, -1, -1) if self.dbg_cut > 2 else []:
                    t0 = s0 + ti * N
                    load_norm(t0, cond)
                    r_proj()
                    kb_proj()
                    vb_proj()
                    gates(1, True)
                    e3_and_decay(True)
                    for c in range(NCH - 1, -1, -1):
                        gc = ti * NCH + c
                        for p in range(2):
                            P.op("pool", I("tensor_copy", out=SBS[0:64, gc, p, :], in_=SB_[0:64, p, 0:128]), reads=[rSB], writes=[rSBS[ti]])
                            P.op("pool", I("tensor_copy", out=SBS[64:128, gc, p, :], in_=SB_[64:128, p, 128:256]), reads=[rSB], writes=[rSBS[ti]])
                        kv_update(c, SB_, rSB)
                if not sample:
                    for p in range(2):
                        for hh in range(2):
                            P.dma("sp", self.ngla[pb, j, 1, 2 * p + hh], SB_[hh * 64:(hh + 1) * 64, p, hh * 128:(hh + 1) * 128], reads=[rSB], writes=[self.r_out])
                P.op("act", I("copy", out=SFb[:], in_=SF[:]), reads=[rSF], writes=[rSFb])

                def phase_a(ti):
                    t0 = s0 + ti * N
                    bq = ti % 2
                    load_norm(t0, cond)
                    if sample:
                        P.dma("sp", ROP[:], self.rope[:, :, ti * N:(ti + 1) * N], writes=[rROP])

                    def rope_or_copy(pt, rpt, dst, rdst):
                        if not sample:
                            P.op("act", I("copy", out=dst, in_=pt[:, 0:N]), reads=[rpt], writes=[rdst])
                            return
                        P.op("act", I("copy", out=QS[:], in_=pt[:, 0:N]), reads=[rpt], writes=[rQS])
                        p2, rp2 = pj()
                        P.op("pe", MM([dict(out=p2[:, 0:N], lhsT=permf, rhs=QS[:], start=True, stop=True)]), reads=[rQS, self.r_con], writes=[rp2])
                        P.op("pool", I("tensor_tensor", out=T1[:], in0=QS[:], in1=ROP[:, 0, :], op=ALU.mult), reads=[rQS, rROP], writes=[rT1])
                        P.op("dve", I("tensor_tensor", out=T2[:], in0=p2[:, 0:N], in1=ROP[:, 1, :], op=ALU.mult), reads=[rp2, rROP], writes=[rT2])
                        P.op("pool", I("tensor_tensor", out=dst, in0=T1[:], in1=T2[:], op=ALU.add), reads=[rT1, rT2], writes=[rdst])

                    if self.dbg_cut <= 4.1:
                        return
                    for cc in range(4):
                        pt, rpt = proj_fm(QA + cc * 128, 128)
                        rope_or_copy(pt, rpt, QR[bq][:, cc, :], rQR[bq])
                    for kvh in range(2):
                        pt, rpt = proj_fm(KD + kvh * 128, 128)
                        rope_or_copy(pt, rpt, KR[:, kvh, ti * N:(ti + 1) * N], rKR[ti])
                    if self.dbg_cut <= 4.2:
                        return
                    for p in range(2):
                        pt, rpt = proj_fm(QB + p * 128, 128)
                        P.op("act", I("activation", out=QBr[:, p, :], in_=pt[:, 0:N], func=AF.Identity, scale=0.125), reads=[rpt], writes=[rQBr])
                    kb_proj()
                    for h in range(4):
                        pt, rpt = proj_fm(GB + h * 128, 128)
                        P.op("act", I("activation", out=GS[:, h, :], in_=pt[:, 0:N], func=AF.Silu), reads=[rpt], writes=[rGS])
                    r_proj()
                    vb_proj()
                    if self.dbg_cut <= 4.3:
                        return
                    for c in range(NCH):
                        pt, rpt = pj()
                        mms = [dict(out=pt[:, 0:256], lhsT=HT[:, k, c * 128:(c + 1) * 128], rhs=WE[:, k, TM:TM + 256], start=(k == 0), stop=(k == KC - 1))
                               for k in range(KC)]
                        P.op("pe", MM(mms), reads=[rW, rH], writes=[rpt])
                        P.op("act", I("copy", out=VA[:, ti * NCH + c, :], in_=pt[:, 128:256]), reads=[rpt], writes=[rVA[ti]])
                        if not sample:
                            P.op("dve", I("tensor_copy", out=KVO[:, c, :], in_=pt[:, 0:256]), reads=[rpt], writes=[rKVO])
                        if not sample and self.dbg_cut != 4.35:
                            P.dma("sp", self.nk[pb, j, (ti * NCH + c) * 128:(ti * NCH + c + 1) * 128, :], KVO[:, c, 0:128], reads=[rKVO], writes=[self.r_out])
                            P.dma("sp", self.nv[pb, j, (ti * NCH + c) * 128:(ti * NCH + c + 1) * 128, :], KVO[:, c, 128:256], reads=[rKVO], writes=[self.r_out])
                    if self.dbg_cut <= 4.4:
                        return
                    for d in range(2):
                        gates(d, d == 1)
                        flat = lambda t: t[:].rearrange("p a n -> p (a n)")
                        P.op("act", I("activation", out=flat(E1), in_=flat(CS), func=AF.Exp, scale=-1.0 / 16.0), reads=[rCS], writes=[rE1])
                        P.op("act", I("activation", out=flat(E2), in_=flat(CS), func=AF.Exp, scale=1.0 / 16.0), reads=[rCS], writes=[rE2])
                        P.op("dve", I("tensor_tensor", out=QT[d][:], in0=QBr[:], in1=E1[:], op=ALU.mult), reads=[rQBr, rE1], writes=[rQT[d]])
                        P.op("pool", I("tensor_tensor", out=KT[d][:], in0=KBr[:], in1=E2[:], op=ALU.mult), reads=[rKBr, rE2], writes=[rKT[d]])
                        if d == 0:
                            e3_and_decay(False)
                    if self.dbg_cut <= 4.5:
                        return
                    for c in range(NCH):
                        gc = ti * NCH + c
                        cs_ = slice(c * 128, (c + 1) * 128)
                        for hh in range(2):
                            mms = []
                            for d in range(2):
                                for p in range(2):
                                    mms.append(dict(out=ATp[hh][:, (d * 2 + p) * 128:(d * 2 + p + 1) * 128], lhsT=KT[d][hh * 64:(hh + 1) * 64, p, cs_],
                                                    rhs=QT[d][hh * 64:(hh + 1) * 64, p, cs_], start=True, stop=True))
                            P.op("pe", MM(mms), reads=[rKT[0], rKT[1], rQT[0], rQT[1]], writes=[rAT[hh]])
                            for d in range(2):
                                mo_ = C_MF if d == 0 else C_MB
                                mk = self.CON[:, mo_:mo_ + 128].unsqueeze(1).to_broadcast([128, 2, 128])
                                P.op("dve", I("tensor_tensor", out=ATm[hh][:, d * 256:(d + 1) * 256].rearrange("p (h n) -> p h n", h=2),
                                              in0=ATp[hh][:, d * 256:(d + 1) * 256].rearrange("p (h n) -> p h n", h=2), in1=mk, op=ALU.mult),
                                     reads=[rAT[hh], self.r_con], writes=[rATm[hh]])
                        mms = []
                        for h in range(4):
                            p, hh = h // 2, h % 2
                            o = OTp[:, h * 128:(h + 1) * 128]
                            mms.append(dict(out=o, lhsT=VB[:, c, h * 128:(h + 1) * 128], rhs=ATm[hh][:, (0 * 2 + p) * 128:(0 * 2 + p + 1) * 128], start=True, stop=False))
                            mms.append(dict(out=o, lhsT=VB[:, c, h * 128:(h + 1) * 128], rhs=ATm[hh][:, (1 * 2 + p) * 128:(1 * 2 + p + 1) * 128], start=False, stop=False))
                            mms.append(dict(out=o, lhsT=SFb[hh * 64:(hh + 1) * 64, p, hh * 128:(hh + 1) * 128], rhs=QT[0][hh * 64:(hh + 1) * 64, p, cs_],
                                            start=False, stop=False))
                            mms.append(dict(out=o, lhsT=SBS[hh * 64:(hh + 1) * 64, gc, p, :], rhs=QT[1][hh * 64:(hh + 1) * 64, p, cs_],
                                            start=False, stop=True))
                        P.op("pe", MM(mms), reads=[rVB, rATm[0], rATm[1], rSFb, rSBS[ti], rQT[0], rQT[1]], writes=[rOT])
                        kv_update(c, SF, rSF)
                        P.op("act", I("copy", out=SFb[:], in_=SF[:]), reads=[rSF], writes=[rSFb])
                        P.op("act", I("activation", out=OSQ[:], in_=OTp[:], func=AF.Square), reads=[rOT], writes=[rOSQ])
                        pt, rpt = pj()
                        P.op("pe", MM([dict(out=pt[:, 0:512], lhsT=ones, rhs=OSQ[:], start=True, stop=True)]), reads=[rOSQ, self.r_con], writes=[rpt])
                        P.op("act", I("activation", out=ORS[:], in_=pt[:, 0:512], func=AF.Sqrt, bias=self.EPSB[:, 0:1], scale=8.0), reads=[rpt], writes=[rORS])
                        P.op("dve", I("reciprocal", out=ORS[:], in_=ORS[:]), reads=[rORS], writes=[rORS])
                        P.op("dve", I("tensor_tensor", out=OBF[:], in0=OTp[:], in1=ORS[:], op=ALU.mult), reads=[rOT, rORS], writes=[rOBF])
                        P.op("dve", I("scalar_tensor_tensor", out=OB[bq][:, :, cs_], in0=OBF[:].rearrange("p (h n) -> p h n", h=4), scalar=gn,
                                      in1=GS[:, :, cs_], op0=ALU.mult, op1=ALU.mult), reads=[rOBF, rGS, self.r_vec], writes=[rOB[bq]])

                def phase_b(ti):
                    t0 = s0 + ti * N
                    bq = ti % 2
                    P.dma("sp", XR[:], self.XT[:, :, t0:t0 + N], reads=[self.r_xt], writes=[rXR])
                    for c in range(NCH):
                        qb = ti * NCH + c
                        cs_ = slice(c * 128, (c + 1) * 128)
                        if sample:
                            kb0, kb1 = max(0, qb - 1), min(nblk, qb + 2)
                        else:
                            kb0, kb1 = 0, nblk
                        nw = (kb1 - kb0) * 128
                        ntot = nw + (256 if sample else 0)
                        kread = sorted(set((b * 128) // N for b in range(kb0, kb1)))
                        pa, rpa = PJ[2], rPJ[2]
                        rd, rrd = RDEN[qb % 2], rRDEN[qb % 2]
                        def qk(h):
                            cc, hh, kvh = h // 2, h % 2, h // 4
                            par = h % 2
                            rows = slice(hh * 64, (hh + 1) * 64)
                            sw, rsw = (ATp[0], rAT[0]) if par == 0 else (OTp, rOT)
                            sc, rsc = (ATp[1], rAT[1]) if par == 0 else (KVp, rKV)
                            P.op("pe", MM([dict(out=sw[:, 0:nw], lhsT=QR[bq][rows, cc, cs_], rhs=KR[rows, kvh, kb0 * 128:kb1 * 128], start=True, stop=True)]),
                                 reads=[rQR[bq]] + [rKR[t] for t in kread], writes=[rsw])
                            if sample:
                                P.op("pe", MM([dict(out=sc[:, 0:256], lhsT=QR[bq][rows, cc, cs_], rhs=KCX[rows, kvh, :], start=True, stop=True)]),
                                     reads=[rQR[bq], rKCX], writes=[rsc])

                        def stage1(h):
                            par = h % 2
                            sw, rsw = (ATp[0], rAT[0]) if par == 0 else (OTp, rOT)
                            sc, rsc = (ATp[1], rAT[1]) if par == 0 else (KVp, rKV)
                            SMh, rSMh, PXh, rPXh, SMALLh, rSMALLh = SM[par], rSMl[par], PX[par], rPXl[par], SMALL[par], rSMALLl[par]
                            if sample:
                                mo = C_AMASK + (kb0 - (qb - 1)) * 128
                                P.op("dve", I("tensor_tensor", out=SMh[:, 0:nw], in0=sw[:, 0:nw], in1=self.CON[:, mo:mo + nw], op=ALU.add),
                                     reads=[rsw, self.r_con], writes=[rSMh])
                                P.op("act", I("copy", out=SMh[:, nw:ntot], in_=sc[:, 0:256]), reads=[rsc], writes=[rSMh])
                            else:
                                P.op("act", I("copy", out=SMh[:, 0:nw], in_=sw[:, 0:nw]), reads=[rsw], writes=[rSMh])
                            sk = self.VEC[:, osk + j * 8 + h: osk + j * 8 + h + 1]
                            P.op("dve", I("reduce_max", out=SMALLh[:, 0:1], in_=SMh[:, 0:ntot], axis=mybir.AxisListType.X), reads=[rSMh], writes=[rSMALLh])
                            P.op("dve", I("tensor_scalar", out=SMALLh[:, 1:2], in0=SMALLh[:, 0:1], scalar1=0.125, scalar2=sk, op0=ALU.mult, op1=ALU.max),
                                 reads=[rSMALLh, self.r_vec], writes=[rSMALLh])
                            P.op("dve", I("tensor_scalar", out=SMALLh[:, 2:3], in0=SMALLh[:, 1:2], scalar1=-1.0, scalar2=None, op0=ALU.mult),
                                 reads=[rSMALLh], writes=[rSMALLh])
                            P.op("act", I("activation", out=PXh[:, 0:ntot], in_=SMh[:, 0:ntot], func=AF.Exp, bias=SMALLh[:, 2:3], scale=0.125, accum_out=SMALLh[:, 3:4]),
                                 reads=[rSMh, rSMALLh], writes=[rPXh, rSMALLh])
                            P.op("act", I("activation", out=SMALLh[:, 4:5], in_=sk, func=AF.Exp, bias=SMALLh[:, 2:3], scale=1.0),
                                 reads=[rSMALLh, self.r_vec], writes=[rSMALLh])

                        def stage2(h):
                            kvh = h // 4
                            par = h % 2
                            PXh, rPXh, PTh, rPTh, SMALLh, rSMALLh = PX[par], rPXl[par], PTs[par], rPTl[par], SMALL[par], rSMALLl[par]
                            P.op("dve", I("tensor_tensor", out=SMALLh[:, 5:6], in0=SMALLh[:, 3:4], in1=SMALLh[:, 4:5], op=ALU.add), reads=[rSMALLh], writes=[rSMALLh])
                            P.op("dve", I("reciprocal", out=rd[:, h:h + 1], in_=SMALLh[:, 5:6]), reads=[rSMALLh], writes=[rrd])
                            nkb = ntot // 128
                            trs = [(TRb[:, b * 128:(b + 1) * 128], PXh[:, b * 128:(b + 1) * 128]) for b in range(nkb)]
                            P.op("pe", TR(trs, identb), reads=[rPXh, self.r_conb], writes=[rTRb])
                            P.op("dve", I("tensor_copy", out=PTh[:, 0:ntot], in_=TRb[:, 0:ntot]), reads=[rTRb], writes=[rPTh])
                            mms = []
                            for b in range(nkb):
                                if b < kb1 - kb0:
                                    vv = VA[:, kb0 + b, kvh * 64:(kvh + 1) * 64]
                                else:
                                    vv = VCX[:, b - (kb1 - kb0), kvh * 64:(kvh + 1) * 64]
                                mms.append(dict(out=pa[:, h * 64:(h + 1) * 64], lhsT=PTh[:, b * 128:(b + 1) * 128], rhs=vv, start=(b == 0), stop=(b == nkb - 1)))
                            P.op("pe", MM(mms), reads=[rPTh, rVCX] + [rVA[t] for t in kread], writes=[rpa])

                        qk(0)
                        qk(1)
                        stage1(0)
                        for h in range(8):
                            if h + 1 < 8:
                                stage1(h + 1)
                            if h + 2 < 8:
                                qk(h + 2)
                            stage2(h)
                        P.op("dve", I("tensor_tensor", out=OAs[:].rearrange("p (h d) -> p h d", h=8), in0=pa[:].rearrange("p (h d) -> p h d", h=8),
                                      in1=rd[:].unsqueeze(2).to_broadcast([128, 8, 64]), op=ALU.mult), reads=[rpa, rrd], writes=[rOAs])
                        trs = [(TRb[:, k4 * 128:(k4 + 1) * 128], OAs[:, k4 * 128:(k4 + 1) * 128]) for k4 in range(4)]
                        P.op("pe", TR(trs, identb), reads=[rOAs, self.r_conb], writes=[rTRb])
                        P.op("dve", I("tensor_copy", out=OAT[:, :, cs_], in_=TRb[:, 0:512].rearrange("p (k n) -> p k n", k=4)), reads=[rTRb], writes=[rOAT])
                    for m in range(KC):
                        pt, rpt = pj()
                        mms = []
                        for k in range(KC):
                            rhs = OAT[:, k, :] if k < 4 else OB[bq][:, k - 4, :]
                            mms.append(dict(out=pt[:, 0:N], lhsT=WO[:, k, m * 128:(m + 1) * 128], rhs=rhs, start=(k == 0), stop=(k == KC - 1)))
                        P.op("pe", MM(mms), reads=[rW, rOAT, rOB[bq]], writes=[rpt])
                        P.op("dve", I("scalar_tensor_tensor", out=XR[:, m, :], in0=pt[:, 0:N], scalar=self.modG(l, 0, cond, m), in1=XR[:, m, :], op0=ALU.mult, op1=ALU.add),
                             reads=[rpt, rXR, self.r_mod], writes=[rXR])
                    P.dma("sp", self.XT[:, :, t0:t0 + N], XR[:], reads=[rXR], writes=[self.r_xt])

                for ti in range(ntile + 1):
                    if ti < ntile and self.dbg_cut > 3:
                        phase_a(ti)
                    if ti >= 1 and self.dbg_cut > 4:
                        phase_b(ti - 1)
                if not sample:
                    for p in range(2):
                        for hh in range(2):
                            P.dma("sp", self.ngla[pb, j, 0, 2 * p + hh], SF[hh * 64:(hh + 1) * 64, p, hh * 128:(hh + 1) * 128], reads=[rSF], writes=[self.r_out])

    def odd_stage(self, l):
        P = self.P
        j = l // 2
        NMAX = 512
        WMAX = NMAX + 3
        with P.stage():
            WIN = P.sb("o_win", [128, KC, 2048], BF16)
            WOUT = P.sb("o_wout", [128, KC, D], BF16)
            BD = P.sb("o_bd", [128, 4, KC, 128], BF16)
            rW = Res()
            P.dma("pool", WIN[:], self.od_w_in[j].rearrange("(k p) n -> p k n", p=128), writes=[rW])
            P.dma("pool", WOUT[:], self.od_w_out[j].rearrange("(k p) n -> p k n", p=128), writes=[rW])
            for di in range(2):
                for ai in range(2):
                    P.dma("pool", BD[:, di * 2 + ai].rearrange("p k n -> p (k n)"), self.od_bd[j, di, ai], writes=[rW])
            CL = P.sb("o_cl", [128, 2, 2, KC], F32)
            rCL = Res()
            ol, _ = VEC_OFF["od_lambda"]
            lam = self.VEC[:, ol + j * 16: ol + j * 16 + 16]
            cl0 = CL[:, 0].rearrange("p d k -> p (d k)")
            cl1 = CL[:, 1].rearrange("p d k -> p (d k)")
            P.op("act", I("activation", out=cl0, in_=lam, func=AF.Exp, scale=-1.0), reads=[self.r_vec], writes=[rCL])
            P.op("act", I("activation", out=cl0, in_=cl0, func=AF.Ln, bias=self.ONEB[:, 0:1], scale=1.0), reads=[rCL], writes=[rCL])
            P.op("dve", I("tensor_scalar", out=cl1, in0=cl0, scalar1=-8.0, scalar2=None, op0=ALU.mult), reads=[rCL], writes=[rCL])
            P.op("dve", I("tensor_scalar", out=cl0, in0=cl0, scalar1=-4.0, scalar2=None, op0=ALU.mult), reads=[rCL], writes=[rCL])
            HBA = P.sb("o_hba", [128, 2, 16], F32)
            rHBA = Res()
            oba_, _ = VEC_OFF["od_b_a"]
            obi_, _ = VEC_OFF["od_b_i"]
            P.op("dve", I("tensor_scalar", out=HBA[:, 0, :], in0=self.VEC[:, oba_ + j * 16: oba_ + j * 16 + 16], scalar1=0.5, scalar2=None, op0=ALU.mult),
                 reads=[self.r_vec], writes=[rHBA])
            P.op("dve", I("tensor_scalar", out=HBA[:, 1, :], in0=self.VEC[:, obi_ + j * 16: obi_ + j * 16 + 16], scalar1=0.5, scalar2=None, op0=ALU.mult),
                 reads=[self.r_vec], writes=[rHBA])

            XTt = P.sb("o_xt", [128, KC, WMAX + 1], F32)
            HT = P.sb("o_ht", [128, KC, WMAX + 1], BF16)
            GYt = P.sb("o_gy", [128, KC, NMAX], BF16)
            U = [P.sb(f"o_u{i}", [128, WMAX + 1], F32) for i in range(2)]
            XCt = P.sb("o_xc", [128, KC, NMAX], F32)
            XCb = P.sb("o_xcb", [128, KC, NMAX], BF16)
            HFt = P.sb("o_hf", [128, KC, NMAX], F32)
            HBt = P.sb("o_hb", [128, KC, NMAX], F32)
            Zt = P.sb("o_z", [128, KC, NMAX], BF16)
            RSTD = P.sb("o_rstd", [128, WMAX + 1], F32)
            CAR = P.sb("o_car", [128, KC], F32)
            GG = 4
            Rr = [P.sb(f"o_r{i}", [128, NMAX], F32) for i in range(GG)]
            Ii = [P.sb(f"o_i{i}", [128, NMAX], F32) for i in range(GG)]
            Aa = [P.sb(f"o_a{i}", [128, NMAX], F32) for i in range(GG)]
            Mm = [P.sb(f"o_m{i}", [128, NMAX], F32) for i in range(GG)]
            rX, rH_, rGY, rXC_, rXCb_, rHF_, rHB_, rZ_, rRSTD, rCAR = (Res() for _ in range(10))
            rH = RG(Res() for _ in range(KC))
            rXC = RG(Res() for _ in range(KC))
            rXCb = RG(Res() for _ in range(KC))
            rHF = RG(Res() for _ in range(KC))
            rHB = RG(Res() for _ in range(KC))
            rZ = RG(Res() for _ in range(KC))
            rU = [Res() for _ in range(2)]
            rR = [Res() for _ in range(4)]
            rI = [Res() for _ in range(4)]
            rA = [Res() for _ in range(4)]
            rM = [Res() for _ in range(4)]
            PA = [P.ps(f"o_pa{i}", [128, 512], F32) for i in range(2)]
            PBk = [P.ps(f"o_pb{i}", [128, 512], F32) for i in range(2)]
            PHh = [P.ps(f"o_ph{i}", [128, 512], F32) for i in range(2)]
            PD = [P.ps(f"o_pd{i}", [128, 512], F32) for i in range(2)]
            rPA = [PRes() for _ in range(2)]
            rPB = [PRes() for _ in range(2)]
            rPH = [PRes() for _ in range(2)]
            rPD = [PRes() for _ in range(2)]
            ocw, _ = VEC_OFF["od_conv_w"]
            ocb, _ = VEC_OFF["od_conv_b"]
            oba, _ = VEC_OFF["od_b_a"]
            obi, _ = VEC_OFF["od_b_i"]
            ost, _ = VEC_OFF["st_rg"]

            def vcol(o):
                return self.VEC[:, o:o + 1]

            cwt = lambda tap, k: vcol(ocw + (j * 4 + tap) * 8 + k)
            cbt = lambda k: vcol(ocb + j * 8 + k)
            bat = lambda di, k: vcol(oba + (j * 2 + di) * 8 + k)
            bit = lambda di, k: vcol(obi + (j * 2 + di) * 8 + k)
            h0t = lambda di, k: vcol(ost + (j * 2 + di) * 8 + k)
            cnt = [0]

            def gates_group(di, N, ks, HO, rHO, inits):
                for g, k in enumerate(ks):
                    b = cnt[0] % 2
                    cnt[0] += 1
                    P.op("pe", MM([dict(out=PA[b][:, 0:N], lhsT=BD[:, di * 2 + 0, k, :], rhs=XCb[:, k, 0:N], start=True, stop=True)]),
                         reads=[rW, rXCb[k]], writes=[rPA[b]])
                    P.op("pe", MM([dict(out=PBk[b][:, 0:N], lhsT=BD[:, di * 2 + 1, k, :], rhs=XCb[:, k, 0:N], start=True, stop=True)]),
                         reads=[rW, rXCb[k]], writes=[rPB[b]])
                    P.op("act", I("activation", out=Rr[g][:, 0:N], in_=PA[b][:, 0:N], func=AF.Tanh, bias=HBA[:, 0, di * 8 + k: di * 8 + k + 1], scale=0.5),
                         reads=[rPA[b], rHBA], writes=[rR[g]])
                    P.op("act", I("activation", out=Ii[g][:, 0:N], in_=PBk[b][:, 0:N], func=AF.Tanh, bias=HBA[:, 1, di * 8 + k: di * 8 + k + 1], scale=0.5),
                         reads=[rPB[b], rHBA], writes=[rI[g]])
                for g, k in enumerate(ks):
                    P.op("act", I("activation", out=Aa[g][:, 0:N], in_=Rr[g][:, 0:N], func=AF.Exp, scale=CL[:, 0, di, k:k + 1], bias=CL[:, 0, di, k:k + 1]),
                         reads=[rR[g], rCL], writes=[rA[g]])
                    P.op("act", I("activation", out=Mm[g][:, 0:N], in_=Rr[g][:, 0:N], func=AF.Exp, scale=CL[:, 1, di, k:k + 1], bias=CL[:, 1, di, k:k + 1]),
                         reads=[rR[g], rCL], writes=[rM[g]])
                    P.op("dve", I("scalar_tensor_tensor", out=Ii[g][:, 0:N], in0=Ii[g][:, 0:N], scalar=1.0, in1=XCt[:, k, 0:N], op0=ALU.add, op1=ALU.mult),
                         reads=[rI[g], rXC[k]], writes=[rI[g]])
                for g, k in enumerate(ks):
                    P.op("act", I("activation", out=Mm[g][:, 0:N], in_=Mm[g][:, 0:N], func=AF.Sqrt, bias=self.QTRB[:, 0:1], scale=-0.25),
                         reads=[rM[g]], writes=[rM[g]])
                    P.op("dve", I("tensor_tensor", out=Mm[g][:, 0:N], in0=Mm[g][:, 0:N], in1=Ii[g][:, 0:N], op=ALU.mult),
                         reads=[rM[g], rI[g]], writes=[rM[g]])
                    init_ap, rinit = inits[g]
                    if di == 0:
                        P.op("dve", I("tensor_tensor_scan", out=HO[:, k, 0:N], data0=Aa[g][:, 0:N], data1=Mm[g][:, 0:N], initial=init_ap,
                                      op0=ALU.mult, op1=ALU.add), reads=[rA[g], rM[g]] + rinit, writes=[rHO[k]])
                    else:
                        P.op("dve", I("tensor_tensor_scan", out=HO[:, k, 0:N][:, ::-1],
                                      data0=Aa[g][:, 0:N][:, ::-1], data1=Mm[g][:, 0:N][:, ::-1], initial=init_ap,
                                      op0=ALU.mult, op1=ALU.add), reads=[rA[g], rM[g]] + rinit, writes=[rHO[k]])

            seqs = [(0, TS, 1, None)] + [(TS + b * SEQ, SEQ, 0, b) for b in range(NPC)]
            for (s0, T, cond, pb) in seqs:
                N = min(NMAX, T)
                ntile = T // N
                W = N + 3
                for ti in range(ntile):
                    t0 = s0 + ti * N
                    hl = ti > 0
                    hr = ti < ntile - 1
                    lo = t0 - 2 if hl else t0
                    hi = t0 + N + 1 if hr else t0 + N
                    dlo = 0 if hl else 2
                    P.dma("sp", XTt[:, :, dlo:dlo + (hi - lo)], self.XT[:, :, lo:hi], reads=[self.r_xt], writes=[rX])
                    if not hl:
                        P.op("pool", I("memset", XTt[:, :, 0:2], 0.0), writes=[rX])
                    if not hr:
                        P.op("pool", I("memset", XTt[:, :, W - 1:W], 0.0), writes=[rX])
                    tl = dict(sq=HBt, rsq=rHB, rstd=RSTD, rrstd=rRSTD)
                    self.norm_mod_wide(XTt, rX, HT, rH, 0, W, l, 0, cond, tl, PD, rPD, sqw=NMAX)
                    if not hl:
                        P.op("pool", I("memset", HT[:, :, 0:2], 0.0), writes=[rH])
                    if not hr:
                        P.op("pool", I("memset", HT[:, :, W - 1:W], 0.0), writes=[rH])
                    for k in range(KC):
                        b = cnt[0] % 2
                        cnt[0] += 1
                        mms = [dict(out=PA[b][:, 0:N], lhsT=WIN[:, kk, k * 128:(k + 1) * 128], rhs=HT[:, kk, 2:N + 2],
                                    start=(kk == 0), stop=(kk == KC - 1)) for kk in range(KC)]
                        P.op("pe", MM(mms), reads=[rW, rH], writes=[rPA[b]])
                        P.op("act", I("activation", out=GYt[:, k, 0:N], in_=PA[b][:, 0:N], func=AF.Gelu), reads=[rPA[b]], writes=[rGY])
                    P.dma("sp", self.GY[:, :, t0:t0 + N], GYt[:, :, 0:N], reads=[rGY], writes=[self.r_gy])
                    for k in range(KC):
                        b = cnt[0] % 2
                        cnt[0] += 1
                        Wm = min(W, 512)
                        mms = [dict(out=PBk[b][:, 0:Wm], lhsT=WIN[:, kk, D + k * 128:D + (k + 1) * 128], rhs=HT[:, kk, 0:Wm],
                                    start=(kk == 0), stop=(kk == KC - 1)) for kk in range(KC)]
                        P.op("pe", MM(mms), reads=[rW, rH], writes=[rPB[b]])
                        P.op("act", I("copy", out=U[b][:, 0:Wm], in_=PBk[b][:, 0:Wm]), reads=[rPB[b]], writes=[rU[b]])
                        if W > 512:
                            mms = [dict(out=PHh[b][:, 0:W - 512], lhsT=WIN[:, kk, D + k * 128:D + (k + 1) * 128], rhs=HT[:, kk, 512:W],
                                        start=(kk == 0), stop=(kk == KC - 1)) for kk in range(KC)]
                            P.op("pe", MM(mms), reads=[rW, rH], writes=[rPH[b]])
                            P.op("act", I("copy", out=U[b][:, 512:W], in_=PHh[b][:, 0:W - 512]), reads=[rPH[b]], writes=[rU[b]])
                        P.op("dve", I("tensor_scalar", out=XCt[:, k, 0:N], in0=U[b][:, 0:N], scalar1=cwt(0, k), scalar2=cbt(k), op0=ALU.mult, op1=ALU.add),
                             reads=[rU[b], self.r_vec], writes=[rXC[k]])
                        for tap in (1, 2, 3):
                            P.op("dve", I("scalar_tensor_tensor", out=XCt[:, k, 0:N], in0=U[b][:, tap:N + tap], scalar=cwt(tap, k), in1=XCt[:, k, 0:N],
                                          op0=ALU.mult, op1=ALU.add), reads=[rU[b], rXC[k], self.r_vec], writes=[rXC[k]])
                        P.op("pool", I("tensor_copy", out=XCb[:, k, 0:N], in_=XCt[:, k, 0:N]), reads=[rXC[k]], writes=[rXCb[k]])
                    P.dma("sp", self.XC[:, :, t0:t0 + N], XCt[:, :, 0:N], reads=[rXC], writes=[self.r_xc])
                    for k0 in range(0, KC, GG):
                        inits = []
                        for k in range(k0, k0 + GG):
                            if ti == 0:
                                inits.append((h0t(0, k), [self.r_vec]) if pb is None else (0.0, []))
                            else:
                                inits.append((CAR[:, k:k + 1], [rCAR]))
                        gates_group(0, N, list(range(k0, k0 + GG)), HFt, rHF, inits)
                    P.op("pool", I("tensor_copy", out=CAR[:], in_=HFt[:, :, N - 1]), reads=[rHF], writes=[rCAR])
                    P.dma("sp", self.HF[:, :, t0:t0 + N], HFt[:, :, 0:N], reads=[rHF], writes=[self.r_hf])
                    if pb is not None and ti == ntile - 1:
                        c0 = ((pb * 2 + j) * 2 + 0) * 8
                        P.op("pool", I("tensor_copy", out=self.RGO[:, c0:c0 + 8], in_=HFt[:, :, N - 1]), reads=[rHF], writes=[self.r_rgo])
                for ti in range(ntile - 1, -1, -1):
                    t0 = s0 + ti * N
                    P.dma("sp", XCt[:, :, 0:N], self.XC[:, :, t0:t0 + N], reads=[self.r_xc], writes=[rXC])
                    for k in range(KC):
                        P.op("pool", I("tensor_copy", out=XCb[:, k, 0:N], in_=XCt[:, k, 0:N]), reads=[rXC[k]], writes=[rXCb[k]])
                    P.dma("sp", HFt[:, :, 0:N], self.HF[:, :, t0:t0 + N], reads=[self.r_hf], writes=[rHF])
                    P.dma("sp", GYt[:, :, 0:N], self.GY[:, :, t0:t0 + N], reads=[self.r_gy], writes=[rGY])
                    P.dma("sp", XTt[:, :, 0:N], self.XT[:, :, t0:t0 + N], reads=[self.r_xt], writes=[rX])
                    for k0 in range(0, KC, GG):
                        inits = []
                        for k in range(k0, k0 + GG):
                            if ti == ntile - 1:
                                inits.append((h0t(1, k), [self.r_vec]) if pb is None else (0.0, []))
                            else:
                                inits.append((CAR[:, k:k + 1], [rCAR]))
                        gates_group(1, N, list(range(k0, k0 + GG)), HBt, rHB, inits)
                    P.op("pool", I("tensor_copy", out=CAR[:], in_=HBt[:, :, 0]), reads=[rHB], writes=[rCAR])
                    if pb is not None and ti == 0:
                        c0 = ((pb * 2 + j) * 2 + 1) * 8
                        P.op("pool", I("tensor_copy", out=self.RGO[:, c0:c0 + 8], in_=HBt[:, :, 0]), reads=[rHB], writes=[self.r_rgo])
                    for k in range(KC):
                        P.op("dve", I("tensor_tensor", out=HFt[:, k, 0:N], in0=HFt[:, k, 0:N], in1=HBt[:, k, 0:N], op=ALU.add),
                             reads=[rHB[k], rHF[k]], writes=[rHF[k]])
                        P.op("dve", I("tensor_tensor", out=Zt[:, k, 0:N], in0=HFt[:, k, 0:N], in1=GYt[:, k, 0:N], op=ALU.mult),
                             reads=[rHF[k], rGY], writes=[rZ[k]])
                    for m in range(KC):
                        b = cnt[0] % 2
                        cnt[0] += 1
                        mms = [dict(out=PD[b][:, 0:N], lhsT=WOUT[:, kk, m * 128:(m + 1) * 128], rhs=Zt[:, kk, 0:N],
                                    start=(kk == 0), stop=(kk == KC - 1)) for kk in range(KC)]
                        P.op("pe", MM(mms), reads=[rW, rZ], writes=[rPD[b]])
                        xs = XTt[:, m, 0:N]
                        P.op("dve", I("scalar_tensor_tensor", out=xs, in0=PD[b][:, 0:N], scalar=self.modG(l, 0, cond, m), in1=xs, op0=ALU.mult, op1=ALU.add),
                             reads=[rPD[b], rX, self.r_mod], writes=[rX])
                    P.dma("sp", self.XT[:, :, t0:t0 + N], XTt[:, :, 0:N], reads=[rX], writes=[self.r_xt])

    def final_stage(self):
        P = self.P
        with P.stage():
            XF = [P.sb(f"fin_x{i}", [128, KC, 128], F32) for i in range(2)]
            SQ = [P.sb(f"fin_sq{i}", [128, KC, 128], F32) for i in range(2)]
            RS = [P.sb(f"fin_rs{i}", [128, 128], F32) for i in range(2)]
            YB = [P.sb(f"fin_y{i}", [128, D], F32) for i in range(2)]
            rXF = [Res() for _ in range(2)]
            rSQ = [Res() for _ in range(2)]
            rRS = [Res() for _ in range(2)]
            rYB = [Res() for _ in range(2)]
            PSS = [P.ps(f"fin_ps{i}", [128, 512], F32) for i in range(2)]
            rPSS = [PRes() for _ in range(2)]
            TPp = [P.ps(f"fin_tp{i}", [128, 512], F32) for i in range(4)]
            rTP = [PRes() for _ in range(4)]
            ident = self.CON[:, C_IDENT:C_IDENT + 128]
            ones = self.CON[:, C_ONES:C_ONES + 128]
            ofn, _ = VEC_OFF["final_norm"]
            fn_b = self.VEC[:, ofn:ofn + 8].unsqueeze(2).to_broadcast([128, KC, 128])
            for blk in range(TT // 128):
                b = blk % 2
                P.dma("sp", XF[b][:], self.XT[:, :, blk * 128:(blk + 1) * 128], reads=[self.r_xt], writes=[rXF[b]])
                P.op("act", I("activation", out=SQ[b][:], in_=XF[b][:], func=AF.Square), reads=[rXF[b]], writes=[rSQ[b]])
                mms = [dict(out=PSS[b][:, 0:128], lhsT=ones, rhs=SQ[b][:, k, :], start=(k == 0), stop=(k == KC - 1)) for k in range(KC)]
                P.op("pe", MM(mms), reads=[rSQ[b], self.r_con], writes=[rPSS[b]])
                P.op("act", I("activation", out=RS[b][:], in_=PSS[b][:, 0:128], func=AF.Sqrt, bias=self.EPSB[:, 0:1], scale=1.0),
                     reads=[rPSS[b]], writes=[rRS[b]])
                P.op("dve", I("reciprocal", out=RS[b][:], in_=RS[b][:]), reads=[rRS[b]], writes=[rRS[b]])
                P.op("dve", I("tensor_tensor", out=SQ[b][:], in0=XF[b][:], in1=RS[b][:].unsqueeze(1).to_broadcast([128, KC, 128]), op=ALU.mult),
                     reads=[rXF[b], rRS[b], rSQ[b]], writes=[rSQ[b]])
                P.op("pool", I("tensor_tensor", out=SQ[b][:], in0=SQ[b][:], in1=fn_b, op=ALU.mult),
                     reads=[rSQ[b], self.r_vec], writes=[rSQ[b]])
                for hf in range(2):
                    tp, rtp = TPp[b * 2 + hf], rTP[b * 2 + hf]
                    trs = [(tp[:, kk * 128:(kk + 1) * 128], SQ[b][:, hf * 4 + kk, :]) for kk in range(4)]
                    P.op("pe", TR(trs, ident), reads=[rSQ[b], self.r_con], writes=[rtp])
                    if hf == 0:
                        P.op("act", I("copy", out=YB[b][:, 0:512], in_=tp[:]), reads=[rtp], writes=[rYB[b]])
                    else:
                        P.op("dve", I("tensor_copy", out=YB[b][:, 512:1024], in_=tp[:]), reads=[rtp], writes=[rYB[b]])
                P.dma("sp", self.y[blk * 128:(blk + 1) * 128, :], YB[b][:], reads=[rYB[b]], writes=[self.r_y])


    def out_stage(self):
        P = self.P
        with P.stage():
            ident = self.CON[:, C_IDENT:C_IDENT + 128]
            TPp = P.ps("os_tp", [128, 512], F32)
            rTP = PRes()
            OB = P.sb("os_ob", [128, 128], F32)
            rOB = Res()
            ncol = NPC * 2 * 2 * KC
            for g in range((ncol + 127) // 128):
                w = min(128, ncol - g * 128)
                P.op("pe", TR([(TPp[0:w, 0:128], self.RGO[:, g * 128:g * 128 + w])], ident), reads=[self.r_rgo, self.r_con], writes=[rTP])
                P.op("dve", I("tensor_copy", out=OB[0:w, :], in_=TPp[0:w, 0:128]), reads=[rTP], writes=[rOB])
                P.dma("sp", self.nrg[g * 128:g * 128 + w, :], OB[0:w, :], reads=[rOB], writes=[self.r_out])

    def build(self):
        P = self.P
        self.r_xt = Res("XT")
        self.r_y = Res("y")
        self.r_gy, self.r_xc, self.r_hf, self.r_rgo, self.r_out = Res(), Res(), Res(), Res(), Res()
        self.setup_globals()
        self.EPSB = P.sb("EPSB", [128, 1], F32, glob=True)
        P.op("pool", I("memset", self.EPSB[:], EPS), writes=[self.r_con])
        self.QTRB = P.sb("QTRB", [128, 1], F32, glob=True)
        P.op("pool", I("memset", self.QTRB[:], 0.25), writes=[self.r_con])
        self.ONEB = P.sb("ONEB", [128, 1], F32, glob=True)
        P.op("pool", I("memset", self.ONEB[:], 1.0), writes=[self.r_con])
        self.RGO = P.sb("RGO", [128, NPC * 2 * 2 * KC], F32, glob=True)
        P.op("pool", I("memset", self.RGO[:], 0.0), writes=[self.r_rgo])
        self.stage0()
        for l in range(self.layers):
            if self.do_mixer:
                if l % 2 == 1:
                    self.odd_stage(l)
                elif self.do_even:
                    self.even_stage(l)
            if self.do_ffn:
                self.ffn_stage(l)
        self.final_stage()
        self.out_stage()
        P.barrier(engines=["sp"])


_CACHE = {}


def pack_vec(inp, core):
    s = core % DEC_BATCH
    vp = VecPack()
    vp.add("cvec", np.stack([fm(inp["c_ctx"]), fm(inp["c"][s])], axis=-1))
    vp.add("b_ada", fm(inp["b_ada"]))
    vp.add("norm_mix", fm(inp["norm_mix"]))
    vp.add("norm_ffn", fm(inp["norm_ffn"]))
    vp.add("final_norm", fm(inp["final_norm"]))
    vp.add("ffn_conv_w", fm(inp["ffn_conv_w"]))
    vp.add("ffn_conv_b", fm(inp["ffn_conv_b"]))
    vp.add("od_conv_w", fm(inp["od_conv_w"]))
    vp.add("od_conv_b", fm(inp["od_conv_b"]))
    vp.add("od_b_a", fm(inp["od_b_a"]))
    vp.add("od_b_i", fm(inp["od_b_i"]))
    vp.add("od_lambda", fm(inp["od_lambda"]))
    vp.add("st_rg", fm(inp["state_rglru"][s]))
    vp.add("gla_norm", fm(inp["ev_gla_norm"]))
    vp.add("b_gate", np.stack([fm(inp["ev_b_gate_f"]), fm(inp["ev_b_gate_b"])], axis=2))
    vp.add("sink", np.broadcast_to(inp["ev_sink"].reshape(1, 16), (128, 16)))
    for n, c in VEC_SPEC:
        assert vp.off[n] == VEC_OFF[n], (n, vp.off[n], VEC_OFF[n])
    return vp.build()


def pack_shared(inp):
    sh = {}
    sh["consts"] = make_consts()
    sh["w_ada"] = np.ascontiguousarray(inp["w_ada"], np.float32)
    up = np.asarray(inp["ffn_w_up"], np.float32)
    upk = up.reshape(DEPTH, KC, 128, 2, NJ, 128)
    sh["ffn_up_t"] = np.ascontiguousarray(upk.transpose(0, 4, 2, 1, 3, 5)).reshape(DEPTH, NJ, 128, KC * 256)
    dn = np.asarray(inp["ffn_w_down"], np.float32)
    dnk = dn.reshape(DEPTH, NJ, 128, KC, 128)
    sh["ffn_dn_t"] = np.ascontiguousarray(dnk.transpose(0, 3, 2, 1, 4)).reshape(DEPTH, KC, 128, NJ * 128)
    wi = np.asarray(inp["ev_w_in"], np.float32)
    z16 = np.zeros((2, D, 16), np.float32)
    cols = [wi[:, :, 0:512],
            wi[:, :, 512:576], wi[:, :, 512:576], wi[:, :, 576:640], wi[:, :, 576:640],
            wi[:, :, 768:1024], wi[:, :, 1024:1280], wi[:, :, 1792:2304],
            wi[:, :, 2304:2320], z16, wi[:, :, 2320:2336], z16,
            wi[:, :, 512:640], wi[:, :, 640:768], wi[:, :, 1280:1792]]
    we = np.concatenate(cols, axis=2)
    assert we.shape[2] == 2624
    sh["w_even"] = np.ascontiguousarray(we.reshape(2, KC, 128, 2624).transpose(0, 2, 1, 3)).reshape(2, 128, KC * 2624)
    sh["ev_w_out"] = np.ascontiguousarray(inp["ev_w_out"], np.float32)
    wg = np.zeros((2, 64, 512), np.float32)
    wg[:, 0:16, 0:256] = inp["ev_w_gate_f"]
    wg[:, 32:48, 256:512] = inp["ev_w_gate_b"]
    sh["wg"] = wg
    sh["rope"] = make_rope()
    sh["od_w_in"] = np.ascontiguousarray(inp["od_w_in"], np.float32)
    sh["od_w_out"] = np.ascontiguousarray(inp["od_w_out"], np.float32)
    bd = np.zeros((2, 2, 2, 128, KC, 128), np.float32)
    for ai, nm in enumerate(("od_w_a", "od_w_i")):
        w = np.asarray(inp[nm], np.float32)
        for k in range(KC):
            for hb in range(2):
                bd[:, :, ai, hb * 64:(hb + 1) * 64, k, hb * 64:(hb + 1) * 64] = w[:, :, 2 * k + hb]
    sh["od_bd"] = bd.reshape(2, 2, 2, 128, KC * 128)
    return sh


def core_inputs(inp, sh, core):
    s = core % DEC_BATCH
    xin = np.concatenate([inp["x_sample"][s], inp["x_prompt"][core * NPC:(core + 1) * NPC].reshape(TP, D)], axis=0)
    m = dict(sh)
    m["xin"] = np.ascontiguousarray(xin, np.float32)
    m["vec"] = pack_vec(inp, core)
    m["cache_k"] = np.ascontiguousarray(inp["cache_attn_k"][s].reshape(2, 256, 128), np.float32)
    m["cache_v"] = np.ascontiguousarray(inp["cache_attn_v"][s].reshape(2, 256, 128), np.float32)
    m["state_gla"] = np.ascontiguousarray(inp["state_gla"][s], np.float32)
    return m


def get_builder(**kw):
    key = tuple(sorted(kw.items()))
    if key not in _CACHE:
        _CACHE[key] = Builder(**kw)
    return _CACHE[key]


CFG = {}


def set_cores(n):
    global N_CORES, NPC, TP, TT
    N_CORES = n
    NPC = BATCH // n
    TP = NPC * SEQ
    TT = TS + TP

CORE_OFF = [0]


def kernel(**inputs):
    inp = {k: np.asarray(v) for k, v in inputs.items()}
    cfg = dict(CFG)
    ncr = cfg.pop("n_cores", N_CORES)
    bld = Builder(**cfg)
    sh = pack_shared(inp)
    coff = cfg.pop("core_off", 0) if False else CORE_OFF[0]
    in_maps = [core_inputs(inp, sh, c + coff) for c in range(ncr)]
    res = run_bass_kernel_spmd(bld.nc, in_maps, core_ids=list(range(ncr)))
    R = list(res.results)
    while len(R) < N_CORES:
        R.append(R[0])
    y_prompt = np.stack([R[c]["y"][TS:].reshape(NPC, SEQ, D) for c in range(N_CORES)]).reshape(BATCH, SEQ, D)
    y_sample = np.stack([R[s]["y"][:TS] for s in range(DEC_BATCH)])
    nrg = np.stack([R[c]["nrg"].reshape(NPC, 2, 2, D) for c in range(N_CORES)]).reshape(BATCH, 2, 2, D)
    nk = np.stack([R[c]["nk"] for c in range(N_CORES)]).reshape(BATCH, 2, SEQ, 2, 64)
    nv = np.stack([R[c]["nv"] for c in range(N_CORES)]).reshape(BATCH, 2, SEQ, 2, 64)
    ngla = np.stack([R[c]["ngla"] for c in range(N_CORES)]).reshape(BATCH, 2, 2, 4, 64, 128)
    f = lambda a: np.ascontiguousarray(a, dtype=np.float32)
    return (f(y_prompt), f(y_sample), f(nk), f(nv), f(ngla), f(nrg))
```

```python
import contextlib
import numpy as np
import concourse.bass as bass
import concourse.mybir as mybir
from concourse.bass_utils import run_bass_kernel_spmd

F32 = mybir.dt.float32
BF16 = mybir.dt.bfloat16
AF = mybir.ActivationFunctionType
ALU = mybir.AluOpType

D = 1024
KC = 8
BATCH, SEQ = 32, 256
DEC_BATCH, DEC_SEQ = 4, 4096
DEPTH = 4
N_CORES = 8
NPC = BATCH // N_CORES
TS = DEC_SEQ
TP = NPC * SEQ
TT = TS + TP
D_FF = 2816
NJ = D_FF // 128
EPS = 1e-6

SAME_SYNC = True
RELAX = [10 ** 9]
PE_RELAX = [False]
DMA_K = 8


class Tok:
    __slots__ = ("sem", "semid", "val", "key")

    def __init__(self, sem, semid, val, key):
        self.sem, self.semid, self.val, self.key = sem, semid, val, key


class Res:
    __slots__ = ("name", "w", "r", "ex")

    def __init__(self, name="", ex=False):
        self.name = name
        self.w = None
        self.r = []
        self.ex = ex


def PRes():
    return Res("psum", ex=True)


class RG(list):
    pass


def _flat(lst):
    out = []
    for x in lst:
        if isinstance(x, RG):
            out.extend(x)
        else:
            out.append(x)
    return out


class Prog:
    ENG = ("pe", "dve", "act", "pool", "sp")

    def __init__(self, nc):
        self.nc = nc
        self.es = contextlib.ExitStack()
        self.eng = {"pe": nc.tensor, "dve": nc.vector, "act": nc.scalar, "pool": nc.gpsimd, "sp": nc.sync}
        self.sem = {}
        self.seq = {}
        self.waited = {}
        self._nid = 0
        for e in self.ENG:
            self.sem[e] = (self.es.enter_context(nc.semaphore("s_" + e)), self._sid())
            self.seq[e] = 0
        self.dsem = {}
        self.dman = {}
        for q in ("sp", "act", "pool"):
            self.dsem[q] = [(self.es.enter_context(nc.semaphore(f"d_{q}{i}")), self._sid()) for i in range(DMA_K)]
            self.dman[q] = 0
        self.stage_es = None
        self.n_inst = 0

    def _sid(self):
        self._nid += 1
        return self._nid

    def sb(self, name, shape, dtype, glob=False):
        es = self.es if (glob or self.stage_es is None) else self.stage_es
        self._nid += 1
        return es.enter_context(self.nc.sbuf_tensor(f"{name}_{self._nid}", list(shape), dtype))

    def ps(self, name, shape, dtype=F32):
        es = self.es if self.stage_es is None else self.stage_es
        self._nid += 1
        return es.enter_context(self.nc.psum_tensor(f"{name}_{self._nid}", list(shape), dtype))

    @contextlib.contextmanager
    def stage(self):
        assert self.stage_es is None
        self.stage_es = contextlib.ExitStack()
        try:
            yield
        finally:
            self.barrier()
            self.stage_es.close()
            self.stage_es = None

    def _deps(self, reads, writes):
        toks = []
        for r in reads:
            if r.w is not None:
                toks.append(r.w)
            if r.ex:
                toks.extend(r.r)
        for w in writes:
            if w.w is not None:
                toks.append(w.w)
            toks.extend(w.r)
        return toks

    def _need(self, e, toks, strict_same=False):
        need = {}
        for t in toks:
            if t.key == e and not strict_same:
                if (not SAME_SYNC) or (e == "pe" and PE_RELAX[0]) or (e != "pe" and (self.seq[e] - t.val) >= RELAX[0]):
                    continue
            if self.waited.get((e, t.semid), 0) >= t.val:
                continue
            if t.semid not in need or need[t.semid].val < t.val:
                need[t.semid] = t
        return list(need.values())

    def _emit_waits(self, e, need):
        for t in need:
            self.eng[e].wait_ge(t.sem, t.val)
            self.waited[(e, t.semid)] = t.val
            self.n_inst += 1

    def _record(self, tok, reads, writes):
        for r in reads:
            if r.ex:
                r.w = tok
                r.r = []
            else:
                r.r = [x for x in r.r if x.key != tok.key] + [tok]
        for w in writes:
            w.w = tok
            w.r = []

    def op(self, e, fn, reads=(), writes=()):
        reads, writes = _flat(reads), _flat(writes)
        need = self._need(e, self._deps(reads, writes))
        attach = need.pop() if need else None
        self._emit_waits(e, need)
        first_holder = []
        ins = fn(self.eng[e], first_holder)
        tgt = first_holder[0] if first_holder else ins
        if attach is not None:
            tgt._wait_ge(attach.sem, attach.val)
            self.waited[(e, attach.semid)] = attach.val
        self.seq[e] += 1
        sem, semid = self.sem[e]
        ins.then_inc(sem, 1)
        self.n_inst += 1
        tok = Tok(sem, semid, self.seq[e], e)
        self._record(tok, reads, writes)
        return tok

    def dma(self, q, out, in_, reads=(), writes=(), **kw):
        e = q
        reads, writes = _flat(reads), _flat(writes)
        need = self._need(e, self._deps(reads, writes), strict_same=True)
        n = self.dman[q]
        slot, rnd = n % DMA_K, n // DMA_K
        sem, semid = self.dsem[q][slot]
        if rnd > 0 and self.waited.get((e, semid), 0) < 16 * rnd:
            need.append(Tok(sem, semid, 16 * rnd, ("q", q, slot)))
        self._emit_waits(e, need)
        self.eng[e].dma_start(out=out, in_=in_, **kw).then_inc(sem, 16)
        self.n_inst += 1
        self.dman[q] = n + 1
        tok = Tok(sem, semid, 16 * (rnd + 1), ("q", q, slot))
        self._record(tok, reads, writes)
        return tok

    def all_tokens(self):
        toks = []
        for e in self.ENG:
            if self.seq[e] > 0:
                sem, semid = self.sem[e]
                toks.append(Tok(sem, semid, self.seq[e], e))
        for q in self.dsem:
            n = self.dman[q]
            for slot in range(DMA_K):
                cnt = (n - slot + DMA_K - 1) // DMA_K if n > slot else 0
                if cnt > 0:
                    sem, semid = self.dsem[q][slot]
                    toks.append(Tok(sem, semid, 16 * cnt, ("q", q, slot)))
        return toks

    def barrier(self, engines=None):
        toks = self.all_tokens()
        for e in (engines or self.ENG):
            need = []
            for t in toks:
                if t.key == e:
                    continue
                if self.waited.get((e, t.semid), 0) >= t.val:
                    continue
                need.append(t)
            self._emit_waits(e, need)


def I(method, *a, **k):
    def fn(eng, fh):
        return getattr(eng, method)(*a, **k)
    return fn


def MM(mms):
    def fn(eng, fh):
        ins = None
        for i, m in enumerate(mms):
            ins = eng.matmul(m["out"], lhsT=m["lhsT"], rhs=m["rhs"], start=m["start"], stop=m["stop"])
            if i == 0:
                fh.append(ins)
        return ins
    return fn


def TR(trs, ident):
    def fn(eng, fh):
        ins = None
        for i, (o, a) in enumerate(trs):
            ins = eng.transpose(out=o, in_=a, identity=ident)
            if i == 0:
                fh.append(ins)
        return ins
    return fn


def fm(v):
    v = np.asarray(v, np.float32)
    n = v.shape[-1] // 128
    a = v.reshape(v.shape[:-1] + (n, 128))
    return np.moveaxis(a, -1, 0)


class VecPack:
    def __init__(self):
        self.items = []
        self.off = {}
        self.n = 0

    def add(self, name, arr):
        arr = np.ascontiguousarray(arr, np.float32).reshape(128, -1)
        self.off[name] = (self.n, arr.shape[1])
        self.items.append(arr)
        self.n += arr.shape[1]

    def build(self):
        return np.ascontiguousarray(np.concatenate(self.items, axis=1))


VEC_SPEC = [
    ("cvec", 16), ("b_ada", 4 * 48), ("norm_mix", 32), ("norm_ffn", 32), ("final_norm", 8),
    ("ffn_conv_w", 4 * 3 * 44), ("ffn_conv_b", 4 * 44),
    ("od_conv_w", 2 * 4 * 8), ("od_conv_b", 16), ("od_b_a", 32), ("od_b_i", 32), ("od_lambda", 32),
    ("st_rg", 32), ("gla_norm", 2), ("b_gate", 2 * 2 * 2), ("sink", 16),
]
VEC_OFF = {}
_o = 0
for _n, _c in VEC_SPEC:
    VEC_OFF[_n] = (_o, _c)
    _o += _c
NV = _o

C_IDENT = 0
C_MF = 128
C_MB = 256
C_RST = 384
C_AMASK = 896
C_ONES = 1280
C_PERM = 1408
NCONST = 1536


def make_consts():
    c = np.zeros((128, NCONST), np.float32)
    c[:, C_IDENT:C_IDENT + 128] = np.eye(128, dtype=np.float32)
    j = np.arange(128)[:, None]
    i = np.arange(128)[None, :]
    c[:, C_MF:C_MF + 128] = (j <= i)
    c[:, C_MB:C_MB + 128] = (j >= i)
    t = np.arange(512)
    c[:, C_RST:C_RST + 512] = (t % 128 != 0)[None, :]
    qi = np.arange(128)[:, None]
    kj = np.arange(384)[None, :]
    rel = kj - 128 - qi
    c[:, C_AMASK:C_AMASK + 384] = np.where(np.abs(rel) <= 128, 0.0, -1e30)
    c[:, C_ONES:C_ONES + 128] = 1.0 / 1024.0
    for m in range(128):
        d = m % 64
        if (d % 32) < 16:
            c[m + 16, C_PERM + m] = -1.0
        else:
            c[m - 16, C_PERM + m] = 1.0
    return c


def make_rope():
    t = np.arange(DEC_SEQ)
    row = (t // 64).astype(np.float32)
    col = (t % 64).astype(np.float32)
    inv = np.power(np.float32(10000.0), -np.arange(16, dtype=np.float32) / np.float32(16.0)).astype(np.float32)
    ang_r = (row[None, :] * inv[:, None]).astype(np.float32)
    ang_c = (col[None, :] * inv[:, None]).astype(np.float32)
    tab = np.zeros((128, 2, DEC_SEQ), np.float32)
    for p in range(128):
        d = p % 64
        a = ang_r if d < 32 else ang_c
        f = d % 16
        tab[p, 0] = np.cos(a[f])
        tab[p, 1] = np.sin(a[f])
    return tab


class Builder:
    def __init__(self, layers=DEPTH, do_mixer=True, do_ffn=True, do_even=True, dbg_seqs=None, dbg_cut=99):
        self.dbg_seqs = dbg_seqs
        self.dbg_cut = dbg_cut
        self.layers = layers
        self.do_mixer = do_mixer
        self.do_ffn = do_ffn
        self.do_even = do_even
        nc = bass.Bass("TRN2", target_bir_lowering=False)
        self.nc = nc
        self.P = Prog(nc)
        dt = lambda name, shape, kind=None, dtype=F32: (
            nc.dram_tensor(name, list(shape), dtype, kind=kind) if kind else nc.dram_tensor(name, list(shape), dtype))
        self.xin = dt("xin", [TT, D], "ExternalInput").ap()
        self.vec_d = dt("vec", [128, NV], "ExternalInput").ap()
        self.const_d = dt("consts", [128, NCONST], "ExternalInput").ap()
        self.w_ada = dt("w_ada", [DEPTH, D, 6 * D], "ExternalInput").ap()
        self.ffn_up = dt("ffn_up_t", [DEPTH, NJ, 128, KC * 256], "ExternalInput").ap()
        self.ffn_dn = dt("ffn_dn_t", [DEPTH, KC, 128, NJ * 128], "ExternalInput").ap()
        self.od_w_in = dt("od_w_in", [2, D, 2 * D], "ExternalInput").ap()
        self.od_w_out = dt("od_w_out", [2, D, D], "ExternalInput").ap()
        self.od_bd = dt("od_bd", [2, 2, 2, 128, KC * 128], "ExternalInput").ap()
        self.w_even = dt("w_even", [2, 128, KC * 2624], "ExternalInput").ap()
        self.ev_w_out = dt("ev_w_out", [2, D, D], "ExternalInput").ap()
        self.wg = dt("wg", [2, 64, 512], "ExternalInput").ap()
        self.cache_k = dt("cache_k", [2, 256, 128], "ExternalInput").ap()
        self.cache_v = dt("cache_v", [2, 256, 128], "ExternalInput").ap()
        self.state_gla = dt("state_gla", [2, 2, 4, 64, 128], "ExternalInput").ap()
        self.rope = dt("rope", [128, 2, DEC_SEQ], "ExternalInput").ap()
        self.y = dt("y", [TT, D], "ExternalOutput").ap()
        self.nk = dt("nk", [NPC, 2, SEQ, 128], "ExternalOutput").ap()
        self.nv = dt("nv", [NPC, 2, SEQ, 128], "ExternalOutput").ap()
        self.ngla = dt("ngla", [NPC, 2, 2, 4, 64, 128], "ExternalOutput").ap()
        self.nrg = dt("nrg", [NPC * 2 * 2 * KC, 128], "ExternalOutput").ap()
        self.GY = dt("GY", [D, TT], dtype=BF16).ap().rearrange("(k p) t -> p k t", p=128)
        self.XC = dt("XC", [D, TT]).ap().rearrange("(k p) t -> p k t", p=128)
        self.HF = dt("HF", [D, TT]).ap().rearrange("(k p) t -> p k t", p=128)
        self.XT = dt("XT", [D, TT]).ap().rearrange("(k p) t -> p k t", p=128)
        self.build()

    def setup_globals(self):
        P = self.P
        self.VEC = P.sb("VEC", [128, NV], F32, glob=True)
        self.CON = P.sb("CON", [128, NCONST], F32, glob=True)
        self.CONB = P.sb("CONB", [128, 384], BF16, glob=True)
        self.MOD = P.sb("MOD", [128, DEPTH, 48, 2], F32, glob=True)
        self.AMF = P.sb("AMF", [128, DEPTH, 2, 2, 8], F32, glob=True)
        self.r_vec = Res("VEC")
        self.r_con = Res("CON")
        self.r_conb = Res("CONB")
        self.r_mod = Res("MOD")
        self.r_amf = Res("AMF")
        P.dma("sp", self.VEC[:], self.vec_d[:, :], writes=[self.r_vec])
        P.dma("sp", self.CON[:], self.const_d[:, :], writes=[self.r_con])
        P.op("dve", I("tensor_copy", out=self.CONB[:], in_=self.CON[:, 0:384]), reads=[self.r_con], writes=[self.r_conb])

    def vec(self, name, *idx_shape):
        o, n = VEC_OFF[name]
        return self.VEC[:, o:o + n]

    def stage0(self):
        P = self.P
        with P.stage():
            XA = [P.sb(f"s0_xa{i}", [128, D], F32) for i in range(3)]
            XB = [P.sb(f"s0_xb{i}", [128, KC, 128], F32) for i in range(2)]
            rXA = [Res() for _ in range(3)]
            rXB = [Res() for _ in range(2)]
            TPp = [P.ps(f"s0_tp{i}", [128, 512], F32) for i in range(4)]
            rTP = [PRes() for _ in range(4)]
            ident = self.CON[:, C_IDENT:C_IDENT + 128]
            SC = P.sb("s0_sc", [128, KC, 2], BF16)
            rSC = Res()
            WA = [P.sb(f"s0_wa{i}", [128, KC, 1024], BF16) for i in range(2)]
            rWA = [Res() for _ in range(2)]
            PM = P.ps("s0_pm", [128, 512], F32)
            rPM = PRes()
            o, n = VEC_OFF["cvec"]
            P.op("act", I("activation", out=SC[:].rearrange("p k c -> p (k c)"), in_=self.VEC[:, o:o + n], func=AF.Silu),
                 reads=[self.r_vec], writes=[rSC])
            nblk = TT // 128
            mod_jobs = [(l, pc) for l in range(DEPTH) for pc in range(6)]
            mj = 0

            mod_issued = [0]

            def issue_mod_dma(idx):
                l_, pc_ = mod_jobs[idx]
                wv = self.w_ada[l_].rearrange("(k p) n -> p k n", p=128)
                P.dma("pool", WA[idx % 2][:], wv[:, :, pc_ * 1024:(pc_ + 1) * 1024], writes=[rWA[idx % 2]])

            def do_mod(l, pc, idx):
                while mod_issued[0] <= min(idx + 1, len(mod_jobs) - 1):
                    issue_mod_dma(mod_issued[0])
                    mod_issued[0] += 1
                wa, rwa = WA[idx % 2], rWA[idx % 2]
                for jj in range(8):
                    mms = [dict(out=PM[:, jj * 2:jj * 2 + 2], lhsT=wa[:, k, jj * 128:(jj + 1) * 128], rhs=SC[:, k, :],
                                start=(k == 0), stop=(k == KC - 1)) for k in range(KC)]
                    P.op("pe", MM(mms), reads=[rwa, rSC], writes=[rPM])
                ob, _ = VEC_OFF["b_ada"]
                bsl = self.VEC[:, ob + l * 48 + pc * 8: ob + l * 48 + pc * 8 + 8]
                for c in range(2):
                    P.op("dve", I("tensor_tensor", out=self.MOD[:, l, pc * 8:(pc + 1) * 8, c], in0=PM[:, c:16:2], in1=bsl, op=ALU.add),
                         reads=[rPM, self.r_vec], writes=[self.r_mod])

            ld_issued = [0]
            for blk in range(nblk):
                xa, rxa = XA[blk % 3], rXA[blk % 3]
                xb, rxb = XB[blk % 2], rXB[blk % 2]
                while ld_issued[0] <= min(blk + 2, nblk - 1):
                    q = ld_issued[0]
                    P.dma("sp", XA[q % 3][:], self.xin[q * 128:(q + 1) * 128, :], writes=[rXA[q % 3]])
                    ld_issued[0] += 1
                for hf in range(2):
                    tp, rtp = TPp[(blk % 2) * 2 + hf], rTP[(blk % 2) * 2 + hf]
                    trs = [(tp[:, kk * 128:(kk + 1) * 128], xa[:, (hf * 4 + kk) * 128:(hf * 4 + kk + 1) * 128]) for kk in range(4)]
                    P.op("pe", TR(trs, ident), reads=[rxa, self.r_con], writes=[rtp])
                    dst = xb[:, hf * 4:(hf + 1) * 4, :]
                    src = tp[:].rearrange("p (a b) -> p a b", a=4)
                    if hf == 0:
                        P.op("act", I("copy", out=dst, in_=src), reads=[rtp], writes=[rxb])
                    else:
                        P.op("dve", I("tensor_copy", out=dst, in_=src), reads=[rtp], writes=[rxb])
                P.dma("sp", self.XT[:, :, blk * 128:(blk + 1) * 128], xb[:], reads=[rxb], writes=[self.r_xt])
                if mj < len(mod_jobs) and blk % 1 == 0:
                    do_mod(*mod_jobs[mj], mj)
                    mj += 1
            while mj < len(mod_jobs):
                do_mod(*mod_jobs[mj], mj)
                mj += 1
            for l in range(DEPTH):
                for mf, (nname, sc0) in enumerate((("norm_mix", 8), ("norm_ffn", 32))):
                    on, _ = VEC_OFF[nname]
                    for c in range(2):
                        P.op("dve", I("scalar_tensor_tensor", out=self.AMF[:, l, mf, c, :], in0=self.MOD[:, l, sc0:sc0 + 8, c], scalar=1.0,
                                      in1=self.VEC[:, on + l * 8: on + l * 8 + 8], op0=ALU.add, op1=ALU.mult),
                             reads=[self.r_mod, self.r_vec], writes=[self.r_amf])

    def modA(self, l, mf, c, k):
        return self.AMF[:, l, mf, c, k:k + 1]

    def modB(self, l, mf, c, k):
        base = 0 if mf == 0 else 24
        return self.MOD[:, l, base + k, c:c + 1]

    def modG(self, l, mf, c, k):
        base = 16 if mf == 0 else 40
        return self.MOD[:, l, base + k, c:c + 1]

    def norm_mod(self, xt, rx, ht, rh, W, l, mf, c, pfx, tiles):
        P = self.P
        sq, rsq = tiles["sq"], tiles["rsq"]
        ps, rps = tiles["ps"], tiles["rps"]
        rstd, rrstd = tiles["rstd"], tiles["rrstd"]
        ones = self.CON[:, C_ONES:C_ONES + 128]
        P.op("act", I("activation", out=sq[:, :, 0:W], in_=xt[:, :, 0:W], func=AF.Square), reads=[rx], writes=[rsq])
        mms = [dict(out=ps[:, 0:W], lhsT=ones, rhs=sq[:, k, 0:W], start=(k == 0), stop=(k == KC - 1)) for k in range(KC)]
        P.op("pe", MM(mms), reads=[rsq, self.r_con], writes=[rps])
        P.op("act", I("activation", out=rstd[:, 0:W], in_=ps[:, 0:W], func=AF.Sqrt, bias=self.EPSB[:, 0:1], scale=1.0),
             reads=[rps], writes=[rrstd])
        P.op("dve", I("reciprocal", out=rstd[:, 0:W], in_=rstd[:, 0:W]), reads=[rrstd], writes=[rrstd])
        P.op("dve", I("tensor_tensor", out=sq[:, :, 0:W], in0=xt[:, :, 0:W], in1=rstd[:, 0:W].unsqueeze(1).to_broadcast([128, KC, W]), op=ALU.mult),
             reads=[rx, rrstd, rsq], writes=[rsq])
        for k in range(KC):
            e = "act" if k % 2 == 0 else "pool"
            if e == "act":
                P.op("act", I("activation", out=ht[:, k, 0:W], in_=sq[:, k, 0:W], func=AF.Identity,
                              bias=self.modB(l, mf, c, k), scale=self.modA(l, mf, c, k)),
                     reads=[rsq, self.r_amf, self.r_mod], writes=[rh])
            else:
                P.op("pool", I("tensor_scalar", out=ht[:, k, 0:W], in0=sq[:, k, 0:W], scalar1=self.modA(l, mf, c, k),
                               scalar2=self.modB(l, mf, c, k), op0=ALU.mult, op1=ALU.add),
                     reads=[rsq, self.r_amf, self.r_mod], writes=[rh])

    def ffn_stage(self, l):
        P = self.P
        with P.stage():
            NT = 1024
            XTt = P.sb("f_xt", [128, KC, NT + 8], F32)
            rX = Res()
            SQ = P.sb("f_sq", [128, KC, 520], F32)
            rSQ = Res()
            RSTD = P.sb("f_rstd", [128, 520], F32)
            rRSTD = Res()
            HT = P.sb("f_ht", [128, KC, NT + 8], BF16)
            rH = RG(Res() for _ in range(KC))
            PB = P.sb("f_p", [128, NJ, NT], BF16)
            rPB = [Res() for _ in range(NJ)]
            UPW = [P.sb(f"f_upw{i}", [128, KC, 256], BF16) for i in range(4)]
            rUPW = [Res() for _ in range(4)]
            DNW = [P.sb(f"f_dnw{i}", [128, NJ, 128], BF16) for i in range(2)]
            rDNW = [Res() for _ in range(2)]
            UG = [P.sb(f"f_ug{i}", [128, 516], F32) for i in range(2)]
            UV = [P.sb(f"f_uv{i}", [128, 516], F32) for i in range(2)]
            rUG = [Res() for _ in range(2)]
            rUV = [Res() for _ in range(2)]
            AG = [P.sb(f"f_ag{i}", [128, 512], F32) for i in range(2)]
            AV = [P.sb(f"f_av{i}", [128, 512], F32) for i in range(2)]
            SG = [P.sb(f"f_sg{i}", [128, 512], F32) for i in range(2)]
            rAG = [Res() for _ in range(2)]
            rAV = [Res() for _ in range(2)]
            rSG = [Res() for _ in range(2)]
            SG2 = [P.sb(f"f_sg2{i}", [128, 512], F32) for i in range(2)]
            rSG2 = [Res() for _ in range(2)]
            PG = [P.ps(f"f_pg{i}", [128, 512], F32) for i in range(2)]
            PV = [P.ps(f"f_pv{i}", [128, 512], F32) for i in range(2)]
            PH = [P.ps(f"f_ph{i}", [128, 512], F32) for i in range(2)]
            PD = [P.ps(f"f_pd{i}", [128, 512], F32) for i in range(2)]
            rPG = [PRes() for _ in range(2)]
            rPV = [PRes() for _ in range(2)]
            rPH = [PRes() for _ in range(2)]
            rPD = [PRes() for _ in range(2)]
            ocw, _ = VEC_OFF["ffn_conv_w"]
            ocb, _ = VEC_OFF["ffn_conv_b"]

            def cw(tap, ch):
                o = ocw + (l * 3 + tap) * 44 + ch
                return self.VEC[:, o:o + 1]

            def cb(ch):
                o = ocb + l * 44 + ch
                return self.VEC[:, o:o + 1]

            tiles = []
            for i in range(TS // NT):
                units = []
                for u in range(NT // 512):
                    c0 = i * NT + u * 512
                    units.append((c0, 512, c0 > 0, c0 + 512 < TS))
                tiles.append((1, units))
            for g in range(NPC // 4):
                tiles.append((0, [(TS + (g * 4 + b) * SEQ, SEQ, False, False) for b in range(4)]))
            upi = 0
            dni = 0
            uix = 0
            n_up = len(tiles) * NJ
            n_dn = len(tiles) * KC
            issued = {"up": 0, "dn": 0}

            def ensure_up(i):
                while issued["up"] < min(i + 4, n_up):
                    q = issued["up"]
                    P.dma("pool", UPW[q % 4][:].rearrange("p k n -> p (k n)"), self.ffn_up[l, q % NJ], writes=[rUPW[q % 4]])
                    issued["up"] = q + 1

            def ensure_dn(i):
                while issued["dn"] < min(i + 2, n_dn):
                    q = issued["dn"]
                    P.dma("pool", DNW[q % 2][:].rearrange("p j n -> p (j n)"), self.ffn_dn[l, q % KC], writes=[rDNW[q % 2]])
                    issued["dn"] = q + 1

            for cond, units in tiles:
                offs = []
                off = 0
                for (c0, N, hl, hr) in units:
                    offs.append(off)
                    W = N + 2
                    lo = c0 - 1 if hl else c0
                    hi = c0 + N + 1 if hr else c0 + N
                    dlo = off + (0 if hl else 1)
                    P.dma("sp", XTt[:, :, dlo:dlo + (hi - lo)], self.XT[:, :, lo:hi], reads=[self.r_xt], writes=[rX])
                    if not hl:
                        P.op("pool", I("memset", XTt[:, :, off:off + 1], 0.0), writes=[rX])
                    if not hr:
                        P.op("pool", I("memset", XTt[:, :, off + W - 1:off + W], 0.0), writes=[rX])
                    off += W
                for ui, (c0, N, hl, hr) in enumerate(units):
                    W = N + 2
                    o = offs[ui]
                    tl = dict(sq=SQ, rsq=rSQ, ps=PD[0], rps=rPD[0], rstd=RSTD, rrstd=rRSTD)
                    self.norm_mod_wide(XTt, rX, HT, rH, o, W, l, 1, cond, tl, PD, rPD)
                    if not hl:
                        P.op("pool", I("memset", HT[:, :, o:o + 1], 0.0), writes=[rH])
                    if not hr:
                        P.op("pool", I("memset", HT[:, :, o + W - 1:o + W], 0.0), writes=[rH])
                for j in range(NJ):
                    ensure_up(upi)
                    ensure_dn(dni)
                    w, rw = UPW[upi % 4], rUPW[upi % 4]
                    upi += 1
                    pcol = 0
                    for ui, (c0, N, hl, hr) in enumerate(units):
                        W = N + 2
                        o = offs[ui]
                        b = uix % 2
                        uix += 1
                        Wm = min(W, 512)
                        for half, (pt, rpt) in enumerate(((PG[b], rPG[b]), (PV[b], rPV[b]))):
                            mms = [dict(out=pt[:, 0:Wm], lhsT=w[:, k, half * 128:(half + 1) * 128], rhs=HT[:, k, o:o + Wm],
                                        start=(k == 0), stop=(k == KC - 1)) for k in range(KC)]
                            P.op("pe", MM(mms), reads=[rw, rH], writes=[rpt])
                        if W > 512:
                            mms = []
                            for half in range(2):
                                mms += [dict(out=PH[b][:, half * 2:half * 2 + 2], lhsT=w[:, k, half * 128:(half + 1) * 128],
                                             rhs=HT[:, k, o + 512:o + W], start=(k == 0), stop=(k == KC - 1)) for k in range(KC)]
                            P.op("pe", MM(mms), reads=[rw, rH], writes=[rPH[b]])
                        P.op("act", I("copy", out=UG[b][:, 0:Wm], in_=PG[b][:, 0:Wm]), reads=[rPG[b]], writes=[rUG[b]])
                        P.op("act", I("copy", out=UV[b][:, 0:Wm], in_=PV[b][:, 0:Wm]), reads=[rPV[b]], writes=[rUV[b]])
                        if W > 512:
                            P.op("act", I("copy", out=UG[b][:, 512:514], in_=PH[b][:, 0:2]), reads=[rPH[b]], writes=[rUG[b]])
                            P.op("act", I("copy", out=UV[b][:, 512:514], in_=PH[b][:, 2:4]), reads=[rPH[b]], writes=[rUV[b]])
                        chg, chv = j, NJ + j
                        P.op("act", I("activation", out=AG[b][:, 0:N], in_=PG[b][:, 0:N], func=AF.Identity, scale=cw(0, chg), bias=cb(chg)),
                             reads=[rPG[b], self.r_vec], writes=[rAG[b]])
                        P.op("act", I("activation", out=AV[b][:, 0:N], in_=PV[b][:, 0:N], func=AF.Identity, scale=cw(0, chv), bias=cb(chv)),
                             reads=[rPV[b], self.r_vec], writes=[rAV[b]])
                        for tap in (1, 2):
                            P.op("dve", I("scalar_tensor_tensor", out=AG[b][:, 0:N], in0=UG[b][:, tap:N + tap], scalar=cw(tap, chg), in1=AG[b][:, 0:N],
                                          op0=ALU.mult, op1=ALU.add), reads=[rUG[b], rAG[b], self.r_vec], writes=[rAG[b]])
                            P.op("dve", I("scalar_tensor_tensor", out=AV[b][:, 0:N], in0=UV[b][:, tap:N + tap], scalar=cw(tap, chv), in1=AV[b][:, 0:N],
                                          op0=ALU.mult, op1=ALU.add), reads=[rUV[b], rAV[b], self.r_vec], writes=[rAV[b]])
                        P.op("act", I("activation", out=SG[b][:, 0:N], in_=AG[b][:, 0:N], func=AF.Silu), reads=[rAG[b]], writes=[rSG[b]])
                        P.op("pool", I("tensor_tensor", out=PB[:, j, pcol:pcol + N], in0=SG[b][:, 0:N], in1=AV[b][:, 0:N], op=ALU.mult),
                             reads=[rSG[b], rAV[b]], writes=[rPB[j]])
                        pcol += N
                for m in range(KC):
                    ensure_dn(dni)
                    ensure_up(upi)
                    w, rw = DNW[dni % 2], rDNW[dni % 2]
                    dni += 1
                    pcol = 0
                    for ui, (c0, N, hl, hr) in enumerate(units):
                        o = offs[ui]
                        b = uix % 2
                        uix += 1
                        mms = [dict(out=PD[b][:, 0:N], lhsT=w[:, j, :], rhs=PB[:, j, pcol:pcol + N], start=(j == 0), stop=(j == NJ - 1))
                               for j in range(NJ)]
                        P.op("pe", MM(mms), reads=[rw] + rPB, writes=[rPD[b]])
                        xs = XTt[:, m, o + 1:o + 1 + N]
                        P.op("dve", I("scalar_tensor_tensor", out=xs, in0=PD[b][:, 0:N], scalar=self.modG(l, 1, cond, m), in1=xs, op0=ALU.mult, op1=ALU.add),
                             reads=[rPD[b], rX, self.r_mod], writes=[rX])
                        pcol += N
                for ui, (c0, N, hl, hr) in enumerate(units):
                    o = offs[ui]
                    P.dma("sp", self.XT[:, :, c0:c0 + N], XTt[:, :, o + 1:o + 1 + N], reads=[rX], writes=[self.r_xt])

    def norm_mod_wide(self, XTt, rX, HT, rH, o, W, l, mf, cond, tl, PD, rPD, sqw=None):
        P = self.P
        SQ, rSQ, RSTD, rRSTD = tl["sq"], tl["rsq"], tl["rstd"], tl["rrstd"]
        ones = self.CON[:, C_ONES:C_ONES + 128]
        segs = [(0, min(W, 512))] + ([(512, W)] if W > 512 else [])
        for si, (a, bnd) in enumerate(segs):
            w = bnd - a
            ps, rps = PD[si % 2], rPD[si % 2]
            P.op("act", I("activation", out=SQ[:, :, 0:w], in_=XTt[:, :, o + a:o + bnd], func=AF.Square), reads=[rX], writes=[rSQ])
            mms = [dict(out=ps[:, 0:w], lhsT=ones, rhs=SQ[:, k, 0:w], start=(k == 0), stop=(k == KC - 1)) for k in range(KC)]
            P.op("pe", MM(mms), reads=[rSQ, self.r_con], writes=[rps])
            P.op("act", I("activation", out=RSTD[:, a:bnd], in_=ps[:, 0:w], func=AF.Sqrt, bias=self.EPSB[:, 0:1], scale=1.0),
                 reads=[rps], writes=[rRSTD])
            P.op("dve", I("reciprocal", out=RSTD[:, a:bnd], in_=RSTD[:, a:bnd]), reads=[rRSTD], writes=[rRSTD])
            P.op("dve", I("tensor_tensor", out=SQ[:, :, 0:w], in0=XTt[:, :, o + a:o + bnd], in1=RSTD[:, a:bnd].unsqueeze(1).to_broadcast([128, KC, w]), op=ALU.mult),
                 reads=[rX, rRSTD, rSQ], writes=[rSQ])
            for k in range(KC):
                if k % 2 == 0:
                    P.op("act", I("activation", out=HT[:, k, o + a:o + bnd], in_=SQ[:, k, 0:w], func=AF.Identity,
                                  bias=self.modB(l, mf, cond, k), scale=self.modA(l, mf, cond, k)),
                         reads=[rSQ, self.r_amf, self.r_mod], writes=[rH[k] if isinstance(rH, RG) else rH])
                else:
                    P.op("pool", I("tensor_scalar", out=HT[:, k, o + a:o + bnd], in0=SQ[:, k, 0:w], scalar1=self.modA(l, mf, cond, k),
                                   scalar2=self.modB(l, mf, cond, k), op0=ALU.mult, op1=ALU.add),
                         reads=[rSQ, self.r_amf, self.r_mod], writes=[rH[k] if isinstance(rH, RG) else rH])


    def even_stage(self, l):
        P = self.P
        j = l // 2
        N = 256
        NCH = N // 128
        QA, KD, QB, KB, GB, RR, TM = 0, 512, 768, 1024, 1280, 1792, 1856
        NCE = TM + 768
        with P.stage():
            WE = P.sb("e_we", [128, KC, NCE], BF16)
            WO = P.sb("e_wo", [128, KC, D], BF16)
            WG = P.sb("e_wg", [64, 512], BF16)
            rW = Res()
            P.dma("pool", WE[:].rearrange("p k n -> p (k n)"), self.w_even[j], writes=[rW])
            P.dma("pool", WO[:], self.ev_w_out[j].rearrange("(k p) n -> p k n", p=128), writes=[rW])
            P.dma("pool", WG[:], self.wg[j], writes=[rW])
            identf = self.CON[:, C_IDENT:C_IDENT + 128]
            identb = self.CONB[:, 0:128]
            ones = self.CON[:, C_ONES:C_ONES + 128]
            permf = self.CON[:, C_PERM:C_PERM + 128]
            PJ = [P.ps(f"e_pj{i}", [128, 512], F32) for i in range(3)]
            rPJ = [PRes() for _ in range(3)]
            ATp = [P.ps(f"e_at{i}", [128, 512], F32) for i in range(2)]
            rAT = [PRes() for _ in range(2)]
            OTp = P.ps("e_ot", [128, 512], F32)
            rOT = PRes()
            KVp = P.ps("e_kv", [128, 512], F32)
            rKV = PRes()
            TRb = P.ps("e_trb", [128, 1024], BF16)
            rTRb = PRes()
            pjc = [0]

            def pj():
                b = pjc[0] % 2
                pjc[0] += 1
                return PJ[b], rPJ[b]

            KCX = P.sb("e_kcx", [128, 2, 256], BF16)
            VCX = P.sb("e_vcx", [128, 2, 128], BF16)
            CKD = P.sb("e_ckd", [128, 2, 2, 2, 64], F32)
            rKCX, rVCX, rCKD = Res(), Res(), Res()
            ckv = self.cache_k[j].rearrange("(b t) (h d) -> t b h d", t=128, h=2)
            for dup in range(2):
                for blk in range(2):
                    P.dma("sp", CKD[:, blk, :, dup, :], ckv[:, blk], writes=[rCKD])
            for blk in range(2):
                for kvh in range(2):
                    pt, rpt = pj()
                    P.op("pe", TR([(pt[:, 0:128], CKD[:, blk, kvh].rearrange("t u d -> t (u d)"))], identf), reads=[rCKD, self.r_con], writes=[rpt])
                    P.op("act", I("copy", out=KCX[:, kvh, blk * 128:(blk + 1) * 128], in_=pt[:, 0:128]), reads=[rpt], writes=[rKCX])
            P.dma("pool", VCX[:], self.cache_v[j].rearrange("(b t) f -> t b f", t=128), writes=[rVCX])

            KR = P.sb("e_kr", [128, 2, TS], BF16)
            VA = P.sb("e_va", [128, TS // 128, 128], BF16)
            SBS = P.sb("e_sbs", [128, TS // 128, 2, 128], BF16)
            rKR = [Res() for _ in range(TS // N)]
            rVA = [Res() for _ in range(TS // N)]
            rSBS = [Res() for _ in range(TS // N)]
            SF = P.sb("e_sf", [128, 2, 256], F32)
            SB_ = P.sb("e_sb", [128, 2, 256], F32)
            SFb = P.sb("e_sfb", [128, 2, 256], BF16)
            rSF, rSB, rSFb = Res(), Res(), Res()
            XTt = P.sb("e_xt", [128, KC, N], F32)
            XR = P.sb("e_xr", [128, KC, N], F32)
            RSTD = P.sb("e_rstd", [128, N], F32)
            HT = P.sb("e_ht", [128, KC, N], BF16)
            ROP = P.sb("e_rop", [128, 2, N], F32)
            QS = P.sb("e_qs", [128, N], F32)
            T1 = P.sb("e_t1", [128, N], F32)
            T2 = P.sb("e_t2", [128, N], F32)
            QR = [P.sb(f"e_qr{i}", [128, 4, N], BF16) for i in range(2)]
            OB = [P.sb(f"e_ob{i}", [128, 4, N], BF16) for i in range(2)]
            OAT = P.sb("e_oat", [128, 4, N], BF16)
            GS = P.sb("e_gs", [128, 4, N], BF16)
            QBr = P.sb("e_qbr", [128, 2, N], F32)
            KBr = P.sb("e_kbr", [128, 2, N], F32)
            RT = P.sb("e_rt", [64, N], BF16)
            VB = P.sb("e_vb", [128, NCH, 512], BF16)
            KVO = P.sb("e_kvo", [128, NCH, 256], F32)
            Lg = P.sb("e_l", [128, 2, N], F32)
            CS = P.sb("e_cs", [128, 2, N], F32)
            E1 = P.sb("e_e1", [128, 2, N], F32)
            E2 = P.sb("e_e2", [128, 2, N], F32)
            E3 = P.sb("e_e3", [128, 2, N], F32)
            NB = P.sb("e_nb", [128, 2, NCH], F32)
            AD = P.sb("e_ad", [128, 2, NCH], F32)
            QT = [P.sb(f"e_qt{d}", [128, 2, N], BF16) for d in range(2)]
            KT = [P.sb(f"e_kt{d}", [128, 2, N], BF16) for d in range(2)]
            KHT = P.sb("e_kht", [128, 2, N], BF16)
            KH = P.sb("e_kh", [128, NCH, 2, 128], BF16)
            ATm = [P.sb(f"e_atm{d}", [128, 512], BF16) for d in range(2)]
            OSQ = P.sb("e_osq", [128, 512], F32)
            ORS = P.sb("e_ors", [128, 512], F32)
            OBF = P.sb("e_obf", [128, 512], F32)
            SM = [P.sb(f"e_sm{i}", [128, 640], F32) for i in range(2)]
            PX = [P.sb(f"e_px{i}", [128, 640], BF16) for i in range(2)]
            PTs = [P.sb(f"e_pts{i}", [128, 640], BF16) for i in range(2)]
            OAs = P.sb("e_oas", [128, 512], BF16)
            SMALL = [P.sb(f"e_small{i}", [128, 8], F32) for i in range(2)]
            RDEN = [P.sb(f"e_rden{i}", [128, 8], F32) for i in range(2)]
            rSMl = [Res() for _ in range(2)]
            rPXl = [Res() for _ in range(2)]
            rPTl = [Res() for _ in range(2)]
            rSMALLl = [Res() for _ in range(2)]
            rRDEN = [Res() for _ in range(2)]
            NBG = P.sb("e_nbg", [128, 4], F32)
            (rX, rXR, rRSTD, rH_, rROP, rQS, rT1, rT2, rOAT, rGS, rQBr, rKBr, rRT, rVB, rKVO, rL, rCS, rE1, rE2, rE3, rNB, rAD,
             rKHT, rKH, rOSQ, rORS, rOBF, rSM, rPX, rPTs, rOAs, rSMALL, rNBG) = (Res() for _ in range(33))
            rH = RG(Res() for _ in range(KC))
            rQR = [Res() for _ in range(2)]
            rOB = [Res() for _ in range(2)]
            rQT = [Res() for _ in range(2)]
            rKT = [Res() for _ in range(2)]
            rATm = [Res() for _ in range(2)]
            obg, _ = VEC_OFF["b_gate"]
            P.op("dve", I("tensor_scalar", out=NBG[:], in0=self.VEC[:, obg + j * 4: obg + j * 4 + 4], scalar1=-1.0, scalar2=None, op0=ALU.mult),
                 reads=[self.r_vec], writes=[rNBG])
            osk, _ = VEC_OFF["sink"]
            ogn, _ = VEC_OFF["gla_norm"]
            gn = self.VEC[:, ogn + j: ogn + j + 1]

            def proj_fm(col, M, dst_rows=128):
                pt, rpt = pj()
                mms = [dict(out=pt[0:M, 0:N], lhsT=WE[:, k, col:col + M], rhs=HT[:, k, 0:N], start=(k == 0), stop=(k == KC - 1)) for k in range(KC)]
                P.op("pe", MM(mms), reads=[rW, rH], writes=[rpt])
                return pt, rpt

            def gates(d, rev):
                for p in range(2):
                    pt, rpt = pj()
                    P.op("pe", MM([dict(out=pt[:, 0:N], lhsT=WG[0:64, d * 256 + p * 128: d * 256 + (p + 1) * 128], rhs=RT[0:64, 0:N], start=True, stop=True)]),
                         reads=[rW, rRT], writes=[rpt])
                    P.op("act", I("activation", out=Lg[:, p, :], in_=pt[:, 0:N], func=AF.Exp, bias=NBG[:, d * 2 + p: d * 2 + p + 1], scale=-1.0),
                         reads=[rpt, rNBG], writes=[rL])
                P.op("act", I("activation", out=Lg[:].rearrange("p a n -> p (a n)"), in_=Lg[:].rearrange("p a n -> p (a n)"), func=AF.Ln,
                              bias=self.ONEB[:, 0:1], scale=1.0), reads=[rL], writes=[rL])
                rst = self.CON[:, C_RST:C_RST + N]
                for p in range(2):
                    if not rev:
                        P.op("dve", I("tensor_tensor_scan", out=CS[:, p, :], data0=rst, data1=Lg[:, p, :], initial=0.0, op0=ALU.mult, op1=ALU.add),
                             reads=[rL, self.r_con], writes=[rCS])
                    else:
                        P.op("dve", I("tensor_tensor_scan", out=CS[:, p, :][:, ::-1], data0=rst, data1=Lg[:, p, :][:, ::-1], initial=0.0,
                                      op0=ALU.mult, op1=ALU.add), reads=[rL, self.r_con], writes=[rCS])

            def e3_and_decay(rev):
                first = 0 if rev else 127
                P.op("dve", I("tensor_scalar", out=NB[:], in0=CS[:, :, first::128], scalar1=-1.0 / 16.0, scalar2=None, op0=ALU.mult),
                     reads=[rCS], writes=[rNB])
                P.op("act", I("activation", out=AD[:].rearrange("p a c -> p (a c)"), in_=NB[:].rearrange("p a c -> p (a c)"), func=AF.Exp),
                     reads=[rNB], writes=[rAD])
                for p in range(2):
                    for c in range(NCH):
                        P.op("act", I("activation", out=E3[:, p, c * 128:(c + 1) * 128], in_=CS[:, p, c * 128:(c + 1) * 128], func=AF.Exp,
                                      bias=NB[:, p, c:c + 1], scale=1.0 / 16.0), reads=[rCS, rNB], writes=[rE3])
                P.op("dve", I("tensor_tensor", out=KHT[:], in0=KBr[:], in1=E3[:], op=ALU.mult), reads=[rKBr, rE3], writes=[rKHT])
                for c in range(NCH):
                    trs = [(TRb[:, (c * 2 + p) * 128:(c * 2 + p + 1) * 128], KHT[:, p, c * 128:(c + 1) * 128]) for p in range(2)]
                    P.op("pe", TR(trs, identb), reads=[rKHT, self.r_conb], writes=[rTRb])
                P.op("act", I("copy", out=KH[:].rearrange("t c p f -> t (c p f)"), in_=TRb[:, 0:NCH * 256]), reads=[rTRb], writes=[rKH])

            def kv_update(c, S, rS, extra_reads=()):
                mms = [dict(out=KVp[:, p * 256:(p + 1) * 256], lhsT=KH[:, c, p, :], rhs=VB[:, c, p * 256:(p + 1) * 256], start=True, stop=True) for p in range(2)]
                P.op("pe", MM(mms), reads=[rKH, rVB], writes=[rKV])
                for p in range(2):
                    P.op("dve", I("scalar_tensor_tensor", out=S[:, p, :], in0=S[:, p, :], scalar=AD[:, p, c:c + 1], in1=KVp[:, p * 256:(p + 1) * 256],
                                  op0=ALU.mult, op1=ALU.add), reads=[rKV, rAD, rS] + list(extra_reads), writes=[rS])

            def load_norm(t0, cond):
                P.dma("sp", XTt[:], self.XT[:, :, t0:t0 + N], reads=[self.r_xt], writes=[rX])
                tl = dict(sq=XR, rsq=rXR, rstd=RSTD, rrstd=rRSTD)
                self.norm_mod_wide(XTt, rX, HT, rH, 0, N, l, 0, cond, tl, [PJ[0], PJ[1]], [rPJ[0], rPJ[1]])

            def vb_proj():
                for c in range(NCH):
                    pt, rpt = pj()
                    mms = [dict(out=pt[:, 0:512], lhsT=HT[:, k, c * 128:(c + 1) * 128], rhs=WE[:, k, TM + 256:TM + 768], start=(k == 0), stop=(k == KC - 1))
                           for k in range(KC)]
                    P.op("pe", MM(mms), reads=[rW, rH], writes=[rpt])
                    P.op("act", I("copy", out=VB[:, c, :], in_=pt[:, 0:512]), reads=[rpt], writes=[rVB])

            def r_proj():
                pt, rpt = proj_fm(RR, 64)
                P.op("act", I("copy", out=RT[:, :], in_=pt[0:64, 0:N]), reads=[rpt], writes=[rRT])

            def kb_proj():
                for p in range(2):
                    pt, rpt = proj_fm(KB + p * 128, 128)
                    P.op("act", I("copy", out=KBr[:, p, :], in_=pt[:, 0:N]), reads=[rpt], writes=[rKBr])

            seqs = [(0, TS, 1, None)] + [(TS + b * SEQ, SEQ, 0, b) for b in range(NPC)]
            if self.dbg_seqs is not None:
                seqs = [seqs[i] for i in self.dbg_seqs]
            if self.dbg_cut <= 1:
                seqs = []
            for (s0, T, cond, pb) in seqs:
                ntile = T // N
                nblk = T // 128
                sample = pb is None
                for (S, rS, d) in ((SF, rSF, 0), (SB_, rSB, 1)):
                    P.op("pool", I("memset", S[:], 0.0), writes=[rS])
                    if sample:
                        for p in range(2):
                            for hh in range(2):
                                P.dma("sp", S[hh * 64:(hh + 1) * 64, p, hh * 128:(hh + 1) * 128], self.state_gla[j, d, 2 * p + hh], writes=[rS])
                for ti in range(ntile - 1, -1, -1) if self.dbg_cut > 2 else []:
                    t0 = s0 + ti * N
                    load_norm(t0, cond)
                    r_proj()
                    kb_proj()
                    vb_proj()
                    gates(1, True)
                    e3_and_decay(True)
                    for c in range(NCH - 1, -1, -1):
                        gc = ti * NCH + c
                        for p in range(2):
                            P.op("pool", I("tensor_copy", out=SBS[0:64, gc, p, :], in_=SB_[0:64, p, 0:128]), reads=[rSB], writes=[rSBS[ti]])
                            P.op("pool", I("tensor_copy", out=SBS[64:128, gc, p, :], in_=SB_[64:128, p, 128:256]), reads=[rSB], writes=[rSBS[ti]])
                        kv_update(c, SB_, rSB)
                if not sample:
                    for p in range(2):
                        for hh in range(2):
                            P.dma("sp", self.ngla[pb, j, 1, 2 * p + hh], SB_[hh * 64:(hh + 1) * 64, p, hh * 128:(hh + 1) * 128], reads=[rSB], writes=[self.r_out])
                P.op("act", I("copy", out=SFb[:], in_=SF[:]), reads=[rSF], writes=[rSFb])

                def phase_a(ti):
                    t0 = s0 + ti * N
                    bq = ti % 2
                    load_norm(t0, cond)
                    if sample:
                        P.dma("sp", ROP[:], self.rope[:, :, ti * N:(ti + 1) * N], writes=[rROP])

                    def rope_or_copy(pt, rpt, dst, rdst):
                        if not sample:
                            P.op("act", I("copy", out=dst, in_=pt[:, 0:N]), reads=[rpt], writes=[rdst])
                            return
                        P.op("act", I("copy", out=QS[:], in_=pt[:, 0:N]), reads=[rpt], writes=[rQS])
                        p2, rp2 = pj()
                        P.op("pe", MM([dict(out=p2[:, 0:N], lhsT=permf, rhs=QS[:], start=True, stop=True)]), reads=[rQS, self.r_con], writes=[rp2])
                        P.op("pool", I("tensor_tensor", out=T1[:], in0=QS[:], in1=ROP[:, 0, :], op=ALU.mult), reads=[rQS, rROP], writes=[rT1])
                        P.op("dve", I("tensor_tensor", out=T2[:], in0=p2[:, 0:N], in1=ROP[:, 1, :], op=ALU.mult), reads=[rp2, rROP], writes=[rT2])
                        P.op("pool", I("tensor_tensor", out=dst, in0=T1[:], in1=T2[:], op=ALU.add), reads=[rT1, rT2], writes=[rdst])

                    if self.dbg_cut <= 4.1:
                        return
                    for cc in range(4):
                        pt, rpt = proj_fm(QA + cc * 128, 128)
                        rope_or_copy(pt, rpt, QR[bq][:, cc, :], rQR[bq])
                    for kvh in range(2):
                        pt, rpt = proj_fm(KD + kvh * 128, 128)
                        rope_or_copy(pt, rpt, KR[:, kvh, ti * N:(ti + 1) * N], rKR[ti])
                    if self.dbg_cut <= 4.2:
                        return
                    for p in range(2):
                        pt, rpt = proj_fm(QB + p * 128, 128)
                        P.op("act", I("activation", out=QBr[:, p, :], in_=pt[:, 0:N], func=AF.Identity, scale=0.125), reads=[rpt], writes=[rQBr])
                    kb_proj()
                    for h in range(4):
                        pt, rpt = proj_fm(GB + h * 128, 128)
                        P.op("act", I("activation", out=GS[:, h, :], in_=pt[:, 0:N], func=AF.Silu), reads=[rpt], writes=[rGS])
                    r_proj()
                    vb_proj()
                    if self.dbg_cut <= 4.3:
                        return
                    for c in range(NCH):
                        pt, rpt = pj()
                        mms = [dict(out=pt[:, 0:256], lhsT=HT[:, k, c * 128:(c + 1) * 128], rhs=WE[:, k, TM:TM + 256], start=(k == 0), stop=(k == KC - 1))
                               for k in range(KC)]
                        P.op("pe", MM(mms), reads=[rW, rH], writes=[rpt])
                        P.op("act", I("copy", out=VA[:, ti * NCH + c, :], in_=pt[:, 128:256]), reads=[rpt], writes=[rVA[ti]])
                        if not sample:
                            P.op("dve", I("tensor_copy", out=KVO[:, c, :], in_=pt[:, 0:256]), reads=[rpt], writes=[rKVO])
                        if not sample and self.dbg_cut != 4.35:
                            P.dma("sp", self.nk[pb, j, (ti * NCH + c) * 128:(ti * NCH + c + 1) * 128, :], KVO[:, c, 0:128], reads=[rKVO], writes=[self.r_out])
                            P.dma("sp", self.nv[pb, j, (ti * NCH + c) * 128:(ti * NCH + c + 1) * 128, :], KVO[:, c, 128:256], reads=[rKVO], writes=[self.r_out])
                    if self.dbg_cut <= 4.4:
                        return
                    for d in range(2):
                        gates(d, d == 1)
                        flat = lambda t: t[:].rearrange("p a n -> p (a n)")
                        P.op("act", I("activation", out=flat(E1), in_=flat(CS), func=AF.Exp, scale=-1.0 / 16.0), reads=[rCS], writes=[rE1])
                        P.op("act", I("activation", out=flat(E2), in_=flat(CS), func=AF.Exp, scale=1.0 / 16.0), reads=[rCS], writes=[rE2])
                        P.op("dve", I("tensor_tensor", out=QT[d][:], in0=QBr[:], in1=E1[:], op=ALU.mult), reads=[rQBr, rE1], writes=[rQT[d]])
                        P.op("pool", I("tensor_tensor", out=KT[d][:], in0=KBr[:], in1=E2[:], op=ALU.mult), reads=[rKBr, rE2], writes=[rKT[d]])
                        if d == 0:
                            e3_and_decay(False)
                    if self.dbg_cut <= 4.5:
                        return
                    for c in range(NCH):
                        gc = ti * NCH + c
                        cs_ = slice(c * 128, (c + 1) * 128)
                        for hh in range(2):
                            mms = []
                            for d in range(2):
                                for p in range(2):
                                    mms.append(dict(out=ATp[hh][:, (d * 2 + p) * 128:(d * 2 + p + 1) * 128], lhsT=KT[d][hh * 64:(hh + 1) * 64, p, cs_],
                                                    rhs=QT[d][hh * 64:(hh + 1) * 64, p, cs_], start=True, stop=True))
                            P.op("pe", MM(mms), reads=[rKT[0], rKT[1], rQT[0], rQT[1]], writes=[rAT[hh]])
                            for d in range(2):
                                mo_ = C_MF if d == 0 else C_MB
                                mk = self.CON[:, mo_:mo_ + 128].unsqueeze(1).to_broadcast([128, 2, 128])
                                P.op("dve", I("tensor_tensor", out=ATm[hh][:, d * 256:(d + 1) * 256].rearrange("p (h n) -> p h n", h=2),
                                              in0=ATp[hh][:, d * 256:(d + 1) * 256].rearrange("p (h n) -> p h n", h=2), in1=mk, op=ALU.mult),
                                     reads=[rAT[hh], self.r_con], writes=[rATm[hh]])
                        mms = []
                        for h in range(4):
                            p, hh = h // 2, h % 2
                            o = OTp[:, h * 128:(h + 1) * 128]
                            mms.append(dict(out=o, lhsT=VB[:, c, h * 128:(h + 1) * 128], rhs=ATm[hh][:, (0 * 2 + p) * 128:(0 * 2 + p + 1) * 128], start=True, stop=False))
                            mms.append(dict(out=o, lhsT=VB[:, c, h * 128:(h + 1) * 128], rhs=ATm[hh][:, (1 * 2 + p) * 128:(1 * 2 + p + 1) * 128], start=False, stop=False))
                            mms.append(dict(out=o, lhsT=SFb[hh * 64:(hh + 1) * 64, p, hh * 128:(hh + 1) * 128], rhs=QT[0][hh * 64:(hh + 1) * 64, p, cs_],
                                            start=False, stop=False))
                            mms.append(dict(out=o, lhsT=SBS[hh * 64:(hh + 1) * 64, gc, p, :], rhs=QT[1][hh * 64:(hh + 1) * 64, p, cs_],
                                            start=False, stop=True))
                        P.op("pe", MM(mms), reads=[rVB, rATm[0], rATm[1], rSFb, rSBS[ti], rQT[0], rQT[1]], writes=[rOT])
                        kv_update(c, SF, rSF)
                        P.op("act", I("copy", out=SFb[:], in_=SF[:]), reads=[rSF], writes=[rSFb])
                        P.op("act", I("activation", out=OSQ[:], in_=OTp[:], func=AF.Square), reads=[rOT], writes=[rOSQ])
                        pt, rpt = pj()
                        P.op("pe", MM([dict(out=pt[:, 0:512], lhsT=ones, rhs=OSQ[:], start=True, stop=True)]), reads=[rOSQ, self.r_con], writes=[rpt])
                        P.op("act", I("activation", out=ORS[:], in_=pt[:, 0:512], func=AF.Sqrt, bias=self.EPSB[:, 0:1], scale=8.0), reads=[rpt], writes=[rORS])
                        P.op("dve", I("reciprocal", out=ORS[:], in_=ORS[:]), reads=[rORS], writes=[rORS])
                        P.op("dve", I("tensor_tensor", out=OBF[:], in0=OTp[:], in1=ORS[:], op=ALU.mult), reads=[rOT, rORS], writes=[rOBF])
                        P.op("dve", I("scalar_tensor_tensor", out=OB[bq][:, :, cs_], in0=OBF[:].rearrange("p (h n) -> p h n", h=4), scalar=gn,
                                      in1=GS[:, :, cs_], op0=ALU.mult, op1=ALU.mult), reads=[rOBF, rGS, self.r_vec], writes=[rOB[bq]])

                def phase_b(ti):
                    t0 = s0 + ti * N
                    bq = ti % 2
                    P.dma("sp", XR[:], self.XT[:, :, t0:t0 + N], reads=[self.r_xt], writes=[rXR])
                    for c in range(NCH):
                        qb = ti * NCH + c
                        cs_ = slice(c * 128, (c + 1) * 128)
                        if sample:
                            kb0, kb1 = max(0, qb - 1), min(nblk, qb + 2)
                        else:
                            kb0, kb1 = 0, nblk
                        nw = (kb1 - kb0) * 128
                        ntot = nw + (256 if sample else 0)
                        kread = sorted(set((b * 128) // N for b in range(kb0, kb1)))
                        pa, rpa = PJ[2], rPJ[2]
                        rd, rrd = RDEN[qb % 2], rRDEN[qb % 2]
                        def qk(h):
                            cc, hh, kvh = h // 2, h % 2, h // 4
                            par = h % 2
                            rows = slice(hh * 64, (hh + 1) * 64)
                            sw, rsw = (ATp[0], rAT[0]) if par == 0 else (OTp, rOT)
                            sc, rsc = (ATp[1], rAT[1]) if par == 0 else (KVp, rKV)
                            P.op("pe", MM([dict(out=sw[:, 0:nw], lhsT=QR[bq][rows, cc, cs_], rhs=KR[rows, kvh, kb0 * 128:kb1 * 128], start=True, stop=True)]),
                                 reads=[rQR[bq]] + [rKR[t] for t in kread], writes=[rsw])
                            if sample:
                                P.op("pe", MM([dict(out=sc[:, 0:256], lhsT=QR[bq][rows, cc, cs_], rhs=KCX[rows, kvh, :], start=True, stop=True)]),
                                     reads=[rQR[bq], rKCX], writes=[rsc])

                        def stage1(h):
                            par = h % 2
                            sw, rsw = (ATp[0], rAT[0]) if par == 0 else (OTp, rOT)
                            sc, rsc = (ATp[1], rAT[1]) if par == 0 else (KVp, rKV)
                            SMh, rSMh, PXh, rPXh, SMALLh, rSMALLh = SM[par], rSMl[par], PX[par], rPXl[par], SMALL[par], rSMALLl[par]
                            if sample:
                                mo = C_AMASK + (kb0 - (qb - 1)) * 128
                                P.op("dve", I("tensor_tensor", out=SMh[:, 0:nw], in0=sw[:, 0:nw], in1=self.CON[:, mo:mo + nw], op=ALU.add),
                                     reads=[rsw, self.r_con], writes=[rSMh])
                                P.op("act", I("copy", out=SMh[:, nw:ntot], in_=sc[:, 0:256]), reads=[rsc], writes=[rSMh])
                            else:
                                P.op("act", I("copy", out=SMh[:, 0:nw], in_=sw[:, 0:nw]), reads=[rsw], writes=[rSMh])
                            sk = self.VEC[:, osk + j * 8 + h: osk + j * 8 + h + 1]
                            P.op("dve", I("reduce_max", out=SMALLh[:, 0:1], in_=SMh[:, 0:ntot], axis=mybir.AxisListType.X), reads=[rSMh], writes=[rSMALLh])
                            P.op("dve", I("tensor_scalar", out=SMALLh[:, 1:2], in0=SMALLh[:, 0:1], scalar1=0.125, scalar2=sk, op0=ALU.mult, op1=ALU.max),
                                 reads=[rSMALLh, self.r_vec], writes=[rSMALLh])
                            P.op("dve", I("tensor_scalar", out=SMALLh[:, 2:3], in0=SMALLh[:, 1:2], scalar1=-1.0, scalar2=None, op0=ALU.mult),
                                 reads=[rSMALLh], writes=[rSMALLh])
                            P.op("act", I("activation", out=PXh[:, 0:ntot], in_=SMh[:, 0:ntot], func=AF.Exp, bias=SMALLh[:, 2:3], scale=0.125, accum_out=SMALLh[:, 3:4]),
                                 reads=[rSMh, rSMALLh], writes=[rPXh, rSMALLh])
                            P.op("act", I("activation", out=SMALLh[:, 4:5], in_=sk, func=AF.Exp, bias=SMALLh[:, 2:3], scale=1.0),
                                 reads=[rSMALLh, self.r_vec], writes=[rSMALLh])

                        def stage2(h):
                            kvh = h // 4
                            par = h % 2
                            PXh, rPXh, PTh, rPTh, SMALLh, rSMALLh = PX[par], rPXl[par], PTs[par], rPTl[par], SMALL[par], rSMALLl[par]
                            P.op("dve", I("tensor_tensor", out=SMALLh[:, 5:6], in0=SMALLh[:, 3:4], in1=SMALLh[:, 4:5], op=ALU.add), reads=[rSMALLh], writes=[rSMALLh])
                            P.op("dve", I("reciprocal", out=rd[:, h:h + 1], in_=SMALLh[:, 5:6]), reads=[rSMALLh], writes=[rrd])
                            nkb = ntot // 128
                            trs = [(TRb[:, b * 128:(b + 1) * 128], PXh[:, b * 128:(b + 1) * 128]) for b in range(nkb)]
                            P.op("pe", TR(trs, identb), reads=[rPXh, self.r_conb], writes=[rTRb])
                            P.op("dve", I("tensor_copy", out=PTh[:, 0:ntot], in_=TRb[:, 0:ntot]), reads=[rTRb], writes=[rPTh])
                            mms = []
                            for b in range(nkb):
                                if b < kb1 - kb0:
                                    vv = VA[:, kb0 + b, kvh * 64:(kvh + 1) * 64]
                                else:
                                    vv = VCX[:, b - (kb1 - kb0), kvh * 64:(kvh + 1) * 64]
                                mms.append(dict(out=pa[:, h * 64:(h + 1) * 64], lhsT=PTh[:, b * 128:(b + 1) * 128], rhs=vv, start=(b == 0), stop=(b == nkb - 1)))
                            P.op("pe", MM(mms), reads=[rPTh, rVCX] + [rVA[t] for t in kread], writes=[rpa])

                        qk(0)
                        qk(1)
                        stage1(0)
                        for h in range(8):
                            if h + 1 < 8:
                                stage1(h + 1)
                            if h + 2 < 8:
                                qk(h + 2)
                            stage2(h)
                        P.op("dve", I("tensor_tensor", out=OAs[:].rearrange("p (h d) -> p h d", h=8), in0=pa[:].rearrange("p (h d) -> p h d", h=8),
                                      in1=rd[:].unsqueeze(2).to_broadcast([128, 8, 64]), op=ALU.mult), reads=[rpa, rrd], writes=[rOAs])
                        trs = [(TRb[:, k4 * 128:(k4 + 1) * 128], OAs[:, k4 * 128:(k4 + 1) * 128]) for k4 in range(4)]
                        P.op("pe", TR(trs, identb), reads=[rOAs, self.r_conb], writes=[rTRb])
                        P.op("dve", I("tensor_copy", out=OAT[:, :, cs_], in_=TRb[:, 0:512].rearrange("p (k n) -> p k n", k=4)), reads=[rTRb], writes=[rOAT])
                    for m in range(KC):
                        pt, rpt = pj()
                        mms = []
                        for k in range(KC):
                            rhs = OAT[:, k, :] if k < 4 else OB[bq][:, k - 4, :]
                            mms.append(dict(out=pt[:, 0:N], lhsT=WO[:, k, m * 128:(m + 1) * 128], rhs=rhs, start=(k == 0), stop=(k == KC - 1)))
                        P.op("pe", MM(mms), reads=[rW, rOAT, rOB[bq]], writes=[rpt])
                        P.op("dve", I("scalar_tensor_tensor", out=XR[:, m, :], in0=pt[:, 0:N], scalar=self.modG(l, 0, cond, m), in1=XR[:, m, :], op0=ALU.mult, op1=ALU.add),
                             reads=[rpt, rXR, self.r_mod], writes=[rXR])
                    P.dma("sp", self.XT[:, :, t0:t0 + N], XR[:], reads=[rXR], writes=[self.r_xt])

                for ti in range(ntile + 1):
                    if ti < ntile and self.dbg_cut > 3:
                        phase_a(ti)
                    if ti >= 1 and self.dbg_cut > 4:
                        phase_b(ti - 1)
                if not sample:
                    for p in range(2):
                        for hh in range(2):
                            P.dma("sp", self.ngla[pb, j, 0, 2 * p + hh], SF[hh * 64:(hh + 1) * 64, p, hh * 128:(hh + 1) * 128], reads=[rSF], writes=[self.r_out])

    def odd_stage(self, l):
        P = self.P
        j = l // 2
        NMAX = 512
        WMAX = NMAX + 3
        with P.stage():
            WIN = P.sb("o_win", [128, KC, 2048], BF16)
            WOUT = P.sb("o_wout", [128, KC, D], BF16)
            BD = P.sb("o_bd", [128, 4, KC, 128], BF16)
            rW = Res()
            P.dma("pool", WIN[:], self.od_w_in[j].rearrange("(k p) n -> p k n", p=128), writes=[rW])
            P.dma("pool", WOUT[:], self.od_w_out[j].rearrange("(k p) n -> p k n", p=128), writes=[rW])
            for di in range(2):
                for ai in range(2):
                    P.dma("pool", BD[:, di * 2 + ai].rearrange("p k n -> p (k n)"), self.od_bd[j, di, ai], writes=[rW])
            CL = P.sb("o_cl", [128, 2, 2, KC], F32)
            rCL = Res()
            ol, _ = VEC_OFF["od_lambda"]
            lam = self.VEC[:, ol + j * 16: ol + j * 16 + 16]
            cl0 = CL[:, 0].rearrange("p d k -> p (d k)")
            cl1 = CL[:, 1].rearrange("p d k -> p (d k)")
            P.op("act", I("activation", out=cl0, in_=lam, func=AF.Exp, scale=-1.0), reads=[self.r_vec], writes=[rCL])
            P.op("act", I("activation", out=cl0, in_=cl0, func=AF.Ln, bias=self.ONEB[:, 0:1], scale=1.0), reads=[rCL], writes=[rCL])
            P.op("dve", I("tensor_scalar", out=cl1, in0=cl0, scalar1=-8.0, scalar2=None, op0=ALU.mult), reads=[rCL], writes=[rCL])
            P.op("dve", I("tensor_scalar", out=cl0, in0=cl0, scalar1=-4.0, scalar2=None, op0=ALU.mult), reads=[rCL], writes=[rCL])
            HBA = P.sb("o_hba", [128, 2, 16], F32)
            rHBA = Res()
            oba_, _ = VEC_OFF["od_b_a"]
            obi_, _ = VEC_OFF["od_b_i"]
            P.op("dve", I("tensor_scalar", out=HBA[:, 0, :], in0=self.VEC[:, oba_ + j * 16: oba_ + j * 16 + 16], scalar1=0.5, scalar2=None, op0=ALU.mult),
                 reads=[self.r_vec], writes=[rHBA])
            P.op("dve", I("tensor_scalar", out=HBA[:, 1, :], in0=self.VEC[:, obi_ + j * 16: obi_ + j * 16 + 16], scalar1=0.5, scalar2=None, op0=ALU.mult),
                 reads=[self.r_vec], writes=[rHBA])

            XTt = P.sb("o_xt", [128, KC, WMAX + 1], F32)
            HT = P.sb("o_ht", [128, KC, WMAX + 1], BF16)
            GYt = P.sb("o_gy", [128, KC, NMAX], BF16)
            U = [P.sb(f"o_u{i}", [128, WMAX + 1], F32) for i in range(2)]
            XCt = P.sb("o_xc", [128, KC, NMAX], F32)
            XCb = P.sb("o_xcb", [128, KC, NMAX], BF16)
            HFt = P.sb("o_hf", [128, KC, NMAX], F32)
            HBt = P.sb("o_hb", [128, KC, NMAX], F32)
            Zt = P.sb("o_z", [128, KC, NMAX], BF16)
            RSTD = P.sb("o_rstd", [128, WMAX + 1], F32)
            CAR = P.sb("o_car", [128, KC], F32)
            GG = 4
            Rr = [P.sb(f"o_r{i}", [128, NMAX], F32) for i in range(GG)]
            Ii = [P.sb(f"o_i{i}", [128, NMAX], F32) for i in range(GG)]
            Aa = [P.sb(f"o_a{i}", [128, NMAX], F32) for i in range(GG)]
            Mm = [P.sb(f"o_m{i}", [128, NMAX], F32) for i in range(GG)]
            rX, rH_, rGY, rXC_, rXCb_, rHF_, rHB_, rZ_, rRSTD, rCAR = (Res() for _ in range(10))
            rH = RG(Res() for _ in range(KC))
            rXC = RG(Res() for _ in range(KC))
            rXCb = RG(Res() for _ in range(KC))
            rHF = RG(Res() for _ in range(KC))
            rHB = RG(Res() for _ in range(KC))
            rZ = RG(Res() for _ in range(KC))
            rU = [Res() for _ in range(2)]
            rR = [Res() for _ in range(4)]
            rI = [Res() for _ in range(4)]
            rA = [Res() for _ in range(4)]
            rM = [Res() for _ in range(4)]
            PA = [P.ps(f"o_pa{i}", [128, 512], F32) for i in range(2)]
            PBk = [P.ps(f"o_pb{i}", [128, 512], F32) for i in range(2)]
            PHh = [P.ps(f"o_ph{i}", [128, 512], F32) for i in range(2)]
            PD = [P.ps(f"o_pd{i}", [128, 512], F32) for i in range(2)]
            rPA = [PRes() for _ in range(2)]
            rPB = [PRes() for _ in range(2)]
            rPH = [PRes() for _ in range(2)]
            rPD = [PRes() for _ in range(2)]
            ocw, _ = VEC_OFF["od_conv_w"]
            ocb, _ = VEC_OFF["od_conv_b"]
            oba, _ = VEC_OFF["od_b_a"]
            obi, _ = VEC_OFF["od_b_i"]
            ost, _ = VEC_OFF["st_rg"]

            def vcol(o):
                return self.VEC[:, o:o + 1]

            cwt = lambda tap, k: vcol(ocw + (j * 4 + tap) * 8 + k)
            cbt = lambda k: vcol(ocb + j * 8 + k)
            bat = lambda di, k: vcol(oba + (j * 2 + di) * 8 + k)
            bit = lambda di, k: vcol(obi + (j * 2 + di) * 8 + k)
            h0t = lambda di, k: vcol(ost + (j * 2 + di) * 8 + k)
            cnt = [0]

            def gates_group(di, N, ks, HO, rHO, inits):
                for g, k in enumerate(ks):
                    b = cnt[0] % 2
                    cnt[0] += 1
                    P.op("pe", MM([dict(out=PA[b][:, 0:N], lhsT=BD[:, di * 2 + 0, k, :], rhs=XCb[:, k, 0:N], start=True, stop=True)]),
                         reads=[rW, rXCb[k]], writes=[rPA[b]])
                    P.op("pe", MM([dict(out=PBk[b][:, 0:N], lhsT=BD[:, di * 2 + 1, k, :], rhs=XCb[:, k, 0:N], start=True, stop=True)]),
                         reads=[rW, rXCb[k]], writes=[rPB[b]])
                    P.op("act", I("activation", out=Rr[g][:, 0:N], in_=PA[b][:, 0:N], func=AF.Tanh, bias=HBA[:, 0, di * 8 + k: di * 8 + k + 1], scale=0.5),
                         reads=[rPA[b], rHBA], writes=[rR[g]])
                    P.op("act", I("activation", out=Ii[g][:, 0:N], in_=PBk[b][:, 0:N], func=AF.Tanh, bias=HBA[:, 1, di * 8 + k: di * 8 + k + 1], scale=0.5),
                         reads=[rPB[b], rHBA], writes=[rI[g]])
                for g, k in enumerate(ks):
                    P.op("act", I("activation", out=Aa[g][:, 0:N], in_=Rr[g][:, 0:N], func=AF.Exp, scale=CL[:, 0, di, k:k + 1], bias=CL[:, 0, di, k:k + 1]),
                         reads=[rR[g], rCL], writes=[rA[g]])
                    P.op("act", I("activation", out=Mm[g][:, 0:N], in_=Rr[g][:, 0:N], func=AF.Exp, scale=CL[:, 1, di, k:k + 1], bias=CL[:, 1, di, k:k + 1]),
                         reads=[rR[g], rCL], writes=[rM[g]])
                    P.op("dve", I("scalar_tensor_tensor", out=Ii[g][:, 0:N], in0=Ii[g][:, 0:N], scalar=1.0, in1=XCt[:, k, 0:N], op0=ALU.add, op1=ALU.mult),
                         reads=[rI[g], rXC[k]], writes=[rI[g]])
                for g, k in enumerate(ks):
                    P.op("act", I("activation", out=Mm[g][:, 0:N], in_=Mm[g][:, 0:N], func=AF.Sqrt, bias=self.QTRB[:, 0:1], scale=-0.25),
                         reads=[rM[g]], writes=[rM[g]])
                    P.op("dve", I("tensor_tensor", out=Mm[g][:, 0:N], in0=Mm[g][:, 0:N], in1=Ii[g][:, 0:N], op=ALU.mult),
                         reads=[rM[g], rI[g]], writes=[rM[g]])
                    init_ap, rinit = inits[g]
                    if di == 0:
                        P.op("dve", I("tensor_tensor_scan", out=HO[:, k, 0:N], data0=Aa[g][:, 0:N], data1=Mm[g][:, 0:N], initial=init_ap,
                                      op0=ALU.mult, op1=ALU.add), reads=[rA[g], rM[g]] + rinit, writes=[rHO[k]])
                    else:
                        P.op("dve", I("tensor_tensor_scan", out=HO[:, k, 0:N][:, ::-1],
                                      data0=Aa[g][:, 0:N][:, ::-1], data1=Mm[g][:, 0:N][:, ::-1], initial=init_ap,
                                      op0=ALU.mult, op1=ALU.add), reads=[rA[g], rM[g]] + rinit, writes=[rHO[k]])

            seqs = [(0, TS, 1, None)] + [(TS + b * SEQ, SEQ, 0, b) for b in range(NPC)]
            for (s0, T, cond, pb) in seqs:
                N = min(NMAX, T)
                ntile = T // N
                W = N + 3
                for ti in range(ntile):
                    t0 = s0 + ti * N
                    hl = ti > 0
                    hr = ti < ntile - 1
                    lo = t0 - 2 if hl else t0
                    hi = t0 + N + 1 if hr else t0 + N
                    dlo = 0 if hl else 2
                    P.dma("sp", XTt[:, :, dlo:dlo + (hi - lo)], self.XT[:, :, lo:hi], reads=[self.r_xt], writes=[rX])
                    if not hl:
                        P.op("pool", I("memset", XTt[:, :, 0:2], 0.0), writes=[rX])
                    if not hr:
                        P.op("pool", I("memset", XTt[:, :, W - 1:W], 0.0), writes=[rX])
                    tl = dict(sq=HBt, rsq=rHB, rstd=RSTD, rrstd=rRSTD)
                    self.norm_mod_wide(XTt, rX, HT, rH, 0, W, l, 0, cond, tl, PD, rPD, sqw=NMAX)
                    if not hl:
                        P.op("pool", I("memset", HT[:, :, 0:2], 0.0), writes=[rH])
                    if not hr:
                        P.op("pool", I("memset", HT[:, :, W - 1:W], 0.0), writes=[rH])
                    for k in range(KC):
                        b = cnt[0] % 2
                        cnt[0] += 1
                        mms = [dict(out=PA[b][:, 0:N], lhsT=WIN[:, kk, k * 128:(k + 1) * 128], rhs=HT[:, kk, 2:N + 2],
                                    start=(kk == 0), stop=(kk == KC - 1)) for kk in range(KC)]
                        P.op("pe", MM(mms), reads=[rW, rH], writes=[rPA[b]])
                        P.op("act", I("activation", out=GYt[:, k, 0:N], in_=PA[b][:, 0:N], func=AF.Gelu), reads=[rPA[b]], writes=[rGY])
                    P.dma("sp", self.GY[:, :, t0:t0 + N], GYt[:, :, 0:N], reads=[rGY], writes=[self.r_gy])
                    for k in range(KC):
                        b = cnt[0] % 2
                        cnt[0] += 1
                        Wm = min(W, 512)
                        mms = [dict(out=PBk[b][:, 0:Wm], lhsT=WIN[:, kk, D + k * 128:D + (k + 1) * 128], rhs=HT[:, kk, 0:Wm],
                                    start=(kk == 0), stop=(kk == KC - 1)) for kk in range(KC)]
                        P.op("pe", MM(mms), reads=[rW, rH], writes=[rPB[b]])
                        P.op("act", I("copy", out=U[b][:, 0:Wm], in_=PBk[b][:, 0:Wm]), reads=[rPB[b]], writes=[rU[b]])
                        if W > 512:
                            mms = [dict(out=PHh[b][:, 0:W - 512], lhsT=WIN[:, kk, D + k * 128:D + (k + 1) * 128], rhs=HT[:, kk, 512:W],
                                        start=(kk == 0), stop=(kk == KC - 1)) for kk in range(KC)]
                            P.op("pe", MM(mms), reads=[rW, rH], writes=[rPH[b]])
                            P.op("act", I("copy", out=U[b][:, 512:W], in_=PHh[b][:, 0:W - 512]), reads=[rPH[b]], writes=[rU[b]])
                        P.op("dve", I("tensor_scalar", out=XCt[:, k, 0:N], in0=U[b][:, 0:N], scalar1=cwt(0, k), scalar2=cbt(k), op0=ALU.mult, op1=ALU.add),
                             reads=[rU[b], self.r_vec], writes=[rXC[k]])
                        for tap in (1, 2, 3):
                            P.op("dve", I("scalar_tensor_tensor", out=XCt[:, k, 0:N], in0=U[b][:, tap:N + tap], scalar=cwt(tap, k), in1=XCt[:, k, 0:N],
                                          op0=ALU.mult, op1=ALU.add), reads=[rU[b], rXC[k], self.r_vec], writes=[rXC[k]])
                        P.op("pool", I("tensor_copy", out=XCb[:, k, 0:N], in_=XCt[:, k, 0:N]), reads=[rXC[k]], writes=[rXCb[k]])
                    P.dma("sp", self.XC[:, :, t0:t0 + N], XCt[:, :, 0:N], reads=[rXC], writes=[self.r_xc])
                    for k0 in range(0, KC, GG):
                        inits = []
                        for k in range(k0, k0 + GG):
                            if ti == 0:
                                inits.append((h0t(0, k), [self.r_vec]) if pb is None else (0.0, []))
                            else:
                                inits.append((CAR[:, k:k + 1], [rCAR]))
                        gates_group(0, N, list(range(k0, k0 + GG)), HFt, rHF, inits)
                    P.op("pool", I("tensor_copy", out=CAR[:], in_=HFt[:, :, N - 1]), reads=[rHF], writes=[rCAR])
                    P.dma("sp", self.HF[:, :, t0:t0 + N], HFt[:, :, 0:N], reads=[rHF], writes=[self.r_hf])
                    if pb is not None and ti == ntile - 1:
                        c0 = ((pb * 2 + j) * 2 + 0) * 8
                        P.op("pool", I("tensor_copy", out=self.RGO[:, c0:c0 + 8], in_=HFt[:, :, N - 1]), reads=[rHF], writes=[self.r_rgo])
                for ti in range(ntile - 1, -1, -1):
                    t0 = s0 + ti * N
                    P.dma("sp", XCt[:, :, 0:N], self.XC[:, :, t0:t0 + N], reads=[self.r_xc], writes=[rXC])
                    for k in range(KC):
                        P.op("pool", I("tensor_copy", out=XCb[:, k, 0:N], in_=XCt[:, k, 0:N]), reads=[rXC[k]], writes=[rXCb[k]])
                    P.dma("sp", HFt[:, :, 0:N], self.HF[:, :, t0:t0 + N], reads=[self.r_hf], writes=[rHF])
                    P.dma("sp", GYt[:, :, 0:N], self.GY[:, :, t0:t0 + N], reads=[self.r_gy], writes=[rGY])
                    P.dma("sp", XTt[:, :, 0:N], self.XT[:, :, t0:t0 + N], reads=[self.r_xt], writes=[rX])
                    for k0 in range(0, KC, GG):
                        inits = []
                        for k in range(k0, k0 + GG):
                            if ti == ntile - 1:
                                inits.append((h0t(1, k), [self.r_vec]) if pb is None else (0.0, []))
                            else:
                                inits.append((CAR[:, k:k + 1], [rCAR]))
                        gates_group(1, N, list(range(k0, k0 + GG)), HBt, rHB, inits)
                    P.op("pool", I("tensor_copy", out=CAR[:], in_=HBt[:, :, 0]), reads=[rHB], writes=[rCAR])
                    if pb is not None and ti == 0:
                        c0 = ((pb * 2 + j) * 2 + 1) * 8
                        P.op("pool", I("tensor_copy", out=self.RGO[:, c0:c0 + 8], in_=HBt[:, :, 0]), reads=[rHB], writes=[self.r_rgo])
                    for k in range(KC):
                        P.op("dve", I("tensor_tensor", out=HFt[:, k, 0:N], in0=HFt[:, k, 0:N], in1=HBt[:, k, 0:N], op=ALU.add),
                             reads=[rHB[k], rHF[k]], writes=[rHF[k]])
                        P.op("dve", I("tensor_tensor", out=Zt[:, k, 0:N], in0=HFt[:, k, 0:N], in1=GYt[:, k, 0:N], op=ALU.mult),
                             reads=[rHF[k], rGY], writes=[rZ[k]])
                    for m in range(KC):
                        b = cnt[0] % 2
                        cnt[0] += 1
                        mms = [dict(out=PD[b][:, 0:N], lhsT=WOUT[:, kk, m * 128:(m + 1) * 128], rhs=Zt[:, kk, 0:N],
                                    start=(kk == 0), stop=(kk == KC - 1)) for kk in range(KC)]
                        P.op("pe", MM(mms), reads=[rW, rZ], writes=[rPD[b]])
                        xs = XTt[:, m, 0:N]
                        P.op("dve", I("scalar_tensor_tensor", out=xs, in0=PD[b][:, 0:N], scalar=self.modG(l, 0, cond, m), in1=xs, op0=ALU.mult, op1=ALU.add),
                             reads=[rPD[b], rX, self.r_mod], writes=[rX])
                    P.dma("sp", self.XT[:, :, t0:t0 + N], XTt[:, :, 0:N], reads=[rX], writes=[self.r_xt])

    def final_stage(self):
        P = self.P
        with P.stage():
            XF = [P.sb(f"fin_x{i}", [128, KC, 128], F32) for i in range(2)]
            SQ = [P.sb(f"fin_sq{i}", [128, KC, 128], F32) for i in range(2)]
            RS = [P.sb(f"fin_rs{i}", [128, 128], F32) for i in range(2)]
            YB = [P.sb(f"fin_y{i}", [128, D], F32) for i in range(2)]
            rXF = [Res() for _ in range(2)]
            rSQ = [Res() for _ in range(2)]
            rRS = [Res() for _ in range(2)]
            rYB = [Res() for _ in range(2)]
            PSS = [P.ps(f"fin_ps{i}", [128, 512], F32) for i in range(2)]
            rPSS = [PRes() for _ in range(2)]
            TPp = [P.ps(f"fin_tp{i}", [128, 512], F32) for i in range(4)]
            rTP = [PRes() for _ in range(4)]
            ident = self.CON[:, C_IDENT:C_IDENT + 128]
            ones = self.CON[:, C_ONES:C_ONES + 128]
            ofn, _ = VEC_OFF["final_norm"]
            fn_b = self.VEC[:, ofn:ofn + 8].unsqueeze(2).to_broadcast([128, KC, 128])
            fl_issued = [0]
            for blk in range(TT // 128):
                b = blk % 2
                while fl_issued[0] <= min(blk + 1, TT // 128 - 1):
                    q = fl_issued[0]
                    P.dma("sp", XF[q % 2][:], self.XT[:, :, q * 128:(q + 1) * 128], reads=[self.r_xt], writes=[rXF[q % 2]])
                    fl_issued[0] += 1
                P.op("act", I("activation", out=SQ[b][:], in_=XF[b][:], func=AF.Square), reads=[rXF[b]], writes=[rSQ[b]])
                mms = [dict(out=PSS[b][:, 0:128], lhsT=ones, rhs=SQ[b][:, k, :], start=(k == 0), stop=(k == KC - 1)) for k in range(KC)]
                P.op("pe", MM(mms), reads=[rSQ[b], self.r_con], writes=[rPSS[b]])
                P.op("act", I("activation", out=RS[b][:], in_=PSS[b][:, 0:128], func=AF.Sqrt, bias=self.EPSB[:, 0:1], scale=1.0),
                     reads=[rPSS[b]], writes=[rRS[b]])
                P.op("dve", I("reciprocal", out=RS[b][:], in_=RS[b][:]), reads=[rRS[b]], writes=[rRS[b]])
                P.op("dve", I("tensor_tensor", out=SQ[b][:], in0=XF[b][:], in1=RS[b][:].unsqueeze(1).to_broadcast([128, KC, 128]), op=ALU.mult),
                     reads=[rXF[b], rRS[b], rSQ[b]], writes=[rSQ[b]])
                P.op("pool", I("tensor_tensor", out=SQ[b][:], in0=SQ[b][:], in1=fn_b, op=ALU.mult),
                     reads=[rSQ[b], self.r_vec], writes=[rSQ[b]])
                for hf in range(2):
                    tp, rtp = TPp[b * 2 + hf], rTP[b * 2 + hf]
                    trs = [(tp[:, kk * 128:(kk + 1) * 128], SQ[b][:, hf * 4 + kk, :]) for kk in range(4)]
                    P.op("pe", TR(trs, ident), reads=[rSQ[b], self.r_con], writes=[rtp])
                    if hf == 0:
                        P.op("act", I("copy", out=YB[b][:, 0:512], in_=tp[:]), reads=[rtp], writes=[rYB[b]])
                    else:
                        P.op("dve", I("tensor_copy", out=YB[b][:, 512:1024], in_=tp[:]), reads=[rtp], writes=[rYB[b]])
                P.dma("sp", self.y[blk * 128:(blk + 1) * 128, :], YB[b][:], reads=[rYB[b]], writes=[self.r_y])


    def out_stage(self):
        P = self.P
        with P.stage():
            ident = self.CON[:, C_IDENT:C_IDENT + 128]
            TPp = P.ps("os_tp", [128, 512], F32)
            rTP = PRes()
            OB = P.sb("os_ob", [128, 128], F32)
            rOB = Res()
            ncol = NPC * 2 * 2 * KC
            for g in range((ncol + 127) // 128):
                w = min(128, ncol - g * 128)
                P.op("pe", TR([(TPp[0:w, 0:128], self.RGO[:, g * 128:g * 128 + w])], ident), reads=[self.r_rgo, self.r_con], writes=[rTP])
                P.op("dve", I("tensor_copy", out=OB[0:w, :], in_=TPp[0:w, 0:128]), reads=[rTP], writes=[rOB])
                P.dma("sp", self.nrg[g * 128:g * 128 + w, :], OB[0:w, :], reads=[rOB], writes=[self.r_out])

    def build(self):
        P = self.P
        self.r_xt = Res("XT")
        self.r_y = Res("y")
        self.r_gy, self.r_xc, self.r_hf, self.r_rgo, self.r_out = Res(), Res(), Res(), Res(), Res()
        self.setup_globals()
        self.EPSB = P.sb("EPSB", [128, 1], F32, glob=True)
        P.op("pool", I("memset", self.EPSB[:], EPS), writes=[self.r_con])
        self.QTRB = P.sb("QTRB", [128, 1], F32, glob=True)
        P.op("pool", I("memset", self.QTRB[:], 0.25), writes=[self.r_con])
        self.ONEB = P.sb("ONEB", [128, 1], F32, glob=True)
        P.op("pool", I("memset", self.ONEB[:], 1.0), writes=[self.r_con])
        self.RGO = P.sb("RGO", [128, NPC * 2 * 2 * KC], F32, glob=True)
        P.op("pool", I("memset", self.RGO[:], 0.0), writes=[self.r_rgo])
        self.stage0()
        for l in range(self.layers):
            if self.do_mixer:
                if l % 2 == 1:
                    self.odd_stage(l)
                elif self.do_even:
                    self.even_stage(l)
            if self.do_ffn:
                self.ffn_stage(l)
        self.final_stage()
        self.out_stage()
        P.barrier(engines=["sp"])


_CACHE = {}


def pack_vec(inp, core):
    s = core % DEC_BATCH
    vp = VecPack()
    vp.add("cvec", np.stack([fm(inp["c_ctx"]), fm(inp["c"][s])], axis=-1))
    vp.add("b_ada", fm(inp["b_ada"]))
    vp.add("norm_mix", fm(inp["norm_mix"]))
    vp.add("norm_ffn", fm(inp["norm_ffn"]))
    vp.add("final_norm", fm(inp["final_norm"]))
    vp.add("ffn_conv_w", fm(inp["ffn_conv_w"]))
    vp.add("ffn_conv_b", fm(inp["ffn_conv_b"]))
    vp.add("od_conv_w", fm(inp["od_conv_w"]))
    vp.add("od_conv_b", fm(inp["od_conv_b"]))
    vp.add("od_b_a", fm(inp["od_b_a"]))
    vp.add("od_b_i", fm(inp["od_b_i"]))
    vp.add("od_lambda", fm(inp["od_lambda"]))
    vp.add("st_rg", fm(inp["state_rglru"][s]))
    vp.add("gla_norm", fm(inp["ev_gla_norm"]))
    vp.add("b_gate", np.stack([fm(inp["ev_b_gate_f"]), fm(inp["ev_b_gate_b"])], axis=2))
    vp.add("sink", np.broadcast_to(inp["ev_sink"].reshape(1, 16), (128, 16)))
    for n, c in VEC_SPEC:
        assert vp.off[n] == VEC_OFF[n], (n, vp.off[n], VEC_OFF[n])
    return vp.build()


def pack_shared(inp):
    sh = {}
    sh["consts"] = make_consts()
    sh["w_ada"] = np.ascontiguousarray(inp["w_ada"], np.float32)
    up = np.asarray(inp["ffn_w_up"], np.float32)
    upk = up.reshape(DEPTH, KC, 128, 2, NJ, 128)
    sh["ffn_up_t"] = np.ascontiguousarray(upk.transpose(0, 4, 2, 1, 3, 5)).reshape(DEPTH, NJ, 128, KC * 256)
    dn = np.asarray(inp["ffn_w_down"], np.float32)
    dnk = dn.reshape(DEPTH, NJ, 128, KC, 128)
    sh["ffn_dn_t"] = np.ascontiguousarray(dnk.transpose(0, 3, 2, 1, 4)).reshape(DEPTH, KC, 128, NJ * 128)
    wi = np.asarray(inp["ev_w_in"], np.float32)
    z16 = np.zeros((2, D, 16), np.float32)
    cols = [wi[:, :, 0:512],
            wi[:, :, 512:576], wi[:, :, 512:576], wi[:, :, 576:640], wi[:, :, 576:640],
            wi[:, :, 768:1024], wi[:, :, 1024:1280], wi[:, :, 1792:2304],
            wi[:, :, 2304:2320], z16, wi[:, :, 2320:2336], z16,
            wi[:, :, 512:640], wi[:, :, 640:768], wi[:, :, 1280:1792]]
    we = np.concatenate(cols, axis=2)
    assert we.shape[2] == 2624
    sh["w_even"] = np.ascontiguousarray(we.reshape(2, KC, 128, 2624).transpose(0, 2, 1, 3)).reshape(2, 128, KC * 2624)
    sh["ev_w_out"] = np.ascontiguousarray(inp["ev_w_out"], np.float32)
    wg = np.zeros((2, 64, 512), np.float32)
    wg[:, 0:16, 0:256] = inp["ev_w_gate_f"]
    wg[:, 32:48, 256:512] = inp["ev_w_gate_b"]
    sh["wg"] = wg
    sh["rope"] = make_rope()
    sh["od_w_in"] = np.ascontiguousarray(inp["od_w_in"], np.float32)
    sh["od_w_out"] = np.ascontiguousarray(inp["od_w_out"], np.float32)
    bd = np.zeros((2, 2, 2, 128, KC, 128), np.float32)
    for ai, nm in enumerate(("od_w_a", "od_w_i")):
        w = np.asarray(inp[nm], np.float32)
        for k in range(KC):
            for hb in range(2):
                bd[:, :, ai, hb * 64:(hb + 1) * 64, k, hb * 64:(hb + 1) * 64] = w[:, :, 2 * k + hb]
    sh["od_bd"] = bd.reshape(2, 2, 2, 128, KC * 128)
    return sh


def core_inputs(inp, sh, core):
    s = core % DEC_BATCH
    xin = np.concatenate([inp["x_sample"][s], inp["x_prompt"][core * NPC:(core + 1) * NPC].reshape(TP, D)], axis=0)
    m = dict(sh)
    m["xin"] = np.ascontiguousarray(xin, np.float32)
    m["vec"] = pack_vec(inp, core)
    m["cache_k"] = np.ascontiguousarray(inp["cache_attn_k"][s].reshape(2, 256, 128), np.float32)
    m["cache_v"] = np.ascontiguousarray(inp["cache_attn_v"][s].reshape(2, 256, 128), np.float32)
    m["state_gla"] = np.ascontiguousarray(inp["state_gla"][s], np.float32)
    return m


def get_builder(**kw):
    key = tuple(sorted(kw.items()))
    if key not in _CACHE:
        _CACHE[key] = Builder(**kw)
    return _CACHE[key]


CFG = {}


def set_cores(n):
    global N_CORES, NPC, TP, TT
    N_CORES = n
    NPC = BATCH // n
    TP = NPC * SEQ
    TT = TS + TP

CORE_OFF = [0]


def kernel(**inputs):
    inp = {k: np.asarray(v) for k, v in inputs.items()}
    cfg = dict(CFG)
    ncr = cfg.pop("n_cores", N_CORES)
    bld = Builder(**cfg)
    sh = pack_shared(inp)
    coff = cfg.pop("core_off", 0) if False else CORE_OFF[0]
    in_maps = [core_inputs(inp, sh, c + coff) for c in range(ncr)]
    res = run_bass_kernel_spmd(bld.nc, in_maps, core_ids=list(range(ncr)))
    R = list(res.results)
    while len(R) < N_CORES:
        R.append(R[0])
    y_prompt = np.stack([R[c]["y"][TS:].reshape(NPC, SEQ, D) for c in range(N_CORES)]).reshape(BATCH, SEQ, D)
    y_sample = np.stack([R[s]["y"][:TS] for s in range(DEC_BATCH)])
    nrg = np.stack([R[c]["nrg"].reshape(NPC, 2, 2, D) for c in range(N_CORES)]).reshape(BATCH, 2, 2, D)
    nk = np.stack([R[c]["nk"] for c in range(N_CORES)]).reshape(BATCH, 2, SEQ, 2, 64)
    nv = np.stack([R[c]["nv"] for c in range(N_CORES)]).reshape(BATCH, 2, SEQ, 2, 64)
    ngla = np.stack([R[c]["ngla"] for c in range(N_CORES)]).reshape(BATCH, 2, 2, 4, 64, 128)
    f = lambda a: np.ascontiguousarray(a, dtype=np.float32)
    return (f(y_prompt), f(y_sample), f(nk), f(nv), f(ngla), f(nrg))
```
